# Optimizing a Trainium2 kernel written in Bass

```python
import math
import jax
import jax.numpy as jnp
from jax import lax
import numpy as np


D_MODEL = 1024
BATCH = 8
SEQ = 4096
DEPTH = 2

CTX_LEN = 256
GRID_W = 64
NA_HEADS = 8
NA_HEAD_DIM = 64
NA_WIDTH = NA_HEADS * NA_HEAD_DIM
NA_WIN_ROWS = 8
NA_WIN_COLS = 16
SSM_HEADS = 16
SSM_HEAD_DIM = 64
SSM_INNER = SSM_HEADS * SSM_HEAD_DIM
SSM_GROUPS = 4
SSM_STATE = 128
SSM_CONV = 3
SSD_CHUNK = 128
SSM_CONV_DIM = SSM_INNER + 2 * SSM_GROUPS * SSM_STATE
DT_MIN = 0.001
DT_MAX = 0.1
POOL_WINDOWS = (2, 4, 8, 16)
N_POOL = len(POOL_WINDOWS)
POOL_GROUP = D_MODEL // N_POOL
FFN_HIDDEN = 2816
FFN_CONV = 3
RMS_EPS = 1e-6
N_EVEN = (DEPTH + 1) // 2
N_ODD = DEPTH // 2
IN_SPLITS = (NA_WIDTH, SSM_INNER, NA_WIDTH, NA_WIDTH, SSM_CONV_DIM, 2 * SSM_HEADS)
IN_WIDTH = sum(IN_SPLITS)
CTX_KV_OFFSET = NA_WIDTH + SSM_INNER
MIX_WIDTH = NA_WIDTH + SSM_INNER

kernel_name = "hybrid_natten_ssd_pool_dit_block"


def rmsnorm(x, g):
    xf = x.astype(jnp.float32)
    xf = xf * lax.rsqrt(jnp.mean(xf * xf, axis=-1, keepdims=True) + RMS_EPS)
    return xf.astype(x.dtype) * g


def modulate(h, shift, scale):
    return h * (1 + scale) + shift


def split_cols(p, sizes):
    return jnp.split(p, [int(v) for v in np.cumsum(sizes)[:-1]], axis=-1)


def dwconv_centred(x, w, b):
    k = w.shape[0]
    pad = k // 2
    l = x.shape[1]
    xp = jnp.pad(x, ((0, 0), (pad, pad), (0, 0)))
    return sum(xp[:, j:j + l] * w[j] for j in range(k)) + b


def neighbourhood_attention(q, k, v, kc, vc, rpb):
    bsz, s, h, dh = q.shape
    rows = s // GRID_W
    kh = min(NA_WIN_ROWS, rows)
    kw = NA_WIN_COLS
    scale = dh ** -0.5
    qg = q.reshape(bsz, rows, GRID_W, h, dh)
    kg = k.reshape(bsz, rows, GRID_W, h, dh)
    vg = v.reshape(bsz, rows, GRID_W, h, dh)
    cols = jnp.arange(GRID_W)
    col_idx = jnp.clip(cols - kw // 2, 0, GRID_W - kw)[:, None] + jnp.arange(kw)
    rel_col = col_idx - cols[:, None] + (NA_WIN_COLS - 1)

    def row_block(r):
        rs = jnp.clip(r - kh // 2, 0, rows - kh)
        q_r = lax.dynamic_index_in_dim(qg, r, axis=1, keepdims=False)
        k_win = lax.dynamic_slice_in_dim(kg, rs, kh, axis=1)[:, :, col_idx]
        v_win = lax.dynamic_slice_in_dim(vg, rs, kh, axis=1)[:, :, col_idx]
        rel_row = rs + jnp.arange(kh) - r + (NA_WIN_ROWS - 1)
        bias = rpb[:, rel_row[None, :, None], rel_col[:, None, :]]
        s_lat = jnp.einsum('bqhd,biqjhd->bhqij', q_r, k_win).astype(jnp.float32) * scale + bias.astype(jnp.float32)
        s_ctx = jnp.einsum('bqhd,blhd->bhql', q_r, kc).astype(jnp.float32) * scale
        p = jax.nn.softmax(jnp.concatenate([s_lat.reshape(bsz, h, GRID_W, kh * kw), s_ctx], axis=-1), axis=-1).astype(v.dtype)
        p_lat = p[..., :kh * kw].reshape(bsz, h, GRID_W, kh, kw)
        p_ctx = p[..., kh * kw:]
        return jnp.einsum('bhqij,biqjhd->bqhd', p_lat, v_win) + jnp.einsum('bhql,blhd->bqhd', p_ctx, vc)

    out = lax.map(row_block, jnp.arange(rows))
    return out.transpose(1, 0, 2, 3, 4).reshape(bsz, s, h * dh)


def context_attention(qc, kc, vc):
    bsz, l, h, dh = qc.shape
    s = jnp.einsum('blhd,bmhd->bhlm', qc, kc).astype(jnp.float32) * dh ** -0.5
    p = jax.nn.softmax(s, axis=-1).astype(vc.dtype)
    return jnp.einsum('bhlm,bmhd->blhd', p, vc).reshape(bsz, l, h * dh)


def ssd_scan(x, a, bm, cm, h0):
    bsz, l, h, p = x.shape
    g, n = bm.shape[-2:]
    r = h // g
    q = SSD_CHUNK
    nc = l // q
    dtype = x.dtype
    xc = x.reshape(bsz, nc, q, g, r, p)
    bc = bm.reshape(bsz, nc, q, g, n)
    cc = cm.reshape(bsz, nc, q, g, n)
    a_cs = jnp.cumsum(a.astype(jnp.float32).reshape(bsz, nc, q, g, r).transpose(0, 3, 4, 1, 2), axis=-1)
    lower = jnp.tril(jnp.ones((q, q), dtype=bool))
    seg = jnp.exp(jnp.where(lower, a_cs[..., :, None] - a_cs[..., None, :], -jnp.inf)).astype(dtype)
    cb = jnp.einsum('bclgn,bcsgn->bcgls', cc, bc)
    y_diag = jnp.einsum('bcgls,bgrcls,bcsgrp->bclgrp', cb, seg, xc)
    to_end = jnp.exp(a_cs[..., -1:] - a_cs).astype(dtype)
    chunk_states = jnp.einsum('bcsgn,bgrcs,bcsgrp->cbgrpn', bc, to_end, xc)
    chunk_decay = jnp.exp(a_cs[..., -1]).astype(dtype).transpose(3, 0, 1, 2)

    def step(state, inp):
        s_c, dec = inp
        return state * dec[..., None, None] + s_c, state

    h_final, h_in = lax.scan(step, h0.astype(dtype), (chunk_states, chunk_decay))
    from_start = jnp.exp(a_cs).astype(dtype)
    y_off = jnp.einsum('bclgn,cbgrpn,bgrcl->bclgrp', cc, h_in, from_start)
    return (y_diag + y_off).reshape(bsz, l, h, p), h_final


def bidirectional_ssd(xbc, dtr, xbc_c, dtr_c, a_log, dt_bias, d_skip):
    def parts(t):
        xs, bm, cm = split_cols(t, (SSM_INNER, SSM_GROUPS * SSM_STATE, SSM_GROUPS * SSM_STATE))
        b_, l_ = t.shape[:2]
        return (xs.reshape(b_, l_, SSM_HEADS, SSM_HEAD_DIM),
                bm.reshape(b_, l_, SSM_GROUPS, SSM_STATE),
                cm.reshape(b_, l_, SSM_GROUPS, SSM_STATE))

    lat = parts(xbc)
    ctxp = parts(xbc_c)
    h_zero = jnp.zeros((xbc_c.shape[0], SSM_GROUPS, SSM_HEADS // SSM_GROUPS, SSM_HEAD_DIM, SSM_STATE), xbc.dtype)

    def run(pp, dt_raw, h0, d):
        xs, bm, cm = pp
        dt = jax.nn.softplus(dt_raw[:, :, d].astype(jnp.float32) + dt_bias[d].astype(jnp.float32))
        a = -dt * jnp.exp(a_log[d].astype(jnp.float32))
        xdt = xs * dt[..., None].astype(xs.dtype)
        order = (lambda t: jnp.flip(t, axis=1)) if d == 1 else (lambda t: t)
        y, h_last = ssd_scan(order(xdt), order(a), order(bm), order(cm), h0)
        return order(y) + d_skip[d][:, None] * xs, h_last

    y_lat = 0
    y_ctx = 0
    for d in range(2):
        yc, h_ctx = run(ctxp, dtr_c, h_zero, d)
        yl, _ = run(lat, dtr, h_ctx, d)
        y_lat = y_lat + yl
        y_ctx = y_ctx + yc
    return y_lat, y_ctx


def gated_rmsnorm(y, z, g):
    yz = (y * jax.nn.silu(z)).astype(jnp.float32)
    shp = yz.shape
    yz = yz.reshape(shp[:-1] + (SSM_GROUPS, shp[-1] // SSM_GROUPS))
    yz = yz * lax.rsqrt(jnp.mean(yz * yz, axis=-1, keepdims=True) + RMS_EPS)
    return yz.reshape(shp).astype(y.dtype) * g


def attn_ssd_mixer(h, hc, w_in, w_out, rpb, conv_w, conv_b, a_log, dt_bias, d_skip, norm_g, ctx_out):
    bsz, s, _ = h.shape
    l = hc.shape[1]
    heads = lambda t: t.reshape(t.shape[0], t.shape[1], NA_HEADS, NA_HEAD_DIM)
    q, z, k, v, xbc, dtr = split_cols(h @ w_in, IN_SPLITS)
    if ctx_out:
        qc, zc, kc, vc, xbc_c, dtr_c = split_cols(hc @ w_in, IN_SPLITS)
    else:
        kc, vc, xbc_c, dtr_c = split_cols(hc @ w_in[:, CTX_KV_OFFSET:], IN_SPLITS[2:])
    att = neighbourhood_attention(heads(q), heads(k), heads(v), heads(kc), heads(vc), rpb)
    xbc = jax.nn.silu(dwconv_centred(xbc, conv_w, conv_b))
    xbc_c = jax.nn.silu(dwconv_centred(xbc_c, conv_w, conv_b))
    y_ssm, y_ssm_c = bidirectional_ssd(xbc, dtr.reshape(bsz, s, 2, SSM_HEADS), xbc_c,
                                       dtr_c.reshape(bsz, l, 2, SSM_HEADS), a_log, dt_bias, d_skip)
    ssm = gated_rmsnorm(y_ssm.reshape(bsz, s, SSM_INNER), z, norm_g)
    y = jnp.concatenate([att, ssm], axis=-1) @ w_out
    if not ctx_out:
        return y, None
    att_c = context_attention(heads(qc), heads(kc), heads(vc))
    ssm_c = gated_rmsnorm(y_ssm_c.reshape(bsz, l, SSM_INNER), zc, norm_g)
    return y, jnp.concatenate([att_c, ssm_c], axis=-1) @ w_out


def multiscale_pool_mixer(h, pool_w, pool_b, pool_scale):
    bsz, l, d = h.shape
    hf = h.astype(jnp.float32)
    cs = jnp.concatenate([jnp.zeros((bsz, 1, d), jnp.float32), jnp.cumsum(hf, axis=1)], axis=1)
    t = jnp.arange(l)
    outs = []
    for gi, w in enumerate(POOL_WINDOWS):
        lo = jnp.clip(t - w // 2, 0, l)
        hi = jnp.clip(t - w // 2 + w, 0, l)
        sl = slice(gi * POOL_GROUP, (gi + 1) * POOL_GROUP)
        csg = cs[..., sl]
        mean = (csg[:, hi] - csg[:, lo]) / (hi - lo).astype(jnp.float32)[:, None]
        outs.append(mean - hf[..., sl])
    pooled = jnp.stack(outs, axis=2).astype(h.dtype)
    y = jnp.einsum('blgc,gcd->blgd', pooled, pool_w) + pool_b
    return y.reshape(bsz, l, d) * pool_scale


def conv_ffn(h, w_up, conv_w, conv_b, w_down):
    u, v = jnp.split(h @ w_up, 2, axis=-1)
    return (jax.nn.gelu(dwconv_centred(u, conv_w, conv_b), approximate=False) * v) @ w_down


def setup_inputs(seed: int = 0) -> dict:
    key = jax.random.key(seed)
    ks = jax.random.split(key, 24)
    f32 = jnp.float32

    def nrm(k, shape, s):
        return jax.random.normal(k, shape, f32) * s

    dt0 = jnp.exp(jax.random.uniform(ks[13], (N_EVEN, 2, SSM_HEADS), f32, math.log(DT_MIN), math.log(DT_MAX)))
    return {
        'x': nrm(ks[0], (BATCH, SEQ, D_MODEL), 1.0),
        'c': nrm(ks[1], (BATCH, D_MODEL), 1.0),
        'ctx': nrm(ks[2], (BATCH, CTX_LEN, D_MODEL), 1.0),
        'c_ctx': nrm(ks[3], (D_MODEL,), 1.0),
        'ada_w': nrm(ks[4], (DEPTH, D_MODEL, 6 * D_MODEL), D_MODEL ** -0.5),
        'ada_b': nrm(ks[5], (DEPTH, 6 * D_MODEL), 0.01),
        'norm_g': 1.0 + nrm(ks[6], (DEPTH, 4, D_MODEL), 0.05),
        'w_in': nrm(ks[7], (N_EVEN, D_MODEL, IN_WIDTH), D_MODEL ** -0.5),
        'w_out': nrm(ks[8], (N_EVEN, MIX_WIDTH, D_MODEL), MIX_WIDTH ** -0.5),
        'na_rpb': nrm(ks[9], (N_EVEN, NA_HEADS, 2 * NA_WIN_ROWS - 1, 2 * NA_WIN_COLS - 1), 0.1),
        'ssm_conv_w': nrm(ks[10], (N_EVEN, SSM_CONV, SSM_CONV_DIM), SSM_CONV ** -0.5),
        'ssm_conv_b': nrm(ks[11], (N_EVEN, SSM_CONV_DIM), 0.01),
        'ssm_a_log': jnp.log(jax.random.uniform(ks[12], (N_EVEN, 2, SSM_HEADS), f32, 1.0, 16.0)),
        'ssm_dt_bias': dt0 + jnp.log(-jnp.expm1(-dt0)),
        'ssm_d': 1.0 + nrm(ks[14], (N_EVEN, 2, SSM_HEADS), 0.1),
        'ssm_norm_g': 1.0 + nrm(ks[15], (N_EVEN, SSM_INNER), 0.05),
        'pool_w': nrm(ks[16], (N_ODD, N_POOL, POOL_GROUP, POOL_GROUP), POOL_GROUP ** -0.5),
        'pool_b': nrm(ks[17], (N_ODD, N_POOL, POOL_GROUP), 0.01),
        'pool_scale': 1.0 + nrm(ks[18], (N_ODD, D_MODEL), 0.1),
        'ffn_w_up': nrm(ks[19], (DEPTH, D_MODEL, 2 * FFN_HIDDEN), D_MODEL ** -0.5),
        'ffn_conv_w': nrm(ks[20], (DEPTH, FFN_CONV, FFN_HIDDEN), FFN_CONV ** -0.5),
        'ffn_conv_b': nrm(ks[21], (DEPTH, FFN_HIDDEN), 0.01),
        'ffn_w_down': nrm(ks[22], (DEPTH, FFN_HIDDEN, D_MODEL), FFN_HIDDEN ** -0.5),
    }


def reference(x, c, ctx, c_ctx, ada_w, ada_b, norm_g, w_in, w_out, na_rpb, ssm_conv_w, ssm_conv_b,
              ssm_a_log, ssm_dt_bias, ssm_d, ssm_norm_g, pool_w, pool_b, pool_scale,
              ffn_w_up, ffn_conv_w, ffn_conv_b, ffn_w_down):
    xc = ctx
    for i in range(DEPTH):
        ctx_live = any(j % 2 == 0 for j in range(i + 1, DEPTH))
        m = (jax.nn.silu(c) @ ada_w[i] + ada_b[i])[:, None, :]
        sh1, sc1, g1, sh2, sc2, g2 = jnp.split(m, 6, axis=-1)
        mc = jax.nn.silu(c_ctx) @ ada_w[i] + ada_b[i]
        csh1, csc1, cg1, csh2, csc2, cg2 = jnp.split(mc, 6, axis=-1)
        g_pre_mix, g_post_mix, g_pre_ffn, g_post_ffn = norm_g[i]

        h = modulate(rmsnorm(x, g_pre_mix), sh1, sc1)
        if i % 2 == 0:
            e = i // 2
            hc = modulate(rmsnorm(xc, g_pre_mix), csh1, csc1)
            y, yc = attn_ssd_mixer(h, hc, w_in[e], w_out[e], na_rpb[e], ssm_conv_w[e], ssm_conv_b[e],
                                   ssm_a_log[e], ssm_dt_bias[e], ssm_d[e], ssm_norm_g[e], ctx_live)
        else:
            o = i // 2
            y = multiscale_pool_mixer(h, pool_w[o], pool_b[o], pool_scale[o])
            yc = None
            if ctx_live:
                hc = modulate(rmsnorm(xc, g_pre_mix), csh1, csc1)
                yc = multiscale_pool_mixer(hc, pool_w[o], pool_b[o], pool_scale[o])
        x = x + g1 * rmsnorm(y, g_post_mix)
        hf = modulate(rmsnorm(x, g_pre_ffn), sh2, sc2)
        x = x + g2 * rmsnorm(conv_ffn(hf, ffn_w_up[i], ffn_conv_w[i], ffn_conv_b[i], ffn_w_down[i]), g_post_ffn)
        if ctx_live:
            xc = xc + cg1 * rmsnorm(yc, g_post_mix)
            hfc = modulate(rmsnorm(xc, g_pre_ffn), csh2, csc2)
            xc = xc + cg2 * rmsnorm(conv_ffn(hfc, ffn_w_up[i], ffn_conv_w[i], ffn_conv_b[i], ffn_w_down[i]), g_post_ffn)
    return x
```

```python
import numpy as np
from contextlib import ExitStack
import concourse.bass as bass
import concourse.mybir as mybir
from concourse.bass_utils import run_bass_kernel_spmd

F32 = mybir.dt.float32
BF16 = mybir.dt.bfloat16
ALU = mybir.AluOpType
AF = mybir.ActivationFunctionType

D = 1024
S = 4096
LC = 256
NCORES = 8
FH = 2816
EPS = 1e-6
NEG = -30000.0


class Res:
    __slots__ = ("w", "r", "name")

    def __init__(self, name=""):
        self.w = {}
        self.r = {}
        self.name = name


class KB:
    def __init__(self):
        nc = bass.Bass("TRN2", target_bir_lowering=False)
        self.nc = nc
        self.h = {"pe": nc.tensor, "act": nc.scalar, "dve": nc.vector, "pool": nc.gpsimd, "sp": nc.sync}
        self.sems = {}
        self.cnt = {}
        for e in ("pe", "act", "dve", "pool"):
            self.sems[e] = nc.alloc_semaphore(name="c_" + e)
            self.cnt[e] = 0
        self.dq = {"sp": [], "pool": [], "act": []}
        for q, n in (("sp", 28), ("pool", 20), ("act", 8)):
            for i in range(n):
                k = "d_%s%d" % (q, i)
                self.sems[k] = nc.alloc_semaphore(name=k)
                self.cnt[k] = 0
                self.dq[q].append(k)
        self.dqi = {"sp": 0, "pool": 0, "act": 0}
        self.sems["bar"] = nc.alloc_semaphore(name="bar")
        self.cnt["bar"] = 0
        self.waited = {e: {} for e in self.h}
        self.nres = 0

    def res(self, name=""):
        return Res(name)

    def _wait(self, E, need):
        for key, v in need.items():
            if key == E and E == "pe":
                continue
            if self.waited[E].get(key, 0) < v:
                self.h[E].wait_ge(self.sems[key], v)
                self.waited[E][key] = v

    @staticmethod
    def _merge(need, d):
        for k, v in d.items():
            if need.get(k, 0) < v:
                need[k] = v

    def op(self, E, fn, reads=(), writes=(), inc=True):
        need = {}
        for r in reads:
            self._merge(need, r.w)
        for w in writes:
            self._merge(need, w.w)
            self._merge(need, w.r)
        self._wait(E, need)
        ins = fn(self.h[E])
        if inc:
            self.cnt[E] += 1
            ins.then_inc(self.sems[E], 1)
            tv = self.cnt[E]
        else:
            tv = self.cnt[E] + 1
        for r in reads:
            if r.r.get(E, 0) < tv:
                r.r[E] = tv
        for w in writes:
            w.w = {E: tv}
            w.r = {}
        return ins

    def dma(self, Q, out, in_, reads=(), writes=(), **kw):
        need = {}
        for r in reads:
            self._merge(need, r.w)
        for w in writes:
            self._merge(need, w.w)
            self._merge(need, w.r)
        i = self.dqi[Q]
        self.dqi[Q] = i + 1
        key = self.dq[Q][i % len(self.dq[Q])]
        if self.cnt[key] > 0:
            need[key] = max(need.get(key, 0), self.cnt[key])
        self._wait(Q, need)
        ins = self.h[Q].dma_start(out=out, in_=in_, **kw)
        self.cnt[key] += 16
        ins.then_inc(self.sems[key], 16)
        tv = self.cnt[key]
        for r in reads:
            r.r[key] = tv
        for w in writes:
            w.w = {key: tv}
            w.r = {}
        return ins

    def barrier(self):
        need = {k: v for k, v in self.cnt.items() if k != "bar" and v > 0}
        self._wait("sp", need)
        self.cnt["bar"] += 1
        self.h["sp"].sem_inc(self.sems["bar"], 1)
        for e in ("pe", "act", "dve", "pool"):
            self.h[e].wait_ge(self.sems["bar"], self.cnt["bar"])
            self.waited[e]["bar"] = self.cnt["bar"]
            for k, v in need.items():
                self.waited[e][k] = max(self.waited[e].get(k, 0), v)


def _r(ap, pat, **kw):
    return ap.rearrange(pat, **kw)


class Prog:
    def __init__(self, dumps=(), stop_after=None, mode="full", max_tiles=None):
        self.mode = mode
        self.max_tiles = max_tiles
        self.kb = KB()
        self.nc = self.kb.nc
        self.dumps = set(dumps)
        self.stop_after = stop_after
        self.dram = {}

    def din(self, name, shape, dt=F32):
        t = self.nc.dram_tensor(name, list(shape), dt, kind="ExternalInput").ap()
        self.dram[name] = t
        return t

    SHAPES = {
        "x": [S, D], "ctx": [LC, D], "cc": [128, 8, 2], "ada_w": [2, D, 6 * D], "ada_bT": [128, 2, 48],
        "norm_gT": [128, 2, 4, 8], "w_in": [D, 4640], "w_out": [1536, D], "rpbtab": [8, 5, 128, 5, 128],
        "conv_wT": [128, 16, 3], "conv_bT": [128, 16], "ssm_rows": [3, 32], "ssm_norm_g": [D],
        "pool_w": [4, 256, 256], "pool_b": [D], "pool_scale": [D], "pool_inv": [2, 4, 256],
        "ffn_w_up": [2, D, 2 * FH], "fconv_wT": [128, 2, 22, 3], "fconv_bT": [128, 2, 22], "ffn_w_down": [2, FH, D],
        "consts": [128, 6, 128],
    }

    def I(self, name):
        if name not in self.dram:
            self.din(name, self.SHAPES[name])
        return self.dram[name]

    def dscr(self, name, shape, dt=F32, out=False):
        kind = "ExternalOutput" if (out or name in self.dumps) else "Internal"
        t = self.nc.dram_tensor(name, list(shape), dt, kind=kind).ap()
        self.dram[name] = t
        return t

    def sb(self, es, name, shape, dt):
        self.kb.nres += 1
        return es.enter_context(self.nc.sbuf_tensor("s%d_%s" % (self.kb.nres, name), list(shape), dt))

    def build(self):
        nc, kb = self.nc, self.kb
        x_in = self.I("x")
        ctx_in = self.I("ctx") if self.mode == "full" else None
        cc_in = self.I("cc")
        ada_w = self.I("ada_w")
        ada_bT = self.I("ada_bT")
        norm_gT = self.I("norm_gT")
        consts_in = self.I("consts")
        out = self.dscr("out", [S, D], out=True)
        xa = self.dscr("xa", [S, D])
        xb = self.dscr("xb", [S, D])
        xc = self.dscr("xc", [S, D])
        self.vec = self.dscr("vec", [2, 2, D])

        with ExitStack() as g:
            self.cst = self.sb(g, "cst", [128, 6, 128], F32)
            self.cstb = self.sb(g, "cstb", [128, 6, 128], BF16)
            self.modv = self.sb(g, "modv", [128, 2, 4, 8, 2], F32)
            self.epsb = self.sb(g, "epsb", [128, 1], F32)
            self.r_cst = kb.res("cst")
            self.r_modv = kb.res("modv")
            self.psA = [g.enter_context(nc.psum_tensor("psA%d" % i, [128, 1024], F32)) for i in range(2)]
            self.psB = [g.enter_context(nc.psum_tensor("psB%d" % i, [128, 512], F32)) for i in range(4)]
            self.r_psA = [kb.res("psA%d" % i) for i in range(2)]
            self.r_psB = [kb.res("psB%d" % i) for i in range(4)]

            kb.dma("sp", self.cst[:], consts_in, writes=[self.r_cst])
            kb.op("dve", lambda e: e.tensor_copy(out=self.cstb[:], in_=self.cst[:]), reads=[self.r_cst], writes=[self.r_cst])
            kb.op("pool", lambda e: e.memset(self.epsb[:], EPS), writes=[self.r_cst])
            self.identb = self.cstb[:, 0, :]

            self.phase_adaln(cc_in, ada_w, ada_bT, norm_gT)
            kb.barrier()
            if self.stop_after == "adaln":
                return self.finish()
            if self.mode == "l1":
                self.phase_pool(x_in, xc)
                kb.barrier()
                self.phase_ffn(1, xc, out)
                return self.finish()
            if self.mode == "ffn0":
                self.phase_ffn(0, x_in, out)
                return self.finish()
            self.phase_l0_mixer(x_in, ctx_in, xa)
            kb.barrier()
            if self.stop_after in ("proj", "ssd", "att", "l0mix"):
                return self.finish()
            self.phase_ffn(0, xa, xb)
            kb.barrier()
            if self.stop_after == "l0ffn":
                return self.finish()
            self.phase_pool(xb, xc)
            kb.barrier()
            if self.stop_after == "l1mix":
                return self.finish()
            self.phase_ffn(1, xc, out)
            kb.barrier()
        return self.finish()

    def finish(self):
        self.kb.barrier()
        return self.nc

    def phase_adaln(self, cc_in, ada_w, ada_bT, norm_gT):
        nc, kb = self.nc, self.kb
        with ExitStack() as es:
            cc = self.sb(es, "cc", [128, 8, 2], F32)
            sg = self.sb(es, "sg", [128, 8, 2], F32)
            scc = self.sb(es, "scc", [128, 8, 2], F32)
            abT = self.sb(es, "abT", [128, 2, 48], F32)
            ngT = self.sb(es, "ngT", [128, 2, 4, 8], F32)
            mod = self.sb(es, "mod", [128, 2, 48, 2], F32)
            gv = self.sb(es, "gv", [128, 2, 2, 8], F32)
            wbuf = [self.sb(es, "adw%d" % i, [128, 8, 768], F32) for i in range(2)]
            r_w = [[kb.res(), kb.res()] for _ in range(2)]
            r_cc, r_small, r_mod, r_gv = kb.res(), kb.res(), kb.res(), kb.res()
            kb.dma("sp", cc[:], cc_in, writes=[r_cc])
            kb.dma("sp", abT[:], ada_bT, writes=[r_small])
            kb.dma("sp", ngT[:], norm_gT, writes=[r_small])
            kb.op("act", lambda e: e.activation(out=sg[:], in_=cc[:], func=AF.Sigmoid), reads=[r_cc], writes=[r_gv])
            kb.op("dve", lambda e: e.tensor_tensor(out=scc[:], in0=cc[:], in1=sg[:], op=ALU.mult), reads=[r_cc, r_gv], writes=[r_cc])
            it = 0
            for l in range(2):
                for jb in range(8):
                    wb, rw = wbuf[it % 2], r_w[it % 2]
                    for half in range(2):
                        kb.dma("sp", wb[:, half * 4:(half + 1) * 4, :],
                               _r(ada_w[l, half * 512:(half + 1) * 512, jb * 768:(jb + 1) * 768], "(k p) n -> p k n", p=128),
                               writes=[rw[half]])
                    ps = self.psB[it % 2]
                    rps = self.r_psB[it % 2]
                    it += 1
                    self._ada_mm(wb, rw, scc, r_cc, ps, rps, mod, r_mod, abT, r_small, l, jb)
            for l in range(2):
                for i, (sci, shi, gi) in enumerate(((8, 0, 0), (32, 24, 2))):
                    kb.op("dve", lambda e, l=l, i=i, sci=sci, gi=gi: e.scalar_tensor_tensor(
                        out=self.modv[:, l, 2 * i, :, :], in0=mod[:, l, sci:sci + 8, :], scalar=1.0,
                        in1=ngT[:, l, gi, :].unsqueeze(2).to_broadcast([128, 8, 2]), op0=ALU.add, op1=ALU.mult),
                        reads=[r_mod, r_small], writes=[self.r_modv])
                    kb.op("dve", lambda e, l=l, i=i, shi=shi: e.tensor_copy(
                        out=self.modv[:, l, 2 * i + 1, :, :], in_=mod[:, l, shi:shi + 8, :]),
                        reads=[r_mod], writes=[self.r_modv])
                for i, (gti, gpi) in enumerate(((16, 1), (40, 3))):
                    kb.op("dve", lambda e, l=l, i=i, gti=gti, gpi=gpi: e.tensor_tensor(
                        out=gv[:, l, i, :], in0=mod[:, l, gti:gti + 8, 0], in1=ngT[:, l, gpi, :], op=ALU.mult),
                        reads=[r_mod, r_small], writes=[r_gv])
            for l in range(2):
                for i in range(2):
                    kb.dma("pool", _r(self.vec[l, i], "(j p) -> p j", p=128), gv[:, l, i, :], reads=[r_gv],
                           allow_slow_non_contiguous=True)

    def _ada_mm(self, wb, rw, scc, r_cc, ps, rps, mod, r_mod, abT, r_small, l, jb):
        kb = self.kb
        for jj in range(6):
            for kc in range(8):
                kb.op("pe", lambda e, jj=jj, kc=kc: e.matmul(ps[:, jj * 2:jj * 2 + 2], lhsT=wb[:, kc, jj * 128:(jj + 1) * 128],
                                                         rhs=scc[:, kc, :], start=(kc == 0), stop=(kc == 7)),
                      reads=[rw[kc // 4], r_cc], writes=[rps], inc=(jj == 5 and kc == 7))
        kb.op("dve", lambda e: e.tensor_tensor(
            out=mod[:, l, jb * 6:(jb + 1) * 6, :], in0=_r(ps[:, 0:12], "p (j n) -> p j n", n=2),
            in1=abT[:, l, jb * 6:(jb + 1) * 6].unsqueeze(2).to_broadcast([128, 6, 2]), op=ALU.add),
            reads=[rps, r_small], writes=[r_mod])

    def alloc_norm_bufs(self, es, nsub, nh):
        nhp = max(nh, 2)
        self.r_nt = [self.kb.res() for _ in range(6)]
        return dict(junk=self.sb(es, "n_junk", [128, D], BF16), xn=self.sb(es, "n_xn", [128, nsub, D], BF16),
                    ss=self.sb(es, "n_ss", [128, 4], F32), rstd=self.sb(es, "n_rstd", [128, 4], F32),
                    xnh=self.sb(es, "n_xnh", [nhp, D], BF16), ssh=self.sb(es, "n_ssh", [nhp, 1], F32),
                    rstdh=self.sb(es, "n_rstdh", [nhp, 1], F32))

    def norm_T(self, nb, xt, r_xts, nsub, xh, r_xh, nh, halo, hT, r_hT, main0, hl0, hr0, l, mi, col, psT, r_psT, psH, r_psH):
        kb = self.kb
        junk, xn, ss, rstd, xnh, ssh, rstdh = (nb[k] for k in ("junk", "xn", "ss", "rstd", "xnh", "ssh", "rstdh"))
        rt = self.r_nt
        ntok = nsub * 128
        for s in range(nsub):
            kb.op("act", lambda e, s=s: e.activation(out=junk[:], in_=xt[:, s, :], func=AF.Square, accum_out=ss[:, s:s + 1]),
                  reads=[r_xts[s]], writes=[rt[0]])
        kb.op("act", lambda e: e.activation(out=rstd[:, 0:nsub], in_=ss[:, 0:nsub], func=AF.Sqrt, bias=self.epsb[:], scale=1.0 / D),
              reads=[rt[0], self.r_cst], writes=[rt[1]])
        kb.op("dve", lambda e: e.reciprocal(out=rstd[:, 0:nsub], in_=rstd[:, 0:nsub]), reads=[rt[1]], writes=[rt[1]])
        for s in range(nsub):
            kb.op("dve", lambda e, s=s: e.tensor_scalar(out=xn[:, s, :], in0=xt[:, s, :], scalar1=rstd[:, s:s + 1], scalar2=None, op0=ALU.mult),
                  reads=[r_xts[s], rt[1]], writes=[rt[2]])
        A = self.modv[:, l, mi, :, col]
        B = self.modv[:, l, mi + 1, :, col]
        for kc in range(8):
            p = psT[kc % 2]
            rp = r_psT[kc % 2]
            for s in range(nsub):
                kb.op("pe", lambda e, s=s, kc=kc, p=p: e.matmul(p[:, s * 128:(s + 1) * 128], lhsT=xn[:, s, kc * 128:(kc + 1) * 128],
                                                             rhs=self.identb, start=True, stop=True),
                      reads=[rt[2], self.r_cst], writes=[rp], inc=(s == nsub - 1))
            kb.op("act", lambda e, kc=kc, p=p: e.activation(out=hT[:, kc, main0:main0 + ntok], in_=p[:, 0:ntok], func=AF.Identity,
                                                          bias=B[:, kc:kc + 1], scale=A[:, kc:kc + 1]),
                  reads=[rp, self.r_modv], writes=[r_hT])
        if nh == 0:
            return
        lv, rv = halo
        hh = nh // 2
        if lv or rv:
            kb.op("act", lambda e: e.activation(out=junk[0:nh, :], in_=xh[0:nh, :], func=AF.Square, accum_out=ssh[0:nh, 0:1]),
                  reads=[r_xh], writes=[rt[3]])
            kb.op("act", lambda e: e.activation(out=rstdh[0:nh, :], in_=ssh[0:nh, :], func=AF.Sqrt, bias=self.epsb[0:nh, :], scale=1.0 / D),
                  reads=[rt[3], self.r_cst], writes=[rt[4]])
            kb.op("dve", lambda e: e.reciprocal(out=rstdh[0:nh, :], in_=rstdh[0:nh, :]), reads=[rt[4]], writes=[rt[4]])
            kb.op("dve", lambda e: e.tensor_scalar(out=xnh[0:nh, :], in0=xh[0:nh, :], scalar1=rstdh[0:nh, 0:1], scalar2=None, op0=ALU.mult),
                  reads=[r_xh, rt[4]], writes=[rt[5]])
            for kc in range(8):
                kb.op("pe", lambda e, kc=kc: e.matmul(psH[:, kc * nh:(kc + 1) * nh], lhsT=xnh[0:nh, kc * 128:(kc + 1) * 128],
                                                   rhs=self.cstb[0:nh, 0, 0:nh], start=True, stop=True),
                      reads=[rt[5], self.r_cst], writes=[r_psH], inc=(kc == 7))
            for kc in range(8):
                for side, c0 in ((0, hl0), (1, hr0)):
                    kb.op("dve", lambda e, kc=kc, side=side, c0=c0: e.tensor_scalar(
                        out=hT[:, kc, c0:c0 + hh], in0=psH[:, kc * nh + side * hh:kc * nh + (side + 1) * hh],
                        scalar1=A[:, kc:kc + 1], scalar2=B[:, kc:kc + 1], op0=ALU.mult, op1=ALU.add),
                        reads=[r_psH, self.r_modv], writes=[r_hT])
        if not lv:
            kb.op("dve", lambda e: e.memset(hT[:, :, hl0:hl0 + hh], 0.0), writes=[r_hT])
        if not rv:
            kb.op("dve", lambda e: e.memset(hT[:, :, hr0:hr0 + hh], 0.0), writes=[r_hT])

    def load_rows_bcast(self, es, name, src_row, n):
        t = self.sb(es, name, [128, n], F32)
        r = self.kb.res(name)
        self.kb.dma("sp", t[:], src_row.partition_broadcast(128), writes=[r])
        return t, r

    def resid_epilogue(self, y, r_y, xt_s, r_xt, Grow, r_G, tmp, r_tmp, ss2, r_ss2, junk, r_junk):
        kb = self.kb
        for hb in range(2):
            kb.op("act", lambda e, hb=hb: e.activation(out=junk[:, 0:512], in_=y[:, hb * 512:(hb + 1) * 512], func=AF.Square,
                                                     accum_out=ss2[:, hb:hb + 1]), reads=r_y, writes=[r_ss2[0], r_junk])
        kb.op("dve", lambda e: e.tensor_tensor(out=ss2[:, 2:3], in0=ss2[:, 0:1], in1=ss2[:, 1:2], op=ALU.add), reads=[r_ss2[0]], writes=[r_ss2[1]])
        kb.op("act", lambda e: e.activation(out=ss2[:, 3:4], in_=ss2[:, 2:3], func=AF.Sqrt, bias=self.epsb[:], scale=1.0 / D),
              reads=[r_ss2[1], self.r_cst], writes=[r_ss2[2]])
        kb.op("dve", lambda e: e.reciprocal(out=ss2[:, 3:4], in_=ss2[:, 3:4]), reads=[r_ss2[2]], writes=[r_ss2[2]])
        kb.op("dve", lambda e: e.scalar_tensor_tensor(out=tmp[:], in0=y, scalar=ss2[:, 3:4], in1=Grow[:], op0=ALU.mult, op1=ALU.mult),
              reads=r_y + [r_ss2[2], r_G], writes=[r_tmp])
        kb.op("pool", lambda e: e.tensor_tensor(out=xt_s, in0=xt_s, in1=tmp[:], op=ALU.add), reads=[r_tmp, r_xt], writes=[r_xt])

    def phase_ffn(self, l, src, dst):
        nc, kb = self.nc, self.kb
        T = 256
        NT = S // T
        with ExitStack() as es:
            Wup = self.sb(es, "Wup", [128, 8, 2 * FH], BF16)
            Wdn = self.sb(es, "Wdn", [128, 22, D], BF16)
            r_wup = [kb.res() for _ in range(8)]
            r_wdn = [kb.res() for _ in range(2)]
            for kc in range(8):
                kb.dma("pool", Wup[:, kc, :], self.I("ffn_w_up")[l, kc * 128:(kc + 1) * 128, :], writes=[r_wup[kc]])
            for hh in range(2):
                kb.dma("pool", Wdn[:, hh * 11:(hh + 1) * 11, :], _r(self.I("ffn_w_down")[l, hh * 1408:(hh + 1) * 1408, :], "(j p) d -> p j d", p=128),
                       writes=[r_wdn[hh]])
            cw = self.sb(es, "f_cw", [128, 22, 3], F32)
            cb = self.sb(es, "f_cb", [128, 22], F32)
            r_cw = [kb.res(), kb.res()]
            kb.dma("sp", cw[:], self.I("fconv_wT")[:, l], writes=[r_cw[0]])
            kb.dma("sp", cb[:], self.I("fconv_bT")[:, l], writes=[r_cw[1]])
            Grow, r_G = self.load_rows_bcast(es, "f_G", self.vec[l, 1], D)
            nb = self.alloc_norm_bufs(es, 2, 2)
            xt = [self.sb(es, "f_xt%d" % i, [128, 2, D], F32) for i in range(2)]
            r_xt = [[kb.res() for _ in range(2)] for _ in range(2)]
            xh = [self.sb(es, "f_xh%d" % i, [2, D], F32) for i in range(2)]
            r_xh = [kb.res() for _ in range(2)]
            hT = [self.sb(es, "f_hT%d" % i, [128, 8, T + 2], BF16) for i in range(2)]
            r_hT = [kb.res() for _ in range(2)]
            ub = [self.sb(es, "f_ub%d" % i, [128, T + 2], F32) for i in range(2)]
            acc = [self.sb(es, "f_acc%d" % i, [128, T], F32) for i in range(2)]
            gl = [self.sb(es, "f_gl%d" % i, [128, T], F32) for i in range(2)]
            r_ub = [kb.res() for _ in range(2)]
            r_acc = [kb.res() for _ in range(2)]
            r_gl = [kb.res() for _ in range(2)]
            gT = self.sb(es, "f_gT", [128, 22, T], BF16)
            r_gT = [kb.res() for _ in range(22)]
            tmp = [self.sb(es, "f_tmp%d" % i, [128, D], F32) for i in range(2)]
            r_tmp = [kb.res() for _ in range(2)]
            ss2 = [self.sb(es, "f_ss2%d" % i, [128, 4], F32) for i in range(2)]
            r_ss2 = [[kb.res() for _ in range(3)] for _ in range(2)]
            junk2 = self.sb(es, "f_junk2", [128, 512], BF16)
            r_junk2 = kb.res()
            psT = [self.psB[2], self.psB[3]]
            r_psT = [self.r_psB[2], self.r_psB[3]]
            psH = self.psB[3][:, 256:272]
            r_psH = self.r_psB[3]
            bankU = [self.psB[0], self.psA[1][:, 0:512]]
            bankV = [self.psB[1], self.psA[1][:, 512:1024]]
            psU = [bk[:, 0:T] for bk in bankU]
            psUh = [bk[:, T:T + 2] for bk in bankU]
            psV = [bk[:, 0:T] for bk in bankV]
            r_psU = [self.r_psB[0], kb.res()]
            r_psV = [self.r_psB[1], kb.res()]
            r_psUh = r_psU
            for b in range(2):
                kb.op("dve", lambda e, b=b: e.memset(xh[b][:], 0.0), writes=[r_xh[b]])

            def load(t):
                b = t % 2
                t0 = t * T
                for s in range(2):
                    kb.dma("sp", xt[b][:, s, :], src[t0 + s * 128:t0 + (s + 1) * 128, :], writes=[r_xt[b][s]])
                if 0 < t < NT - 1:
                    kb.dma("sp", xh[b][0:2, :], src[t0 - 1:t0 + T + 1:T + 1, :], writes=[r_xh[b]])
                elif t > 0:
                    kb.dma("sp", xh[b][0:1, :], src[t0 - 1:t0, :], writes=[r_xh[b]])
                else:
                    kb.dma("sp", xh[b][1:2, :], src[t0 + T:t0 + T + 1, :], writes=[r_xh[b]])

            load(0)
            for t in range(NT if self.max_tiles is None else self.max_tiles):
                b = t % 2
                t0 = t * T
                if t + 1 < NT:
                    load(t + 1)
                self.norm_T(nb, xt[b], r_xt[b], 2, xh[b], r_xh[b], 2, (t > 0, t < NT - 1), hT[b], r_hT[b], 0, T, T + 1,
                            l, 2, 0, psT, r_psT, psH, r_psH)
                for j in range(22):
                    pb = j % 2
                    for (pp, rr, c0, n0, n1) in ((psU[pb], r_psU[pb], j * 128, 0, T), (psUh[pb], r_psUh[pb], j * 128, T, T + 2),
                                                 (psV[pb], r_psV[pb], FH + j * 128, 0, T)):
                        for kc in range(8):
                            kb.op("pe", lambda e, kc=kc, pp=pp, c0=c0, n0=n0, n1=n1: e.matmul(
                                pp, lhsT=Wup[:, kc, c0:c0 + 128], rhs=hT[b][:, kc, n0:n1], start=(kc == 0), stop=(kc == 7)),
                                reads=[r_wup[kc], r_hT[b]], writes=[rr], inc=(kc == 7))
                    u, a = ub[pb], acc[pb]
                    kb.op("act", lambda e, u=u, pb=pb: e.activation(out=u[:, 1:T + 1], in_=psU[pb], func=AF.Identity),
                          reads=[r_psU[pb]], writes=[r_ub[pb]])
                    kb.op("dve", lambda e, u=u, pb=pb: e.tensor_copy(out=u[:, 0:T + 2:T + 1], in_=psUh[pb]),
                          reads=[r_psUh[pb]], writes=[r_ub[pb]])
                    kb.op("dve", lambda e, u=u, a=a, j=j: e.tensor_scalar(out=a[:], in0=u[:, 1:T + 1], scalar1=cw[:, j, 1:2], scalar2=cb[:, j:j + 1],
                                                                        op0=ALU.mult, op1=ALU.add), reads=[r_ub[pb]] + r_cw, writes=[r_acc[pb]])
                    kb.op("dve", lambda e, u=u, a=a, j=j: e.scalar_tensor_tensor(out=a[:], in0=u[:, 0:T], scalar=cw[:, j, 0:1], in1=a[:],
                                                                               op0=ALU.mult, op1=ALU.add), reads=[r_ub[pb], r_acc[pb]] + r_cw, writes=[r_acc[pb]])
                    kb.op("dve", lambda e, u=u, a=a, j=j: e.scalar_tensor_tensor(out=a[:], in0=u[:, 2:T + 2], scalar=cw[:, j, 2:3], in1=a[:],
                                                                               op0=ALU.mult, op1=ALU.add), reads=[r_ub[pb], r_acc[pb]] + r_cw, writes=[r_acc[pb]])
                    kb.op("act", lambda e, a=a, pb=pb: e.activation(out=gl[pb][:], in_=a[:], func=AF.Gelu), reads=[r_acc[pb]], writes=[r_gl[pb]])
                    kb.op("dve", lambda e, j=j, pb=pb: e.tensor_tensor(out=gT[:, j, :], in0=gl[pb][:], in1=psV[pb], op=ALU.mult),
                          reads=[r_gl[pb], r_psV[pb]], writes=[r_gT[j]])
                for s in range(2):
                    py = self.psA[0]
                    rpy = self.r_psA[0]
                    for hb in range(2):
                        for j in range(22):
                            kb.op("pe", lambda e, j=j, s=s, hb=hb: e.matmul(py[:, hb * 512:(hb + 1) * 512], lhsT=gT[:, j, s * 128:(s + 1) * 128],
                                                                          rhs=Wdn[:, j, hb * 512:(hb + 1) * 512], start=(j == 0), stop=(j == 21)),
                                  reads=[r_gT[j], r_wdn[j // 11]], writes=[rpy], inc=(j == 21))
                    self.resid_epilogue(py[:, :], [rpy], xt[b][:, s, :], r_xt[b][s], Grow, r_G, tmp[s], r_tmp[s], ss2[s], r_ss2[s], junk2, r_junk2)
                    kb.dma("pool", dst[t0 + s * 128:t0 + (s + 1) * 128, :], xt[b][:, s, :], reads=[r_xt[b][s]])

    def phase_pool(self, src, dst):
        nc, kb = self.nc, self.kb
        T = 256
        NT = S // T
        HH = 8
        W = T + 2 * HH
        l = 1
        with ExitStack() as es:
            PW = self.sb(es, "p_PW", [128, 4, 2, 256], BF16)
            r_pw = kb.res()
            kb.dma("pool", PW[:], _r(self.I("pool_w"), "g (k p) n -> p g k n", p=128), writes=[r_pw])
            pbrow, r_pb = self.load_rows_bcast(es, "p_pb", self.I("pool_b"), D)
            psrow, r_psr = self.load_rows_bcast(es, "p_ps", self.I("pool_scale"), D)
            Grow, r_G = self.load_rows_bcast(es, "p_G", self.vec[l, 0], D)
            inv = self.sb(es, "p_inv", [128, 2, 4, 256], F32)
            r_inv = kb.res()
            kb.dma("sp", inv[:], _r(self.I("pool_inv"), "a g t -> (a g t)").partition_broadcast(128), writes=[r_inv])
            nb = self.alloc_norm_bufs(es, 2, 2 * HH)
            xt = [self.sb(es, "p_xt%d" % i, [128, 2, D], F32) for i in range(2)]
            r_xt = [[kb.res() for _ in range(2)] for _ in range(2)]
            xh = [self.sb(es, "p_xh%d" % i, [2 * HH, D], F32) for i in range(2)]
            r_xh = [kb.res() for _ in range(2)]
            hT = self.sb(es, "p_hT", [128, 8, W], F32)
            r_hT = kb.res()
            bufA = self.sb(es, "p_bA", [128, 2, W], F32)
            bufB = self.sb(es, "p_bB", [128, 2, W], F32)
            r_bA, r_bB = kb.res(), kb.res()
            pl = self.sb(es, "p_pl", [128, 8, T], BF16)
            r_pl = [kb.res() for _ in range(4)]
            ysb = [self.sb(es, "p_ysb%d" % i, [128, D], F32) for i in range(2)]
            r_ysb = [kb.res() for _ in range(2)]
            tmp = [self.sb(es, "p_tmp%d" % i, [128, D], F32) for i in range(2)]
            r_tmp = [kb.res() for _ in range(2)]
            ss2 = [self.sb(es, "p_ss2%d" % i, [128, 4], F32) for i in range(2)]
            r_ss2 = [[kb.res() for _ in range(3)] for _ in range(2)]
            junk2 = self.sb(es, "p_junk2", [128, 512], BF16)
            r_junk2 = kb.res()
            psT = [self.psB[2], self.psB[3]]
            r_psT = [self.r_psB[2], self.r_psB[3]]
            psH = self.psB[1][:, 0:128]
            r_psH = self.r_psB[1]
            for b in range(2):
                kb.op("dve", lambda e, b=b: e.memset(xh[b][:], 0.0), writes=[r_xh[b], self._rxh2(b)])

            def load(t):
                b = t % 2
                t0 = t * T
                for s in range(2):
                    kb.dma("sp", xt[b][:, s, :], src[t0 + s * 128:t0 + (s + 1) * 128, :], writes=[r_xt[b][s]])
                if 0 < t < NT - 1:
                    kb.dma("sp", xh[b][0:HH, :], src[t0 - HH:t0, :], writes=[r_xh[b]])
                    kb.dma("sp", xh[b][HH:2 * HH, :], src[t0 + T:t0 + T + HH, :], writes=[self._rxh2(b)])
                elif t > 0:
                    kb.dma("sp", xh[b][0:HH, :], src[t0 - HH:t0, :], writes=[r_xh[b]])
                else:
                    kb.dma("sp", xh[b][HH:2 * HH, :], src[t0 + T:t0 + T + HH, :], writes=[self._rxh2(b)])

            load(0)
            for t in range(NT):
                b = t % 2
                t0 = t * T
                if t + 1 < NT:
                    load(t + 1)
                rxh = Res()
                kb._merge(rxh.w, r_xh[b].w)
                kb._merge(rxh.w, self._rxh2(b).w)
                self.norm_T(nb, xt[b], r_xt[b], 2, xh[b], rxh, 2 * HH, (t > 0, t < NT - 1), hT, r_hT, HH, 0, HH + T,
                            l, 0, 0, psT, r_psT, psH, r_psH)
                kb._merge(r_xh[b].r, rxh.r)
                kb._merge(self._rxh2(b).r, rxh.r)
                for g in range(4):
                    hg = hT[:, 2 * g:2 * g + 2, :]
                    kb.op("dve", lambda e, hg=hg: e.tensor_tensor(out=bufA[:, :, 1:W], in0=hg[:, :, 0:W - 1], in1=hg[:, :, 1:W], op=ALU.add),
                          reads=[r_hT], writes=[r_bA])
                    cur, rcur, oth, roth = bufA, r_bA, bufB, r_bB
                    lo, hi = 1, W
                    for lvl in range(1, g + 1):
                        sh = 1 << (lvl - 1)
                        nlo, nhi = lo + sh, hi - sh
                        kb.op("dve", lambda e, cur=cur, oth=oth, sh=sh, nlo=nlo, nhi=nhi: e.tensor_tensor(
                            out=oth[:, :, nlo:nhi], in0=cur[:, :, nlo - sh:nhi - sh], in1=cur[:, :, nlo + sh:nhi + sh], op=ALU.add),
                            reads=[rcur], writes=[roth])
                        cur, rcur, oth, roth = oth, roth, cur, rcur
                        lo, hi = nlo, nhi
                    w = 2 << g
                    if 0 < t < NT - 1:
                        kb.op("dve", lambda e, cur=cur, hg=hg, g=g, w=w: e.scalar_tensor_tensor(
                            out=pl[:, 2 * g:2 * g + 2, :], in0=cur[:, :, HH:HH + T], scalar=1.0 / w, in1=hg[:, :, HH:HH + T],
                            op0=ALU.mult, op1=ALU.subtract), reads=[rcur, r_hT], writes=[r_pl[g]])
                    else:
                        a = 0 if t == 0 else 1
                        kb.op("dve", lambda e, cur=cur, oth=oth, g=g, a=a: e.tensor_tensor(
                            out=oth[:, :, HH:HH + T], in0=cur[:, :, HH:HH + T], in1=inv[:, a, g, :].unsqueeze(1).to_broadcast([128, 2, T]),
                            op=ALU.mult), reads=[rcur, r_inv], writes=[roth])
                        kb.op("dve", lambda e, oth=oth, hg=hg, g=g: e.tensor_tensor(
                            out=pl[:, 2 * g:2 * g + 2, :], in0=oth[:, :, HH:HH + T], in1=hg[:, :, HH:HH + T], op=ALU.subtract),
                            reads=[roth, r_hT], writes=[r_pl[g]])
                for s in range(2):
                    py = self.psA[s]
                    rpy = self.r_psA[s]
                    for g in range(4):
                        for kk in range(2):
                            kb.op("pe", lambda e, g=g, kk=kk, s=s, py=py: e.matmul(py[:, g * 256:(g + 1) * 256], lhsT=pl[:, 2 * g + kk, s * 128:(s + 1) * 128],
                                                                                 rhs=PW[:, g, kk, :], start=(kk == 0), stop=(kk == 1)),
                                  reads=[r_pl[g], r_pw], writes=[rpy], inc=(g == 3 and kk == 1))
                    kb.op("dve", lambda e, s=s, py=py: e.tensor_tensor(out=ysb[s][:], in0=py[:, :], in1=pbrow[:], op=ALU.add),
                          reads=[rpy, r_pb], writes=[r_ysb[s]])
                    kb.op("pool", lambda e, s=s: e.tensor_tensor(out=ysb[s][:], in0=ysb[s][:], in1=psrow[:], op=ALU.mult),
                          reads=[r_ysb[s], r_psr], writes=[r_ysb[s]])
                    self.resid_epilogue(ysb[s][:], [r_ysb[s]], xt[b][:, s, :], r_xt[b][s], Grow, r_G, tmp[s], r_tmp[s], ss2[s], r_ss2[s], junk2, r_junk2)
                    kb.dma("pool", dst[t0 + s * 128:t0 + (s + 1) * 128, :], xt[b][:, s, :], reads=[r_xt[b][s]])

    def _rxh2(self, b):
        if not hasattr(self, "_rxh2_l"):
            self._rxh2_l = [self.kb.res(), self.kb.res()]
        return self._rxh2_l[b]

    def phase_l0_mixer(self, x_in, ctx_in, dst):
        kb = self.kb
        ST = S + LC
        self.qT = self.dscr("qT", [512, S], BF16)
        self.kT = self.dscr("kT", [512, ST], BF16)
        self.v_tok = self.dscr("v_tok", [ST, 512], BF16)
        self.sz_tok = self.dscr("sz_tok", [S, D], BF16)
        self.xs_tok = self.dscr("xs_tok", [ST, D], BF16)
        self.B_tok = self.dscr("B_tok", [ST, 512], BF16)
        self.BT = self.dscr("BT", [512, ST], BF16)
        self.CT = self.dscr("CT", [512, ST], BF16)
        self.dt_tok = self.dscr("dt_tok", [ST, 32], F32)
        self.yf = self.dscr("yf", [S, D], F32)
        self.ssm_tok = self.dscr("ssm_tok", [S, D], BF16)
        self.att_tok = self.dscr("att_tok", [S, 512], BF16)
        self.l0_proj(x_in, ctx_in)
        kb.barrier()
        if self.stop_after == "proj":
            return
        self.l0_ssd(0)
        kb.barrier()
        self.l0_ssd(1)
        kb.barrier()
        if self.stop_after == "ssd":
            return
        self.l0_att()
        kb.barrier()
        if self.stop_after == "att":
            return
        self.l0_out(x_in, dst)

    def l0_proj(self, x_in, ctx_in):
        kb = self.kb
        T = 256
        NT = S // T
        w_in = self.I("w_in")
        with ExitStack() as es:
            W = self.sb(es, "Win", [128, 8, 4640], BF16)
            r_w = [kb.res() for _ in range(8)]
            for kc in range(8):
                kb.dma("pool", W[:, kc, :], w_in[kc * 128:(kc + 1) * 128, :], writes=[r_w[kc]])
            cw = self.sb(es, "c_cw", [128, 16, 3], F32)
            cb = self.sb(es, "c_cb", [128, 16], F32)
            r_cw = [kb.res(), kb.res()]
            kb.dma("sp", cw[:], self.I("conv_wT"), writes=[r_cw[0]])
            kb.dma("sp", cb[:], self.I("conv_bT"), writes=[r_cw[1]])
            nb = self.alloc_norm_bufs(es, 2, 2)
            xt = [self.sb(es, "j_xt%d" % i, [128, 2, D], F32) for i in range(2)]
            r_xt = [[kb.res() for _ in range(2)] for _ in range(2)]
            xh = [self.sb(es, "j_xh%d" % i, [2, D], F32) for i in range(2)]
            r_xh = [kb.res() for _ in range(2)]
            hT = self.sb(es, "j_hT", [128, 8, T + 2], BF16)
            r_hT = kb.res()
            ub = [self.sb(es, "j_ub%d" % i, [128, T + 2], F32) for i in range(2)]
            acc = [self.sb(es, "j_acc%d" % i, [128, T], F32) for i in range(2)]
            sx = [self.sb(es, "j_sx%d" % i, [128, T], BF16) for i in range(2)]
            r_ub = [kb.res() for _ in range(2)]
            r_acc = [kb.res() for _ in range(2)]
            r_sx = [kb.res() for _ in range(2)]
            fst = [self.sb(es, "j_fst%d" % i, [128, T], BF16) for i in range(2)]
            r_fst = [kb.res() for _ in range(2)]
            xs_st = self.sb(es, "j_xsst", [128, 2, D], BF16)
            b_st = self.sb(es, "j_bst", [128, 2, 512], BF16)
            sz_st = self.sb(es, "j_szst", [128, 2, D], BF16)
            v_st = self.sb(es, "j_vst", [128, 2, 512], BF16)
            dt_st = self.sb(es, "j_dtst", [128, 2, 32], F32)
            r_xsst, r_bst, r_szst, r_vst, r_dtst = (kb.res() for _ in range(5))
            psT = [self.psB[2], self.psB[3]]
            r_psT = [self.r_psB[2], self.r_psB[3]]
            psH = self.psB[3][:, 256:272]
            r_psH = self.r_psB[3]
            bankF = [self.psB[0], self.psB[1]]
            r_bankF = [self.r_psB[0], self.r_psB[1]]
            bankX = self.psA[1][:, 0:512]
            r_bankX = kb.res()
            bankD = self.psA[1][:, 512:1024]
            r_bankD = kb.res()
            bankM = [self.psA[0][:, 0:512], self.psA[0][:, 512:1024]]
            r_bankM = [kb.res(), kb.res()]
            for b in range(2):
                kb.op("dve", lambda e, b=b: e.memset(xh[b][:], 0.0), writes=[r_xh[b]])

            def srcrows(t):
                return (x_in, t * T) if t < NT else (ctx_in, 0)

            def load(t, b):
                src, t0 = srcrows(t)
                for s in range(2):
                    kb.dma("sp", xt[b][:, s, :], src[t0 + s * 128:t0 + (s + 1) * 128, :], writes=[r_xt[b][s]])
                if t >= NT:
                    return
                if 0 < t < NT - 1:
                    kb.dma("sp", xh[b][0:2, :], src[t0 - 1:t0 + T + 1:T + 1, :], writes=[r_xh[b]])
                elif t > 0:
                    kb.dma("sp", xh[b][0:1, :], src[t0 - 1:t0, :], writes=[r_xh[b]])
                else:
                    kb.dma("sp", xh[b][1:2, :], src[t0 + T:t0 + T + 1, :], writes=[r_xh[b]])

            tiles = list(range(NT + 1))
            if self.max_tiles is not None:
                tiles = list(range(self.max_tiles)) + [NT]
            load(tiles[0], 0)
            fi = 0
            for ti, t in enumerate(tiles):
                b = ti % 2
                isctx = t == NT
                g0 = S if isctx else t * T
                if ti + 1 < len(tiles):
                    load(tiles[ti + 1], (ti + 1) % 2)
                halo = (False, False) if isctx else (t > 0, t < NT - 1)
                self.norm_T(nb, xt[b], r_xt[b], 2, xh[b], r_xh[b], 2, halo, hT, r_hT, 0, T, T + 1,
                            0, 0, 1 if isctx else 0, psT, r_psT, psH, r_psH)
                chunks = []
                if not isctx:
                    chunks += [("q", c, c * 128) for c in range(4)]
                chunks += [("k", c, 1536 + c * 128) for c in range(4)]
                chunks += [("x", c, 2560 + c * 128) for c in range(16)]
                for kind, c, col in chunks:
                    pb = fi % 2
                    fi += 1
                    bk, rbk = bankF[pb], r_bankF[pb]
                    parts = [(bk[:, 0:T], 0, T)] + ([(bk[:, T:T + 2], T, T + 2)] if kind == "x" else [])
                    for (pp, n0, n1) in parts:
                        for kc in range(8):
                            kb.op("pe", lambda e, kc=kc, pp=pp, col=col, n0=n0, n1=n1: e.matmul(
                                pp, lhsT=W[:, kc, col:col + 128], rhs=hT[:, kc, n0:n1], start=(kc == 0), stop=(kc == 7)),
                                reads=[r_w[kc], r_hT], writes=[rbk], inc=(kc == 7))
                    if kind in ("q", "k"):
                        st, rst = fst[pb], r_fst[pb]
                        kb.op("act", lambda e, st=st, bk=bk, kind=kind: e.activation(out=st[:], in_=bk[:, 0:T], func=AF.Identity,
                                                                                   scale=(0.125 if kind == "q" else 1.0)),
                              reads=[rbk], writes=[rst])
                        dstT = self.qT if kind == "q" else self.kT
                        kb.dma("pool", dstT[c * 128:(c + 1) * 128, g0:g0 + T], st[:], reads=[rst])
                        continue
                    u, a, sxx = ub[pb], acc[pb], sx[pb]
                    kb.op("act", lambda e, u=u, bk=bk: e.activation(out=u[:, 1:T + 1], in_=bk[:, 0:T], func=AF.Identity),
                          reads=[rbk], writes=[r_ub[pb]])
                    kb.op("dve", lambda e, u=u, bk=bk: e.tensor_copy(out=u[:, 0:T + 2:T + 1], in_=bk[:, T:T + 2]),
                          reads=[rbk], writes=[r_ub[pb]])
                    kb.op("dve", lambda e, u=u, a=a, c=c: e.tensor_scalar(out=a[:], in0=u[:, 1:T + 1], scalar1=cw[:, c, 1:2], scalar2=cb[:, c:c + 1],
                                                                        op0=ALU.mult, op1=ALU.add), reads=[r_ub[pb]] + r_cw, writes=[r_acc[pb]])
                    kb.op("dve", lambda e, u=u, a=a, c=c: e.scalar_tensor_tensor(out=a[:], in0=u[:, 0:T], scalar=cw[:, c, 0:1], in1=a[:],
                                                                               op0=ALU.mult, op1=ALU.add), reads=[r_ub[pb], r_acc[pb]] + r_cw, writes=[r_acc[pb]])
                    kb.op("dve", lambda e, u=u, a=a, c=c: e.scalar_tensor_tensor(out=a[:], in0=u[:, 2:T + 2], scalar=cw[:, c, 2:3], in1=a[:],
                                                                               op0=ALU.mult, op1=ALU.add), reads=[r_ub[pb], r_acc[pb]] + r_cw, writes=[r_acc[pb]])
                    kb.op("act", lambda e, a=a, sxx=sxx: e.activation(out=sxx[:], in_=a[:], func=AF.Silu), reads=[r_acc[pb]], writes=[r_sx[pb]])
                    if c >= 8:
                        dstT = self.BT if c < 12 else self.CT
                        cc = (c - 8) % 4
                        kb.dma("pool", dstT[cc * 128:(cc + 1) * 128, g0:g0 + T], sxx[:], reads=[r_sx[pb]])
                    if c < 12:
                        for s in range(2):
                            kb.op("pe", lambda e, s=s, sxx=sxx: e.matmul(bankX[:, s * 128:(s + 1) * 128], lhsT=sxx[:, s * 128:(s + 1) * 128],
                                                                       rhs=self.identb, start=True, stop=True),
                                  reads=[r_sx[pb], self.r_cst], writes=[r_bankX], inc=(s == 1))
                        if c < 8:
                            kb.op("dve", lambda e, c=c: e.tensor_copy(out=xs_st[:, :, c * 128:(c + 1) * 128],
                                                                     in_=_r(bankX[:, 0:256], "p (s f) -> p s f", s=2)),
                                  reads=[r_bankX], writes=[r_xsst])
                        else:
                            kb.op("dve", lambda e, c=c: e.tensor_copy(out=b_st[:, :, (c - 8) * 128:(c - 7) * 128],
                                                                     in_=_r(bankX[:, 0:256], "p (s f) -> p s f", s=2)),
                                  reads=[r_bankX], writes=[r_bst])
                kb.dma("pool", _r(self.xs_tok[g0:g0 + T, :], "(s p) d -> p s d", p=128), xs_st[:], reads=[r_xsst])
                kb.dma("pool", _r(self.B_tok[g0:g0 + T, :], "(s p) d -> p s d", p=128), b_st[:], reads=[r_bst])
                mi = 0
                for s in range(2):
                    tm = [("v", 2048, 512), ("d", 4608, 32)]
                    if not isctx:
                        tm = [("z", 512, 512), ("z", 1024, 512)] + tm
                    for kind, col, n in tm:
                        if kind == "d":
                            bk, rbk = bankD, r_bankD
                        else:
                            bk, rbk = bankM[mi % 2], r_bankM[mi % 2]
                            mi += 1
                        for kc in range(8):
                            kb.op("pe", lambda e, kc=kc, bk=bk, col=col, n=n, s=s: e.matmul(
                                bk[:, 0:n], lhsT=hT[:, kc, s * 128:(s + 1) * 128], rhs=W[:, kc, col:col + n], start=(kc == 0), stop=(kc == 7)),
                                reads=[r_w[kc], r_hT], writes=[rbk], inc=(kc == 7))
                        if kind == "z":
                            kb.op("act", lambda e, bk=bk, col=col, s=s: e.activation(out=sz_st[:, s, col - 512:col], in_=bk[:, 0:512], func=AF.Silu),
                                  reads=[rbk], writes=[r_szst])
                        elif kind == "v":
                            kb.op("act", lambda e, bk=bk, s=s: e.activation(out=v_st[:, s, :], in_=bk[:, 0:512], func=AF.Identity),
                                  reads=[rbk], writes=[r_vst])
                        else:
                            kb.op("dve", lambda e, bk=bk, s=s: e.tensor_copy(out=dt_st[:, s, :], in_=bk[:, 0:32]), reads=[rbk], writes=[r_dtst])
                if not isctx:
                    kb.dma("pool", _r(self.sz_tok[g0:g0 + T, :], "(s p) d -> p s d", p=128), sz_st[:], reads=[r_szst])
                kb.dma("pool", _r(self.v_tok[g0:g0 + T, :], "(s p) d -> p s d", p=128), v_st[:], reads=[r_vst])
                kb.dma("pool", _r(self.dt_tok[g0:g0 + T, :], "(s p) d -> p s d", p=128), dt_st[:], reads=[r_dtst])

    def l0_ssd(self, d):
        kb = self.kb
        NCH = S // 128
        with ExitStack() as es:
            rows = self.sb(es, "d_rows", [128, 3, 32], F32)
            r_rows = kb.res()
            kb.dma("sp", rows[:], _r(self.I("ssm_rows"), "a b -> (a b)").partition_broadcast(128), writes=[r_rows])
            negA = self.sb(es, "d_negA", [128, 32], F32)
            dsum = self.sb(es, "d_dsum", [128, 16], F32)
            r_negA = kb.res()
            kb.op("act", lambda e: e.activation(out=negA[:], in_=rows[:, 0, :], func=AF.Exp), reads=[r_rows], writes=[r_negA])
            kb.op("dve", lambda e: e.tensor_scalar(out=negA[:], in0=negA[:], scalar1=-1.0, scalar2=None, op0=ALU.mult), reads=[r_negA], writes=[r_negA])
            kb.op("dve", lambda e: e.tensor_tensor(out=dsum[:], in0=rows[:, 2, 0:16], in1=rows[:, 2, 16:32], op=ALU.add), reads=[r_rows], writes=[r_negA])
            ngrow, r_ng = self.load_rows_bcast(es, "d_ng", self.I("ssm_norm_g"), D)
            xs = [self.sb(es, "d_xs%d" % i, [128, D], BF16) for i in range(2)]
            Bt = [self.sb(es, "d_Bt%d" % i, [128, 512], BF16) for i in range(2)]
            BTt = [self.sb(es, "d_BTt%d" % i, [128, 4, 128], BF16) for i in range(2)]
            CTt = [self.sb(es, "d_CTt%d" % i, [128, 4, 128], BF16) for i in range(2)]
            dtr = [self.sb(es, "d_dtr%d" % i, [128, 32], F32) for i in range(2)]
            yfl = [self.sb(es, "d_yfl%d" % i, [128, D], F32) for i in range(2)]
            szl = [self.sb(es, "d_szl%d" % i, [128, D], BF16) for i in range(2)]
            r_ld = [[kb.res() for _ in range(7)] for _ in range(2)]
            H = self.sb(es, "d_H", [128, D], F32)
            Hb = self.sb(es, "d_Hb", [128, D], BF16)
            r_H, r_Hb = kb.res(), kb.res()
            kb.op("dve", lambda e: e.memset(H[:], 0.0), writes=[r_H])
            kb.op("pool", lambda e: e.memset(Hb[:], 0.0), writes=[r_Hb])
            sm = self.sb(es, "d_sm", [128, 8, 16], F32)
            r_sm = [kb.res() for _ in range(8)]
            dw = self.sb(es, "d_dw", [128, 16], F32)
            r_dw = kb.res()
            Lh = self.sb(es, "d_Lh", [128, 16, 128], F32)
            r_Lh = kb.res()
            seg = self.sb(es, "d_seg", [128, 4, 128], F32)
            r_seg = kb.res()
            cbm = self.sb(es, "d_cbm", [128, 4, 128], F32)
            r_cbm = kb.res()
            M = self.sb(es, "d_M", [128, 16, 128], BF16)
            r_M = [kb.res() for _ in range(4)]
            xdt = self.sb(es, "d_xdt", [128, D], BF16)
            xdtw = self.sb(es, "d_xdtw", [128, D], BF16)
            r_xdt, r_xdtw = kb.res(), kb.res()
            yc = self.sb(es, "d_yc", [128, D], F32)
            yt = self.sb(es, "d_yt", [128, D], F32)
            r_yc, r_yt = kb.res(), kb.res()
            ss4 = self.sb(es, "d_ss4", [128, 8], F32)
            r_ss4 = [kb.res() for _ in range(3)]
            junk = self.sb(es, "d_junk", [128, 256], BF16)
            r_junk = kb.res()
            so = self.sb(es, "d_so", [128, D], BF16)
            r_so = kb.res()
            one1 = self.sb(es, "d_one", [128, 1], F32)
            kb.op("pool", lambda e: e.memset(one1[:], 1.0), writes=[r_negA])
            y_ps, r_yps = self.psA[0], [self.r_psA[0], kb.res()]
            yo_ps, r_yops = self.psA[1], [self.r_psA[1], kb.res()]
            D_ps, r_Dps = self.psB[0], self.r_psB[0]
            cb_ps, r_cbps = self.psB[1], self.r_psB[1]
            sm_ps, r_smps = self.psB[2], self.r_psB[2]
            S_ps, r_Sps = self.psB[3], self.r_psB[3]
            Rm = self.cst[:, 1 + d, :]
            Mk = self.cst[:, 3 + d, :]
            ones = self.cst[:, 5, :]
            lat = list(range(NCH)) if d == 0 else list(range(NCH - 1, -1, -1))
            if self.max_tiles is not None:
                lat = lat[:self.max_tiles]
            order = [("c", 0), ("c", 1)] if d == 0 else [("c", 1), ("c", 0)]
            order += [("l", c) for c in lat]

            def tok0(kc):
                return S + kc[1] * 128 if kc[0] == "c" else kc[1] * 128

            def load(i):
                b = i % 2
                kc = order[i]
                t0 = tok0(kc)
                kb.dma("sp", xs[b][:], self.xs_tok[t0:t0 + 128, :], writes=[r_ld[b][0]])
                kb.dma("sp", Bt[b][:], self.B_tok[t0:t0 + 128, :], writes=[r_ld[b][1]])
                kb.dma("sp", dtr[b][:], self.dt_tok[t0:t0 + 128, :], writes=[r_ld[b][2]])
                if kc[0] == "l":
                    kb.dma("sp", BTt[b][:], _r(self.BT[:, t0:t0 + 128], "(g n) t -> n g t", n=128), writes=[r_ld[b][3]])
                    kb.dma("sp", CTt[b][:], _r(self.CT[:, t0:t0 + 128], "(g n) t -> n g t", n=128), writes=[r_ld[b][4]])
                    if d == 1:
                        kb.dma("sp", yfl[b][:], self.yf[t0:t0 + 128, :], writes=[r_ld[b][5]])
                        kb.dma("sp", szl[b][:], self.sz_tok[t0:t0 + 128, :], writes=[r_ld[b][6]])

            load(0)
            for i, kc in enumerate(order):
                b = i % 2
                t0 = tok0(kc)
                islat = kc[0] == "l"
                if i + 1 < len(order):
                    load(i + 1)
                rl = r_ld[b]
                hs = slice(d * 16, (d + 1) * 16)
                kb.op("dve", lambda e: e.tensor_tensor(out=sm[:, 0, :], in0=dtr[b][:, hs], in1=rows[:, 1, hs], op=ALU.add), reads=[rl[2], r_rows], writes=[r_sm[0]])
                kb.op("act", lambda e: e.activation(out=sm[:, 1, :], in_=sm[:, 0, :], func=AF.Exp), reads=[r_sm[0]], writes=[r_sm[1]])
                kb.op("act", lambda e: e.activation(out=sm[:, 2, :], in_=sm[:, 1, :], func=AF.Ln, bias=one1[:]), reads=[r_sm[1], r_negA], writes=[r_sm[2]])
                kb.op("dve", lambda e: e.tensor_tensor(out=sm[:, 3, :], in0=sm[:, 2, :], in1=negA[:, hs], op=ALU.mult), reads=[r_sm[2], r_negA], writes=[r_sm[3]])
                kb.op("pe", lambda e: e.matmul(sm_ps[:, 0:16], lhsT=Rm, rhs=sm[:, 3, :], start=True, stop=True), reads=[r_sm[3], self.r_cst], writes=[r_smps], inc=False)
                kb.op("pe", lambda e: e.matmul(sm_ps[:, 16:32], lhsT=ones, rhs=sm[:, 3, :], start=True, stop=True), reads=[r_sm[3], self.r_cst], writes=[r_smps])
                kb.op("dve", lambda e: e.tensor_copy(out=sm[:, 4, :], in_=sm_ps[:, 0:16]), reads=[r_smps], writes=[r_sm[4]])
                kb.op("dve", lambda e: e.tensor_tensor(out=sm[:, 0, :], in0=sm_ps[:, 16:32], in1=sm[:, 4, :], op=ALU.subtract), reads=[r_smps, r_sm[4]], writes=[r_sm[0]])
                kb.op("act", lambda e: e.activation(out=sm[:, 5, :], in_=sm[:, 0, :], func=AF.Exp), reads=[r_sm[0]], writes=[r_sm[5]])
                kb.op("act", lambda e: e.activation(out=sm[:, 6, :], in_=sm[:, 4, :], func=AF.Exp), reads=[r_sm[4]], writes=[r_sm[6]])
                kb.op("act", lambda e: e.activation(out=sm[:, 7, :], in_=sm_ps[:, 16:32], func=AF.Exp), reads=[r_smps], writes=[r_sm[7]])
                kb.op("dve", lambda e: e.tensor_tensor(out=dw[:], in0=sm[:, 2, :], in1=sm[:, 5, :], op=ALU.mult), reads=[r_sm[2], r_sm[5]], writes=[r_dw])
                xs3 = _r(xs[b][:], "p (h q) -> p h q", h=16)
                kb.op("dve", lambda e: e.tensor_tensor(out=_r(xdtw[:], "p (h q) -> p h q", h=16), in0=xs3, in1=dw[:].unsqueeze(2).to_broadcast([128, 16, 64]), op=ALU.mult),
                      reads=[rl[0], r_dw], writes=[r_xdtw])
                if islat:
                    kb.op("dve", lambda e: e.tensor_tensor(out=_r(xdt[:], "p (h q) -> p h q", h=16), in0=xs3, in1=sm[:, 2, :].unsqueeze(2).to_broadcast([128, 16, 64]), op=ALU.mult),
                          reads=[rl[0], r_sm[2]], writes=[r_xdt])
                    kb.op("dve", lambda e: e.tensor_tensor(out=Lh[:], in0=Mk.unsqueeze(1).to_broadcast([128, 16, 128]),
                                                           in1=sm[:, 3, :].unsqueeze(2).to_broadcast([128, 16, 128]), op=ALU.mult),
                          reads=[r_sm[3], self.r_cst], writes=[r_Lh])
                    for g in range(4):
                        kb.op("pe", lambda e, g=g: e.matmul(cb_ps[:, g * 128:(g + 1) * 128], lhsT=BTt[b][:, g, :], rhs=CTt[b][:, g, :], start=True, stop=True),
                              reads=[rl[3], rl[4]], writes=[r_cbps], inc=(g == 3))
                    kb.op("dve", lambda e: e.tensor_tensor(out=cbm[:], in0=_r(cb_ps[:, :], "p (g l) -> p g l", g=4),
                                                           in1=Rm.unsqueeze(1).to_broadcast([128, 4, 128]), op=ALU.mult),
                          reads=[r_cbps, self.r_cst], writes=[r_cbm])
                    for g in range(4):
                        for r4 in range(4):
                            h = g * 4 + r4
                            kb.op("pe", lambda e, h=h, r4=r4: e.matmul(D_ps[:, r4 * 128:(r4 + 1) * 128], lhsT=Lh[:, h, :], rhs=Rm, start=True, stop=True),
                                  reads=[r_Lh, self.r_cst], writes=[r_Dps], inc=(r4 == 3))
                        kb.op("act", lambda e: e.activation(out=_r(seg[:], "p r l -> p (r l)"), in_=D_ps[:, :], func=AF.Exp), reads=[r_Dps], writes=[r_seg])
                        kb.op("dve", lambda e, g=g: e.tensor_tensor(out=M[:, g * 4:(g + 1) * 4, :], in0=seg[:], in1=cbm[:, g, :].unsqueeze(1).to_broadcast([128, 4, 128]), op=ALU.mult),
                              reads=[r_seg, r_cbm], writes=[r_M[g]])
                    for h in range(16):
                        kb.op("pe", lambda e, h=h: e.matmul(y_ps[:, h * 64:(h + 1) * 64], lhsT=M[:, h, :], rhs=xdt[:, h * 64:(h + 1) * 64], start=True, stop=True),
                              reads=[r_M[h // 4], r_xdt], writes=r_yps, inc=(h == 15))
                    for g in range(4):
                        kb.op("pe", lambda e, g=g: e.matmul(yo_ps[:, g * 256:(g + 1) * 256], lhsT=CTt[b][:, g, :], rhs=Hb[:, g * 256:(g + 1) * 256], start=True, stop=True),
                              reads=[rl[4], r_Hb], writes=r_yops, inc=(g == 3))
                    kb.op("dve", lambda e: e.tensor_tensor(out=_r(yt[:], "p (h q) -> p h q", h=16), in0=_r(yo_ps[:, :], "p (h q) -> p h q", h=16),
                                                           in1=sm[:, 6, :].unsqueeze(2).to_broadcast([128, 16, 64]), op=ALU.mult),
                          reads=r_yops + [r_sm[6]], writes=[r_yt])
                    kb.op("dve", lambda e: e.tensor_tensor(out=yc[:], in0=yt[:], in1=y_ps[:, :], op=ALU.add), reads=[r_yt] + r_yps, writes=[r_yc])
                    if d == 0:
                        kb.dma("pool", self.yf[t0:t0 + 128, :], yc[:], reads=[r_yc])
                    else:
                        kb.op("pool", lambda e: e.tensor_tensor(out=yc[:], in0=yc[:], in1=yfl[b][:], op=ALU.add), reads=[r_yc, rl[5]], writes=[r_yc])
                        kb.op("dve", lambda e: e.tensor_tensor(out=_r(yt[:], "p (h q) -> p h q", h=16), in0=xs3, in1=dsum[:].unsqueeze(2).to_broadcast([128, 16, 64]), op=ALU.mult),
                              reads=[rl[0], r_negA], writes=[r_yt])
                        kb.op("pool", lambda e: e.tensor_tensor(out=yc[:], in0=yc[:], in1=yt[:], op=ALU.add), reads=[r_yc, r_yt], writes=[r_yc])
                        kb.op("dve", lambda e: e.tensor_tensor(out=yc[:], in0=yc[:], in1=szl[b][:], op=ALU.mult), reads=[r_yc, rl[6]], writes=[r_yc])
                        for g in range(4):
                            kb.op("act", lambda e, g=g: e.activation(out=junk[:], in_=yc[:, g * 256:(g + 1) * 256], func=AF.Square, accum_out=ss4[:, g:g + 1]),
                                  reads=[r_yc], writes=[r_ss4[0], r_junk])
                        kb.op("act", lambda e: e.activation(out=ss4[:, 4:8], in_=ss4[:, 0:4], func=AF.Sqrt, bias=self.epsb[:], scale=1.0 / 256), reads=[r_ss4[0], self.r_cst], writes=[r_ss4[1]])
                        kb.op("dve", lambda e: e.reciprocal(out=ss4[:, 4:8], in_=ss4[:, 4:8]), reads=[r_ss4[1]], writes=[r_ss4[1]])
                        kb.op("dve", lambda e: e.tensor_tensor(out=_r(yt[:], "p (g q) -> p g q", g=4), in0=_r(yc[:], "p (g q) -> p g q", g=4),
                                                               in1=ss4[:, 4:8].unsqueeze(2).to_broadcast([128, 4, 256]), op=ALU.mult), reads=[r_yc, r_ss4[1]], writes=[r_yt])
                        kb.op("pool", lambda e: e.tensor_tensor(out=so[:], in0=yt[:], in1=ngrow[:], op=ALU.mult), reads=[r_yt, r_ng], writes=[r_so])
                        kb.dma("pool", self.ssm_tok[t0:t0 + 128, :], so[:], reads=[r_so])
                for gp in range(2):
                    for gg in range(2):
                        g = gp * 2 + gg
                        kb.op("pe", lambda e, g=g, gg=gg: e.matmul(S_ps[:, gg * 256:(gg + 1) * 256], lhsT=Bt[b][:, g * 128:(g + 1) * 128], rhs=xdtw[:, g * 256:(g + 1) * 256], start=True, stop=True),
                              reads=[rl[1], r_xdtw], writes=[r_Sps], inc=(gg == 1))
                    Hs = H[:, gp * 512:(gp + 1) * 512]
                    kb.op("dve", lambda e, Hs=Hs, gp=gp: e.tensor_tensor(out=_r(Hs, "p (h q) -> p h q", h=8), in0=_r(Hs, "p (h q) -> p h q", h=8),
                                                                        in1=sm[:, 7, gp * 8:(gp + 1) * 8].unsqueeze(2).to_broadcast([128, 8, 64]), op=ALU.mult),
                          reads=[r_H, r_sm[7], r_Hb] + r_yops, writes=[r_H])
                    kb.op("dve", lambda e, Hs=Hs: e.tensor_tensor(out=Hs, in0=Hs, in1=S_ps[:, :], op=ALU.add), reads=[r_H, r_Sps], writes=[r_H])
                kb.op("act", lambda e: e.activation(out=Hb[:], in_=H[:], func=AF.Identity), reads=[r_H], writes=[r_Hb])

    def l0_att(self):
        kb = self.kb
        ST = S + LC
        NTL = S // 128
        tab_in = self.I("rpbtab")
        with ExitStack() as es:
            qh = [self.sb(es, "a_q%d" % i, [64, S], BF16) for i in range(2)]
            kh = [self.sb(es, "a_k%d" % i, [64, ST], BF16) for i in range(2)]
            va = [self.sb(es, "a_v%d" % i, [128, 34, 65], BF16) for i in range(2)]
            tab = [self.sb(es, "a_tab%d" % i, [128, 5, 640], F32) for i in range(2)]
            r_hd = [[kb.res() for _ in range(4)] for _ in range(2)]
            sc = [self.sb(es, "a_sc%d" % i, [128, 640], F32) for i in range(2)]
            P = [self.sb(es, "a_P%d" % i, [128, 896], BF16) for i in range(2)]
            r_sc = [kb.res() for _ in range(2)]
            r_P = [kb.res() for _ in range(2)]
            rc = [self.sb(es, "a_rc%d" % i, [128, 1], F32) for i in range(2)]
            r_rc = [kb.res() for _ in range(2)]
            ast = [self.sb(es, "a_st%d" % i, [128, 32, 64], BF16) for i in range(2)]
            r_ast = [kb.res() for _ in range(2)]
            S_ps = [self.psA[0], self.psA[1]]
            r_Sps = [[self.r_psA[0], kb.res()], [self.r_psA[1], kb.res()]]
            o_ps = [self.psB[0], self.psB[1]]
            r_ops = [self.r_psB[0], self.r_psB[1]]
            for i in range(2):
                kb.op("dve", lambda e, i=i: e.memset(va[i][:, :, 64:65], 1.0), writes=[r_hd[i][2]])

            def loadh(h):
                b = h % 2
                kb.dma("sp", qh[b][:], self.qT[h * 64:(h + 1) * 64, :], writes=[r_hd[b][0]])
                kb.dma("sp", kh[b][:], self.kT[h * 64:(h + 1) * 64, :], writes=[r_hd[b][1]])
                for part in range(2):
                    kb.dma("sp", va[b][:, part * 17:(part + 1) * 17, 0:64],
                           _r(self.v_tok[part * 17 * 128:(part + 1) * 17 * 128, h * 64:(h + 1) * 64], "(j p) d -> p j d", p=128),
                           writes=[r_hd[b][2]] if part == 0 else [self._rva2(b)])
                kb.dma("sp", tab[b][:], _r(tab_in[h], "v p c q -> p v (c q)"), writes=[r_hd[b][3]])

            nh = 8
            loadh(0)
            it = 0
            for h in range(nh):
                b = h % 2
                if h + 1 < nh:
                    loadh(h + 1)
                rq, rk, rv, rt = r_hd[b]
                rv2 = self._rva2(b)
                tl = range(NTL) if self.max_tiles is None else range(self.max_tiles)
                for j in tl:
                    pb = it % 2
                    it += 1
                    var = 0 if j == 0 else 1 if j == 1 else 3 if j == 30 else 4 if j == 31 else 2
                    u0 = min(max(2 * j - 4, 0), 54)
                    kt0 = u0 // 2
                    sp_ = S_ps[pb]
                    for c in range(7):
                        k0 = (kt0 + c) * 128 if c < 5 else S + (c - 5) * 128
                        kb.op("pe", lambda e, c=c, k0=k0, sp_=sp_: e.matmul(sp_[:, c * 128:(c + 1) * 128], lhsT=kh[b][:, k0:k0 + 128], rhs=qh[b][:, j * 128:(j + 1) * 128],
                                                                          start=True, stop=True), reads=[rq, rk], writes=r_Sps[pb], inc=(c == 6))
                    kb.op("dve", lambda e, sp_=sp_, var=var: e.tensor_tensor(out=sc[pb][:, 0:512], in0=sp_[:, 0:512], in1=tab[b][:, var, 0:512], op=ALU.add),
                          reads=r_Sps[pb] + [rt], writes=[r_sc[pb]])
                    kb.op("dve", lambda e, sp_=sp_, var=var: e.tensor_tensor(out=sc[pb][:, 512:640], in0=sp_[:, 512:640], in1=tab[b][:, var, 512:640], op=ALU.add),
                          reads=r_Sps[pb] + [rt], writes=[r_sc[pb]])
                    kb.op("act", lambda e: e.activation(out=P[pb][:, 0:640], in_=sc[pb][:], func=AF.Exp), reads=[r_sc[pb]], writes=[r_P[pb]])
                    kb.op("act", lambda e, sp_=sp_: e.activation(out=P[pb][:, 640:896], in_=sp_[:, 640:896], func=AF.Exp), reads=r_Sps[pb], writes=[r_P[pb]])
                    for c in range(7):
                        kt = kt0 + c if c < 5 else 32 + (c - 5)
                        kb.op("pe", lambda e, c=c, kt=kt: e.matmul(o_ps[pb][:, 0:65], lhsT=P[pb][:, c * 128:(c + 1) * 128], rhs=va[b][:, kt, :], start=(c == 0), stop=(c == 6)),
                              reads=[r_P[pb], rv, rv2], writes=[r_ops[pb]], inc=(c == 6))
                    kb.op("dve", lambda e: e.reciprocal(out=rc[pb][:], in_=o_ps[pb][:, 64:65]), reads=[r_ops[pb]], writes=[r_rc[pb]])
                    kb.op("dve", lambda e, j=j: e.tensor_scalar(out=ast[b][:, j, :], in0=o_ps[pb][:, 0:64], scalar1=rc[pb][:, 0:1], scalar2=None, op0=ALU.mult),
                          reads=[r_ops[pb], r_rc[pb]], writes=[r_ast[b]])
                for part in range(4):
                    kb.dma("pool", _r(self.att_tok[part * 1024:(part + 1) * 1024, h * 64:(h + 1) * 64], "(j p) d -> p j d", p=128),
                           ast[b][:, part * 8:(part + 1) * 8, :], reads=[r_ast[b]])

    def _rva2(self, b):
        if not hasattr(self, "_rva2_l"):
            self._rva2_l = [self.kb.res(), self.kb.res()]
        return self._rva2_l[b]

    def l0_out(self, x_in, dst):
        kb = self.kb
        NTL = S // 128
        with ExitStack() as es:
            Wo = self.sb(es, "o_W", [128, 12, D], BF16)
            r_wo = kb.res()
            kb.dma("pool", Wo[:], _r(self.I("w_out"), "(c p) d -> p c d", p=128), writes=[r_wo])
            Grow, r_G = self.load_rows_bcast(es, "o_G", self.vec[0, 0], D)
            cat = [self.sb(es, "o_cat%d" % i, [128, 1536], BF16) for i in range(2)]
            xt = [self.sb(es, "o_xt%d" % i, [128, D], F32) for i in range(2)]
            r_cat = [[kb.res(), kb.res()] for _ in range(2)]
            r_xt = [kb.res() for _ in range(2)]
            catT = self.sb(es, "o_catT", [128, 12, 128], BF16)
            r_catT = [kb.res() for _ in range(3)]
            tmp = self.sb(es, "o_tmp", [128, D], F32)
            r_tmp = kb.res()
            ss2 = self.sb(es, "o_ss2", [128, 4], F32)
            r_ss2 = [kb.res() for _ in range(3)]
            junk2 = self.sb(es, "o_junk2", [128, 512], BF16)
            r_junk2 = kb.res()
            psT = [self.psB[0], self.psB[1], self.psB[2]]
            r_psT = [self.r_psB[0], self.r_psB[1], self.r_psB[2]]

            def load(i):
                b = i % 2
                kb.dma("sp", cat[b][:, 0:512], self.att_tok[i * 128:(i + 1) * 128, :], writes=[r_cat[b][0]])
                kb.dma("sp", cat[b][:, 512:1536], self.ssm_tok[i * 128:(i + 1) * 128, :], writes=[r_cat[b][1]])
                kb.dma("sp", xt[b][:], x_in[i * 128:(i + 1) * 128, :], writes=[r_xt[b]])

            n = NTL if self.max_tiles is None else self.max_tiles
            load(0)
            for i in range(n):
                b = i % 2
                if i + 1 < n:
                    load(i + 1)
                for q3 in range(3):
                    for cc in range(4):
                        c = q3 * 4 + cc
                        kb.op("pe", lambda e, c=c, cc=cc, q3=q3: e.matmul(psT[q3][:, cc * 128:(cc + 1) * 128], lhsT=cat[b][:, c * 128:(c + 1) * 128], rhs=self.identb,
                                                                        start=True, stop=True), reads=r_cat[b] + [self.r_cst], writes=[r_psT[q3]], inc=(cc == 3))
                    kb.op("act", lambda e, q3=q3: e.activation(out=_r(catT[:, q3 * 4:(q3 + 1) * 4, :], "p c t -> p (c t)"), in_=psT[q3][:, :], func=AF.Identity),
                          reads=[r_psT[q3]], writes=[r_catT[q3]])
                py = self.psA[i % 2]
                rpy = self.r_psA[i % 2]
                for hb in range(2):
                    for c in range(12):
                        kb.op("pe", lambda e, c=c, hb=hb: e.matmul(py[:, hb * 512:(hb + 1) * 512], lhsT=catT[:, c, :], rhs=Wo[:, c, hb * 512:(hb + 1) * 512],
                                                                 start=(c == 0), stop=(c == 11)), reads=[r_catT[c // 4], r_wo], writes=[rpy], inc=(c == 11))
                self.resid_epilogue(py[:, :], [rpy], xt[b][:], r_xt[b], Grow, r_G, tmp, r_tmp, ss2, r_ss2, junk2, r_junk2)
                kb.dma("pool", dst[i * 128:(i + 1) * 128, :], xt[b][:], reads=[r_xt[b]])


def _consts():
    t = np.arange(128)
    c = np.zeros((128, 6, 128), np.float32)
    c[:, 0] = (t[:, None] == t[None, :])
    c[:, 1] = (t[:, None] <= t[None, :])
    c[:, 2] = (t[:, None] >= t[None, :])
    c[:, 3] = (t[:, None] > t[None, :])
    c[:, 4] = (t[:, None] < t[None, :])
    c[:, 5] = 1.0
    return c


def _pool_inv():
    out = np.zeros((2, 4, 256), np.float32)
    for a, base in enumerate((0, S - 256)):
        t = base + np.arange(256)
        for gi, w in enumerate((2, 4, 8, 16)):
            lo = np.clip(t - w // 2, 0, S)
            hi = np.clip(t - w // 2 + w, 0, S)
            out[a, gi] = np.float32(1.0) / (hi - lo).astype(np.float32)
    return out


def _rpb_table(rpb):
    padded = np.concatenate([rpb.reshape(8, -1), np.full((8, 1), NEG, np.float32)], axis=1)
    sent = 15 * 31
    variants = [(0, (0, 1)), (0, (2, 3)), (0, (4, 5)), (54, (60, 61)), (54, (62, 63))]
    k = np.arange(640)
    ki = k // 64
    kc = k % 64
    q = np.arange(128)
    qc = q % 64
    idx = np.zeros((5, 640, 128), np.int64)
    for v, (u0, rows) in enumerate(variants):
        r = np.array(rows)[q // 64]
        rs = np.clip(r - 4, 0, 56)
        cs = np.clip(qc - 8, 0, 48)
        i = u0 + ki
        valid = (i[:, None] >= rs[None, :]) & (i[:, None] < rs[None, :] + 8) & (kc[:, None] >= cs[None, :]) & (kc[:, None] < cs[None, :] + 16)
        rr = i[:, None] - r[None, :] + 7
        cc = kc[:, None] - qc[None, :] + 15
        flat = np.clip(rr, 0, 14) * 31 + np.clip(cc, 0, 30)
        idx[v] = np.where(valid, flat, sent)
    tab = padded[:, idx]
    tab = tab.reshape(8, 5, 5, 128, 128).transpose(0, 1, 3, 2, 4)
    return np.ascontiguousarray(tab, dtype=np.float32)


def make_in_maps(inp):
    f = lambda a: np.ascontiguousarray(np.asarray(a), dtype=np.float32)
    x, c, ctx, c_ctx = f(inp["x"]), f(inp["c"]), f(inp["ctx"]), f(inp["c_ctx"])
    shared = {
        "ada_w": f(inp["ada_w"]),
        "ada_bT": f(f(inp["ada_b"]).reshape(2, 48, 128).transpose(2, 0, 1)),
        "norm_gT": f(f(inp["norm_g"]).reshape(2, 4, 8, 128).transpose(3, 0, 1, 2)),
        "w_in": f(inp["w_in"])[0],
        "w_out": f(inp["w_out"])[0],
        "rpbtab": _rpb_table(f(inp["na_rpb"])[0]),
        "conv_wT": f(f(inp["ssm_conv_w"])[0].reshape(3, 16, 128).transpose(2, 1, 0)),
        "conv_bT": f(f(inp["ssm_conv_b"])[0].reshape(16, 128).transpose(1, 0)),
        "ssm_rows": f(np.stack([f(inp["ssm_a_log"])[0].reshape(32), f(inp["ssm_dt_bias"])[0].reshape(32), f(inp["ssm_d"])[0].reshape(32)])),
        "ssm_norm_g": f(inp["ssm_norm_g"])[0],
        "pool_w": f(inp["pool_w"])[0],
        "pool_b": f(inp["pool_b"])[0].reshape(D),
        "pool_scale": f(inp["pool_scale"])[0],
        "pool_inv": _pool_inv(),
        "ffn_w_up": f(inp["ffn_w_up"]),
        "fconv_wT": f(f(inp["ffn_conv_w"]).reshape(2, 3, 22, 128).transpose(3, 0, 2, 1)),
        "fconv_bT": f(f(inp["ffn_conv_b"]).reshape(2, 22, 128).transpose(2, 0, 1)),
        "ffn_w_down": f(inp["ffn_w_down"]),
        "consts": _consts(),
    }
    maps = []
    for b in range(x.shape[0]):
        cc = np.stack([c[b].reshape(8, 128).T, c_ctx.reshape(8, 128).T], axis=2)
        m = dict(shared)
        m["x"] = f(x[b])
        m["ctx"] = f(ctx[b])
        m["cc"] = f(cc)
        maps.append(m)
    return maps


_NC_CACHE = {}


def kernel(**inputs):
    if "full" not in _NC_CACHE:
        p = Prog()
        _NC_CACHE["full"] = p.build()
        _NC_CACHE["names"] = [k for k in p.dram if k in Prog.SHAPES]
    nc = _NC_CACHE["full"]
    maps = make_in_maps(inputs)
    names = _NC_CACHE["names"]
    maps = [{k: m[k] for k in names} for m in maps]
    res = run_bass_kernel_spmd(nc, maps, core_ids=list(range(NCORES)))
    return np.stack([np.asarray(r["out"], dtype=np.float32) for r in res.results], axis=0)
```

```python
import numpy as np
from contextlib import ExitStack
import concourse.bass as bass
import concourse.mybir as mybir
from concourse.bass_utils import run_bass_kernel_spmd

F32 = mybir.dt.float32
BF16 = mybir.dt.bfloat16
ALU = mybir.AluOpType
AF = mybir.ActivationFunctionType

D = 1024
S = 4096
LC = 256
NCORES = 8
FH = 2816
EPS = 1e-6
NEG = -30000.0


class Res:
    __slots__ = ("w", "r", "name")

    def __init__(self, name=""):
        self.w = {}
        self.r = {}
        self.name = name


class KB:
    def __init__(self):
        nc = bass.Bass("TRN2", target_bir_lowering=False)
        self.nc = nc
        self.h = {"pe": nc.tensor, "act": nc.scalar, "dve": nc.vector, "pool": nc.gpsimd, "sp": nc.sync}
        self.sems = {}
        self.cnt = {}
        for e in ("pe", "act", "dve", "pool"):
            self.sems[e] = nc.alloc_semaphore(name="c_" + e)
            self.cnt[e] = 0
        self.dq = {"sp": [], "pool": [], "act": []}
        for q, n in (("sp", 28), ("pool", 20), ("act", 8)):
            for i in range(n):
                k = "d_%s%d" % (q, i)
                self.sems[k] = nc.alloc_semaphore(name=k)
                self.cnt[k] = 0
                self.dq[q].append(k)
        self.dqi = {"sp": 0, "pool": 0, "act": 0}
        self.sems["bar"] = nc.alloc_semaphore(name="bar")
        self.cnt["bar"] = 0
        self.waited = {e: {} for e in self.h}
        self.nres = 0

    def res(self, name=""):
        return Res(name)

    def _wait(self, E, need):
        for key, v in need.items():
            if key == E and E == "pe":
                continue
            if self.waited[E].get(key, 0) < v:
                self.h[E].wait_ge(self.sems[key], v)
                self.waited[E][key] = v

    @staticmethod
    def _merge(need, d):
        for k, v in d.items():
            if need.get(k, 0) < v:
                need[k] = v

    def op(self, E, fn, reads=(), writes=(), inc=True):
        need = {}
        for r in reads:
            self._merge(need, r.w)
        for w in writes:
            self._merge(need, w.w)
            self._merge(need, w.r)
        self._wait(E, need)
        ins = fn(self.h[E])
        if inc:
            self.cnt[E] += 1
            ins.then_inc(self.sems[E], 1)
            tv = self.cnt[E]
        else:
            tv = self.cnt[E] + 1
        for r in reads:
            if r.r.get(E, 0) < tv:
                r.r[E] = tv
        for w in writes:
            w.w = {E: tv}
            w.r = {}
        return ins

    def dma(self, Q, out, in_, reads=(), writes=(), **kw):
        need = {}
        for r in reads:
            self._merge(need, r.w)
        for w in writes:
            self._merge(need, w.w)
            self._merge(need, w.r)
        i = self.dqi[Q]
        self.dqi[Q] = i + 1
        key = self.dq[Q][i % len(self.dq[Q])]
        if self.cnt[key] > 0:
            need[key] = max(need.get(key, 0), self.cnt[key])
        self._wait(Q, need)
        ins = self.h[Q].dma_start(out=out, in_=in_, **kw)
        self.cnt[key] += 16
        ins.then_inc(self.sems[key], 16)
        tv = self.cnt[key]
        for r in reads:
            r.r[key] = tv
        for w in writes:
            w.w = {key: tv}
            w.r = {}
        return ins

    def barrier(self):
        need = {k: v for k, v in self.cnt.items() if k != "bar" and v > 0}
        self._wait("sp", need)
        self.cnt["bar"] += 1
        self.h["sp"].sem_inc(self.sems["bar"], 1)
        for e in ("pe", "act", "dve", "pool"):
            self.h[e].wait_ge(self.sems["bar"], self.cnt["bar"])
            self.waited[e]["bar"] = self.cnt["bar"]
            for k, v in need.items():
                self.waited[e][k] = max(self.waited[e].get(k, 0), v)


def _r(ap, pat, **kw):
    return ap.rearrange(pat, **kw)


class Prog:
    def __init__(self, dumps=(), stop_after=None, mode="full", max_tiles=None):
        self.mode = mode
        self.max_tiles = max_tiles
        self.kb = KB()
        self.nc = self.kb.nc
        self.dumps = set(dumps)
        self.stop_after = stop_after
        self.dram = {}

    def din(self, name, shape, dt=F32):
        t = self.nc.dram_tensor(name, list(shape), dt, kind="ExternalInput").ap()
        self.dram[name] = t
        return t

    SHAPES = {
        "x": [S, D], "ctx": [LC, D], "cc": [128, 8, 2], "ada_w": [2, D, 6 * D], "ada_bT": [128, 2, 48],
        "norm_gT": [128, 2, 4, 8], "w_in": [D, 4640], "w_out": [1536, D], "rpbtab": [8, 5, 128, 5, 128],
        "conv_wT": [128, 16, 3], "conv_bT": [128, 16], "ssm_rows": [3, 32], "ssm_norm_g": [D],
        "pool_w": [4, 256, 256], "pool_b": [D], "pool_scale": [D], "pool_inv": [2, 4, 256],
        "ffn_w_up": [2, D, 2 * FH], "fconv_wT": [128, 2, 22, 3], "fconv_bT": [128, 2, 22], "ffn_w_down": [2, FH, D],
        "consts": [128, 6, 128],
    }

    def I(self, name):
        if name not in self.dram:
            self.din(name, self.SHAPES[name])
        return self.dram[name]

    def dscr(self, name, shape, dt=F32, out=False):
        kind = "ExternalOutput" if (out or name in self.dumps) else "Internal"
        t = self.nc.dram_tensor(name, list(shape), dt, kind=kind).ap()
        self.dram[name] = t
        return t

    def sb(self, es, name, shape, dt):
        self.kb.nres += 1
        return es.enter_context(self.nc.sbuf_tensor("s%d_%s" % (self.kb.nres, name), list(shape), dt))

    def build(self):
        nc, kb = self.nc, self.kb
        x_in = self.I("x")
        ctx_in = self.I("ctx") if self.mode == "full" else None
        cc_in = self.I("cc")
        ada_w = self.I("ada_w")
        ada_bT = self.I("ada_bT")
        norm_gT = self.I("norm_gT")
        consts_in = self.I("consts")
        out = self.dscr("out", [S, D], out=True)
        xa = self.dscr("xa", [S, D])
        xb = self.dscr("xb", [S, D])
        xc = self.dscr("xc", [S, D])
        self.vec = self.dscr("vec", [2, 2, D])

        with ExitStack() as g:
            self.cst = self.sb(g, "cst", [128, 6, 128], F32)
            self.cstb = self.sb(g, "cstb", [128, 6, 128], BF16)
            self.modv = self.sb(g, "modv", [128, 2, 4, 8, 2], F32)
            self.epsb = self.sb(g, "epsb", [128, 1], F32)
            self.r_cst = kb.res("cst")
            self.r_modv = kb.res("modv")
            self.psA = [g.enter_context(nc.psum_tensor("psA%d" % i, [128, 1024], F32)) for i in range(2)]
            self.psB = [g.enter_context(nc.psum_tensor("psB%d" % i, [128, 512], F32)) for i in range(4)]
            self.r_psA = [kb.res("psA%d" % i) for i in range(2)]
            self.r_psB = [kb.res("psB%d" % i) for i in range(4)]

            kb.dma("sp", self.cst[:], consts_in, writes=[self.r_cst])
            kb.op("dve", lambda e: e.tensor_copy(out=self.cstb[:], in_=self.cst[:]), reads=[self.r_cst], writes=[self.r_cst])
            kb.op("pool", lambda e: e.memset(self.epsb[:], EPS), writes=[self.r_cst])
            self.identb = self.cstb[:, 0, :]

            self.phase_adaln(cc_in, ada_w, ada_bT, norm_gT)
            kb.barrier()
            if self.stop_after == "adaln":
                return self.finish()
            if self.mode == "l1":
                self.phase_pool(x_in, xc)
                kb.barrier()
                self.phase_ffn(1, xc, out)
                return self.finish()
            if self.mode == "ffn0":
                self.phase_ffn(0, x_in, out)
                return self.finish()
            self.phase_l0_mixer(x_in, ctx_in, xa)
            kb.barrier()
            if self.stop_after in ("proj", "ssd", "att", "l0mix"):
                return self.finish()
            self.phase_ffn(0, xa, xb)
            kb.barrier()
            if self.stop_after == "l0ffn":
                return self.finish()
            self.phase_pool(xb, xc)
            kb.barrier()
            if self.stop_after == "l1mix":
                return self.finish()
            self.phase_ffn(1, xc, out)
            kb.barrier()
        return self.finish()

    def finish(self):
        self.kb.barrier()
        return self.nc

    def phase_adaln(self, cc_in, ada_w, ada_bT, norm_gT):
        nc, kb = self.nc, self.kb
        with ExitStack() as es:
            cc = self.sb(es, "cc", [128, 8, 2], F32)
            sg = self.sb(es, "sg", [128, 8, 2], F32)
            scc = self.sb(es, "scc", [128, 8, 2], F32)
            abT = self.sb(es, "abT", [128, 2, 48], F32)
            ngT = self.sb(es, "ngT", [128, 2, 4, 8], F32)
            mod = self.sb(es, "mod", [128, 2, 48, 2], F32)
            gv = self.sb(es, "gv", [128, 2, 2, 8], F32)
            wbuf = [self.sb(es, "adw%d" % i, [128, 8, 768], F32) for i in range(2)]
            r_w = [[kb.res(), kb.res()] for _ in range(2)]
            r_cc, r_small, r_mod, r_gv = kb.res(), kb.res(), kb.res(), kb.res()
            kb.dma("sp", cc[:], cc_in, writes=[r_cc])
            kb.dma("sp", abT[:], ada_bT, writes=[r_small])
            kb.dma("sp", ngT[:], norm_gT, writes=[r_small])
            kb.op("act", lambda e: e.activation(out=sg[:], in_=cc[:], func=AF.Sigmoid), reads=[r_cc], writes=[r_gv])
            kb.op("dve", lambda e: e.tensor_tensor(out=scc[:], in0=cc[:], in1=sg[:], op=ALU.mult), reads=[r_cc, r_gv], writes=[r_cc])
            it = 0
            for l in range(2):
                for jb in range(8):
                    wb, rw = wbuf[it % 2], r_w[it % 2]
                    for half in range(2):
                        kb.dma("sp", wb[:, half * 4:(half + 1) * 4, :],
                               _r(ada_w[l, half * 512:(half + 1) * 512, jb * 768:(jb + 1) * 768], "(k p) n -> p k n", p=128),
                               writes=[rw[half]])
                    ps = self.psB[it % 2]
                    rps = self.r_psB[it % 2]
                    it += 1
                    self._ada_mm(wb, rw, scc, r_cc, ps, rps, mod, r_mod, abT, r_small, l, jb)
            for l in range(2):
                for i, (sci, shi, gi) in enumerate(((8, 0, 0), (32, 24, 2))):
                    kb.op("dve", lambda e, l=l, i=i, sci=sci, gi=gi: e.scalar_tensor_tensor(
                        out=self.modv[:, l, 2 * i, :, :], in0=mod[:, l, sci:sci + 8, :], scalar=1.0,
                        in1=ngT[:, l, gi, :].unsqueeze(2).to_broadcast([128, 8, 2]), op0=ALU.add, op1=ALU.mult),
                        reads=[r_mod, r_small], writes=[self.r_modv])
                    kb.op("dve", lambda e, l=l, i=i, shi=shi: e.tensor_copy(
                        out=self.modv[:, l, 2 * i + 1, :, :], in_=mod[:, l, shi:shi + 8, :]),
                        reads=[r_mod], writes=[self.r_modv])
                for i, (gti, gpi) in enumerate(((16, 1), (40, 3))):
                    kb.op("dve", lambda e, l=l, i=i, gti=gti, gpi=gpi: e.tensor_tensor(
                        out=gv[:, l, i, :], in0=mod[:, l, gti:gti + 8, 0], in1=ngT[:, l, gpi, :], op=ALU.mult),
                        reads=[r_mod, r_small], writes=[r_gv])
            for l in range(2):
                for i in range(2):
                    kb.dma("pool", _r(self.vec[l, i], "(j p) -> p j", p=128), gv[:, l, i, :], reads=[r_gv],
                           allow_slow_non_contiguous=True)

    def _ada_mm(self, wb, rw, scc, r_cc, ps, rps, mod, r_mod, abT, r_small, l, jb):
        kb = self.kb
        for jj in range(6):
            for kc in range(8):
                kb.op("pe", lambda e, jj=jj, kc=kc: e.matmul(ps[:, jj * 2:jj * 2 + 2], lhsT=wb[:, kc, jj * 128:(jj + 1) * 128],
                                                         rhs=scc[:, kc, :], start=(kc == 0), stop=(kc == 7)),
                      reads=[rw[kc // 4], r_cc], writes=[rps], inc=(jj == 5 and kc == 7))
        kb.op("dve", lambda e: e.tensor_tensor(
            out=mod[:, l, jb * 6:(jb + 1) * 6, :], in0=_r(ps[:, 0:12], "p (j n) -> p j n", n=2),
            in1=abT[:, l, jb * 6:(jb + 1) * 6].unsqueeze(2).to_broadcast([128, 6, 2]), op=ALU.add),
            reads=[rps, r_small], writes=[r_mod])

    def alloc_norm_bufs(self, es, nsub, nh):
        nhp = max(nh, 2)
        self.r_nt = [self.kb.res() for _ in range(6)]
        return dict(junk=self.sb(es, "n_junk", [128, D], BF16), xn=self.sb(es, "n_xn", [128, nsub, D], BF16),
                    ss=self.sb(es, "n_ss", [128, 4], F32), rstd=self.sb(es, "n_rstd", [128, 4], F32),
                    xnh=self.sb(es, "n_xnh", [nhp, D], BF16), ssh=self.sb(es, "n_ssh", [nhp, 1], F32),
                    rstdh=self.sb(es, "n_rstdh", [nhp, 1], F32))

    def norm_T(self, nb, xt, r_xts, nsub, xh, r_xh, nh, halo, hT, r_hT, main0, hl0, hr0, l, mi, col, psT, r_psT, psH, r_psH):
        kb = self.kb
        junk, xn, ss, rstd, xnh, ssh, rstdh = (nb[k] for k in ("junk", "xn", "ss", "rstd", "xnh", "ssh", "rstdh"))
        rt = self.r_nt
        ntok = nsub * 128
        for s in range(nsub):
            kb.op("act", lambda e, s=s: e.activation(out=junk[:], in_=xt[:, s, :], func=AF.Square, accum_out=ss[:, s:s + 1]),
                  reads=[r_xts[s]], writes=[rt[0]])
        kb.op("act", lambda e: e.activation(out=rstd[:, 0:nsub], in_=ss[:, 0:nsub], func=AF.Sqrt, bias=self.epsb[:], scale=1.0 / D),
              reads=[rt[0], self.r_cst], writes=[rt[1]])
        kb.op("dve", lambda e: e.reciprocal(out=rstd[:, 0:nsub], in_=rstd[:, 0:nsub]), reads=[rt[1]], writes=[rt[1]])
        for s in range(nsub):
            kb.op("dve", lambda e, s=s: e.tensor_scalar(out=xn[:, s, :], in0=xt[:, s, :], scalar1=rstd[:, s:s + 1], scalar2=None, op0=ALU.mult),
                  reads=[r_xts[s], rt[1]], writes=[rt[2]])
        A = self.modv[:, l, mi, :, col]
        B = self.modv[:, l, mi + 1, :, col]
        for kc in range(8):
            p = psT[kc % 2]
            rp = r_psT[kc % 2]
            for s in range(nsub):
                kb.op("pe", lambda e, s=s, kc=kc, p=p: e.matmul(p[:, s * 128:(s + 1) * 128], lhsT=xn[:, s, kc * 128:(kc + 1) * 128],
                                                             rhs=self.identb, start=True, stop=True),
                      reads=[rt[2], self.r_cst], writes=[rp], inc=(s == nsub - 1))
            kb.op("act", lambda e, kc=kc, p=p: e.activation(out=hT[:, kc, main0:main0 + ntok], in_=p[:, 0:ntok], func=AF.Identity,
                                                          bias=B[:, kc:kc + 1], scale=A[:, kc:kc + 1]),
                  reads=[rp, self.r_modv], writes=[r_hT])
        if nh == 0:
            return
        lv, rv = halo
        hh = nh // 2
        if lv or rv:
            kb.op("act", lambda e: e.activation(out=junk[0:nh, :], in_=xh[0:nh, :], func=AF.Square, accum_out=ssh[0:nh, 0:1]),
                  reads=[r_xh], writes=[rt[3]])
            kb.op("act", lambda e: e.activation(out=rstdh[0:nh, :], in_=ssh[0:nh, :], func=AF.Sqrt, bias=self.epsb[0:nh, :], scale=1.0 / D),
                  reads=[rt[3], self.r_cst], writes=[rt[4]])
            kb.op("dve", lambda e: e.reciprocal(out=rstdh[0:nh, :], in_=rstdh[0:nh, :]), reads=[rt[4]], writes=[rt[4]])
            kb.op("dve", lambda e: e.tensor_scalar(out=xnh[0:nh, :], in0=xh[0:nh, :], scalar1=rstdh[0:nh, 0:1], scalar2=None, op0=ALU.mult),
                  reads=[r_xh, rt[4]], writes=[rt[5]])
            for kc in range(8):
                kb.op("pe", lambda e, kc=kc: e.matmul(psH[:, kc * nh:(kc + 1) * nh], lhsT=xnh[0:nh, kc * 128:(kc + 1) * 128],
                                                   rhs=self.cstb[0:nh, 0, 0:nh], start=True, stop=True),
                      reads=[rt[5], self.r_cst], writes=[r_psH], inc=(kc == 7))
            for kc in range(8):
                for side, c0 in ((0, hl0), (1, hr0)):
                    kb.op("dve", lambda e, kc=kc, side=side, c0=c0: e.tensor_scalar(
                        out=hT[:, kc, c0:c0 + hh], in0=psH[:, kc * nh + side * hh:kc * nh + (side + 1) * hh],
                        scalar1=A[:, kc:kc + 1], scalar2=B[:, kc:kc + 1], op0=ALU.mult, op1=ALU.add),
                        reads=[r_psH, self.r_modv], writes=[r_hT])
        if not lv:
            kb.op("dve", lambda e: e.memset(hT[:, :, hl0:hl0 + hh], 0.0), writes=[r_hT])
        if not rv:
            kb.op("dve", lambda e: e.memset(hT[:, :, hr0:hr0 + hh], 0.0), writes=[r_hT])

    def load_rows_bcast(self, es, name, src_row, n):
        t = self.sb(es, name, [128, n], F32)
        r = self.kb.res(name)
        self.kb.dma("sp", t[:], src_row.partition_broadcast(128), writes=[r])
        return t, r

    def resid_epilogue(self, y, r_y, xt_s, r_xt, Grow, r_G, tmp, r_tmp, ss2, r_ss2, junk, r_junk):
        kb = self.kb
        for hb in range(2):
            kb.op("act", lambda e, hb=hb: e.activation(out=junk[:, 0:512], in_=y[:, hb * 512:(hb + 1) * 512], func=AF.Square,
                                                     accum_out=ss2[:, hb:hb + 1]), reads=r_y, writes=[r_ss2[0], r_junk])
        kb.op("dve", lambda e: e.tensor_tensor(out=ss2[:, 2:3], in0=ss2[:, 0:1], in1=ss2[:, 1:2], op=ALU.add), reads=[r_ss2[0]], writes=[r_ss2[1]])
        kb.op("act", lambda e: e.activation(out=ss2[:, 3:4], in_=ss2[:, 2:3], func=AF.Sqrt, bias=self.epsb[:], scale=1.0 / D),
              reads=[r_ss2[1], self.r_cst], writes=[r_ss2[2]])
        kb.op("dve", lambda e: e.reciprocal(out=ss2[:, 3:4], in_=ss2[:, 3:4]), reads=[r_ss2[2]], writes=[r_ss2[2]])
        kb.op("dve", lambda e: e.scalar_tensor_tensor(out=tmp[:], in0=y, scalar=ss2[:, 3:4], in1=Grow[:], op0=ALU.mult, op1=ALU.mult),
              reads=r_y + [r_ss2[2], r_G], writes=[r_tmp])
        kb.op("pool", lambda e: e.tensor_tensor(out=xt_s, in0=xt_s, in1=tmp[:], op=ALU.add), reads=[r_tmp, r_xt], writes=[r_xt])

    def phase_ffn(self, l, src, dst):
        nc, kb = self.nc, self.kb
        T = 256
        NT = S // T
        with ExitStack() as es:
            Wup = self.sb(es, "Wup", [128, 8, 2 * FH], BF16)
            Wdn = self.sb(es, "Wdn", [128, 22, D], BF16)
            r_wup = [kb.res() for _ in range(8)]
            r_wdn = [kb.res() for _ in range(2)]
            for kc in range(8):
                kb.dma("pool", Wup[:, kc, :], self.I("ffn_w_up")[l, kc * 128:(kc + 1) * 128, :], writes=[r_wup[kc]])
            for hh in range(2):
                kb.dma("pool", Wdn[:, hh * 11:(hh + 1) * 11, :], _r(self.I("ffn_w_down")[l, hh * 1408:(hh + 1) * 1408, :], "(j p) d -> p j d", p=128),
                       writes=[r_wdn[hh]])
            cw = self.sb(es, "f_cw", [128, 22, 3], F32)
            cb = self.sb(es, "f_cb", [128, 22], F32)
            r_cw = [kb.res(), kb.res()]
            kb.dma("sp", cw[:], self.I("fconv_wT")[:, l], writes=[r_cw[0]])
            kb.dma("sp", cb[:], self.I("fconv_bT")[:, l], writes=[r_cw[1]])
            Grow, r_G = self.load_rows_bcast(es, "f_G", self.vec[l, 1], D)
            nb = self.alloc_norm_bufs(es, 2, 2)
            xt = [self.sb(es, "f_xt%d" % i, [128, 2, D], F32) for i in range(2)]
            r_xt = [[kb.res() for _ in range(2)] for _ in range(2)]
            xh = [self.sb(es, "f_xh%d" % i, [2, D], F32) for i in range(2)]
            r_xh = [kb.res() for _ in range(2)]
            hT = [self.sb(es, "f_hT%d" % i, [128, 8, T + 2], BF16) for i in range(2)]
            r_hT = [kb.res() for _ in range(2)]
            ub = [self.sb(es, "f_ub%d" % i, [128, T + 2], F32) for i in range(2)]
            acc = [self.sb(es, "f_acc%d" % i, [128, T], F32) for i in range(2)]
            gl = [self.sb(es, "f_gl%d" % i, [128, T], F32) for i in range(2)]
            r_ub = [kb.res() for _ in range(2)]
            r_acc = [kb.res() for _ in range(2)]
            r_gl = [kb.res() for _ in range(2)]
            gT = self.sb(es, "f_gT", [128, 22, T], BF16)
            r_gT = [kb.res() for _ in range(22)]
            ss2 = [self.sb(es, "f_ss2%d" % i, [128, 4], F32) for i in range(2)]
            r_ss2 = [[kb.res() for _ in range(3)] for _ in range(2)]
            junk2 = self.sb(es, "f_junk2", [128, 512], BF16)
            r_junk2 = kb.res()
            psT = [self.psB[2], self.psB[3]]
            r_psT = [self.r_psB[2], self.r_psB[3]]
            psH = self.psB[3][:, 256:272]
            r_psH = self.r_psB[3]
            bankU = [self.psB[0], self.psA[1][:, 0:512]]
            bankV = [self.psB[1], self.psA[1][:, 512:1024]]
            psU = [bk[:, 0:T] for bk in bankU]
            psUh = [bk[:, T:T + 2] for bk in bankU]
            psV = [bk[:, 0:T] for bk in bankV]
            r_psU = [self.r_psB[0], kb.res()]
            r_psV = [self.r_psB[1], kb.res()]
            r_psUh = r_psU
            for b in range(2):
                kb.op("dve", lambda e, b=b: e.memset(xh[b][:], 0.0), writes=[r_xh[b]])

            def load(t):
                b = t % 2
                t0 = t * T
                for s in range(2):
                    kb.dma("sp", xt[b][:, s, :], src[t0 + s * 128:t0 + (s + 1) * 128, :], writes=[r_xt[b][s]])
                if 0 < t < NT - 1:
                    kb.dma("sp", xh[b][0:2, :], src[t0 - 1:t0 + T + 1:T + 1, :], writes=[r_xh[b]])
                elif t > 0:
                    kb.dma("sp", xh[b][0:1, :], src[t0 - 1:t0, :], writes=[r_xh[b]])
                else:
                    kb.dma("sp", xh[b][1:2, :], src[t0 + T:t0 + T + 1, :], writes=[r_xh[b]])

            ysb = [self.sb(es, "f_ysb%d" % i, [128, D], F32) for i in range(2)]
            r_ysb = [kb.res() for _ in range(2)]
            NTR = NT if self.max_tiles is None else self.max_tiles

            def do_norm(t):
                b = t % 2
                self.norm_T(nb, xt[b], r_xt[b], 2, xh[b], r_xh[b], 2, (t > 0, t < NT - 1), hT[b], r_hT[b], 1, 0, T + 1,
                            l, 2, 0, psT, r_psT, psH, r_psH)

            load(0)
            do_norm(0)
            for t in range(NTR):
                b = t % 2
                t0 = t * T
                if t + 1 < NT:
                    load(t + 1)
                for j in range(23):
                    pb = j % 2
                    qb = (j - 1) % 2
                    if j < 22:
                        for (pp, rr, c0, n0, n1) in ((bankU[pb][:, 0:T + 2], r_psU[pb], j * 128, 0, T + 2),
                                                     (psV[pb], r_psV[pb], FH + j * 128, 1, T + 1)):
                            for kc in range(8):
                                kb.op("pe", lambda e, kc=kc, pp=pp, c0=c0, n0=n0, n1=n1: e.matmul(
                                    pp, lhsT=Wup[:, kc, c0:c0 + 128], rhs=hT[b][:, kc, n0:n1], start=(kc == 0), stop=(kc == 7)),
                                    reads=[r_wup[kc], r_hT[b]], writes=[rr], inc=(kc == 7))
                    if j >= 1:
                        kb.op("act", lambda e, qb=qb: e.activation(out=gl[qb][:], in_=acc[qb][:], func=AF.Gelu), reads=[r_acc[qb]], writes=[r_gl[qb]])
                    if j < 22:
                        a = acc[pb]
                        bu = bankU[pb]
                        kb.op("act", lambda e, a=a, bu=bu, j=j: e.activation(out=a[:], in_=bu[:, 1:T + 1], func=AF.Identity,
                                                                           bias=cb[:, j:j + 1], scale=cw[:, j, 1:2]),
                              reads=[r_psU[pb]] + r_cw, writes=[r_acc[pb]])
                    if j >= 1:
                        kb.op("dve", lambda e, j=j, qb=qb: e.tensor_tensor(out=gT[:, j - 1, :], in0=gl[qb][:], in1=psV[qb], op=ALU.mult),
                              reads=[r_gl[qb], r_psV[qb]], writes=[r_gT[j - 1]])
                    if j < 22:
                        kb.op("dve", lambda e, bu=bu, a=a, j=j: e.scalar_tensor_tensor(out=a[:], in0=bu[:, 0:T], scalar=cw[:, j, 0:1], in1=a[:],
                                                                                    op0=ALU.mult, op1=ALU.add), reads=[r_psU[pb], r_acc[pb]] + r_cw, writes=[r_acc[pb]])
                        kb.op("dve", lambda e, bu=bu, a=a, j=j: e.scalar_tensor_tensor(out=a[:], in0=bu[:, 2:T + 2], scalar=cw[:, j, 2:3], in1=a[:],
                                                                                    op0=ALU.mult, op1=ALU.add), reads=[r_psU[pb], r_acc[pb]] + r_cw, writes=[r_acc[pb]])
                    if j == 8 and t + 1 < NTR:
                        do_norm(t + 1)
                for s in range(2):
                    py = self.psA[0]
                    rpy = self.r_psA[0]
                    for hb in range(2):
                        for j in range(22):
                            kb.op("pe", lambda e, j=j, s=s, hb=hb: e.matmul(py[:, hb * 512:(hb + 1) * 512], lhsT=gT[:, j, s * 128:(s + 1) * 128],
                                                                          rhs=Wdn[:, j, hb * 512:(hb + 1) * 512], start=(j == 0), stop=(j == 21)),
                                  reads=[r_gT[j], r_wdn[j // 11]], writes=[rpy], inc=(j == 21))
                    kb.op("act", lambda e, s=s: e.activation(out=ysb[s][:], in_=py[:, :], func=AF.Identity), reads=[rpy], writes=[r_ysb[s]])
                    self.resid_epilogue(ysb[s][:], [r_ysb[s]], xt[b][:, s, :], r_xt[b][s], Grow, r_G, ysb[s], r_ysb[s], ss2[s], r_ss2[s], junk2, r_junk2)
                    kb.dma("pool", dst[t0 + s * 128:t0 + (s + 1) * 128, :], xt[b][:, s, :], reads=[r_xt[b][s]])

    def phase_pool(self, src, dst):
        nc, kb = self.nc, self.kb
        T = 256
        NT = S // T
        HH = 8
        W = T + 2 * HH
        l = 1
        with ExitStack() as es:
            PW = self.sb(es, "p_PW", [128, 4, 2, 256], BF16)
            r_pw = kb.res()
            kb.dma("pool", PW[:], _r(self.I("pool_w"), "g (k p) n -> p g k n", p=128), writes=[r_pw])
            pbrow, r_pb = self.load_rows_bcast(es, "p_pb", self.I("pool_b"), D)
            psrow, r_psr = self.load_rows_bcast(es, "p_ps", self.I("pool_scale"), D)
            Grow, r_G = self.load_rows_bcast(es, "p_G", self.vec[l, 0], D)
            inv = self.sb(es, "p_inv", [128, 2, 4, 256], F32)
            r_inv = kb.res()
            kb.dma("sp", inv[:], _r(self.I("pool_inv"), "a g t -> (a g t)").partition_broadcast(128), writes=[r_inv])
            nb = self.alloc_norm_bufs(es, 2, 2 * HH)
            xt = [self.sb(es, "p_xt%d" % i, [128, 2, D], F32) for i in range(2)]
            r_xt = [[kb.res() for _ in range(2)] for _ in range(2)]
            xh = [self.sb(es, "p_xh%d" % i, [2 * HH, D], F32) for i in range(2)]
            r_xh = [kb.res() for _ in range(2)]
            hT = self.sb(es, "p_hT", [128, 8, W], F32)
            r_hT = kb.res()
            bufA = self.sb(es, "p_bA", [128, 2, W], F32)
            bufB = self.sb(es, "p_bB", [128, 2, W], F32)
            r_bA, r_bB = kb.res(), kb.res()
            pl = self.sb(es, "p_pl", [128, 8, T], BF16)
            r_pl = [kb.res() for _ in range(4)]
            ysb = [self.sb(es, "p_ysb%d" % i, [128, D], F32) for i in range(2)]
            r_ysb = [kb.res() for _ in range(2)]
            tmp = [self.sb(es, "p_tmp%d" % i, [128, D], F32) for i in range(2)]
            r_tmp = [kb.res() for _ in range(2)]
            ss2 = [self.sb(es, "p_ss2%d" % i, [128, 4], F32) for i in range(2)]
            r_ss2 = [[kb.res() for _ in range(3)] for _ in range(2)]
            junk2 = self.sb(es, "p_junk2", [128, 512], BF16)
            r_junk2 = kb.res()
            psT = [self.psB[2], self.psB[3]]
            r_psT = [self.r_psB[2], self.r_psB[3]]
            psH = self.psB[1][:, 0:128]
            r_psH = self.r_psB[1]
            for b in range(2):
                kb.op("dve", lambda e, b=b: e.memset(xh[b][:], 0.0), writes=[r_xh[b], self._rxh2(b)])

            def load(t):
                b = t % 2
                t0 = t * T
                for s in range(2):
                    kb.dma("sp", xt[b][:, s, :], src[t0 + s * 128:t0 + (s + 1) * 128, :], writes=[r_xt[b][s]])
                if 0 < t < NT - 1:
                    kb.dma("sp", xh[b][0:HH, :], src[t0 - HH:t0, :], writes=[r_xh[b]])
                    kb.dma("sp", xh[b][HH:2 * HH, :], src[t0 + T:t0 + T + HH, :], writes=[self._rxh2(b)])
                elif t > 0:
                    kb.dma("sp", xh[b][0:HH, :], src[t0 - HH:t0, :], writes=[r_xh[b]])
                else:
                    kb.dma("sp", xh[b][HH:2 * HH, :], src[t0 + T:t0 + T + HH, :], writes=[self._rxh2(b)])

            load(0)
            for t in range(NT):
                b = t % 2
                t0 = t * T
                if t + 1 < NT:
                    load(t + 1)
                rxh = Res()
                kb._merge(rxh.w, r_xh[b].w)
                kb._merge(rxh.w, self._rxh2(b).w)
                self.norm_T(nb, xt[b], r_xt[b], 2, xh[b], rxh, 2 * HH, (t > 0, t < NT - 1), hT, r_hT, HH, 0, HH + T,
                            l, 0, 0, psT, r_psT, psH, r_psH)
                kb._merge(r_xh[b].r, rxh.r)
                kb._merge(self._rxh2(b).r, rxh.r)
                for g in range(4):
                    hg = hT[:, 2 * g:2 * g + 2, :]
                    kb.op("dve", lambda e, hg=hg: e.tensor_tensor(out=bufA[:, :, 1:W], in0=hg[:, :, 0:W - 1], in1=hg[:, :, 1:W], op=ALU.add),
                          reads=[r_hT], writes=[r_bA])
                    cur, rcur, oth, roth = bufA, r_bA, bufB, r_bB
                    lo, hi = 1, W
                    for lvl in range(1, g + 1):
                        sh = 1 << (lvl - 1)
                        nlo, nhi = lo + sh, hi - sh
                        kb.op("dve", lambda e, cur=cur, oth=oth, sh=sh, nlo=nlo, nhi=nhi: e.tensor_tensor(
                            out=oth[:, :, nlo:nhi], in0=cur[:, :, nlo - sh:nhi - sh], in1=cur[:, :, nlo + sh:nhi + sh], op=ALU.add),
                            reads=[rcur], writes=[roth])
                        cur, rcur, oth, roth = oth, roth, cur, rcur
                        lo, hi = nlo, nhi
                    w = 2 << g
                    if 0 < t < NT - 1:
                        kb.op("dve", lambda e, cur=cur, hg=hg, g=g, w=w: e.scalar_tensor_tensor(
                            out=pl[:, 2 * g:2 * g + 2, :], in0=cur[:, :, HH:HH + T], scalar=1.0 / w, in1=hg[:, :, HH:HH + T],
                            op0=ALU.mult, op1=ALU.subtract), reads=[rcur, r_hT], writes=[r_pl[g]])
                    else:
                        a = 0 if t == 0 else 1
                        kb.op("dve", lambda e, cur=cur, oth=oth, g=g, a=a: e.tensor_tensor(
                            out=oth[:, :, HH:HH + T], in0=cur[:, :, HH:HH + T], in1=inv[:, a, g, :].unsqueeze(1).to_broadcast([128, 2, T]),
                            op=ALU.mult), reads=[rcur, r_inv], writes=[roth])
                        kb.op("dve", lambda e, oth=oth, hg=hg, g=g: e.tensor_tensor(
                            out=pl[:, 2 * g:2 * g + 2, :], in0=oth[:, :, HH:HH + T], in1=hg[:, :, HH:HH + T], op=ALU.subtract),
                            reads=[roth, r_hT], writes=[r_pl[g]])
                for s in range(2):
                    py = self.psA[s]
                    rpy = self.r_psA[s]
                    for g in range(4):
                        for kk in range(2):
                            kb.op("pe", lambda e, g=g, kk=kk, s=s, py=py: e.matmul(py[:, g * 256:(g + 1) * 256], lhsT=pl[:, 2 * g + kk, s * 128:(s + 1) * 128],
                                                                                 rhs=PW[:, g, kk, :], start=(kk == 0), stop=(kk == 1)),
                                  reads=[r_pl[g], r_pw], writes=[rpy], inc=(g == 3 and kk == 1))
                    kb.op("dve", lambda e, s=s, py=py: e.tensor_tensor(out=ysb[s][:], in0=py[:, :], in1=pbrow[:], op=ALU.add),
                          reads=[rpy, r_pb], writes=[r_ysb[s]])
                    kb.op("pool", lambda e, s=s: e.tensor_tensor(out=ysb[s][:], in0=ysb[s][:], in1=psrow[:], op=ALU.mult),
                          reads=[r_ysb[s], r_psr], writes=[r_ysb[s]])
                    self.resid_epilogue(ysb[s][:], [r_ysb[s]], xt[b][:, s, :], r_xt[b][s], Grow, r_G, ysb[s], r_ysb[s], ss2[s], r_ss2[s], junk2, r_junk2)
                    kb.dma("pool", dst[t0 + s * 128:t0 + (s + 1) * 128, :], xt[b][:, s, :], reads=[r_xt[b][s]])

    def _rxh2(self, b):
        if not hasattr(self, "_rxh2_l"):
            self._rxh2_l = [self.kb.res(), self.kb.res()]
        return self._rxh2_l[b]

    def phase_l0_mixer(self, x_in, ctx_in, dst):
        kb = self.kb
        ST = S + LC
        self.qT = self.dscr("qT", [512, S], BF16)
        self.kT = self.dscr("kT", [512, ST], BF16)
        self.v_tok = self.dscr("v_tok", [ST, 512], BF16)
        self.sz_tok = self.dscr("sz_tok", [S, D], BF16)
        self.xs_tok = self.dscr("xs_tok", [ST, D], BF16)
        self.B_tok = self.dscr("B_tok", [ST, 512], BF16)
        self.BT = self.dscr("BT", [512, ST], BF16)
        self.CT = self.dscr("CT", [512, ST], BF16)
        self.dt_tok = self.dscr("dt_tok", [ST, 32], F32)
        self.yf = self.dscr("yf", [S, D], F32)
        self.ssm_tok = self.dscr("ssm_tok", [S, D], BF16)
        self.att_tok = self.dscr("att_tok", [S, 512], BF16)
        self.l0_proj(x_in, ctx_in)
        kb.barrier()
        if self.stop_after == "proj":
            return
        self.l0_ssd(0)
        kb.barrier()
        self.l0_ssd(1)
        kb.barrier()
        if self.stop_after == "ssd":
            return
        self.l0_att()
        kb.barrier()
        if self.stop_after == "att":
            return
        self.l0_out(x_in, dst)

    def l0_proj(self, x_in, ctx_in):
        kb = self.kb
        T = 256
        NT = S // T
        w_in = self.I("w_in")
        with ExitStack() as es:
            W = self.sb(es, "Win", [128, 8, 4640], BF16)
            r_w = [kb.res() for _ in range(8)]
            for kc in range(8):
                kb.dma("pool", W[:, kc, :], w_in[kc * 128:(kc + 1) * 128, :], writes=[r_w[kc]])
            cw = self.sb(es, "c_cw", [128, 16, 3], F32)
            cb = self.sb(es, "c_cb", [128, 16], F32)
            r_cw = [kb.res(), kb.res()]
            kb.dma("sp", cw[:], self.I("conv_wT"), writes=[r_cw[0]])
            kb.dma("sp", cb[:], self.I("conv_bT"), writes=[r_cw[1]])
            nb = self.alloc_norm_bufs(es, 2, 2)
            xt = [self.sb(es, "j_xt%d" % i, [128, 2, D], F32) for i in range(2)]
            r_xt = [[kb.res() for _ in range(2)] for _ in range(2)]
            xh = [self.sb(es, "j_xh%d" % i, [2, D], F32) for i in range(2)]
            r_xh = [kb.res() for _ in range(2)]
            hTs = [self.sb(es, "j_hT%d" % i, [128, 8, T + 2], BF16) for i in range(2)]
            r_hTs = [kb.res() for _ in range(2)]
            ub = [self.sb(es, "j_ub%d" % i, [128, T + 2], F32) for i in range(2)]
            acc = [self.sb(es, "j_acc%d" % i, [128, T], F32) for i in range(2)]
            sx = [self.sb(es, "j_sx%d" % i, [128, T], BF16) for i in range(2)]
            r_ub = [kb.res() for _ in range(2)]
            r_acc = [kb.res() for _ in range(2)]
            r_sx = [kb.res() for _ in range(2)]
            fst = [self.sb(es, "j_fst%d" % i, [128, T], BF16) for i in range(2)]
            r_fst = [kb.res() for _ in range(2)]
            xs_st = self.sb(es, "j_xsst", [128, 2, D], BF16)
            b_st = self.sb(es, "j_bst", [128, 2, 512], BF16)
            sz_st = self.sb(es, "j_szst", [128, 2, D], BF16)
            v_st = self.sb(es, "j_vst", [128, 2, 512], BF16)
            dt_st = self.sb(es, "j_dtst", [128, 2, 32], F32)
            r_xsst, r_bst, r_szst, r_vst, r_dtst = (kb.res() for _ in range(5))
            psT = [self.psB[2], self.psB[3]]
            r_psT = [self.r_psB[2], self.r_psB[3]]
            psH = self.psB[3][:, 256:272]
            r_psH = self.r_psB[3]
            bankF = [self.psB[0], self.psB[1]]
            r_bankF = [self.r_psB[0], self.r_psB[1]]
            bankX = [self.psA[1][:, 0:512], self.psA[1][:, 512:1024]]
            r_bankX = [kb.res(), kb.res()]
            bankM = [self.psA[0][:, 0:512], self.psA[0][:, 512:1024]]
            r_bankM = [kb.res(), kb.res()]
            for b in range(2):
                kb.op("dve", lambda e, b=b: e.memset(xh[b][:], 0.0), writes=[r_xh[b]])

            def srcrows(t):
                return (x_in, t * T) if t < NT else (ctx_in, 0)

            def load(t, b):
                src, t0 = srcrows(t)
                for s in range(2):
                    kb.dma("sp", xt[b][:, s, :], src[t0 + s * 128:t0 + (s + 1) * 128, :], writes=[r_xt[b][s]])
                if t >= NT:
                    return
                if 0 < t < NT - 1:
                    kb.dma("sp", xh[b][0:2, :], src[t0 - 1:t0 + T + 1:T + 1, :], writes=[r_xh[b]])
                elif t > 0:
                    kb.dma("sp", xh[b][0:1, :], src[t0 - 1:t0, :], writes=[r_xh[b]])
                else:
                    kb.dma("sp", xh[b][1:2, :], src[t0 + T:t0 + T + 1, :], writes=[r_xh[b]])

            tiles = list(range(NT + 1))
            if self.max_tiles is not None:
                tiles = list(range(self.max_tiles)) + [NT]

            def do_norm(ti):
                t = tiles[ti]
                b = ti % 2
                isctx = t == NT
                halo = (False, False) if isctx else (t > 0, t < NT - 1)
                self.norm_T(nb, xt[b], r_xt[b], 2, xh[b], r_xh[b], 2, halo, hTs[b], r_hTs[b], 1, 0, T + 1,
                            0, 0, 1 if isctx else 0, psT, r_psT, psH, r_psH)

            load(tiles[0], 0)
            do_norm(0)
            fi = 0
            for ti, t in enumerate(tiles):
                b = ti % 2
                hT, r_hT = hTs[b], r_hTs[b]
                isctx = t == NT
                g0 = S if isctx else t * T
                if ti + 1 < len(tiles):
                    load(tiles[ti + 1], (ti + 1) % 2)
                chunks = []
                if not isctx:
                    chunks += [("q", c, c * 128) for c in range(4)]
                chunks += [("k", c, 1536 + c * 128) for c in range(4)]
                for kind, c, col in chunks:
                    pb = fi % 2
                    fi += 1
                    bk, rbk = bankF[pb], r_bankF[pb]
                    for kc in range(8):
                        kb.op("pe", lambda e, kc=kc, bk=bk, col=col: e.matmul(
                            bk[:, 0:T], lhsT=W[:, kc, col:col + 128], rhs=hT[:, kc, 1:T + 1], start=(kc == 0), stop=(kc == 7)),
                            reads=[r_w[kc], r_hT], writes=[rbk], inc=(kc == 7))
                    st, rst = fst[pb], r_fst[pb]
                    kb.op("act", lambda e, st=st, bk=bk, kind=kind: e.activation(out=st[:], in_=bk[:, 0:T], func=AF.Identity,
                                                                               scale=(0.125 if kind == "q" else 1.0)),
                          reads=[rbk], writes=[rst])
                    dstT = self.qT if kind == "q" else self.kT
                    kb.dma("pool", dstT[c * 128:(c + 1) * 128, g0:g0 + T], st[:], reads=[rst])
                fbase = fi
                fi += 16
                for i in range(18):
                    if i < 16:
                        pb = (fbase + i) % 2
                        bk, rbk = bankF[pb], r_bankF[pb]
                        col = 2560 + i * 128
                        for kc in range(8):
                            kb.op("pe", lambda e, kc=kc, bk=bk, col=col: e.matmul(
                                bk[:, 0:T + 2], lhsT=W[:, kc, col:col + 128], rhs=hT[:, kc, 0:T + 2], start=(kc == 0), stop=(kc == 7)),
                                reads=[r_w[kc], r_hT], writes=[rbk], inc=(kc == 7))
                    if 1 <= i <= 16:
                        c = i - 1
                        cb_ = c % 2
                        kb.op("act", lambda e, cb_=cb_: e.activation(out=sx[cb_][:], in_=acc[cb_][:], func=AF.Silu), reads=[r_acc[cb_]], writes=[r_sx[cb_]])
                        if c >= 8:
                            dstT = self.BT if c < 12 else self.CT
                            cc = (c - 8) % 4
                            kb.dma("pool", dstT[cc * 128:(cc + 1) * 128, g0:g0 + T], sx[cb_][:], reads=[r_sx[cb_]])
                    if i < 16:
                        ib = i % 2
                        kb.op("act", lambda e, ib=ib, bk=bk, i=i: e.activation(out=acc[ib][:], in_=bk[:, 1:T + 1], func=AF.Identity,
                                                                            bias=cb[:, i:i + 1], scale=cw[:, i, 1:2]),
                              reads=[rbk] + r_cw, writes=[r_acc[ib]])
                    if 1 <= i <= 12:
                        c = i - 1
                        cb_ = c % 2
                        for s in range(2):
                            kb.op("pe", lambda e, s=s, cb_=cb_: e.matmul(bankX[cb_][:, s * 128:(s + 1) * 128], lhsT=sx[cb_][:, s * 128:(s + 1) * 128],
                                                                       rhs=self.identb, start=True, stop=True),
                                  reads=[r_sx[cb_], self.r_cst], writes=[r_bankX[cb_]], inc=(s == 1))
                    if 2 <= i <= 13:
                        c = i - 2
                        cb_ = c % 2
                        if c < 8:
                            kb.op("dve", lambda e, c=c, cb_=cb_: e.tensor_copy(out=xs_st[:, :, c * 128:(c + 1) * 128],
                                                                              in_=_r(bankX[cb_][:, 0:256], "p (s f) -> p s f", s=2)),
                                  reads=[r_bankX[cb_]], writes=[r_xsst])
                        else:
                            kb.op("dve", lambda e, c=c, cb_=cb_: e.tensor_copy(out=b_st[:, :, (c - 8) * 128:(c - 7) * 128],
                                                                              in_=_r(bankX[cb_][:, 0:256], "p (s f) -> p s f", s=2)),
                                  reads=[r_bankX[cb_]], writes=[r_bst])
                    if i < 16:
                        a, c = acc[ib], i
                        kb.op("dve", lambda e, bk=bk, a=a, c=c: e.scalar_tensor_tensor(out=a[:], in0=bk[:, 0:T], scalar=cw[:, c, 0:1], in1=a[:],
                                                                                    op0=ALU.mult, op1=ALU.add), reads=[rbk, r_acc[ib]] + r_cw, writes=[r_acc[ib]])
                        kb.op("dve", lambda e, bk=bk, a=a, c=c: e.scalar_tensor_tensor(out=a[:], in0=bk[:, 2:T + 2], scalar=cw[:, c, 2:3], in1=a[:],
                                                                                    op0=ALU.mult, op1=ALU.add), reads=[rbk, r_acc[ib]] + r_cw, writes=[r_acc[ib]])
                    if i == 8 and ti + 1 < len(tiles):
                        do_norm(ti + 1)
                kb.dma("pool", _r(self.xs_tok[g0:g0 + T, :], "(s p) d -> p s d", p=128), xs_st[:], reads=[r_xsst])
                kb.dma("pool", _r(self.B_tok[g0:g0 + T, :], "(s p) d -> p s d", p=128), b_st[:], reads=[r_bst])
                mi = 0
                for s in range(2):
                    tm = [("v", 2048, 512), ("d", 4608, 32)]
                    if not isctx:
                        tm = [("z", 512, 512), ("z", 1024, 512)] + tm
                    for kind, col, n in tm:
                        bk, rbk = bankM[mi % 2], r_bankM[mi % 2]
                        mi += 1
                        for kc in range(8):
                            kb.op("pe", lambda e, kc=kc, bk=bk, col=col, n=n, s=s: e.matmul(
                                bk[:, 0:n], lhsT=hT[:, kc, 1 + s * 128:1 + (s + 1) * 128], rhs=W[:, kc, col:col + n], start=(kc == 0), stop=(kc == 7)),
                                reads=[r_w[kc], r_hT], writes=[rbk], inc=(kc == 7))
                        if kind == "z":
                            kb.op("act", lambda e, bk=bk, col=col, s=s: e.activation(out=sz_st[:, s, col - 512:col], in_=bk[:, 0:512], func=AF.Silu),
                                  reads=[rbk], writes=[r_szst])
                        elif kind == "v":
                            kb.op("act", lambda e, bk=bk, s=s: e.activation(out=v_st[:, s, :], in_=bk[:, 0:512], func=AF.Identity),
                                  reads=[rbk], writes=[r_vst])
                        else:
                            kb.op("dve", lambda e, bk=bk, s=s: e.tensor_copy(out=dt_st[:, s, :], in_=bk[:, 0:32]), reads=[rbk], writes=[r_dtst])
                if not isctx:
                    kb.dma("pool", _r(self.sz_tok[g0:g0 + T, :], "(s p) d -> p s d", p=128), sz_st[:], reads=[r_szst])
                kb.dma("pool", _r(self.v_tok[g0:g0 + T, :], "(s p) d -> p s d", p=128), v_st[:], reads=[r_vst])
                kb.dma("pool", _r(self.dt_tok[g0:g0 + T, :], "(s p) d -> p s d", p=128), dt_st[:], reads=[r_dtst])

    def l0_ssd(self, d):
        kb = self.kb
        NCH = S // 128
        with ExitStack() as es:
            rows = self.sb(es, "d_rows", [128, 3, 32], F32)
            r_rows = kb.res()
            kb.dma("sp", rows[:], _r(self.I("ssm_rows"), "a b -> (a b)").partition_broadcast(128), writes=[r_rows])
            negA = self.sb(es, "d_negA", [128, 32], F32)
            dsum = self.sb(es, "d_dsum", [128, 16], F32)
            r_negA = kb.res()
            kb.op("act", lambda e: e.activation(out=negA[:], in_=rows[:, 0, :], func=AF.Exp), reads=[r_rows], writes=[r_negA])
            kb.op("dve", lambda e: e.tensor_scalar(out=negA[:], in0=negA[:], scalar1=-1.0, scalar2=None, op0=ALU.mult), reads=[r_negA], writes=[r_negA])
            kb.op("dve", lambda e: e.tensor_tensor(out=dsum[:], in0=rows[:, 2, 0:16], in1=rows[:, 2, 16:32], op=ALU.add), reads=[r_rows], writes=[r_negA])
            ngrow, r_ng = self.load_rows_bcast(es, "d_ng", self.I("ssm_norm_g"), D)
            xs = [self.sb(es, "d_xs%d" % i, [128, D], BF16) for i in range(2)]
            Bt = [self.sb(es, "d_Bt%d" % i, [128, 512], BF16) for i in range(2)]
            BTt = [self.sb(es, "d_BTt%d" % i, [128, 4, 128], BF16) for i in range(2)]
            CTt = [self.sb(es, "d_CTt%d" % i, [128, 4, 128], BF16) for i in range(2)]
            dtr = [self.sb(es, "d_dtr%d" % i, [128, 32], F32) for i in range(2)]
            yfl = [self.sb(es, "d_yfl%d" % i, [128, D], F32) for i in range(2)]
            szl = [self.sb(es, "d_szl%d" % i, [128, D], BF16) for i in range(2)]
            r_ld = [[kb.res() for _ in range(7)] for _ in range(2)]
            H = self.sb(es, "d_H", [128, D], F32)
            Hb = self.sb(es, "d_Hb", [128, D], BF16)
            r_H, r_Hb = kb.res(), kb.res()
            kb.op("dve", lambda e: e.memset(H[:], 0.0), writes=[r_H])
            kb.op("pool", lambda e: e.memset(Hb[:], 0.0), writes=[r_Hb])
            sm = self.sb(es, "d_sm", [128, 8, 16], F32)
            r_sm = [kb.res() for _ in range(8)]
            dw = self.sb(es, "d_dw", [128, 16], F32)
            r_dw = kb.res()
            Lh = self.sb(es, "d_Lh", [128, 16, 128], F32)
            r_Lh = kb.res()
            seg = self.sb(es, "d_seg", [128, 4, 128], F32)
            r_seg = kb.res()
            cbm = self.sb(es, "d_cbm", [128, 4, 128], F32)
            r_cbm = kb.res()
            M = self.sb(es, "d_M", [128, 16, 128], BF16)
            r_M = [kb.res() for _ in range(4)]
            xdt = self.sb(es, "d_xdt", [128, D], BF16)
            xdtw = self.sb(es, "d_xdtw", [128, D], BF16)
            r_xdt, r_xdtw = kb.res(), kb.res()
            yc = self.sb(es, "d_yc", [128, D], F32)
            yt = self.sb(es, "d_yt", [128, D], F32)
            r_yc, r_yt = kb.res(), kb.res()
            ss4 = self.sb(es, "d_ss4", [128, 8], F32)
            r_ss4 = [kb.res() for _ in range(3)]
            junk = self.sb(es, "d_junk", [128, 256], BF16)
            r_junk = kb.res()
            so = self.sb(es, "d_so", [128, D], BF16)
            r_so = kb.res()
            one1 = self.sb(es, "d_one", [128, 1], F32)
            kb.op("pool", lambda e: e.memset(one1[:], 1.0), writes=[r_negA])
            y_ps, r_yps = self.psA[0], [self.r_psA[0], kb.res()]
            yo_ps, r_yops = self.psA[1], [self.r_psA[1], kb.res()]
            D_ps, r_Dps = self.psB[0], self.r_psB[0]
            cb_ps, r_cbps = self.psB[1], self.r_psB[1]
            sm_ps, r_smps = self.psB[2], self.r_psB[2]
            S_ps, r_Sps = self.psB[3], self.r_psB[3]
            Rm = self.cst[:, 1 + d, :]
            Mk = self.cst[:, 3 + d, :]
            ones = self.cst[:, 5, :]
            lat = list(range(NCH)) if d == 0 else list(range(NCH - 1, -1, -1))
            if self.max_tiles is not None:
                lat = lat[:self.max_tiles]
            order = [("c", 0), ("c", 1)] if d == 0 else [("c", 1), ("c", 0)]
            order += [("l", c) for c in lat]

            def tok0(kc):
                return S + kc[1] * 128 if kc[0] == "c" else kc[1] * 128

            def load(i):
                b = i % 2
                kc = order[i]
                t0 = tok0(kc)
                kb.dma("sp", xs[b][:], self.xs_tok[t0:t0 + 128, :], writes=[r_ld[b][0]])
                kb.dma("sp", Bt[b][:], self.B_tok[t0:t0 + 128, :], writes=[r_ld[b][1]])
                kb.dma("sp", dtr[b][:], self.dt_tok[t0:t0 + 128, :], writes=[r_ld[b][2]])
                if kc[0] == "l":
                    kb.dma("sp", BTt[b][:], _r(self.BT[:, t0:t0 + 128], "(g n) t -> n g t", n=128), writes=[r_ld[b][3]])
                    kb.dma("sp", CTt[b][:], _r(self.CT[:, t0:t0 + 128], "(g n) t -> n g t", n=128), writes=[r_ld[b][4]])
                    if d == 1:
                        kb.dma("sp", yfl[b][:], self.yf[t0:t0 + 128, :], writes=[r_ld[b][5]])
                        kb.dma("sp", szl[b][:], self.sz_tok[t0:t0 + 128, :], writes=[r_ld[b][6]])

            load(0)
            for i, kc in enumerate(order):
                b = i % 2
                t0 = tok0(kc)
                islat = kc[0] == "l"
                if i + 1 < len(order):
                    load(i + 1)
                rl = r_ld[b]
                hs = slice(d * 16, (d + 1) * 16)
                kb.op("dve", lambda e: e.tensor_tensor(out=sm[:, 0, :], in0=dtr[b][:, hs], in1=rows[:, 1, hs], op=ALU.add), reads=[rl[2], r_rows], writes=[r_sm[0]])
                kb.op("act", lambda e: e.activation(out=sm[:, 1, :], in_=sm[:, 0, :], func=AF.Exp), reads=[r_sm[0]], writes=[r_sm[1]])
                kb.op("act", lambda e: e.activation(out=sm[:, 2, :], in_=sm[:, 1, :], func=AF.Ln, bias=one1[:]), reads=[r_sm[1], r_negA], writes=[r_sm[2]])
                kb.op("dve", lambda e: e.tensor_tensor(out=sm[:, 3, :], in0=sm[:, 2, :], in1=negA[:, hs], op=ALU.mult), reads=[r_sm[2], r_negA], writes=[r_sm[3]])
                kb.op("pe", lambda e: e.matmul(sm_ps[:, 0:16], lhsT=Rm, rhs=sm[:, 3, :], start=True, stop=True), reads=[r_sm[3], self.r_cst], writes=[r_smps], inc=False)
                kb.op("pe", lambda e: e.matmul(sm_ps[:, 16:32], lhsT=ones, rhs=sm[:, 3, :], start=True, stop=True), reads=[r_sm[3], self.r_cst], writes=[r_smps])
                kb.op("dve", lambda e: e.tensor_copy(out=sm[:, 4, :], in_=sm_ps[:, 0:16]), reads=[r_smps], writes=[r_sm[4]])
                kb.op("dve", lambda e: e.tensor_tensor(out=sm[:, 0, :], in0=sm_ps[:, 16:32], in1=sm[:, 4, :], op=ALU.subtract), reads=[r_smps, r_sm[4]], writes=[r_sm[0]])
                kb.op("act", lambda e: e.activation(out=sm[:, 5, :], in_=sm[:, 0, :], func=AF.Exp), reads=[r_sm[0]], writes=[r_sm[5]])
                kb.op("act", lambda e: e.activation(out=sm[:, 6, :], in_=sm[:, 4, :], func=AF.Exp), reads=[r_sm[4]], writes=[r_sm[6]])
                kb.op("act", lambda e: e.activation(out=sm[:, 7, :], in_=sm_ps[:, 16:32], func=AF.Exp), reads=[r_smps], writes=[r_sm[7]])
                kb.op("dve", lambda e: e.tensor_tensor(out=dw[:], in0=sm[:, 2, :], in1=sm[:, 5, :], op=ALU.mult), reads=[r_sm[2], r_sm[5]], writes=[r_dw])
                xs3 = _r(xs[b][:], "p (h q) -> p h q", h=16)
                kb.op("dve", lambda e: e.tensor_tensor(out=_r(xdtw[:], "p (h q) -> p h q", h=16), in0=xs3, in1=dw[:].unsqueeze(2).to_broadcast([128, 16, 64]), op=ALU.mult),
                      reads=[rl[0], r_dw], writes=[r_xdtw])
                if islat:
                    kb.op("dve", lambda e: e.tensor_tensor(out=_r(xdt[:], "p (h q) -> p h q", h=16), in0=xs3, in1=sm[:, 2, :].unsqueeze(2).to_broadcast([128, 16, 64]), op=ALU.mult),
                          reads=[rl[0], r_sm[2]], writes=[r_xdt])
                    kb.op("dve", lambda e: e.tensor_tensor(out=Lh[:], in0=Mk.unsqueeze(1).to_broadcast([128, 16, 128]),
                                                           in1=sm[:, 3, :].unsqueeze(2).to_broadcast([128, 16, 128]), op=ALU.mult),
                          reads=[r_sm[3], self.r_cst], writes=[r_Lh])
                    for g in range(4):
                        kb.op("pe", lambda e, g=g: e.matmul(cb_ps[:, g * 128:(g + 1) * 128], lhsT=BTt[b][:, g, :], rhs=CTt[b][:, g, :], start=True, stop=True),
                              reads=[rl[3], rl[4]], writes=[r_cbps], inc=(g == 3))
                    kb.op("dve", lambda e: e.tensor_tensor(out=cbm[:], in0=_r(cb_ps[:, :], "p (g l) -> p g l", g=4),
                                                           in1=Rm.unsqueeze(1).to_broadcast([128, 4, 128]), op=ALU.mult),
                          reads=[r_cbps, self.r_cst], writes=[r_cbm])
                    for g in range(4):
                        for r4 in range(4):
                            h = g * 4 + r4
                            kb.op("pe", lambda e, h=h, r4=r4: e.matmul(D_ps[:, r4 * 128:(r4 + 1) * 128], lhsT=Lh[:, h, :], rhs=Rm, start=True, stop=True),
                                  reads=[r_Lh, self.r_cst], writes=[r_Dps], inc=(r4 == 3))
                        kb.op("act", lambda e: e.activation(out=_r(seg[:], "p r l -> p (r l)"), in_=D_ps[:, :], func=AF.Exp), reads=[r_Dps], writes=[r_seg])
                        kb.op("dve", lambda e, g=g: e.tensor_tensor(out=M[:, g * 4:(g + 1) * 4, :], in0=seg[:], in1=cbm[:, g, :].unsqueeze(1).to_broadcast([128, 4, 128]), op=ALU.mult),
                              reads=[r_seg, r_cbm], writes=[r_M[g]])
                    for h in range(16):
                        kb.op("pe", lambda e, h=h: e.matmul(y_ps[:, h * 64:(h + 1) * 64], lhsT=M[:, h, :], rhs=xdt[:, h * 64:(h + 1) * 64], start=True, stop=True),
                              reads=[r_M[h // 4], r_xdt], writes=r_yps, inc=(h == 15))
                    for g in range(4):
                        kb.op("pe", lambda e, g=g: e.matmul(yo_ps[:, g * 256:(g + 1) * 256], lhsT=CTt[b][:, g, :], rhs=Hb[:, g * 256:(g + 1) * 256], start=True, stop=True),
                              reads=[rl[4], r_Hb], writes=r_yops, inc=(g == 3))
                    kb.op("dve", lambda e: e.tensor_tensor(out=_r(yt[:], "p (h q) -> p h q", h=16), in0=_r(yo_ps[:, :], "p (h q) -> p h q", h=16),
                                                           in1=sm[:, 6, :].unsqueeze(2).to_broadcast([128, 16, 64]), op=ALU.mult),
                          reads=r_yops + [r_sm[6]], writes=[r_yt])
                    kb.op("dve", lambda e: e.tensor_tensor(out=yc[:], in0=yt[:], in1=y_ps[:, :], op=ALU.add), reads=[r_yt] + r_yps, writes=[r_yc])
                    if d == 0:
                        kb.dma("pool", self.yf[t0:t0 + 128, :], yc[:], reads=[r_yc])
                    else:
                        kb.op("pool", lambda e: e.tensor_tensor(out=yc[:], in0=yc[:], in1=yfl[b][:], op=ALU.add), reads=[r_yc, rl[5]], writes=[r_yc])
                        kb.op("dve", lambda e: e.tensor_tensor(out=_r(yt[:], "p (h q) -> p h q", h=16), in0=xs3, in1=dsum[:].unsqueeze(2).to_broadcast([128, 16, 64]), op=ALU.mult),
                              reads=[rl[0], r_negA], writes=[r_yt])
                        kb.op("pool", lambda e: e.tensor_tensor(out=yc[:], in0=yc[:], in1=yt[:], op=ALU.add), reads=[r_yc, r_yt], writes=[r_yc])
                        kb.op("dve", lambda e: e.tensor_tensor(out=yc[:], in0=yc[:], in1=szl[b][:], op=ALU.mult), reads=[r_yc, rl[6]], writes=[r_yc])
                        for g in range(4):
                            kb.op("act", lambda e, g=g: e.activation(out=junk[:], in_=yc[:, g * 256:(g + 1) * 256], func=AF.Square, accum_out=ss4[:, g:g + 1]),
                                  reads=[r_yc], writes=[r_ss4[0], r_junk])
                        kb.op("act", lambda e: e.activation(out=ss4[:, 4:8], in_=ss4[:, 0:4], func=AF.Sqrt, bias=self.epsb[:], scale=1.0 / 256), reads=[r_ss4[0], self.r_cst], writes=[r_ss4[1]])
                        kb.op("dve", lambda e: e.reciprocal(out=ss4[:, 4:8], in_=ss4[:, 4:8]), reads=[r_ss4[1]], writes=[r_ss4[1]])
                        kb.op("dve", lambda e: e.tensor_tensor(out=_r(yt[:], "p (g q) -> p g q", g=4), in0=_r(yc[:], "p (g q) -> p g q", g=4),
                                                               in1=ss4[:, 4:8].unsqueeze(2).to_broadcast([128, 4, 256]), op=ALU.mult), reads=[r_yc, r_ss4[1]], writes=[r_yt])
                        kb.op("pool", lambda e: e.tensor_tensor(out=so[:], in0=yt[:], in1=ngrow[:], op=ALU.mult), reads=[r_yt, r_ng], writes=[r_so])
                        kb.dma("pool", self.ssm_tok[t0:t0 + 128, :], so[:], reads=[r_so])
                for gp in range(2):
                    for gg in range(2):
                        g = gp * 2 + gg
                        kb.op("pe", lambda e, g=g, gg=gg: e.matmul(S_ps[:, gg * 256:(gg + 1) * 256], lhsT=Bt[b][:, g * 128:(g + 1) * 128], rhs=xdtw[:, g * 256:(g + 1) * 256], start=True, stop=True),
                              reads=[rl[1], r_xdtw], writes=[r_Sps], inc=(gg == 1))
                    Hs = H[:, gp * 512:(gp + 1) * 512]
                    kb.op("dve", lambda e, Hs=Hs, gp=gp: e.tensor_tensor(out=_r(Hs, "p (h q) -> p h q", h=8), in0=_r(Hs, "p (h q) -> p h q", h=8),
                                                                        in1=sm[:, 7, gp * 8:(gp + 1) * 8].unsqueeze(2).to_broadcast([128, 8, 64]), op=ALU.mult),
                          reads=[r_H, r_sm[7], r_Hb] + r_yops, writes=[r_H])
                    kb.op("dve", lambda e, Hs=Hs: e.tensor_tensor(out=Hs, in0=Hs, in1=S_ps[:, :], op=ALU.add), reads=[r_H, r_Sps], writes=[r_H])
                kb.op("act", lambda e: e.activation(out=Hb[:], in_=H[:], func=AF.Identity), reads=[r_H], writes=[r_Hb])

    def l0_att(self):
        kb = self.kb
        ST = S + LC
        NTL = S // 128
        tab_in = self.I("rpbtab")
        with ExitStack() as es:
            qh = [self.sb(es, "a_q%d" % i, [64, S], BF16) for i in range(2)]
            kh = [self.sb(es, "a_k%d" % i, [64, ST], BF16) for i in range(2)]
            va = [self.sb(es, "a_v%d" % i, [128, 34, 65], BF16) for i in range(2)]
            tab = [self.sb(es, "a_tab%d" % i, [128, 5, 640], F32) for i in range(2)]
            r_hd = [[kb.res() for _ in range(4)] for _ in range(2)]
            sc = [self.sb(es, "a_sc%d" % i, [128, 640], F32) for i in range(2)]
            P = [self.sb(es, "a_P%d" % i, [128, 896], BF16) for i in range(2)]
            r_sc = [kb.res() for _ in range(2)]
            r_P = [kb.res() for _ in range(2)]
            rc = [self.sb(es, "a_rc%d" % i, [128, 1], F32) for i in range(2)]
            r_rc = [kb.res() for _ in range(2)]
            ast = [self.sb(es, "a_st%d" % i, [128, 32, 64], BF16) for i in range(2)]
            r_ast = [kb.res() for _ in range(2)]
            S_ps = [self.psA[0], self.psA[1]]
            r_Sps = [[self.r_psA[0], kb.res()], [self.r_psA[1], kb.res()]]
            o_ps = [self.psB[0], self.psB[1]]
            r_ops = [self.r_psB[0], self.r_psB[1]]
            for i in range(2):
                kb.op("dve", lambda e, i=i: e.memset(va[i][:, :, 64:65], 1.0), writes=[r_hd[i][2]])

            def loadh(h):
                b = h % 2
                kb.dma("sp", qh[b][:], self.qT[h * 64:(h + 1) * 64, :], writes=[r_hd[b][0]])
                kb.dma("sp", kh[b][:], self.kT[h * 64:(h + 1) * 64, :], writes=[r_hd[b][1]])
                for part in range(2):
                    kb.dma("sp", va[b][:, part * 17:(part + 1) * 17, 0:64],
                           _r(self.v_tok[part * 17 * 128:(part + 1) * 17 * 128, h * 64:(h + 1) * 64], "(j p) d -> p j d", p=128),
                           writes=[r_hd[b][2]] if part == 0 else [self._rva2(b)])
                kb.dma("sp", tab[b][:], _r(tab_in[h], "v p c q -> p v (c q)"), writes=[r_hd[b][3]])

            nh = 8
            loadh(0)
            it = 0
            for h in range(nh):
                b = h % 2
                if h + 1 < nh:
                    loadh(h + 1)
                rq, rk, rv, rt = r_hd[b]
                rv2 = self._rva2(b)
                tl = list(range(NTL) if self.max_tiles is None else range(self.max_tiles))

                def emit_S(j, pb):
                    u0 = min(max(2 * j - 4, 0), 54)
                    kt0 = u0 // 2
                    sp_ = S_ps[pb]
                    for c in range(7):
                        k0 = (kt0 + c) * 128 if c < 5 else S + (c - 5) * 128
                        kb.op("pe", lambda e, c=c, k0=k0, sp_=sp_: e.matmul(sp_[:, c * 128:(c + 1) * 128], lhsT=kh[b][:, k0:k0 + 128], rhs=qh[b][:, j * 128:(j + 1) * 128],
                                                                          start=True, stop=True), reads=[rq, rk], writes=r_Sps[pb], inc=(c == 6))

                emit_S(tl[0], it % 2)
                for ji, j in enumerate(tl):
                    pb = it % 2
                    it += 1
                    var = 0 if j == 0 else 1 if j == 1 else 3 if j == 30 else 4 if j == 31 else 2
                    u0 = min(max(2 * j - 4, 0), 54)
                    kt0 = u0 // 2
                    sp_ = S_ps[pb]
                    if ji + 1 < len(tl):
                        emit_S(tl[ji + 1], it % 2)
                    kb.op("dve", lambda e, sp_=sp_, var=var: e.tensor_tensor(out=sc[pb][:, 0:512], in0=sp_[:, 0:512], in1=tab[b][:, var, 0:512], op=ALU.add),
                          reads=r_Sps[pb] + [rt], writes=[r_sc[pb]])
                    kb.op("dve", lambda e, sp_=sp_, var=var: e.tensor_tensor(out=sc[pb][:, 512:640], in0=sp_[:, 512:640], in1=tab[b][:, var, 512:640], op=ALU.add),
                          reads=r_Sps[pb] + [rt], writes=[r_sc[pb]])
                    kb.op("act", lambda e, sp_=sp_: e.activation(out=P[pb][:, 640:896], in_=sp_[:, 640:896], func=AF.Exp), reads=r_Sps[pb], writes=[r_P[pb]])
                    kb.op("act", lambda e: e.activation(out=P[pb][:, 0:640], in_=sc[pb][:], func=AF.Exp), reads=[r_sc[pb]], writes=[r_P[pb]])
                    for c in range(7):
                        kt = kt0 + c if c < 5 else 32 + (c - 5)
                        kb.op("pe", lambda e, c=c, kt=kt: e.matmul(o_ps[pb][:, 0:65], lhsT=P[pb][:, c * 128:(c + 1) * 128], rhs=va[b][:, kt, :], start=(c == 0), stop=(c == 6)),
                              reads=[r_P[pb], rv, rv2], writes=[r_ops[pb]], inc=(c == 6))
                    kb.op("dve", lambda e: e.reciprocal(out=rc[pb][:], in_=o_ps[pb][:, 64:65]), reads=[r_ops[pb]], writes=[r_rc[pb]])
                    kb.op("dve", lambda e, j=j: e.tensor_scalar(out=ast[b][:, j, :], in0=o_ps[pb][:, 0:64], scalar1=rc[pb][:, 0:1], scalar2=None, op0=ALU.mult),
                          reads=[r_ops[pb], r_rc[pb]], writes=[r_ast[b]])
                for part in range(4):
                    kb.dma("pool", _r(self.att_tok[part * 1024:(part + 1) * 1024, h * 64:(h + 1) * 64], "(j p) d -> p j d", p=128),
                           ast[b][:, part * 8:(part + 1) * 8, :], reads=[r_ast[b]])

    def _rva2(self, b):
        if not hasattr(self, "_rva2_l"):
            self._rva2_l = [self.kb.res(), self.kb.res()]
        return self._rva2_l[b]

    def l0_out(self, x_in, dst):
        kb = self.kb
        NTL = S // 128
        with ExitStack() as es:
            Wo = self.sb(es, "o_W", [128, 12, D], BF16)
            r_wo = kb.res()
            kb.dma("pool", Wo[:], _r(self.I("w_out"), "(c p) d -> p c d", p=128), writes=[r_wo])
            Grow, r_G = self.load_rows_bcast(es, "o_G", self.vec[0, 0], D)
            cat = [self.sb(es, "o_cat%d" % i, [128, 1536], BF16) for i in range(2)]
            xt = [self.sb(es, "o_xt%d" % i, [128, D], F32) for i in range(2)]
            r_cat = [[kb.res(), kb.res()] for _ in range(2)]
            r_xt = [kb.res() for _ in range(2)]
            catT = self.sb(es, "o_catT", [128, 12, 128], BF16)
            r_catT = [kb.res() for _ in range(3)]
            tmp = self.sb(es, "o_tmp", [128, D], F32)
            r_tmp = kb.res()
            ss2 = self.sb(es, "o_ss2", [128, 4], F32)
            r_ss2 = [kb.res() for _ in range(3)]
            junk2 = self.sb(es, "o_junk2", [128, 512], BF16)
            r_junk2 = kb.res()
            psT = [self.psB[0], self.psB[1], self.psB[2]]
            r_psT = [self.r_psB[0], self.r_psB[1], self.r_psB[2]]

            def load(i):
                b = i % 2
                kb.dma("sp", cat[b][:, 0:512], self.att_tok[i * 128:(i + 1) * 128, :], writes=[r_cat[b][0]])
                kb.dma("sp", cat[b][:, 512:1536], self.ssm_tok[i * 128:(i + 1) * 128, :], writes=[r_cat[b][1]])
                kb.dma("sp", xt[b][:], x_in[i * 128:(i + 1) * 128, :], writes=[r_xt[b]])

            n = NTL if self.max_tiles is None else self.max_tiles
            load(0)
            for i in range(n):
                b = i % 2
                if i + 1 < n:
                    load(i + 1)
                for q3 in range(3):
                    for cc in range(4):
                        c = q3 * 4 + cc
                        kb.op("pe", lambda e, c=c, cc=cc, q3=q3: e.matmul(psT[q3][:, cc * 128:(cc + 1) * 128], lhsT=cat[b][:, c * 128:(c + 1) * 128], rhs=self.identb,
                                                                        start=True, stop=True), reads=r_cat[b] + [self.r_cst], writes=[r_psT[q3]], inc=(cc == 3))
                    kb.op("act", lambda e, q3=q3: e.activation(out=_r(catT[:, q3 * 4:(q3 + 1) * 4, :], "p c t -> p (c t)"), in_=psT[q3][:, :], func=AF.Identity),
                          reads=[r_psT[q3]], writes=[r_catT[q3]])
                py = self.psA[i % 2]
                rpy = self.r_psA[i % 2]
                for hb in range(2):
                    for c in range(12):
                        kb.op("pe", lambda e, c=c, hb=hb: e.matmul(py[:, hb * 512:(hb + 1) * 512], lhsT=catT[:, c, :], rhs=Wo[:, c, hb * 512:(hb + 1) * 512],
                                                                 start=(c == 0), stop=(c == 11)), reads=[r_catT[c // 4], r_wo], writes=[rpy], inc=(c == 11))
                self.resid_epilogue(py[:, :], [rpy], xt[b][:], r_xt[b], Grow, r_G, tmp, r_tmp, ss2, r_ss2, junk2, r_junk2)
                kb.dma("pool", dst[i * 128:(i + 1) * 128, :], xt[b][:], reads=[r_xt[b]])


def _consts():
    t = np.arange(128)
    c = np.zeros((128, 6, 128), np.float32)
    c[:, 0] = (t[:, None] == t[None, :])
    c[:, 1] = (t[:, None] <= t[None, :])
    c[:, 2] = (t[:, None] >= t[None, :])
    c[:, 3] = (t[:, None] > t[None, :])
    c[:, 4] = (t[:, None] < t[None, :])
    c[:, 5] = 1.0
    return c


def _pool_inv():
    out = np.zeros((2, 4, 256), np.float32)
    for a, base in enumerate((0, S - 256)):
        t = base + np.arange(256)
        for gi, w in enumerate((2, 4, 8, 16)):
            lo = np.clip(t - w // 2, 0, S)
            hi = np.clip(t - w // 2 + w, 0, S)
            out[a, gi] = np.float32(1.0) / (hi - lo).astype(np.float32)
    return out


def _rpb_table(rpb):
    padded = np.concatenate([rpb.reshape(8, -1), np.full((8, 1), NEG, np.float32)], axis=1)
    sent = 15 * 31
    variants = [(0, (0, 1)), (0, (2, 3)), (0, (4, 5)), (54, (60, 61)), (54, (62, 63))]
    k = np.arange(640)
    ki = k // 64
    kc = k % 64
    q = np.arange(128)
    qc = q % 64
    idx = np.zeros((5, 640, 128), np.int64)
    for v, (u0, rows) in enumerate(variants):
        r = np.array(rows)[q // 64]
        rs = np.clip(r - 4, 0, 56)
        cs = np.clip(qc - 8, 0, 48)
        i = u0 + ki
        valid = (i[:, None] >= rs[None, :]) & (i[:, None] < rs[None, :] + 8) & (kc[:, None] >= cs[None, :]) & (kc[:, None] < cs[None, :] + 16)
        rr = i[:, None] - r[None, :] + 7
        cc = kc[:, None] - qc[None, :] + 15
        flat = np.clip(rr, 0, 14) * 31 + np.clip(cc, 0, 30)
        idx[v] = np.where(valid, flat, sent)
    tab = padded[:, idx]
    tab = tab.reshape(8, 5, 5, 128, 128).transpose(0, 1, 3, 2, 4)
    return np.ascontiguousarray(tab, dtype=np.float32)


def make_in_maps(inp):
    f = lambda a: np.ascontiguousarray(np.asarray(a), dtype=np.float32)
    x, c, ctx, c_ctx = f(inp["x"]), f(inp["c"]), f(inp["ctx"]), f(inp["c_ctx"])
    shared = {
        "ada_w": f(inp["ada_w"]),
        "ada_bT": f(f(inp["ada_b"]).reshape(2, 48, 128).transpose(2, 0, 1)),
        "norm_gT": f(f(inp["norm_g"]).reshape(2, 4, 8, 128).transpose(3, 0, 1, 2)),
        "w_in": f(inp["w_in"])[0],
        "w_out": f(inp["w_out"])[0],
        "rpbtab": _rpb_table(f(inp["na_rpb"])[0]),
        "conv_wT": f(f(inp["ssm_conv_w"])[0].reshape(3, 16, 128).transpose(2, 1, 0)),
        "conv_bT": f(f(inp["ssm_conv_b"])[0].reshape(16, 128).transpose(1, 0)),
        "ssm_rows": f(np.stack([f(inp["ssm_a_log"])[0].reshape(32), f(inp["ssm_dt_bias"])[0].reshape(32), f(inp["ssm_d"])[0].reshape(32)])),
        "ssm_norm_g": f(inp["ssm_norm_g"])[0],
        "pool_w": f(inp["pool_w"])[0],
        "pool_b": f(inp["pool_b"])[0].reshape(D),
        "pool_scale": f(inp["pool_scale"])[0],
        "pool_inv": _pool_inv(),
        "ffn_w_up": f(inp["ffn_w_up"]),
        "fconv_wT": f(f(inp["ffn_conv_w"]).reshape(2, 3, 22, 128).transpose(3, 0, 2, 1)),
        "fconv_bT": f(f(inp["ffn_conv_b"]).reshape(2, 22, 128).transpose(2, 0, 1)),
        "ffn_w_down": f(inp["ffn_w_down"]),
        "consts": _consts(),
    }
    maps = []
    for b in range(x.shape[0]):
        cc = np.stack([c[b].reshape(8, 128).T, c_ctx.reshape(8, 128).T], axis=2)
        m = dict(shared)
        m["x"] = f(x[b])
        m["ctx"] = f(ctx[b])
        m["cc"] = f(cc)
        maps.append(m)
    return maps


_NC_CACHE = {}


def kernel(**inputs):
    if "full" not in _NC_CACHE:
        p = Prog()
        _NC_CACHE["full"] = p.build()
        _NC_CACHE["names"] = [k for k in p.dram if k in Prog.SHAPES]
    nc = _NC_CACHE["full"]
    maps = make_in_maps(inputs)
    names = _NC_CACHE["names"]
    maps = [{k: m[k] for k in names} for m in maps]
    res = run_bass_kernel_spmd(nc, maps, core_ids=list(range(NCORES)))
    return np.stack([np.asarray(r["out"], dtype=np.float32) for r in res.results], axis=0)
```

```python
import numpy as np
from contextlib import ExitStack
import concourse.bass as bass
import concourse.mybir as mybir
from concourse.bass_utils import run_bass_kernel_spmd

F32 = mybir.dt.float32
BF16 = mybir.dt.bfloat16
ALU = mybir.AluOpType
AF = mybir.ActivationFunctionType

D = 1024
S = 4096
LC = 256
NCORES = 8
FH = 2816
EPS = 1e-6
NEG = -30000.0


class Res:
    __slots__ = ("w", "r", "name")

    def __init__(self, name=""):
        self.w = {}
        self.r = {}
        self.name = name


class KB:
    def __init__(self):
        nc = bass.Bass("TRN2", target_bir_lowering=False)
        self.nc = nc
        self.h = {"pe": nc.tensor, "act": nc.scalar, "dve": nc.vector, "pool": nc.gpsimd, "sp": nc.sync}
        self.sems = {}
        self.cnt = {}
        for e in ("pe", "act", "dve", "pool"):
            self.sems[e] = nc.alloc_semaphore(name="c_" + e)
            self.cnt[e] = 0
        self.dq = {"sp": [], "pool": [], "act": []}
        for q, n in (("sp", 28), ("pool", 20), ("act", 8)):
            for i in range(n):
                k = "d_%s%d" % (q, i)
                self.sems[k] = nc.alloc_semaphore(name=k)
                self.cnt[k] = 0
                self.dq[q].append(k)
        self.dqi = {"sp": 0, "pool": 0, "act": 0}
        self.sems["bar"] = nc.alloc_semaphore(name="bar")
        self.cnt["bar"] = 0
        self.waited = {e: {} for e in self.h}
        self.nres = 0

    def res(self, name=""):
        return Res(name)

    def _wait(self, E, need):
        for key, v in need.items():
            if key == E and E == "pe":
                continue
            if self.waited[E].get(key, 0) < v:
                self.h[E].wait_ge(self.sems[key], v)
                self.waited[E][key] = v

    @staticmethod
    def _merge(need, d):
        for k, v in d.items():
            if need.get(k, 0) < v:
                need[k] = v

    def op(self, E, fn, reads=(), writes=(), inc=True):
        need = {}
        for r in reads:
            self._merge(need, r.w)
        for w in writes:
            self._merge(need, w.w)
            self._merge(need, w.r)
        self._wait(E, need)
        ins = fn(self.h[E])
        if inc:
            self.cnt[E] += 1
            ins.then_inc(self.sems[E], 1)
            tv = self.cnt[E]
        else:
            tv = self.cnt[E] + 1
        for r in reads:
            if r.r.get(E, 0) < tv:
                r.r[E] = tv
        for w in writes:
            w.w = {E: tv}
            w.r = {}
        return ins

    def dma(self, Q, out, in_, reads=(), writes=(), **kw):
        need = {}
        for r in reads:
            self._merge(need, r.w)
        for w in writes:
            self._merge(need, w.w)
            self._merge(need, w.r)
        i = self.dqi[Q]
        self.dqi[Q] = i + 1
        key = self.dq[Q][i % len(self.dq[Q])]
        if self.cnt[key] > 0:
            need[key] = max(need.get(key, 0), self.cnt[key])
        self._wait(Q, need)
        ins = self.h[Q].dma_start(out=out, in_=in_, **kw)
        self.cnt[key] += 16
        ins.then_inc(self.sems[key], 16)
        tv = self.cnt[key]
        for r in reads:
            r.r[key] = tv
        for w in writes:
            w.w = {key: tv}
            w.r = {}
        return ins

    def barrier(self):
        need = {k: v for k, v in self.cnt.items() if k != "bar" and v > 0}
        self._wait("sp", need)
        self.cnt["bar"] += 1
        self.h["sp"].sem_inc(self.sems["bar"], 1)
        for e in ("pe", "act", "dve", "pool"):
            self.h[e].wait_ge(self.sems["bar"], self.cnt["bar"])
            self.waited[e]["bar"] = self.cnt["bar"]
            for k, v in need.items():
                self.waited[e][k] = max(self.waited[e].get(k, 0), v)


def _r(ap, pat, **kw):
    return ap.rearrange(pat, **kw)


class Prog:
    def __init__(self, dumps=(), stop_after=None, mode="full", max_tiles=None):
        self.mode = mode
        self.max_tiles = max_tiles
        self.kb = KB()
        self.nc = self.kb.nc
        self.dumps = set(dumps)
        self.stop_after = stop_after
        self.dram = {}

    def din(self, name, shape, dt=F32):
        t = self.nc.dram_tensor(name, list(shape), dt, kind="ExternalInput").ap()
        self.dram[name] = t
        return t

    SHAPES = {
        "x": [S, D], "ctx": [LC, D], "cc": [128, 8, 2], "ada_w": [2, D, 6 * D], "ada_bT": [128, 2, 48],
        "norm_gT": [128, 2, 4, 8], "w_in": [D, 4640], "w_out": [1536, D], "rpbtab": [8, 5, 128, 5, 128],
        "conv_wT": [128, 16, 3], "conv_bT": [128, 16], "ssm_rows": [3, 32], "ssm_norm_g": [D],
        "pool_w": [4, 256, 256], "pool_b": [D], "pool_scale": [D], "pool_inv": [2, 4, 256],
        "ffn_w_up": [2, D, 2 * FH], "fconv_wT": [128, 2, 22, 3], "fconv_bT": [128, 2, 22], "ffn_w_down": [2, FH, D],
        "consts": [128, 6, 128],
    }

    def I(self, name):
        if name not in self.dram:
            self.din(name, self.SHAPES[name])
        return self.dram[name]

    def dscr(self, name, shape, dt=F32, out=False):
        kind = "ExternalOutput" if (out or name in self.dumps) else "Internal"
        t = self.nc.dram_tensor(name, list(shape), dt, kind=kind).ap()
        self.dram[name] = t
        return t

    def sb(self, es, name, shape, dt):
        self.kb.nres += 1
        return es.enter_context(self.nc.sbuf_tensor("s%d_%s" % (self.kb.nres, name), list(shape), dt))

    def build(self):
        nc, kb = self.nc, self.kb
        x_in = self.I("x")
        ctx_in = self.I("ctx") if self.mode == "full" else None
        cc_in = self.I("cc")
        ada_w = self.I("ada_w")
        ada_bT = self.I("ada_bT")
        norm_gT = self.I("norm_gT")
        consts_in = self.I("consts")
        out = self.dscr("out", [S, D], out=True)
        xa = self.dscr("xa", [S, D])
        xb = self.dscr("xb", [S, D])
        xc = self.dscr("xc", [S, D])
        self.vec = self.dscr("vec", [2, 2, D])

        with ExitStack() as g:
            self.cst = self.sb(g, "cst", [128, 6, 128], F32)
            self.cstb = self.sb(g, "cstb", [128, 6, 128], BF16)
            self.modv = self.sb(g, "modv", [128, 2, 4, 8, 2], F32)
            self.epsb = self.sb(g, "epsb", [128, 1], F32)
            self.r_cst = kb.res("cst")
            self.r_modv = kb.res("modv")
            self.psA = [g.enter_context(nc.psum_tensor("psA%d" % i, [128, 1024], F32)) for i in range(2)]
            self.psB = [g.enter_context(nc.psum_tensor("psB%d" % i, [128, 512], F32)) for i in range(4)]
            self.r_psA = [kb.res("psA%d" % i) for i in range(2)]
            self.r_psB = [kb.res("psB%d" % i) for i in range(4)]

            kb.dma("sp", self.cst[:], consts_in, writes=[self.r_cst])
            kb.op("dve", lambda e: e.tensor_copy(out=self.cstb[:], in_=self.cst[:]), reads=[self.r_cst], writes=[self.r_cst])
            kb.op("pool", lambda e: e.memset(self.epsb[:], EPS), writes=[self.r_cst])
            self.identb = self.cstb[:, 0, :]

            self.phase_adaln(cc_in, ada_w, ada_bT, norm_gT)
            kb.barrier()
            if self.stop_after == "adaln":
                return self.finish()
            if self.mode == "l1":
                self.phase_pool(x_in, xc)
                kb.barrier()
                self.phase_ffn(1, xc, out)
                return self.finish()
            if self.mode == "ffn0":
                self.phase_ffn(0, x_in, out)
                return self.finish()
            self.phase_l0_mixer(x_in, ctx_in, xa)
            kb.barrier()
            if self.stop_after in ("proj", "ssd", "att", "l0mix"):
                return self.finish()
            self.phase_ffn(0, xa, xb)
            kb.barrier()
            if self.stop_after == "l0ffn":
                return self.finish()
            self.phase_pool(xb, xc)
            kb.barrier()
            if self.stop_after == "l1mix":
                return self.finish()
            self.phase_ffn(1, xc, out)
            kb.barrier()
        return self.finish()

    def finish(self):
        self.kb.barrier()
        return self.nc

    def phase_adaln(self, cc_in, ada_w, ada_bT, norm_gT):
        nc, kb = self.nc, self.kb
        with ExitStack() as es:
            cc = self.sb(es, "cc", [128, 8, 2], F32)
            sg = self.sb(es, "sg", [128, 8, 2], F32)
            scc = self.sb(es, "scc", [128, 8, 2], F32)
            abT = self.sb(es, "abT", [128, 2, 48], F32)
            ngT = self.sb(es, "ngT", [128, 2, 4, 8], F32)
            mod = self.sb(es, "mod", [128, 2, 48, 2], F32)
            gv = self.sb(es, "gv", [128, 2, 2, 8], F32)
            wbuf = [self.sb(es, "adw%d" % i, [128, 8, 768], F32) for i in range(2)]
            r_w = [[kb.res(), kb.res()] for _ in range(2)]
            r_cc, r_small, r_mod, r_gv = kb.res(), kb.res(), kb.res(), kb.res()
            kb.dma("sp", cc[:], cc_in, writes=[r_cc])
            kb.dma("sp", abT[:], ada_bT, writes=[r_small])
            kb.dma("sp", ngT[:], norm_gT, writes=[r_small])
            kb.op("act", lambda e: e.activation(out=sg[:], in_=cc[:], func=AF.Sigmoid), reads=[r_cc], writes=[r_gv])
            kb.op("dve", lambda e: e.tensor_tensor(out=scc[:], in0=cc[:], in1=sg[:], op=ALU.mult), reads=[r_cc, r_gv], writes=[r_cc])
            it = 0
            for l in range(2):
                for jb in range(8):
                    wb, rw = wbuf[it % 2], r_w[it % 2]
                    for half in range(2):
                        kb.dma("sp", wb[:, half * 4:(half + 1) * 4, :],
                               _r(ada_w[l, half * 512:(half + 1) * 512, jb * 768:(jb + 1) * 768], "(k p) n -> p k n", p=128),
                               writes=[rw[half]])
                    ps = self.psB[it % 2]
                    rps = self.r_psB[it % 2]
                    it += 1
                    self._ada_mm(wb, rw, scc, r_cc, ps, rps, mod, r_mod, abT, r_small, l, jb)
            for l in range(2):
                for i, (sci, shi, gi) in enumerate(((8, 0, 0), (32, 24, 2))):
                    kb.op("dve", lambda e, l=l, i=i, sci=sci, gi=gi: e.scalar_tensor_tensor(
                        out=self.modv[:, l, 2 * i, :, :], in0=mod[:, l, sci:sci + 8, :], scalar=1.0,
                        in1=ngT[:, l, gi, :].unsqueeze(2).to_broadcast([128, 8, 2]), op0=ALU.add, op1=ALU.mult),
                        reads=[r_mod, r_small], writes=[self.r_modv])
                    kb.op("dve", lambda e, l=l, i=i, shi=shi: e.tensor_copy(
                        out=self.modv[:, l, 2 * i + 1, :, :], in_=mod[:, l, shi:shi + 8, :]),
                        reads=[r_mod], writes=[self.r_modv])
                for i, (gti, gpi) in enumerate(((16, 1), (40, 3))):
                    kb.op("dve", lambda e, l=l, i=i, gti=gti, gpi=gpi: e.tensor_tensor(
                        out=gv[:, l, i, :], in0=mod[:, l, gti:gti + 8, 0], in1=ngT[:, l, gpi, :], op=ALU.mult),
                        reads=[r_mod, r_small], writes=[r_gv])
            for l in range(2):
                for i in range(2):
                    kb.dma("pool", _r(self.vec[l, i], "(j p) -> p j", p=128), gv[:, l, i, :], reads=[r_gv],
                           allow_slow_non_contiguous=True)

    def _ada_mm(self, wb, rw, scc, r_cc, ps, rps, mod, r_mod, abT, r_small, l, jb):
        kb = self.kb
        for jj in range(6):
            for kc in range(8):
                kb.op("pe", lambda e, jj=jj, kc=kc: e.matmul(ps[:, jj * 2:jj * 2 + 2], lhsT=wb[:, kc, jj * 128:(jj + 1) * 128],
                                                         rhs=scc[:, kc, :], start=(kc == 0), stop=(kc == 7)),
                      reads=[rw[kc // 4], r_cc], writes=[rps], inc=(jj == 5 and kc == 7))
        kb.op("dve", lambda e: e.tensor_tensor(
            out=mod[:, l, jb * 6:(jb + 1) * 6, :], in0=_r(ps[:, 0:12], "p (j n) -> p j n", n=2),
            in1=abT[:, l, jb * 6:(jb + 1) * 6].unsqueeze(2).to_broadcast([128, 6, 2]), op=ALU.add),
            reads=[rps, r_small], writes=[r_mod])

    def alloc_norm_bufs(self, es, nsub, nh):
        nhp = max(nh, 2)
        self.r_nt = [self.kb.res() for _ in range(6)]
        return dict(junk=self.sb(es, "n_junk", [128, D], BF16), xn=self.sb(es, "n_xn", [128, nsub, D], BF16),
                    ss=self.sb(es, "n_ss", [128, 4], F32), rstd=self.sb(es, "n_rstd", [128, 4], F32),
                    xnh=self.sb(es, "n_xnh", [nhp, D], BF16), ssh=self.sb(es, "n_ssh", [nhp, 1], F32),
                    rstdh=self.sb(es, "n_rstdh", [nhp, 1], F32))

    def norm_T(self, nb, xt, r_xts, nsub, xh, r_xh, nh, halo, hT, r_hT, main0, hl0, hr0, l, mi, col, psT, r_psT, psH, r_psH):
        kb = self.kb
        junk, xn, ss, rstd, xnh, ssh, rstdh = (nb[k] for k in ("junk", "xn", "ss", "rstd", "xnh", "ssh", "rstdh"))
        rt = self.r_nt
        ntok = nsub * 128
        for s in range(nsub):
            kb.op("act", lambda e, s=s: e.activation(out=junk[:], in_=xt[:, s, :], func=AF.Square, accum_out=ss[:, s:s + 1]),
                  reads=[r_xts[s]], writes=[rt[0]])
        kb.op("act", lambda e: e.activation(out=rstd[:, 0:nsub], in_=ss[:, 0:nsub], func=AF.Sqrt, bias=self.epsb[:], scale=1.0 / D),
              reads=[rt[0], self.r_cst], writes=[rt[1]])
        kb.op("dve", lambda e: e.reciprocal(out=rstd[:, 0:nsub], in_=rstd[:, 0:nsub]), reads=[rt[1]], writes=[rt[1]])
        for s in range(nsub):
            kb.op("dve", lambda e, s=s: e.tensor_scalar(out=xn[:, s, :], in0=xt[:, s, :], scalar1=rstd[:, s:s + 1], scalar2=None, op0=ALU.mult),
                  reads=[r_xts[s], rt[1]], writes=[rt[2]])
        A = self.modv[:, l, mi, :, col]
        B = self.modv[:, l, mi + 1, :, col]
        for kc in range(8):
            p = psT[kc % 2]
            rp = r_psT[kc % 2]
            for s in range(nsub):
                kb.op("pe", lambda e, s=s, kc=kc, p=p: e.matmul(p[:, s * 128:(s + 1) * 128], lhsT=xn[:, s, kc * 128:(kc + 1) * 128],
                                                             rhs=self.identb, start=True, stop=True),
                      reads=[rt[2], self.r_cst], writes=[rp], inc=(s == nsub - 1))
            kb.op("act", lambda e, kc=kc, p=p: e.activation(out=hT[:, kc, main0:main0 + ntok], in_=p[:, 0:ntok], func=AF.Identity,
                                                          bias=B[:, kc:kc + 1], scale=A[:, kc:kc + 1]),
                  reads=[rp, self.r_modv], writes=[r_hT])
        if nh == 0:
            return
        lv, rv = halo
        hh = nh // 2
        if lv or rv:
            kb.op("act", lambda e: e.activation(out=junk[0:nh, :], in_=xh[0:nh, :], func=AF.Square, accum_out=ssh[0:nh, 0:1]),
                  reads=[r_xh], writes=[rt[3]])
            kb.op("act", lambda e: e.activation(out=rstdh[0:nh, :], in_=ssh[0:nh, :], func=AF.Sqrt, bias=self.epsb[0:nh, :], scale=1.0 / D),
                  reads=[rt[3], self.r_cst], writes=[rt[4]])
            kb.op("dve", lambda e: e.reciprocal(out=rstdh[0:nh, :], in_=rstdh[0:nh, :]), reads=[rt[4]], writes=[rt[4]])
            kb.op("dve", lambda e: e.tensor_scalar(out=xnh[0:nh, :], in0=xh[0:nh, :], scalar1=rstdh[0:nh, 0:1], scalar2=None, op0=ALU.mult),
                  reads=[r_xh, rt[4]], writes=[rt[5]])
            for kc in range(8):
                kb.op("pe", lambda e, kc=kc: e.matmul(psH[:, kc * nh:(kc + 1) * nh], lhsT=xnh[0:nh, kc * 128:(kc + 1) * 128],
                                                   rhs=self.cstb[0:nh, 0, 0:nh], start=True, stop=True),
                      reads=[rt[5], self.r_cst], writes=[r_psH], inc=(kc == 7))
            for kc in range(8):
                for side, c0 in ((0, hl0), (1, hr0)):
                    kb.op("dve", lambda e, kc=kc, side=side, c0=c0: e.tensor_scalar(
                        out=hT[:, kc, c0:c0 + hh], in0=psH[:, kc * nh + side * hh:kc * nh + (side + 1) * hh],
                        scalar1=A[:, kc:kc + 1], scalar2=B[:, kc:kc + 1], op0=ALU.mult, op1=ALU.add),
                        reads=[r_psH, self.r_modv], writes=[r_hT])
        if not lv:
            kb.op("dve", lambda e: e.memset(hT[:, :, hl0:hl0 + hh], 0.0), writes=[r_hT])
        if not rv:
            kb.op("dve", lambda e: e.memset(hT[:, :, hr0:hr0 + hh], 0.0), writes=[r_hT])

    def load_rows_bcast(self, es, name, src_row, n):
        t = self.sb(es, name, [128, n], F32)
        r = self.kb.res(name)
        self.kb.dma("sp", t[:], src_row.partition_broadcast(128), writes=[r])
        return t, r

    def resid_epilogue(self, y, r_y, xt_s, r_xt, Grow, r_G, tmp, r_tmp, ss2, r_ss2, junk, r_junk):
        kb = self.kb
        for hb in range(2):
            kb.op("act", lambda e, hb=hb: e.activation(out=junk[:, 0:512], in_=y[:, hb * 512:(hb + 1) * 512], func=AF.Square,
                                                     accum_out=ss2[:, hb:hb + 1]), reads=r_y, writes=[r_ss2[0], r_junk])
        kb.op("dve", lambda e: e.tensor_tensor(out=ss2[:, 2:3], in0=ss2[:, 0:1], in1=ss2[:, 1:2], op=ALU.add), reads=[r_ss2[0]], writes=[r_ss2[1]])
        kb.op("act", lambda e: e.activation(out=ss2[:, 3:4], in_=ss2[:, 2:3], func=AF.Sqrt, bias=self.epsb[:], scale=1.0 / D),
              reads=[r_ss2[1], self.r_cst], writes=[r_ss2[2]])
        kb.op("dve", lambda e: e.reciprocal(out=ss2[:, 3:4], in_=ss2[:, 3:4]), reads=[r_ss2[2]], writes=[r_ss2[2]])
        kb.op("dve", lambda e: e.scalar_tensor_tensor(out=tmp[:], in0=y, scalar=ss2[:, 3:4], in1=Grow[:], op0=ALU.mult, op1=ALU.mult),
              reads=r_y + [r_ss2[2], r_G], writes=[r_tmp])
        kb.op("pool", lambda e: e.tensor_tensor(out=xt_s, in0=xt_s, in1=tmp[:], op=ALU.add), reads=[r_tmp, r_xt], writes=[r_xt])

    def phase_ffn(self, l, src, dst):
        nc, kb = self.nc, self.kb
        T = 256
        NT = S // T
        with ExitStack() as es:
            Wup = self.sb(es, "Wup", [128, 8, 2 * FH], BF16)
            Wdn = self.sb(es, "Wdn", [128, 22, D], BF16)
            r_wup = [kb.res() for _ in range(8)]
            r_wdn = [kb.res() for _ in range(2)]
            for kc in range(8):
                kb.dma("pool", Wup[:, kc, :], self.I("ffn_w_up")[l, kc * 128:(kc + 1) * 128, :], writes=[r_wup[kc]])
            for hh in range(2):
                kb.dma("pool", Wdn[:, hh * 11:(hh + 1) * 11, :], _r(self.I("ffn_w_down")[l, hh * 1408:(hh + 1) * 1408, :], "(j p) d -> p j d", p=128),
                       writes=[r_wdn[hh]])
            cw = self.sb(es, "f_cw", [128, 22, 3], F32)
            cb = self.sb(es, "f_cb", [128, 22], F32)
            r_cw = [kb.res(), kb.res()]
            kb.dma("sp", cw[:], self.I("fconv_wT")[:, l], writes=[r_cw[0]])
            kb.dma("sp", cb[:], self.I("fconv_bT")[:, l], writes=[r_cw[1]])
            Grow, r_G = self.load_rows_bcast(es, "f_G", self.vec[l, 1], D)
            nb = self.alloc_norm_bufs(es, 2, 2)
            xt = [self.sb(es, "f_xt%d" % i, [128, 2, D], F32) for i in range(2)]
            r_xt = [[kb.res() for _ in range(2)] for _ in range(2)]
            xh = [self.sb(es, "f_xh%d" % i, [2, D], F32) for i in range(2)]
            r_xh = [kb.res() for _ in range(2)]
            hT = [self.sb(es, "f_hT%d" % i, [128, 8, T + 2], BF16) for i in range(2)]
            r_hT = [kb.res() for _ in range(2)]
            ub = [self.sb(es, "f_ub%d" % i, [128, T + 2], F32) for i in range(2)]
            acc = [self.sb(es, "f_acc%d" % i, [128, T], F32) for i in range(2)]
            gl = [self.sb(es, "f_gl%d" % i, [128, T], F32) for i in range(2)]
            r_ub = [kb.res() for _ in range(2)]
            r_acc = [kb.res() for _ in range(2)]
            r_gl = [kb.res() for _ in range(2)]
            gT = self.sb(es, "f_gT", [128, 22, T], BF16)
            r_gT = [kb.res() for _ in range(22)]
            ss2 = [self.sb(es, "f_ss2%d" % i, [128, 4], F32) for i in range(2)]
            r_ss2 = [[kb.res() for _ in range(3)] for _ in range(2)]
            junk2 = self.sb(es, "f_junk2", [128, 512], BF16)
            r_junk2 = kb.res()
            psT = [self.psB[2], self.psB[3]]
            r_psT = [self.r_psB[2], self.r_psB[3]]
            psH = self.psB[3][:, 256:272]
            r_psH = self.r_psB[3]
            bankU = [self.psB[0], self.psA[1][:, 0:512]]
            bankV = [self.psB[1], self.psA[1][:, 512:1024]]
            psU = [bk[:, 0:T] for bk in bankU]
            psUh = [bk[:, T:T + 2] for bk in bankU]
            psV = [bk[:, 0:T] for bk in bankV]
            r_psU = [self.r_psB[0], kb.res()]
            r_psV = [self.r_psB[1], kb.res()]
            r_psUh = r_psU
            for b in range(2):
                kb.op("dve", lambda e, b=b: e.memset(xh[b][:], 0.0), writes=[r_xh[b]])

            def load(t):
                b = t % 2
                t0 = t * T
                for s in range(2):
                    kb.dma("sp", xt[b][:, s, :], src[t0 + s * 128:t0 + (s + 1) * 128, :], writes=[r_xt[b][s]])
                if 0 < t < NT - 1:
                    kb.dma("sp", xh[b][0:2, :], src[t0 - 1:t0 + T + 1:T + 1, :], writes=[r_xh[b]])
                elif t > 0:
                    kb.dma("sp", xh[b][0:1, :], src[t0 - 1:t0, :], writes=[r_xh[b]])
                else:
                    kb.dma("sp", xh[b][1:2, :], src[t0 + T:t0 + T + 1, :], writes=[r_xh[b]])

            ysb = [self.sb(es, "f_ysb%d" % i, [128, D], F32) for i in range(2)]
            r_ysb = [kb.res() for _ in range(2)]
            NTR = NT if self.max_tiles is None else self.max_tiles

            def do_norm(t):
                b = t % 2
                self.norm_T(nb, xt[b], r_xt[b], 2, xh[b], r_xh[b], 2, (t > 0, t < NT - 1), hT[b], r_hT[b], 1, 0, T + 1,
                            l, 2, 0, psT, r_psT, psH, r_psH)

            load(0)
            do_norm(0)
            for t in range(NTR):
                b = t % 2
                t0 = t * T
                if t + 1 < NT:
                    load(t + 1)
                for j in range(23):
                    pb = j % 2
                    qb = (j - 1) % 2
                    if j < 22:
                        for (pp, rr, c0, n0, n1) in ((bankU[pb][:, 0:T + 2], r_psU[pb], j * 128, 0, T + 2),
                                                     (psV[pb], r_psV[pb], FH + j * 128, 1, T + 1)):
                            for kc in range(8):
                                kb.op("pe", lambda e, kc=kc, pp=pp, c0=c0, n0=n0, n1=n1: e.matmul(
                                    pp, lhsT=Wup[:, kc, c0:c0 + 128], rhs=hT[b][:, kc, n0:n1], start=(kc == 0), stop=(kc == 7)),
                                    reads=[r_wup[kc], r_hT[b]], writes=[rr], inc=(kc == 7))
                    if j >= 1:
                        kb.op("act", lambda e, qb=qb: e.activation(out=gl[qb][:], in_=acc[qb][:], func=AF.Gelu), reads=[r_acc[qb]], writes=[r_gl[qb]])
                    if j < 22:
                        a = acc[pb]
                        bu = bankU[pb]
                        kb.op("act", lambda e, a=a, bu=bu, j=j: e.activation(out=a[:], in_=bu[:, 1:T + 1], func=AF.Identity,
                                                                           bias=cb[:, j:j + 1], scale=cw[:, j, 1:2]),
                              reads=[r_psU[pb]] + r_cw, writes=[r_acc[pb]])
                    if j >= 1:
                        kb.op("dve", lambda e, j=j, qb=qb: e.tensor_tensor(out=gT[:, j - 1, :], in0=gl[qb][:], in1=psV[qb], op=ALU.mult),
                              reads=[r_gl[qb], r_psV[qb]], writes=[r_gT[j - 1]])
                    if j < 22:
                        kb.op("dve", lambda e, bu=bu, a=a, j=j: e.scalar_tensor_tensor(out=a[:], in0=bu[:, 0:T], scalar=cw[:, j, 0:1], in1=a[:],
                                                                                    op0=ALU.mult, op1=ALU.add), reads=[r_psU[pb], r_acc[pb]] + r_cw, writes=[r_acc[pb]])
                        kb.op("dve", lambda e, bu=bu, a=a, j=j: e.scalar_tensor_tensor(out=a[:], in0=bu[:, 2:T + 2], scalar=cw[:, j, 2:3], in1=a[:],
                                                                                    op0=ALU.mult, op1=ALU.add), reads=[r_psU[pb], r_acc[pb]] + r_cw, writes=[r_acc[pb]])
                    if j == 8 and t + 1 < NTR:
                        do_norm(t + 1)
                for s in range(2):
                    py = self.psA[0]
                    rpy = self.r_psA[0]
                    for hb in range(2):
                        for j in range(22):
                            kb.op("pe", lambda e, j=j, s=s, hb=hb: e.matmul(py[:, hb * 512:(hb + 1) * 512], lhsT=gT[:, j, s * 128:(s + 1) * 128],
                                                                          rhs=Wdn[:, j, hb * 512:(hb + 1) * 512], start=(j == 0), stop=(j == 21)),
                                  reads=[r_gT[j], r_wdn[j // 11]], writes=[rpy], inc=(j == 21))
                    kb.op("act", lambda e, s=s: e.activation(out=ysb[s][:], in_=py[:, :], func=AF.Identity), reads=[rpy], writes=[r_ysb[s]])
                    self.resid_epilogue(ysb[s][:], [r_ysb[s]], xt[b][:, s, :], r_xt[b][s], Grow, r_G, ysb[s], r_ysb[s], ss2[s], r_ss2[s], junk2, r_junk2)
                    kb.dma("pool", dst[t0 + s * 128:t0 + (s + 1) * 128, :], xt[b][:, s, :], reads=[r_xt[b][s]])

    def phase_pool(self, src, dst):
        nc, kb = self.nc, self.kb
        T = 256
        NT = S // T
        HH = 8
        W = T + 2 * HH
        l = 1
        with ExitStack() as es:
            PW = self.sb(es, "p_PW", [128, 4, 2, 256], BF16)
            r_pw = kb.res()
            kb.dma("pool", PW[:], _r(self.I("pool_w"), "g (k p) n -> p g k n", p=128), writes=[r_pw])
            pbrow, r_pb = self.load_rows_bcast(es, "p_pb", self.I("pool_b"), D)
            psrow, r_psr = self.load_rows_bcast(es, "p_ps", self.I("pool_scale"), D)
            Grow, r_G = self.load_rows_bcast(es, "p_G", self.vec[l, 0], D)
            inv = self.sb(es, "p_inv", [128, 2, 4, 256], F32)
            r_inv = kb.res()
            kb.dma("sp", inv[:], _r(self.I("pool_inv"), "a g t -> (a g t)").partition_broadcast(128), writes=[r_inv])
            nb = self.alloc_norm_bufs(es, 2, 2 * HH)
            xt = [self.sb(es, "p_xt%d" % i, [128, 2, D], F32) for i in range(2)]
            r_xt = [[kb.res() for _ in range(2)] for _ in range(2)]
            xh = [self.sb(es, "p_xh%d" % i, [2 * HH, D], F32) for i in range(2)]
            r_xh = [kb.res() for _ in range(2)]
            hT = self.sb(es, "p_hT", [128, 8, W], F32)
            r_hT = kb.res()
            bufA = self.sb(es, "p_bA", [128, 2, W], F32)
            bufB = self.sb(es, "p_bB", [128, 2, W], F32)
            r_bA, r_bB = kb.res(), kb.res()
            pl = self.sb(es, "p_pl", [128, 8, T], BF16)
            r_pl = [kb.res() for _ in range(4)]
            ysb = [self.sb(es, "p_ysb%d" % i, [128, D], F32) for i in range(2)]
            r_ysb = [kb.res() for _ in range(2)]
            tmp = [self.sb(es, "p_tmp%d" % i, [128, D], F32) for i in range(2)]
            r_tmp = [kb.res() for _ in range(2)]
            ss2 = [self.sb(es, "p_ss2%d" % i, [128, 4], F32) for i in range(2)]
            r_ss2 = [[kb.res() for _ in range(3)] for _ in range(2)]
            junk2 = self.sb(es, "p_junk2", [128, 512], BF16)
            r_junk2 = kb.res()
            psT = [self.psB[2], self.psB[3]]
            r_psT = [self.r_psB[2], self.r_psB[3]]
            psH = self.psB[1][:, 0:128]
            r_psH = self.r_psB[1]
            for b in range(2):
                kb.op("dve", lambda e, b=b: e.memset(xh[b][:], 0.0), writes=[r_xh[b], self._rxh2(b)])

            def load(t):
                b = t % 2
                t0 = t * T
                for s in range(2):
                    kb.dma("sp", xt[b][:, s, :], src[t0 + s * 128:t0 + (s + 1) * 128, :], writes=[r_xt[b][s]])
                if 0 < t < NT - 1:
                    kb.dma("sp", xh[b][0:HH, :], src[t0 - HH:t0, :], writes=[r_xh[b]])
                    kb.dma("sp", xh[b][HH:2 * HH, :], src[t0 + T:t0 + T + HH, :], writes=[self._rxh2(b)])
                elif t > 0:
                    kb.dma("sp", xh[b][0:HH, :], src[t0 - HH:t0, :], writes=[r_xh[b]])
                else:
                    kb.dma("sp", xh[b][HH:2 * HH, :], src[t0 + T:t0 + T + HH, :], writes=[self._rxh2(b)])

            load(0)
            for t in range(NT):
                b = t % 2
                t0 = t * T
                if t + 1 < NT:
                    load(t + 1)
                rxh = Res()
                kb._merge(rxh.w, r_xh[b].w)
                kb._merge(rxh.w, self._rxh2(b).w)
                self.norm_T(nb, xt[b], r_xt[b], 2, xh[b], rxh, 2 * HH, (t > 0, t < NT - 1), hT, r_hT, HH, 0, HH + T,
                            l, 0, 0, psT, r_psT, psH, r_psH)
                kb._merge(r_xh[b].r, rxh.r)
                kb._merge(self._rxh2(b).r, rxh.r)
                for g in range(4):
                    hg = hT[:, 2 * g:2 * g + 2, :]
                    kb.op("dve", lambda e, hg=hg: e.tensor_tensor(out=bufA[:, :, 1:W], in0=hg[:, :, 0:W - 1], in1=hg[:, :, 1:W], op=ALU.add),
                          reads=[r_hT], writes=[r_bA])
                    cur, rcur, oth, roth = bufA, r_bA, bufB, r_bB
                    lo, hi = 1, W
                    for lvl in range(1, g + 1):
                        sh = 1 << (lvl - 1)
                        nlo, nhi = lo + sh, hi - sh
                        kb.op("dve", lambda e, cur=cur, oth=oth, sh=sh, nlo=nlo, nhi=nhi: e.tensor_tensor(
                            out=oth[:, :, nlo:nhi], in0=cur[:, :, nlo - sh:nhi - sh], in1=cur[:, :, nlo + sh:nhi + sh], op=ALU.add),
                            reads=[rcur], writes=[roth])
                        cur, rcur, oth, roth = oth, roth, cur, rcur
                        lo, hi = nlo, nhi
                    w = 2 << g
                    if 0 < t < NT - 1:
                        kb.op("dve", lambda e, cur=cur, hg=hg, g=g, w=w: e.scalar_tensor_tensor(
                            out=pl[:, 2 * g:2 * g + 2, :], in0=cur[:, :, HH:HH + T], scalar=1.0 / w, in1=hg[:, :, HH:HH + T],
                            op0=ALU.mult, op1=ALU.subtract), reads=[rcur, r_hT], writes=[r_pl[g]])
                    else:
                        a = 0 if t == 0 else 1
                        kb.op("dve", lambda e, cur=cur, oth=oth, g=g, a=a: e.tensor_tensor(
                            out=oth[:, :, HH:HH + T], in0=cur[:, :, HH:HH + T], in1=inv[:, a, g, :].unsqueeze(1).to_broadcast([128, 2, T]),
                            op=ALU.mult), reads=[rcur, r_inv], writes=[roth])
                        kb.op("dve", lambda e, oth=oth, hg=hg, g=g: e.tensor_tensor(
                            out=pl[:, 2 * g:2 * g + 2, :], in0=oth[:, :, HH:HH + T], in1=hg[:, :, HH:HH + T], op=ALU.subtract),
                            reads=[roth, r_hT], writes=[r_pl[g]])
                for s in range(2):
                    py = self.psA[s]
                    rpy = self.r_psA[s]
                    for g in range(4):
                        for kk in range(2):
                            kb.op("pe", lambda e, g=g, kk=kk, s=s, py=py: e.matmul(py[:, g * 256:(g + 1) * 256], lhsT=pl[:, 2 * g + kk, s * 128:(s + 1) * 128],
                                                                                 rhs=PW[:, g, kk, :], start=(kk == 0), stop=(kk == 1)),
                                  reads=[r_pl[g], r_pw], writes=[rpy], inc=(g == 3 and kk == 1))
                    kb.op("dve", lambda e, s=s, py=py: e.tensor_tensor(out=ysb[s][:], in0=py[:, :], in1=pbrow[:], op=ALU.add),
                          reads=[rpy, r_pb], writes=[r_ysb[s]])
                    kb.op("pool", lambda e, s=s: e.tensor_tensor(out=ysb[s][:], in0=ysb[s][:], in1=psrow[:], op=ALU.mult),
                          reads=[r_ysb[s], r_psr], writes=[r_ysb[s]])
                    self.resid_epilogue(ysb[s][:], [r_ysb[s]], xt[b][:, s, :], r_xt[b][s], Grow, r_G, ysb[s], r_ysb[s], ss2[s], r_ss2[s], junk2, r_junk2)
                    kb.dma("pool", dst[t0 + s * 128:t0 + (s + 1) * 128, :], xt[b][:, s, :], reads=[r_xt[b][s]])

    def _rxh2(self, b):
        if not hasattr(self, "_rxh2_l"):
            self._rxh2_l = [self.kb.res(), self.kb.res()]
        return self._rxh2_l[b]

    def phase_l0_mixer(self, x_in, ctx_in, dst):
        kb = self.kb
        ST = S + LC
        self.qT = self.dscr("qT", [512, S], BF16)
        self.kT = self.dscr("kT", [512, ST], BF16)
        self.v_tok = self.dscr("v_tok", [ST, 512], BF16)
        self.sz_tok = self.dscr("sz_tok", [S, D], BF16)
        self.xs_tok = self.dscr("xs_tok", [ST, D], BF16)
        self.B_tok = self.dscr("B_tok", [ST, 512], BF16)
        self.BT = self.dscr("BT", [512, ST], BF16)
        self.CT = self.dscr("CT", [512, ST], BF16)
        self.dt_tok = self.dscr("dt_tok", [ST, 32], F32)
        self.yf = self.dscr("yf", [S, D], F32)
        self.ssm_tok = self.dscr("ssm_tok", [S, D], BF16)
        self.att_tok = self.dscr("att_tok", [S, 512], BF16)
        self.l0_proj(x_in, ctx_in)
        kb.barrier()
        if self.stop_after == "proj":
            return
        self.l0_ssd(0)
        kb.barrier()
        self.l0_ssd(1)
        kb.barrier()
        if self.stop_after == "ssd":
            return
        self.l0_att()
        kb.barrier()
        if self.stop_after == "att":
            return
        self.l0_out(x_in, dst)

    def l0_proj(self, x_in, ctx_in):
        kb = self.kb
        T = 256
        NT = S // T
        w_in = self.I("w_in")
        with ExitStack() as es:
            W = self.sb(es, "Win", [128, 8, 4640], BF16)
            r_w = [kb.res() for _ in range(8)]
            for kc in range(8):
                kb.dma("pool", W[:, kc, :], w_in[kc * 128:(kc + 1) * 128, :], writes=[r_w[kc]])
            cw = self.sb(es, "c_cw", [128, 16, 3], F32)
            cb = self.sb(es, "c_cb", [128, 16], F32)
            r_cw = [kb.res(), kb.res()]
            kb.dma("sp", cw[:], self.I("conv_wT"), writes=[r_cw[0]])
            kb.dma("sp", cb[:], self.I("conv_bT"), writes=[r_cw[1]])
            nb = self.alloc_norm_bufs(es, 2, 2)
            xt = [self.sb(es, "j_xt%d" % i, [128, 2, D], F32) for i in range(2)]
            r_xt = [[kb.res() for _ in range(2)] for _ in range(2)]
            xh = [self.sb(es, "j_xh%d" % i, [2, D], F32) for i in range(2)]
            r_xh = [kb.res() for _ in range(2)]
            hTs = [self.sb(es, "j_hT%d" % i, [128, 8, T + 2], BF16) for i in range(2)]
            r_hTs = [kb.res() for _ in range(2)]
            ub = [self.sb(es, "j_ub%d" % i, [128, T + 2], F32) for i in range(2)]
            acc = [self.sb(es, "j_acc%d" % i, [128, T], F32) for i in range(2)]
            sx = [self.sb(es, "j_sx%d" % i, [128, T], BF16) for i in range(2)]
            r_ub = [kb.res() for _ in range(2)]
            r_acc = [kb.res() for _ in range(2)]
            r_sx = [kb.res() for _ in range(2)]
            fst = [self.sb(es, "j_fst%d" % i, [128, T], BF16) for i in range(2)]
            r_fst = [kb.res() for _ in range(2)]
            xs_st = self.sb(es, "j_xsst", [128, 2, D], BF16)
            b_st = self.sb(es, "j_bst", [128, 2, 512], BF16)
            sz_st = self.sb(es, "j_szst", [128, 2, D], BF16)
            v_st = self.sb(es, "j_vst", [128, 2, 512], BF16)
            dt_st = self.sb(es, "j_dtst", [128, 2, 32], F32)
            r_xsst, r_bst, r_szst, r_vst, r_dtst = (kb.res() for _ in range(5))
            psT = [self.psB[2], self.psB[3]]
            r_psT = [self.r_psB[2], self.r_psB[3]]
            psH = self.psB[3][:, 256:272]
            r_psH = self.r_psB[3]
            bankF = [self.psB[0], self.psB[1]]
            r_bankF = [self.r_psB[0], self.r_psB[1]]
            bankX = [self.psA[1][:, 0:512], self.psA[1][:, 512:1024]]
            r_bankX = [kb.res(), kb.res()]
            bankM = [self.psA[0][:, 0:512], self.psA[0][:, 512:1024]]
            r_bankM = [kb.res(), kb.res()]
            for b in range(2):
                kb.op("dve", lambda e, b=b: e.memset(xh[b][:], 0.0), writes=[r_xh[b]])

            def srcrows(t):
                return (x_in, t * T) if t < NT else (ctx_in, 0)

            def load(t, b):
                src, t0 = srcrows(t)
                for s in range(2):
                    kb.dma("sp", xt[b][:, s, :], src[t0 + s * 128:t0 + (s + 1) * 128, :], writes=[r_xt[b][s]])
                if t >= NT:
                    return
                if 0 < t < NT - 1:
                    kb.dma("sp", xh[b][0:2, :], src[t0 - 1:t0 + T + 1:T + 1, :], writes=[r_xh[b]])
                elif t > 0:
                    kb.dma("sp", xh[b][0:1, :], src[t0 - 1:t0, :], writes=[r_xh[b]])
                else:
                    kb.dma("sp", xh[b][1:2, :], src[t0 + T:t0 + T + 1, :], writes=[r_xh[b]])

            tiles = list(range(NT + 1))
            if self.max_tiles is not None:
                tiles = list(range(self.max_tiles)) + [NT]

            def do_norm(ti):
                t = tiles[ti]
                b = ti % 2
                isctx = t == NT
                halo = (False, False) if isctx else (t > 0, t < NT - 1)
                self.norm_T(nb, xt[b], r_xt[b], 2, xh[b], r_xh[b], 2, halo, hTs[b], r_hTs[b], 1, 0, T + 1,
                            0, 0, 1 if isctx else 0, psT, r_psT, psH, r_psH)

            load(tiles[0], 0)
            do_norm(0)
            fi = 0
            for ti, t in enumerate(tiles):
                b = ti % 2
                hT, r_hT = hTs[b], r_hTs[b]
                isctx = t == NT
                g0 = S if isctx else t * T
                if ti + 1 < len(tiles):
                    load(tiles[ti + 1], (ti + 1) % 2)
                chunks = []
                if not isctx:
                    chunks += [("q", c, c * 128) for c in range(4)]
                chunks += [("k", c, 1536 + c * 128) for c in range(4)]
                for kind, c, col in chunks:
                    pb = fi % 2
                    fi += 1
                    bk, rbk = bankF[pb], r_bankF[pb]
                    for kc in range(8):
                        kb.op("pe", lambda e, kc=kc, bk=bk, col=col: e.matmul(
                            bk[:, 0:T], lhsT=W[:, kc, col:col + 128], rhs=hT[:, kc, 1:T + 1], start=(kc == 0), stop=(kc == 7)),
                            reads=[r_w[kc], r_hT], writes=[rbk], inc=(kc == 7))
                    st, rst = fst[pb], r_fst[pb]
                    kb.op("act", lambda e, st=st, bk=bk, kind=kind: e.activation(out=st[:], in_=bk[:, 0:T], func=AF.Identity,
                                                                               scale=(0.125 if kind == "q" else 1.0)),
                          reads=[rbk], writes=[rst])
                    dstT = self.qT if kind == "q" else self.kT
                    kb.dma("pool", dstT[c * 128:(c + 1) * 128, g0:g0 + T], st[:], reads=[rst])
                fbase = fi
                fi += 16
                for i in range(18):
                    if i < 16:
                        pb = (fbase + i) % 2
                        bk, rbk = bankF[pb], r_bankF[pb]
                        col = 2560 + i * 128
                        for kc in range(8):
                            kb.op("pe", lambda e, kc=kc, bk=bk, col=col: e.matmul(
                                bk[:, 0:T + 2], lhsT=W[:, kc, col:col + 128], rhs=hT[:, kc, 0:T + 2], start=(kc == 0), stop=(kc == 7)),
                                reads=[r_w[kc], r_hT], writes=[rbk], inc=(kc == 7))
                    if 1 <= i <= 16:
                        c = i - 1
                        cb_ = c % 2
                        kb.op("act", lambda e, cb_=cb_: e.activation(out=sx[cb_][:], in_=acc[cb_][:], func=AF.Silu), reads=[r_acc[cb_]], writes=[r_sx[cb_]])
                        if c >= 8:
                            dstT = self.BT if c < 12 else self.CT
                            cc = (c - 8) % 4
                            kb.dma("pool", dstT[cc * 128:(cc + 1) * 128, g0:g0 + T], sx[cb_][:], reads=[r_sx[cb_]])
                    if i < 16:
                        ib = i % 2
                        kb.op("act", lambda e, ib=ib, bk=bk, i=i: e.activation(out=acc[ib][:], in_=bk[:, 1:T + 1], func=AF.Identity,
                                                                            bias=cb[:, i:i + 1], scale=cw[:, i, 1:2]),
                              reads=[rbk] + r_cw, writes=[r_acc[ib]])
                    if 1 <= i <= 12:
                        c = i - 1
                        cb_ = c % 2
                        for s in range(2):
                            kb.op("pe", lambda e, s=s, cb_=cb_: e.matmul(bankX[cb_][:, s * 128:(s + 1) * 128], lhsT=sx[cb_][:, s * 128:(s + 1) * 128],
                                                                       rhs=self.identb, start=True, stop=True),
                                  reads=[r_sx[cb_], self.r_cst], writes=[r_bankX[cb_]], inc=(s == 1))
                    if 2 <= i <= 13:
                        c = i - 2
                        cb_ = c % 2
                        if c < 8:
                            kb.op("dve", lambda e, c=c, cb_=cb_: e.tensor_copy(out=xs_st[:, :, c * 128:(c + 1) * 128],
                                                                              in_=_r(bankX[cb_][:, 0:256], "p (s f) -> p s f", s=2)),
                                  reads=[r_bankX[cb_]], writes=[r_xsst])
                        else:
                            kb.op("dve", lambda e, c=c, cb_=cb_: e.tensor_copy(out=b_st[:, :, (c - 8) * 128:(c - 7) * 128],
                                                                              in_=_r(bankX[cb_][:, 0:256], "p (s f) -> p s f", s=2)),
                                  reads=[r_bankX[cb_]], writes=[r_bst])
                    if i < 16:
                        a, c = acc[ib], i
                        kb.op("dve", lambda e, bk=bk, a=a, c=c: e.scalar_tensor_tensor(out=a[:], in0=bk[:, 0:T], scalar=cw[:, c, 0:1], in1=a[:],
                                                                                    op0=ALU.mult, op1=ALU.add), reads=[rbk, r_acc[ib]] + r_cw, writes=[r_acc[ib]])
                        kb.op("dve", lambda e, bk=bk, a=a, c=c: e.scalar_tensor_tensor(out=a[:], in0=bk[:, 2:T + 2], scalar=cw[:, c, 2:3], in1=a[:],
                                                                                    op0=ALU.mult, op1=ALU.add), reads=[rbk, r_acc[ib]] + r_cw, writes=[r_acc[ib]])
                    if i == 8 and ti + 1 < len(tiles):
                        do_norm(ti + 1)
                kb.dma("pool", _r(self.xs_tok[g0:g0 + T, :], "(s p) d -> p s d", p=128), xs_st[:], reads=[r_xsst])
                kb.dma("pool", _r(self.B_tok[g0:g0 + T, :], "(s p) d -> p s d", p=128), b_st[:], reads=[r_bst])
                mi = 0
                for s in range(2):
                    tm = [("v", 2048, 512), ("d", 4608, 32)]
                    if not isctx:
                        tm = [("z", 512, 512), ("z", 1024, 512)] + tm
                    for kind, col, n in tm:
                        bk, rbk = bankM[mi % 2], r_bankM[mi % 2]
                        mi += 1
                        for kc in range(8):
                            kb.op("pe", lambda e, kc=kc, bk=bk, col=col, n=n, s=s: e.matmul(
                                bk[:, 0:n], lhsT=hT[:, kc, 1 + s * 128:1 + (s + 1) * 128], rhs=W[:, kc, col:col + n], start=(kc == 0), stop=(kc == 7)),
                                reads=[r_w[kc], r_hT], writes=[rbk], inc=(kc == 7))
                        if kind == "z":
                            kb.op("act", lambda e, bk=bk, col=col, s=s: e.activation(out=sz_st[:, s, col - 512:col], in_=bk[:, 0:512], func=AF.Silu),
                                  reads=[rbk], writes=[r_szst])
                        elif kind == "v":
                            kb.op("act", lambda e, bk=bk, s=s: e.activation(out=v_st[:, s, :], in_=bk[:, 0:512], func=AF.Identity),
                                  reads=[rbk], writes=[r_vst])
                        else:
                            kb.op("dve", lambda e, bk=bk, s=s: e.tensor_copy(out=dt_st[:, s, :], in_=bk[:, 0:32]), reads=[rbk], writes=[r_dtst])
                if not isctx:
                    kb.dma("pool", _r(self.sz_tok[g0:g0 + T, :], "(s p) d -> p s d", p=128), sz_st[:], reads=[r_szst])
                kb.dma("pool", _r(self.v_tok[g0:g0 + T, :], "(s p) d -> p s d", p=128), v_st[:], reads=[r_vst])
                kb.dma("pool", _r(self.dt_tok[g0:g0 + T, :], "(s p) d -> p s d", p=128), dt_st[:], reads=[r_dtst])

    def l0_ssd(self, d):
        kb = self.kb
        NCH = S // 128
        with ExitStack() as es:
            rows = self.sb(es, "d_rows", [128, 3, 32], F32)
            r_rows = kb.res()
            kb.dma("sp", rows[:], _r(self.I("ssm_rows"), "a b -> (a b)").partition_broadcast(128), writes=[r_rows])
            negA = self.sb(es, "d_negA", [128, 32], F32)
            dsum = self.sb(es, "d_dsum", [128, 16], F32)
            r_negA = kb.res()
            kb.op("act", lambda e: e.activation(out=negA[:], in_=rows[:, 0, :], func=AF.Exp), reads=[r_rows], writes=[r_negA])
            kb.op("dve", lambda e: e.tensor_scalar(out=negA[:], in0=negA[:], scalar1=-1.0, scalar2=None, op0=ALU.mult), reads=[r_negA], writes=[r_negA])
            kb.op("dve", lambda e: e.tensor_tensor(out=dsum[:], in0=rows[:, 2, 0:16], in1=rows[:, 2, 16:32], op=ALU.add), reads=[r_rows], writes=[r_negA])
            ngrow, r_ng = self.load_rows_bcast(es, "d_ng", self.I("ssm_norm_g"), D)
            xs = [self.sb(es, "d_xs%d" % i, [128, D], BF16) for i in range(2)]
            Bt = [self.sb(es, "d_Bt%d" % i, [128, 512], BF16) for i in range(2)]
            BTt = [self.sb(es, "d_BTt%d" % i, [128, 4, 128], BF16) for i in range(2)]
            CTt = [self.sb(es, "d_CTt%d" % i, [128, 4, 128], BF16) for i in range(2)]
            dtr = [self.sb(es, "d_dtr%d" % i, [128, 32], F32) for i in range(2)]
            yfl = [self.sb(es, "d_yfl%d" % i, [128, D], F32) for i in range(2)]
            szl = [self.sb(es, "d_szl%d" % i, [128, D], BF16) for i in range(2)]
            r_ld = [[kb.res() for _ in range(7)] for _ in range(2)]
            H = self.sb(es, "d_H", [128, D], F32)
            Hb = self.sb(es, "d_Hb", [128, D], BF16)
            r_H, r_Hb = kb.res(), kb.res()
            kb.op("dve", lambda e: e.memset(H[:], 0.0), writes=[r_H])
            kb.op("pool", lambda e: e.memset(Hb[:], 0.0), writes=[r_Hb])
            sms = [self.sb(es, "d_sm%d" % i, [128, 8, 16], F32) for i in range(2)]
            r_sms = [[kb.res() for _ in range(8)] for _ in range(2)]
            dws = [self.sb(es, "d_dw%d" % i, [128, 16], F32) for i in range(2)]
            r_dws = [kb.res() for _ in range(2)]
            ysbs = [self.sb(es, "d_ysb%d" % i, [128, D], F32) for i in range(2)]
            r_ysbs = [kb.res() for _ in range(2)]
            Lh = self.sb(es, "d_Lh", [128, 16, 128], F32)
            r_Lh = kb.res()
            seg = self.sb(es, "d_seg", [128, 4, 128], F32)
            r_seg = kb.res()
            cbm = self.sb(es, "d_cbm", [128, 4, 128], F32)
            r_cbm = kb.res()
            M = self.sb(es, "d_M", [128, 16, 128], BF16)
            r_M = [kb.res() for _ in range(4)]
            xdt = self.sb(es, "d_xdt", [128, D], BF16)
            xdtws = [self.sb(es, "d_xdtw%d" % i, [128, D], BF16) for i in range(2)]
            r_xdt = kb.res()
            r_xdtws = [kb.res() for _ in range(2)]
            yc = self.sb(es, "d_yc", [128, D], F32)
            yt = self.sb(es, "d_yt", [128, D], F32)
            r_yc, r_yt = kb.res(), kb.res()
            ss4 = self.sb(es, "d_ss4", [128, 8], F32)
            r_ss4 = [kb.res() for _ in range(3)]
            junk = self.sb(es, "d_junk", [128, 256], BF16)
            r_junk = kb.res()
            so = self.sb(es, "d_so", [128, D], BF16)
            r_so = kb.res()
            one1 = self.sb(es, "d_one", [128, 1], F32)
            kb.op("pool", lambda e: e.memset(one1[:], 1.0), writes=[r_negA])
            y_ps, r_yps = self.psA[0], [self.r_psA[0], kb.res()]
            yo_ps, r_yops = self.psA[1], [self.r_psA[1], kb.res()]
            D_ps, r_Dps = self.psB[0], self.r_psB[0]
            cb_ps, r_cbps = self.psB[1], self.r_psB[1]
            sm_ps, r_smps = self.psB[2], self.r_psB[2]
            S_ps, r_Sps = self.psB[3], self.r_psB[3]
            Rm = self.cst[:, 1 + d, :]
            Mk = self.cst[:, 3 + d, :]
            ones = self.cst[:, 5, :]
            lat = list(range(NCH)) if d == 0 else list(range(NCH - 1, -1, -1))
            if self.max_tiles is not None:
                lat = lat[:self.max_tiles]
            order = [("c", 0), ("c", 1)] if d == 0 else [("c", 1), ("c", 0)]
            order += [("l", c) for c in lat]

            def tok0(kc):
                return S + kc[1] * 128 if kc[0] == "c" else kc[1] * 128

            def load(i):
                b = i % 2
                kc = order[i]
                t0 = tok0(kc)
                kb.dma("sp", xs[b][:], self.xs_tok[t0:t0 + 128, :], writes=[r_ld[b][0]])
                kb.dma("sp", Bt[b][:], self.B_tok[t0:t0 + 128, :], writes=[r_ld[b][1]])
                kb.dma("sp", dtr[b][:], self.dt_tok[t0:t0 + 128, :], writes=[r_ld[b][2]])
                if kc[0] == "l":
                    kb.dma("sp", BTt[b][:], _r(self.BT[:, t0:t0 + 128], "(g n) t -> n g t", n=128), writes=[r_ld[b][3]])
                    kb.dma("sp", CTt[b][:], _r(self.CT[:, t0:t0 + 128], "(g n) t -> n g t", n=128), writes=[r_ld[b][4]])
                    if d == 1:
                        kb.dma("sp", yfl[b][:], self.yf[t0:t0 + 128, :], writes=[r_ld[b][5]])
                        kb.dma("sp", szl[b][:], self.sz_tok[t0:t0 + 128, :], writes=[r_ld[b][6]])

            def partA(i):
                b = i % 2
                kc = order[i]
                islat = kc[0] == "l"
                rl = r_ld[b]
                sm, r_sm, dw, r_dw, xdtw, r_xdtw = sms[b], r_sms[b], dws[b], r_dws[b], xdtws[b], r_xdtws[b]
                hs = slice(d * 16, (d + 1) * 16)
                kb.op("dve", lambda e: e.tensor_tensor(out=sm[:, 0, :], in0=dtr[b][:, hs], in1=rows[:, 1, hs], op=ALU.add), reads=[rl[2], r_rows], writes=[r_sm[0]])
                kb.op("act", lambda e: e.activation(out=sm[:, 1, :], in_=sm[:, 0, :], func=AF.Exp), reads=[r_sm[0]], writes=[r_sm[1]])
                kb.op("act", lambda e: e.activation(out=sm[:, 2, :], in_=sm[:, 1, :], func=AF.Ln, bias=one1[:]), reads=[r_sm[1], r_negA], writes=[r_sm[2]])
                kb.op("dve", lambda e: e.tensor_tensor(out=sm[:, 3, :], in0=sm[:, 2, :], in1=negA[:, hs], op=ALU.mult), reads=[r_sm[2], r_negA], writes=[r_sm[3]])
                kb.op("pe", lambda e: e.matmul(sm_ps[:, 0:16], lhsT=Rm, rhs=sm[:, 3, :], start=True, stop=True), reads=[r_sm[3], self.r_cst], writes=[r_smps], inc=False)
                kb.op("pe", lambda e: e.matmul(sm_ps[:, 16:32], lhsT=ones, rhs=sm[:, 3, :], start=True, stop=True), reads=[r_sm[3], self.r_cst], writes=[r_smps])
                kb.op("dve", lambda e: e.tensor_copy(out=sm[:, 4, :], in_=sm_ps[:, 0:16]), reads=[r_smps], writes=[r_sm[4]])
                kb.op("act", lambda e: e.activation(out=sm[:, 7, :], in_=sm_ps[:, 16:32], func=AF.Exp), reads=[r_smps], writes=[r_sm[7]])
                kb.op("dve", lambda e: e.tensor_tensor(out=sm[:, 0, :], in0=sm_ps[:, 16:32], in1=sm[:, 4, :], op=ALU.subtract), reads=[r_smps, r_sm[4]], writes=[r_sm[0]])
                kb.op("act", lambda e: e.activation(out=sm[:, 6, :], in_=sm[:, 4, :], func=AF.Exp), reads=[r_sm[4]], writes=[r_sm[6]])
                kb.op("act", lambda e: e.activation(out=sm[:, 5, :], in_=sm[:, 0, :], func=AF.Exp), reads=[r_sm[0]], writes=[r_sm[5]])
                xs3 = _r(xs[b][:], "p (h q) -> p h q", h=16)
                if islat:
                    kb.op("dve", lambda e: e.tensor_tensor(out=Lh[:], in0=Mk.unsqueeze(1).to_broadcast([128, 16, 128]),
                                                           in1=sm[:, 3, :].unsqueeze(2).to_broadcast([128, 16, 128]), op=ALU.mult),
                          reads=[r_sm[3], self.r_cst], writes=[r_Lh])
                    for g in range(4):
                        kb.op("pe", lambda e, g=g: e.matmul(cb_ps[:, g * 128:(g + 1) * 128], lhsT=BTt[b][:, g, :], rhs=CTt[b][:, g, :], start=True, stop=True),
                              reads=[rl[3], rl[4]], writes=[r_cbps], inc=(g == 3))
                    kb.op("dve", lambda e: e.tensor_tensor(out=_r(xdt[:], "p (h q) -> p h q", h=16), in0=xs3, in1=sm[:, 2, :].unsqueeze(2).to_broadcast([128, 16, 64]), op=ALU.mult),
                          reads=[rl[0], r_sm[2]], writes=[r_xdt])
                    kb.op("dve", lambda e: e.tensor_tensor(out=cbm[:], in0=_r(cb_ps[:, :], "p (g l) -> p g l", g=4),
                                                           in1=Rm.unsqueeze(1).to_broadcast([128, 4, 128]), op=ALU.mult),
                          reads=[r_cbps, self.r_cst], writes=[r_cbm])
                kb.op("dve", lambda e: e.tensor_tensor(out=dw[:], in0=sm[:, 2, :], in1=sm[:, 5, :], op=ALU.mult), reads=[r_sm[2], r_sm[5]], writes=[r_dw])
                kb.op("dve", lambda e: e.tensor_tensor(out=_r(xdtw[:], "p (h q) -> p h q", h=16), in0=xs3, in1=dw[:].unsqueeze(2).to_broadcast([128, 16, 64]), op=ALU.mult),
                      reads=[rl[0], r_dw], writes=[r_xdtw])
                if islat:
                    for g in range(4):
                        for r4 in range(4):
                            h = g * 4 + r4
                            kb.op("pe", lambda e, h=h, r4=r4: e.matmul(D_ps[:, r4 * 128:(r4 + 1) * 128], lhsT=Lh[:, h, :], rhs=Rm, start=True, stop=True),
                                  reads=[r_Lh, self.r_cst], writes=[r_Dps], inc=(r4 == 3))
                        kb.op("act", lambda e: e.activation(out=_r(seg[:], "p r l -> p (r l)"), in_=D_ps[:, :], func=AF.Exp), reads=[r_Dps], writes=[r_seg])
                        kb.op("dve", lambda e, g=g: e.tensor_tensor(out=M[:, g * 4:(g + 1) * 4, :], in0=seg[:], in1=cbm[:, g, :].unsqueeze(1).to_broadcast([128, 4, 128]), op=ALU.mult),
                              reads=[r_seg, r_cbm], writes=[r_M[g]])
                    for h in range(16):
                        kb.op("pe", lambda e, h=h: e.matmul(y_ps[:, h * 64:(h + 1) * 64], lhsT=M[:, h, :], rhs=xdt[:, h * 64:(h + 1) * 64], start=True, stop=True),
                              reads=[r_M[h // 4], r_xdt], writes=r_yps, inc=(h == 15))
                    kb.op("act", lambda e: e.activation(out=ysbs[b][:], in_=y_ps[:, :], func=AF.Identity), reads=r_yps, writes=[r_ysbs[b]])

            def partB(i):
                b = i % 2
                kc = order[i]
                t0 = tok0(kc)
                islat = kc[0] == "l"
                rl = r_ld[b]
                sm, r_sm, xdtw, r_xdtw = sms[b], r_sms[b], xdtws[b], r_xdtws[b]
                xs3 = _r(xs[b][:], "p (h q) -> p h q", h=16)
                if islat:
                    for g in range(4):
                        kb.op("pe", lambda e, g=g: e.matmul(yo_ps[:, g * 256:(g + 1) * 256], lhsT=CTt[b][:, g, :], rhs=Hb[:, g * 256:(g + 1) * 256], start=True, stop=True),
                              reads=[rl[4], r_Hb], writes=r_yops, inc=(g == 3))
                for gp in range(2):
                    for gg in range(2):
                        g = gp * 2 + gg
                        if gp == 0 or True:
                            pass
                    if gp == 0:
                        for gg in range(2):
                            g = gg
                            kb.op("pe", lambda e, g=g, gg=gg: e.matmul(S_ps[:, gg * 256:(gg + 1) * 256], lhsT=Bt[b][:, g * 128:(g + 1) * 128], rhs=xdtw[:, g * 256:(g + 1) * 256], start=True, stop=True),
                                  reads=[rl[1], r_xdtw], writes=[r_Sps], inc=(gg == 1))
                if islat:
                    kb.op("dve", lambda e: e.tensor_tensor(out=_r(yt[:], "p (h q) -> p h q", h=16), in0=_r(yo_ps[:, :], "p (h q) -> p h q", h=16),
                                                           in1=sm[:, 6, :].unsqueeze(2).to_broadcast([128, 16, 64]), op=ALU.mult),
                          reads=r_yops + [r_sm[6]], writes=[r_yt])
                for gp in range(2):
                    if gp == 1:
                        for gg in range(2):
                            g = 2 + gg
                            kb.op("pe", lambda e, g=g, gg=gg: e.matmul(S_ps[:, gg * 256:(gg + 1) * 256], lhsT=Bt[b][:, g * 128:(g + 1) * 128], rhs=xdtw[:, g * 256:(g + 1) * 256], start=True, stop=True),
                                  reads=[rl[1], r_xdtw], writes=[r_Sps], inc=(gg == 1))
                    Hs = H[:, gp * 512:(gp + 1) * 512]
                    kb.op("dve", lambda e, Hs=Hs, gp=gp: e.tensor_tensor(out=_r(Hs, "p (h q) -> p h q", h=8), in0=_r(Hs, "p (h q) -> p h q", h=8),
                                                                        in1=sm[:, 7, gp * 8:(gp + 1) * 8].unsqueeze(2).to_broadcast([128, 8, 64]), op=ALU.mult),
                          reads=[r_H, r_sm[7]], writes=[r_H])
                    kb.op("dve", lambda e, Hs=Hs: e.tensor_tensor(out=Hs, in0=Hs, in1=S_ps[:, :], op=ALU.add), reads=[r_H, r_Sps], writes=[r_H])
                kb.op("act", lambda e: e.activation(out=Hb[:], in_=H[:], func=AF.Identity), reads=[r_H], writes=[r_Hb])
                if not islat:
                    return
                kb.op("pool", lambda e: e.tensor_tensor(out=yc[:], in0=yt[:], in1=ysbs[b][:], op=ALU.add), reads=[r_yt, r_ysbs[b]], writes=[r_yc])
                if d == 0:
                    kb.dma("pool", self.yf[t0:t0 + 128, :], yc[:], reads=[r_yc])
                    return
                kb.op("pool", lambda e: e.tensor_tensor(out=yc[:], in0=yc[:], in1=yfl[b][:], op=ALU.add), reads=[r_yc, rl[5]], writes=[r_yc])
                kb.op("dve", lambda e: e.tensor_tensor(out=_r(yt[:], "p (h q) -> p h q", h=16), in0=xs3, in1=dsum[:].unsqueeze(2).to_broadcast([128, 16, 64]), op=ALU.mult),
                      reads=[rl[0], r_negA], writes=[r_yt])
                kb.op("pool", lambda e: e.tensor_tensor(out=yc[:], in0=yc[:], in1=yt[:], op=ALU.add), reads=[r_yc, r_yt], writes=[r_yc])
                kb.op("dve", lambda e: e.tensor_tensor(out=yc[:], in0=yc[:], in1=szl[b][:], op=ALU.mult), reads=[r_yc, rl[6]], writes=[r_yc])
                for g in range(4):
                    kb.op("act", lambda e, g=g: e.activation(out=junk[:], in_=yc[:, g * 256:(g + 1) * 256], func=AF.Square, accum_out=ss4[:, g:g + 1]),
                          reads=[r_yc], writes=[r_ss4[0], r_junk])
                kb.op("act", lambda e: e.activation(out=ss4[:, 4:8], in_=ss4[:, 0:4], func=AF.Sqrt, bias=self.epsb[:], scale=1.0 / 256), reads=[r_ss4[0], self.r_cst], writes=[r_ss4[1]])
                kb.op("dve", lambda e: e.reciprocal(out=ss4[:, 4:8], in_=ss4[:, 4:8]), reads=[r_ss4[1]], writes=[r_ss4[1]])
                kb.op("dve", lambda e: e.tensor_tensor(out=_r(yt[:], "p (g q) -> p g q", g=4), in0=_r(yc[:], "p (g q) -> p g q", g=4),
                                                       in1=ss4[:, 4:8].unsqueeze(2).to_broadcast([128, 4, 256]), op=ALU.mult), reads=[r_yc, r_ss4[1]], writes=[r_yt])
                kb.op("pool", lambda e: e.tensor_tensor(out=so[:], in0=yt[:], in1=ngrow[:], op=ALU.mult), reads=[r_yt, r_ng], writes=[r_so])
                kb.dma("pool", self.ssm_tok[t0:t0 + 128, :], so[:], reads=[r_so])

            load(0)
            partA(0)
            for i in range(len(order)):
                if i + 1 < len(order):
                    load(i + 1)
                    partA(i + 1)
                partB(i)

    def l0_att(self):
        kb = self.kb
        ST = S + LC
        NTL = S // 128
        tab_in = self.I("rpbtab")
        with ExitStack() as es:
            qh = [self.sb(es, "a_q%d" % i, [64, S], BF16) for i in range(2)]
            kh = [self.sb(es, "a_k%d" % i, [64, ST], BF16) for i in range(2)]
            va = [self.sb(es, "a_v%d" % i, [128, 34, 65], BF16) for i in range(2)]
            tab = [self.sb(es, "a_tab%d" % i, [128, 5, 640], F32) for i in range(2)]
            r_hd = [[kb.res() for _ in range(4)] for _ in range(2)]
            sc = [self.sb(es, "a_sc%d" % i, [128, 640], F32) for i in range(2)]
            P = [self.sb(es, "a_P%d" % i, [128, 896], BF16) for i in range(2)]
            r_sc = [kb.res() for _ in range(2)]
            r_P = [kb.res() for _ in range(2)]
            rc = [self.sb(es, "a_rc%d" % i, [128, 1], F32) for i in range(2)]
            r_rc = [kb.res() for _ in range(2)]
            ast = [self.sb(es, "a_st%d" % i, [128, 32, 64], BF16) for i in range(2)]
            r_ast = [kb.res() for _ in range(2)]
            S_ps = [self.psA[0], self.psA[1]]
            r_Sps = [[self.r_psA[0], kb.res()], [self.r_psA[1], kb.res()]]
            o_ps = [self.psB[0], self.psB[1]]
            r_ops = [self.r_psB[0], self.r_psB[1]]
            for i in range(2):
                kb.op("dve", lambda e, i=i: e.memset(va[i][:, :, 64:65], 1.0), writes=[r_hd[i][2]])

            def loadh(h):
                b = h % 2
                kb.dma("sp", qh[b][:], self.qT[h * 64:(h + 1) * 64, :], writes=[r_hd[b][0]])
                kb.dma("sp", kh[b][:], self.kT[h * 64:(h + 1) * 64, :], writes=[r_hd[b][1]])
                for part in range(2):
                    kb.dma("sp", va[b][:, part * 17:(part + 1) * 17, 0:64],
                           _r(self.v_tok[part * 17 * 128:(part + 1) * 17 * 128, h * 64:(h + 1) * 64], "(j p) d -> p j d", p=128),
                           writes=[r_hd[b][2]] if part == 0 else [self._rva2(b)])
                kb.dma("sp", tab[b][:], _r(tab_in[h], "v p c q -> p v (c q)"), writes=[r_hd[b][3]])

            nh = 8
            loadh(0)
            it = 0
            for h in range(nh):
                b = h % 2
                if h + 1 < nh:
                    loadh(h + 1)
                rq, rk, rv, rt = r_hd[b]
                rv2 = self._rva2(b)
                tl = list(range(NTL) if self.max_tiles is None else range(self.max_tiles))

                def emit_S(j, pb):
                    u0 = min(max(2 * j - 4, 0), 54)
                    kt0 = u0 // 2
                    sp_ = S_ps[pb]
                    for c in range(7):
                        k0 = (kt0 + c) * 128 if c < 5 else S + (c - 5) * 128
                        kb.op("pe", lambda e, c=c, k0=k0, sp_=sp_: e.matmul(sp_[:, c * 128:(c + 1) * 128], lhsT=kh[b][:, k0:k0 + 128], rhs=qh[b][:, j * 128:(j + 1) * 128],
                                                                          start=True, stop=True), reads=[rq, rk], writes=r_Sps[pb], inc=(c == 6))

                emit_S(tl[0], it % 2)
                for ji, j in enumerate(tl):
                    pb = it % 2
                    it += 1
                    var = 0 if j == 0 else 1 if j == 1 else 3 if j == 30 else 4 if j == 31 else 2
                    u0 = min(max(2 * j - 4, 0), 54)
                    kt0 = u0 // 2
                    sp_ = S_ps[pb]
                    if ji + 1 < len(tl):
                        emit_S(tl[ji + 1], it % 2)
                    kb.op("dve", lambda e, sp_=sp_, var=var: e.tensor_tensor(out=sc[pb][:, 0:512], in0=sp_[:, 0:512], in1=tab[b][:, var, 0:512], op=ALU.add),
                          reads=r_Sps[pb] + [rt], writes=[r_sc[pb]])
                    kb.op("dve", lambda e, sp_=sp_, var=var: e.tensor_tensor(out=sc[pb][:, 512:640], in0=sp_[:, 512:640], in1=tab[b][:, var, 512:640], op=ALU.add),
                          reads=r_Sps[pb] + [rt], writes=[r_sc[pb]])
                    kb.op("act", lambda e, sp_=sp_: e.activation(out=P[pb][:, 640:896], in_=sp_[:, 640:896], func=AF.Exp), reads=r_Sps[pb], writes=[r_P[pb]])
                    kb.op("act", lambda e: e.activation(out=P[pb][:, 0:640], in_=sc[pb][:], func=AF.Exp), reads=[r_sc[pb]], writes=[r_P[pb]])
                    for c in range(7):
                        kt = kt0 + c if c < 5 else 32 + (c - 5)
                        kb.op("pe", lambda e, c=c, kt=kt: e.matmul(o_ps[pb][:, 0:65], lhsT=P[pb][:, c * 128:(c + 1) * 128], rhs=va[b][:, kt, :], start=(c == 0), stop=(c == 6)),
                              reads=[r_P[pb], rv, rv2], writes=[r_ops[pb]], inc=(c == 6))
                    kb.op("dve", lambda e: e.reciprocal(out=rc[pb][:], in_=o_ps[pb][:, 64:65]), reads=[r_ops[pb]], writes=[r_rc[pb]])
                    kb.op("dve", lambda e, j=j: e.tensor_scalar(out=ast[b][:, j, :], in0=o_ps[pb][:, 0:64], scalar1=rc[pb][:, 0:1], scalar2=None, op0=ALU.mult),
                          reads=[r_ops[pb], r_rc[pb]], writes=[r_ast[b]])
                for part in range(4):
                    kb.dma("pool", _r(self.att_tok[part * 1024:(part + 1) * 1024, h * 64:(h + 1) * 64], "(j p) d -> p j d", p=128),
                           ast[b][:, part * 8:(part + 1) * 8, :], reads=[r_ast[b]])

    def _rva2(self, b):
        if not hasattr(self, "_rva2_l"):
            self._rva2_l = [self.kb.res(), self.kb.res()]
        return self._rva2_l[b]

    def l0_out(self, x_in, dst):
        kb = self.kb
        NTL = S // 128
        with ExitStack() as es:
            Wo = self.sb(es, "o_W", [128, 12, D], BF16)
            r_wo = kb.res()
            kb.dma("pool", Wo[:], _r(self.I("w_out"), "(c p) d -> p c d", p=128), writes=[r_wo])
            Grow, r_G = self.load_rows_bcast(es, "o_G", self.vec[0, 0], D)
            cat = [self.sb(es, "o_cat%d" % i, [128, 1536], BF16) for i in range(2)]
            xt = [self.sb(es, "o_xt%d" % i, [128, D], F32) for i in range(2)]
            r_cat = [[kb.res(), kb.res()] for _ in range(2)]
            r_xt = [kb.res() for _ in range(2)]
            catT = self.sb(es, "o_catT", [128, 12, 128], BF16)
            r_catT = [kb.res() for _ in range(3)]
            tmp = self.sb(es, "o_tmp", [128, D], F32)
            r_tmp = kb.res()
            ss2 = self.sb(es, "o_ss2", [128, 4], F32)
            r_ss2 = [kb.res() for _ in range(3)]
            junk2 = self.sb(es, "o_junk2", [128, 512], BF16)
            r_junk2 = kb.res()
            psT = [self.psB[0], self.psB[1], self.psB[2]]
            r_psT = [self.r_psB[0], self.r_psB[1], self.r_psB[2]]

            def load(i):
                b = i % 2
                kb.dma("sp", cat[b][:, 0:512], self.att_tok[i * 128:(i + 1) * 128, :], writes=[r_cat[b][0]])
                kb.dma("sp", cat[b][:, 512:1536], self.ssm_tok[i * 128:(i + 1) * 128, :], writes=[r_cat[b][1]])
                kb.dma("sp", xt[b][:], x_in[i * 128:(i + 1) * 128, :], writes=[r_xt[b]])

            n = NTL if self.max_tiles is None else self.max_tiles
            load(0)
            for i in range(n):
                b = i % 2
                if i + 1 < n:
                    load(i + 1)
                for q3 in range(3):
                    for cc in range(4):
                        c = q3 * 4 + cc
                        kb.op("pe", lambda e, c=c, cc=cc, q3=q3: e.matmul(psT[q3][:, cc * 128:(cc + 1) * 128], lhsT=cat[b][:, c * 128:(c + 1) * 128], rhs=self.identb,
                                                                        start=True, stop=True), reads=r_cat[b] + [self.r_cst], writes=[r_psT[q3]], inc=(cc == 3))
                    kb.op("act", lambda e, q3=q3: e.activation(out=_r(catT[:, q3 * 4:(q3 + 1) * 4, :], "p c t -> p (c t)"), in_=psT[q3][:, :], func=AF.Identity),
                          reads=[r_psT[q3]], writes=[r_catT[q3]])
                py = self.psA[i % 2]
                rpy = self.r_psA[i % 2]
                for hb in range(2):
                    for c in range(12):
                        kb.op("pe", lambda e, c=c, hb=hb: e.matmul(py[:, hb * 512:(hb + 1) * 512], lhsT=catT[:, c, :], rhs=Wo[:, c, hb * 512:(hb + 1) * 512],
                                                                 start=(c == 0), stop=(c == 11)), reads=[r_catT[c // 4], r_wo], writes=[rpy], inc=(c == 11))
                self.resid_epilogue(py[:, :], [rpy], xt[b][:], r_xt[b], Grow, r_G, tmp, r_tmp, ss2, r_ss2, junk2, r_junk2)
                kb.dma("pool", dst[i * 128:(i + 1) * 128, :], xt[b][:], reads=[r_xt[b]])


def _consts():
    t = np.arange(128)
    c = np.zeros((128, 6, 128), np.float32)
    c[:, 0] = (t[:, None] == t[None, :])
    c[:, 1] = (t[:, None] <= t[None, :])
    c[:, 2] = (t[:, None] >= t[None, :])
    c[:, 3] = (t[:, None] > t[None, :])
    c[:, 4] = (t[:, None] < t[None, :])
    c[:, 5] = 1.0
    return c


def _pool_inv():
    out = np.zeros((2, 4, 256), np.float32)
    for a, base in enumerate((0, S - 256)):
        t = base + np.arange(256)
        for gi, w in enumerate((2, 4, 8, 16)):
            lo = np.clip(t - w // 2, 0, S)
            hi = np.clip(t - w // 2 + w, 0, S)
            out[a, gi] = np.float32(1.0) / (hi - lo).astype(np.float32)
    return out


def _rpb_table(rpb):
    padded = np.concatenate([rpb.reshape(8, -1), np.full((8, 1), NEG, np.float32)], axis=1)
    sent = 15 * 31
    variants = [(0, (0, 1)), (0, (2, 3)), (0, (4, 5)), (54, (60, 61)), (54, (62, 63))]
    k = np.arange(640)
    ki = k // 64
    kc = k % 64
    q = np.arange(128)
    qc = q % 64
    idx = np.zeros((5, 640, 128), np.int64)
    for v, (u0, rows) in enumerate(variants):
        r = np.array(rows)[q // 64]
        rs = np.clip(r - 4, 0, 56)
        cs = np.clip(qc - 8, 0, 48)
        i = u0 + ki
        valid = (i[:, None] >= rs[None, :]) & (i[:, None] < rs[None, :] + 8) & (kc[:, None] >= cs[None, :]) & (kc[:, None] < cs[None, :] + 16)
        rr = i[:, None] - r[None, :] + 7
        cc = kc[:, None] - qc[None, :] + 15
        flat = np.clip(rr, 0, 14) * 31 + np.clip(cc, 0, 30)
        idx[v] = np.where(valid, flat, sent)
    tab = padded[:, idx]
    tab = tab.reshape(8, 5, 5, 128, 128).transpose(0, 1, 3, 2, 4)
    return np.ascontiguousarray(tab, dtype=np.float32)


def make_in_maps(inp):
    f = lambda a: np.ascontiguousarray(np.asarray(a), dtype=np.float32)
    x, c, ctx, c_ctx = f(inp["x"]), f(inp["c"]), f(inp["ctx"]), f(inp["c_ctx"])
    shared = {
        "ada_w": f(inp["ada_w"]),
        "ada_bT": f(f(inp["ada_b"]).reshape(2, 48, 128).transpose(2, 0, 1)),
        "norm_gT": f(f(inp["norm_g"]).reshape(2, 4, 8, 128).transpose(3, 0, 1, 2)),
        "w_in": f(inp["w_in"])[0],
        "w_out": f(inp["w_out"])[0],
        "rpbtab": _rpb_table(f(inp["na_rpb"])[0]),
        "conv_wT": f(f(inp["ssm_conv_w"])[0].reshape(3, 16, 128).transpose(2, 1, 0)),
        "conv_bT": f(f(inp["ssm_conv_b"])[0].reshape(16, 128).transpose(1, 0)),
        "ssm_rows": f(np.stack([f(inp["ssm_a_log"])[0].reshape(32), f(inp["ssm_dt_bias"])[0].reshape(32), f(inp["ssm_d"])[0].reshape(32)])),
        "ssm_norm_g": f(inp["ssm_norm_g"])[0],
        "pool_w": f(inp["pool_w"])[0],
        "pool_b": f(inp["pool_b"])[0].reshape(D),
        "pool_scale": f(inp["pool_scale"])[0],
        "pool_inv": _pool_inv(),
        "ffn_w_up": f(inp["ffn_w_up"]),
        "fconv_wT": f(f(inp["ffn_conv_w"]).reshape(2, 3, 22, 128).transpose(3, 0, 2, 1)),
        "fconv_bT": f(f(inp["ffn_conv_b"]).reshape(2, 22, 128).transpose(2, 0, 1)),
        "ffn_w_down": f(inp["ffn_w_down"]),
        "consts": _consts(),
    }
    maps = []
    for b in range(x.shape[0]):
        cc = np.stack([c[b].reshape(8, 128).T, c_ctx.reshape(8, 128).T], axis=2)
        m = dict(shared)
        m["x"] = f(x[b])
        m["ctx"] = f(ctx[b])
        m["cc"] = f(cc)
        maps.append(m)
    return maps


_NC_CACHE = {}


def kernel(**inputs):
    if "full" not in _NC_CACHE:
        p = Prog()
        _NC_CACHE["full"] = p.build()
        _NC_CACHE["names"] = [k for k in p.dram if k in Prog.SHAPES]
    nc = _NC_CACHE["full"]
    maps = make_in_maps(inputs)
    names = _NC_CACHE["names"]
    maps = [{k: m[k] for k in names} for m in maps]
    res = run_bass_kernel_spmd(nc, maps, core_ids=list(range(NCORES)))
    return np.stack([np.asarray(r["out"], dtype=np.float32) for r in res.results], axis=0)
```

```python
import numpy as np
from contextlib import ExitStack
import concourse.bass as bass
import concourse.mybir as mybir
from concourse.bass_utils import run_bass_kernel_spmd

F32 = mybir.dt.float32
BF16 = mybir.dt.bfloat16
ALU = mybir.AluOpType
AF = mybir.ActivationFunctionType

D = 1024
S = 4096
LC = 256
NCORES = 8
FH = 2816
EPS = 1e-6
NEG = -30000.0


class Res:
    __slots__ = ("w", "r", "name")

    def __init__(self, name=""):
        self.w = {}
        self.r = {}
        self.name = name


class KB:
    def __init__(self):
        nc = bass.Bass("TRN2", target_bir_lowering=False)
        self.nc = nc
        self.h = {"pe": nc.tensor, "act": nc.scalar, "dve": nc.vector, "pool": nc.gpsimd, "sp": nc.sync}
        self.sems = {}
        self.cnt = {}
        for e in ("pe", "act", "dve", "pool"):
            self.sems[e] = nc.alloc_semaphore(name="c_" + e)
            self.cnt[e] = 0
        self.dq = {"sp": [], "pool": [], "act": []}
        for q, n in (("sp", 28), ("pool", 20), ("act", 8)):
            for i in range(n):
                k = "d_%s%d" % (q, i)
                self.sems[k] = nc.alloc_semaphore(name=k)
                self.cnt[k] = 0
                self.dq[q].append(k)
        self.dqi = {"sp": 0, "pool": 0, "act": 0}
        self.sems["bar"] = nc.alloc_semaphore(name="bar")
        self.cnt["bar"] = 0
        self.waited = {e: {} for e in self.h}
        self.nres = 0

    def res(self, name=""):
        return Res(name)

    def _wait(self, E, need):
        for key, v in need.items():
            if key == E and E == "pe":
                continue
            if self.waited[E].get(key, 0) < v:
                self.h[E].wait_ge(self.sems[key], v)
                self.waited[E][key] = v

    @staticmethod
    def _merge(need, d):
        for k, v in d.items():
            if need.get(k, 0) < v:
                need[k] = v

    def op(self, E, fn, reads=(), writes=(), inc=True):
        need = {}
        for r in reads:
            self._merge(need, r.w)
        for w in writes:
            self._merge(need, w.w)
            self._merge(need, w.r)
        self._wait(E, need)
        ins = fn(self.h[E])
        if inc:
            self.cnt[E] += 1
            ins.then_inc(self.sems[E], 1)
            tv = self.cnt[E]
        else:
            tv = self.cnt[E] + 1
        for r in reads:
            if r.r.get(E, 0) < tv:
                r.r[E] = tv
        for w in writes:
            w.w = {E: tv}
            w.r = {}
        return ins

    def dma(self, Q, out, in_, reads=(), writes=(), **kw):
        need = {}
        for r in reads:
            self._merge(need, r.w)
        for w in writes:
            self._merge(need, w.w)
            self._merge(need, w.r)
        i = self.dqi[Q]
        self.dqi[Q] = i + 1
        key = self.dq[Q][i % len(self.dq[Q])]
        if self.cnt[key] > 0:
            need[key] = max(need.get(key, 0), self.cnt[key])
        self._wait(Q, need)
        ins = self.h[Q].dma_start(out=out, in_=in_, **kw)
        self.cnt[key] += 16
        ins.then_inc(self.sems[key], 16)
        tv = self.cnt[key]
        for r in reads:
            r.r[key] = tv
        for w in writes:
            w.w = {key: tv}
            w.r = {}
        return ins

    def barrier(self):
        need = {k: v for k, v in self.cnt.items() if k != "bar" and v > 0}
        self._wait("sp", need)
        self.cnt["bar"] += 1
        self.h["sp"].sem_inc(self.sems["bar"], 1)
        for e in ("pe", "act", "dve", "pool"):
            self.h[e].wait_ge(self.sems["bar"], self.cnt["bar"])
            self.waited[e]["bar"] = self.cnt["bar"]
            for k, v in need.items():
                self.waited[e][k] = max(self.waited[e].get(k, 0), v)


def _r(ap, pat, **kw):
    return ap.rearrange(pat, **kw)


class Prog:
    def __init__(self, dumps=(), stop_after=None, mode="full", max_tiles=None):
        self.mode = mode
        self.max_tiles = max_tiles
        self.kb = KB()
        self.nc = self.kb.nc
        self.dumps = set(dumps)
        self.stop_after = stop_after
        self.dram = {}

    def din(self, name, shape, dt=F32):
        t = self.nc.dram_tensor(name, list(shape), dt, kind="ExternalInput").ap()
        self.dram[name] = t
        return t

    SHAPES = {
        "x": [S, D], "ctx": [LC, D], "cc": [128, 8, 2], "ada_w": [2, D, 6 * D], "ada_bT": [128, 2, 48],
        "norm_gT": [128, 2, 4, 8], "w_in": [D, 4640], "w_out": [1536, D], "rpbtab": [8, 5, 128, 5, 128],
        "conv_wT": [128, 16, 3], "conv_bT": [128, 16], "ssm_rows": [3, 32], "ssm_norm_g": [D],
        "pool_w": [4, 256, 256], "pool_b": [D], "pool_scale": [D], "pool_inv": [2, 4, 256],
        "ffn_w_up": [2, D, 2 * FH], "fconv_wT": [128, 2, 22, 3], "fconv_bT": [128, 2, 22], "ffn_w_down": [2, FH, D],
        "consts": [128, 6, 128],
    }

    def I(self, name):
        if name not in self.dram:
            self.din(name, self.SHAPES[name])
        return self.dram[name]

    def dscr(self, name, shape, dt=F32, out=False):
        kind = "ExternalOutput" if (out or name in self.dumps) else "Internal"
        t = self.nc.dram_tensor(name, list(shape), dt, kind=kind).ap()
        self.dram[name] = t
        return t

    def sb(self, es, name, shape, dt):
        self.kb.nres += 1
        return es.enter_context(self.nc.sbuf_tensor("s%d_%s" % (self.kb.nres, name), list(shape), dt))

    def build(self):
        nc, kb = self.nc, self.kb
        x_in = self.I("x")
        ctx_in = self.I("ctx") if self.mode == "full" else None
        cc_in = self.I("cc")
        ada_w = self.I("ada_w")
        ada_bT = self.I("ada_bT")
        norm_gT = self.I("norm_gT")
        consts_in = self.I("consts")
        out = self.dscr("out", [S, D], out=True)
        xa = self.dscr("xa", [S, D])
        xb = self.dscr("xb", [S, D])
        xc = self.dscr("xc", [S, D])
        self.vec = self.dscr("vec", [2, 2, D])

        with ExitStack() as g:
            self.cst = self.sb(g, "cst", [128, 6, 128], F32)
            self.cstb = self.sb(g, "cstb", [128, 6, 128], BF16)
            self.modv = self.sb(g, "modv", [128, 2, 4, 8, 2], F32)
            self.epsb = self.sb(g, "epsb", [128, 1], F32)
            self.r_cst = kb.res("cst")
            self.r_modv = kb.res("modv")
            self.psA = [g.enter_context(nc.psum_tensor("psA%d" % i, [128, 1024], F32)) for i in range(2)]
            self.psB = [g.enter_context(nc.psum_tensor("psB%d" % i, [128, 512], F32)) for i in range(4)]
            self.r_psA = [kb.res("psA%d" % i) for i in range(2)]
            self.r_psB = [kb.res("psB%d" % i) for i in range(4)]

            kb.dma("sp", self.cst[:], consts_in, writes=[self.r_cst])
            kb.op("dve", lambda e: e.tensor_copy(out=self.cstb[:], in_=self.cst[:]), reads=[self.r_cst], writes=[self.r_cst])
            kb.op("pool", lambda e: e.memset(self.epsb[:], EPS), writes=[self.r_cst])
            self.identb = self.cstb[:, 0, :]

            self.phase_adaln(cc_in, ada_w, ada_bT, norm_gT)
            kb.barrier()
            if self.stop_after == "adaln":
                return self.finish()
            if self.mode == "l1":
                self.phase_pool(x_in, xc)
                kb.barrier()
                self.phase_ffn(1, xc, out)
                return self.finish()
            if self.mode == "ffn0":
                self.phase_ffn(0, x_in, out)
                return self.finish()
            self.phase_l0_mixer(x_in, ctx_in, xa)
            kb.barrier()
            if self.stop_after in ("proj", "ssd", "att", "l0mix"):
                return self.finish()
            self.phase_ffn(0, xa, xb)
            kb.barrier()
            if self.stop_after == "l0ffn":
                return self.finish()
            self.phase_pool(xb, xc)
            kb.barrier()
            if self.stop_after == "l1mix":
                return self.finish()
            self.phase_ffn(1, xc, out)
            kb.barrier()
        return self.finish()

    def finish(self):
        self.kb.barrier()
        return self.nc

    def phase_adaln(self, cc_in, ada_w, ada_bT, norm_gT):
        nc, kb = self.nc, self.kb
        with ExitStack() as es:
            cc = self.sb(es, "cc", [128, 8, 2], F32)
            sg = self.sb(es, "sg", [128, 8, 2], F32)
            scc = self.sb(es, "scc", [128, 8, 2], F32)
            abT = self.sb(es, "abT", [128, 2, 48], F32)
            ngT = self.sb(es, "ngT", [128, 2, 4, 8], F32)
            mod = self.sb(es, "mod", [128, 2, 48, 2], F32)
            gv = self.sb(es, "gv", [128, 2, 2, 8], F32)
            wbuf = [self.sb(es, "adw%d" % i, [128, 8, 768], F32) for i in range(2)]
            r_w = [[kb.res(), kb.res()] for _ in range(2)]
            r_cc, r_small, r_mod, r_gv = kb.res(), kb.res(), kb.res(), kb.res()
            kb.dma("sp", cc[:], cc_in, writes=[r_cc])
            kb.dma("sp", abT[:], ada_bT, writes=[r_small])
            kb.dma("sp", ngT[:], norm_gT, writes=[r_small])
            kb.op("act", lambda e: e.activation(out=sg[:], in_=cc[:], func=AF.Sigmoid), reads=[r_cc], writes=[r_gv])
            kb.op("dve", lambda e: e.tensor_tensor(out=scc[:], in0=cc[:], in1=sg[:], op=ALU.mult), reads=[r_cc, r_gv], writes=[r_cc])
            it = 0
            for l in range(2):
                for jb in range(8):
                    wb, rw = wbuf[it % 2], r_w[it % 2]
                    for half in range(2):
                        kb.dma("sp", wb[:, half * 4:(half + 1) * 4, :],
                               _r(ada_w[l, half * 512:(half + 1) * 512, jb * 768:(jb + 1) * 768], "(k p) n -> p k n", p=128),
                               writes=[rw[half]])
                    ps = self.psB[it % 2]
                    rps = self.r_psB[it % 2]
                    it += 1
                    self._ada_mm(wb, rw, scc, r_cc, ps, rps, mod, r_mod, abT, r_small, l, jb)
            for l in range(2):
                for i, (sci, shi, gi) in enumerate(((8, 0, 0), (32, 24, 2))):
                    kb.op("dve", lambda e, l=l, i=i, sci=sci, gi=gi: e.scalar_tensor_tensor(
                        out=self.modv[:, l, 2 * i, :, :], in0=mod[:, l, sci:sci + 8, :], scalar=1.0,
                        in1=ngT[:, l, gi, :].unsqueeze(2).to_broadcast([128, 8, 2]), op0=ALU.add, op1=ALU.mult),
                        reads=[r_mod, r_small], writes=[self.r_modv])
                    kb.op("dve", lambda e, l=l, i=i, shi=shi: e.tensor_copy(
                        out=self.modv[:, l, 2 * i + 1, :, :], in_=mod[:, l, shi:shi + 8, :]),
                        reads=[r_mod], writes=[self.r_modv])
                for i, (gti, gpi) in enumerate(((16, 1), (40, 3))):
                    kb.op("dve", lambda e, l=l, i=i, gti=gti, gpi=gpi: e.tensor_tensor(
                        out=gv[:, l, i, :], in0=mod[:, l, gti:gti + 8, 0], in1=ngT[:, l, gpi, :], op=ALU.mult),
                        reads=[r_mod, r_small], writes=[r_gv])
            for l in range(2):
                for i in range(2):
                    kb.dma("pool", _r(self.vec[l, i], "(j p) -> p j", p=128), gv[:, l, i, :], reads=[r_gv],
                           allow_slow_non_contiguous=True)

    def _ada_mm(self, wb, rw, scc, r_cc, ps, rps, mod, r_mod, abT, r_small, l, jb):
        kb = self.kb
        for jj in range(6):
            for kc in range(8):
                kb.op("pe", lambda e, jj=jj, kc=kc: e.matmul(ps[:, jj * 2:jj * 2 + 2], lhsT=wb[:, kc, jj * 128:(jj + 1) * 128],
                                                         rhs=scc[:, kc, :], start=(kc == 0), stop=(kc == 7)),
                      reads=[rw[kc // 4], r_cc], writes=[rps], inc=(jj == 5 and kc == 7))
        kb.op("dve", lambda e: e.tensor_tensor(
            out=mod[:, l, jb * 6:(jb + 1) * 6, :], in0=_r(ps[:, 0:12], "p (j n) -> p j n", n=2),
            in1=abT[:, l, jb * 6:(jb + 1) * 6].unsqueeze(2).to_broadcast([128, 6, 2]), op=ALU.add),
            reads=[rps, r_small], writes=[r_mod])

    def alloc_norm_bufs(self, es, nsub, nh):
        nhp = max(nh, 2)
        self.r_nt = [self.kb.res() for _ in range(6)]
        return dict(junk=self.sb(es, "n_junk", [128, D], BF16), xn=self.sb(es, "n_xn", [128, nsub, D], BF16),
                    ss=self.sb(es, "n_ss", [128, 4], F32), rstd=self.sb(es, "n_rstd", [128, 4], F32),
                    xnh=self.sb(es, "n_xnh", [nhp, D], BF16), ssh=self.sb(es, "n_ssh", [nhp, 1], F32),
                    rstdh=self.sb(es, "n_rstdh", [nhp, 1], F32))

    def norm_T(self, nb, xt, r_xts, nsub, xh, r_xh, nh, halo, hT, r_hT, main0, hl0, hr0, l, mi, col, psT, r_psT, psH, r_psH):
        kb = self.kb
        junk, xn, ss, rstd, xnh, ssh, rstdh = (nb[k] for k in ("junk", "xn", "ss", "rstd", "xnh", "ssh", "rstdh"))
        rt = self.r_nt
        ntok = nsub * 128
        for s in range(nsub):
            kb.op("act", lambda e, s=s: e.activation(out=junk[:], in_=xt[:, s, :], func=AF.Square, accum_out=ss[:, s:s + 1]),
                  reads=[r_xts[s]], writes=[rt[0]])
        kb.op("act", lambda e: e.activation(out=rstd[:, 0:nsub], in_=ss[:, 0:nsub], func=AF.Sqrt, bias=self.epsb[:], scale=1.0 / D),
              reads=[rt[0], self.r_cst], writes=[rt[1]])
        kb.op("dve", lambda e: e.reciprocal(out=rstd[:, 0:nsub], in_=rstd[:, 0:nsub]), reads=[rt[1]], writes=[rt[1]])
        for s in range(nsub):
            kb.op("dve", lambda e, s=s: e.tensor_scalar(out=xn[:, s, :], in0=xt[:, s, :], scalar1=rstd[:, s:s + 1], scalar2=None, op0=ALU.mult),
                  reads=[r_xts[s], rt[1]], writes=[rt[2]])
        A = self.modv[:, l, mi, :, col]
        B = self.modv[:, l, mi + 1, :, col]
        for kc in range(8):
            p = psT[kc % 2]
            rp = r_psT[kc % 2]
            for s in range(nsub):
                kb.op("pe", lambda e, s=s, kc=kc, p=p: e.matmul(p[:, s * 128:(s + 1) * 128], lhsT=xn[:, s, kc * 128:(kc + 1) * 128],
                                                             rhs=self.identb, start=True, stop=True),
                      reads=[rt[2], self.r_cst], writes=[rp], inc=(s == nsub - 1))
            kb.op("act", lambda e, kc=kc, p=p: e.activation(out=hT[:, kc, main0:main0 + ntok], in_=p[:, 0:ntok], func=AF.Identity,
                                                          bias=B[:, kc:kc + 1], scale=A[:, kc:kc + 1]),
                  reads=[rp, self.r_modv], writes=[r_hT])
        if nh == 0:
            return
        lv, rv = halo
        hh = nh // 2
        if lv or rv:
            kb.op("act", lambda e: e.activation(out=junk[0:nh, :], in_=xh[0:nh, :], func=AF.Square, accum_out=ssh[0:nh, 0:1]),
                  reads=[r_xh], writes=[rt[3]])
            kb.op("act", lambda e: e.activation(out=rstdh[0:nh, :], in_=ssh[0:nh, :], func=AF.Sqrt, bias=self.epsb[0:nh, :], scale=1.0 / D),
                  reads=[rt[3], self.r_cst], writes=[rt[4]])
            kb.op("dve", lambda e: e.reciprocal(out=rstdh[0:nh, :], in_=rstdh[0:nh, :]), reads=[rt[4]], writes=[rt[4]])
            kb.op("dve", lambda e: e.tensor_scalar(out=xnh[0:nh, :], in0=xh[0:nh, :], scalar1=rstdh[0:nh, 0:1], scalar2=None, op0=ALU.mult),
                  reads=[r_xh, rt[4]], writes=[rt[5]])
            for kc in range(8):
                kb.op("pe", lambda e, kc=kc: e.matmul(psH[:, kc * nh:(kc + 1) * nh], lhsT=xnh[0:nh, kc * 128:(kc + 1) * 128],
                                                   rhs=self.cstb[0:nh, 0, 0:nh], start=True, stop=True),
                      reads=[rt[5], self.r_cst], writes=[r_psH], inc=(kc == 7))
            for kc in range(8):
                for side, c0 in ((0, hl0), (1, hr0)):
                    kb.op("dve", lambda e, kc=kc, side=side, c0=c0: e.tensor_scalar(
                        out=hT[:, kc, c0:c0 + hh], in0=psH[:, kc * nh + side * hh:kc * nh + (side + 1) * hh],
                        scalar1=A[:, kc:kc + 1], scalar2=B[:, kc:kc + 1], op0=ALU.mult, op1=ALU.add),
                        reads=[r_psH, self.r_modv], writes=[r_hT])
        if not lv:
            kb.op("dve", lambda e: e.memset(hT[:, :, hl0:hl0 + hh], 0.0), writes=[r_hT])
        if not rv:
            kb.op("dve", lambda e: e.memset(hT[:, :, hr0:hr0 + hh], 0.0), writes=[r_hT])

    def load_rows_bcast(self, es, name, src_row, n):
        t = self.sb(es, name, [128, n], F32)
        r = self.kb.res(name)
        self.kb.dma("sp", t[:], src_row.partition_broadcast(128), writes=[r])
        return t, r

    def resid_epilogue(self, y, r_y, xt_s, r_xt, Grow, r_G, tmp, r_tmp, ss2, r_ss2, junk, r_junk):
        kb = self.kb
        for hb in range(2):
            kb.op("act", lambda e, hb=hb: e.activation(out=junk[:, 0:512], in_=y[:, hb * 512:(hb + 1) * 512], func=AF.Square,
                                                     accum_out=ss2[:, hb:hb + 1]), reads=r_y, writes=[r_ss2[0], r_junk])
        kb.op("dve", lambda e: e.tensor_tensor(out=ss2[:, 2:3], in0=ss2[:, 0:1], in1=ss2[:, 1:2], op=ALU.add), reads=[r_ss2[0]], writes=[r_ss2[1]])
        kb.op("act", lambda e: e.activation(out=ss2[:, 3:4], in_=ss2[:, 2:3], func=AF.Sqrt, bias=self.epsb[:], scale=1.0 / D),
              reads=[r_ss2[1], self.r_cst], writes=[r_ss2[2]])
        kb.op("dve", lambda e: e.reciprocal(out=ss2[:, 3:4], in_=ss2[:, 3:4]), reads=[r_ss2[2]], writes=[r_ss2[2]])
        kb.op("dve", lambda e: e.scalar_tensor_tensor(out=tmp[:], in0=y, scalar=ss2[:, 3:4], in1=Grow[:], op0=ALU.mult, op1=ALU.mult),
              reads=r_y + [r_ss2[2], r_G], writes=[r_tmp])
        kb.op("pool", lambda e: e.tensor_tensor(out=xt_s, in0=xt_s, in1=tmp[:], op=ALU.add), reads=[r_tmp, r_xt], writes=[r_xt])

    def phase_ffn(self, l, src, dst):
        nc, kb = self.nc, self.kb
        T = 256
        NT = S // T
        with ExitStack() as es:
            Wup = self.sb(es, "Wup", [128, 8, 2 * FH], BF16)
            Wdn = self.sb(es, "Wdn", [128, 22, D], BF16)
            r_wup = [kb.res() for _ in range(8)]
            r_wdn = [kb.res() for _ in range(2)]
            for kc in range(8):
                kb.dma("pool", Wup[:, kc, :], self.I("ffn_w_up")[l, kc * 128:(kc + 1) * 128, :], writes=[r_wup[kc]])
            for hh in range(2):
                kb.dma("pool", Wdn[:, hh * 11:(hh + 1) * 11, :], _r(self.I("ffn_w_down")[l, hh * 1408:(hh + 1) * 1408, :], "(j p) d -> p j d", p=128),
                       writes=[r_wdn[hh]])
            cw = self.sb(es, "f_cw", [128, 22, 3], F32)
            cb = self.sb(es, "f_cb", [128, 22], F32)
            r_cw = [kb.res(), kb.res()]
            kb.dma("sp", cw[:], self.I("fconv_wT")[:, l], writes=[r_cw[0]])
            kb.dma("sp", cb[:], self.I("fconv_bT")[:, l], writes=[r_cw[1]])
            Grow, r_G = self.load_rows_bcast(es, "f_G", self.vec[l, 1], D)
            nb = self.alloc_norm_bufs(es, 2, 2)
            xt = [self.sb(es, "f_xt%d" % i, [128, 2, D], F32) for i in range(2)]
            r_xt = [[kb.res() for _ in range(2)] for _ in range(2)]
            xh = [self.sb(es, "f_xh%d" % i, [2, D], F32) for i in range(2)]
            r_xh = [kb.res() for _ in range(2)]
            hT = [self.sb(es, "f_hT%d" % i, [128, 8, T + 2], BF16) for i in range(2)]
            r_hT = [kb.res() for _ in range(2)]
            ub = [self.sb(es, "f_ub%d" % i, [128, T + 2], F32) for i in range(2)]
            acc = [self.sb(es, "f_acc%d" % i, [128, T], F32) for i in range(2)]
            gl = [self.sb(es, "f_gl%d" % i, [128, T], F32) for i in range(2)]
            r_ub = [kb.res() for _ in range(2)]
            r_acc = [kb.res() for _ in range(2)]
            r_gl = [kb.res() for _ in range(2)]
            gT = self.sb(es, "f_gT", [128, 22, T], BF16)
            r_gT = [kb.res() for _ in range(22)]
            ss2 = [self.sb(es, "f_ss2%d" % i, [128, 4], F32) for i in range(2)]
            r_ss2 = [[kb.res() for _ in range(3)] for _ in range(2)]
            junk2 = self.sb(es, "f_junk2", [128, 512], BF16)
            r_junk2 = kb.res()
            psT = [self.psB[2], self.psB[3]]
            r_psT = [self.r_psB[2], self.r_psB[3]]
            psH = self.psB[3][:, 256:272]
            r_psH = self.r_psB[3]
            bankU = [self.psB[0], self.psA[1][:, 0:512]]
            bankV = [self.psB[1], self.psA[1][:, 512:1024]]
            psU = [bk[:, 0:T] for bk in bankU]
            psUh = [bk[:, T:T + 2] for bk in bankU]
            psV = [bk[:, 0:T] for bk in bankV]
            r_psU = [self.r_psB[0], kb.res()]
            r_psV = [self.r_psB[1], kb.res()]
            r_psUh = r_psU
            for b in range(2):
                kb.op("dve", lambda e, b=b: e.memset(xh[b][:], 0.0), writes=[r_xh[b]])

            def load(t):
                b = t % 2
                t0 = t * T
                for s in range(2):
                    kb.dma("sp", xt[b][:, s, :], src[t0 + s * 128:t0 + (s + 1) * 128, :], writes=[r_xt[b][s]])
                if 0 < t < NT - 1:
                    kb.dma("sp", xh[b][0:2, :], src[t0 - 1:t0 + T + 1:T + 1, :], writes=[r_xh[b]])
                elif t > 0:
                    kb.dma("sp", xh[b][0:1, :], src[t0 - 1:t0, :], writes=[r_xh[b]])
                else:
                    kb.dma("sp", xh[b][1:2, :], src[t0 + T:t0 + T + 1, :], writes=[r_xh[b]])

            ysb = [self.sb(es, "f_ysb%d" % i, [128, D], F32) for i in range(2)]
            r_ysb = [kb.res() for _ in range(2)]
            NTR = NT if self.max_tiles is None else self.max_tiles

            def do_norm(t):
                b = t % 2
                self.norm_T(nb, xt[b], r_xt[b], 2, xh[b], r_xh[b], 2, (t > 0, t < NT - 1), hT[b], r_hT[b], 1, 0, T + 1,
                            l, 2, 0, psT, r_psT, psH, r_psH)

            load(0)
            do_norm(0)
            for t in range(NTR):
                b = t % 2
                t0 = t * T
                if t + 1 < NT:
                    load(t + 1)
                for j in range(23):
                    pb = j % 2
                    qb = (j - 1) % 2
                    if j < 22:
                        for (pp, rr, c0, n0, n1) in ((bankU[pb][:, 0:T + 2], r_psU[pb], j * 128, 0, T + 2),
                                                     (psV[pb], r_psV[pb], FH + j * 128, 1, T + 1)):
                            for kc in range(8):
                                kb.op("pe", lambda e, kc=kc, pp=pp, c0=c0, n0=n0, n1=n1: e.matmul(
                                    pp, lhsT=Wup[:, kc, c0:c0 + 128], rhs=hT[b][:, kc, n0:n1], start=(kc == 0), stop=(kc == 7)),
                                    reads=[r_wup[kc], r_hT[b]], writes=[rr], inc=(kc == 7))
                    if j >= 1:
                        kb.op("act", lambda e, qb=qb: e.activation(out=gl[qb][:], in_=acc[qb][:], func=AF.Gelu), reads=[r_acc[qb]], writes=[r_gl[qb]])
                    if j < 22:
                        a = acc[pb]
                        bu = bankU[pb]
                        kb.op("act", lambda e, a=a, bu=bu, j=j: e.activation(out=a[:], in_=bu[:, 1:T + 1], func=AF.Identity,
                                                                           bias=cb[:, j:j + 1], scale=cw[:, j, 1:2]),
                              reads=[r_psU[pb]] + r_cw, writes=[r_acc[pb]])
                    if j >= 1:
                        kb.op("dve", lambda e, j=j, qb=qb: e.tensor_tensor(out=gT[:, j - 1, :], in0=gl[qb][:], in1=psV[qb], op=ALU.mult),
                              reads=[r_gl[qb], r_psV[qb]], writes=[r_gT[j - 1]])
                    if j < 22:
                        kb.op("dve", lambda e, bu=bu, a=a, j=j: e.scalar_tensor_tensor(out=a[:], in0=bu[:, 0:T], scalar=cw[:, j, 0:1], in1=a[:],
                                                                                    op0=ALU.mult, op1=ALU.add), reads=[r_psU[pb], r_acc[pb]] + r_cw, writes=[r_acc[pb]])
                        kb.op("dve", lambda e, bu=bu, a=a, j=j: e.scalar_tensor_tensor(out=a[:], in0=bu[:, 2:T + 2], scalar=cw[:, j, 2:3], in1=a[:],
                                                                                    op0=ALU.mult, op1=ALU.add), reads=[r_psU[pb], r_acc[pb]] + r_cw, writes=[r_acc[pb]])
                    if j == 8 and t + 1 < NTR:
                        do_norm(t + 1)
                for s in range(2):
                    py = self.psA[0]
                    rpy = self.r_psA[0]
                    for hb in range(2):
                        for j in range(22):
                            kb.op("pe", lambda e, j=j, s=s, hb=hb: e.matmul(py[:, hb * 512:(hb + 1) * 512], lhsT=gT[:, j, s * 128:(s + 1) * 128],
                                                                          rhs=Wdn[:, j, hb * 512:(hb + 1) * 512], start=(j == 0), stop=(j == 21)),
                                  reads=[r_gT[j], r_wdn[j // 11]], writes=[rpy], inc=(j == 21))
                    kb.op("act", lambda e, s=s: e.activation(out=ysb[s][:], in_=py[:, :], func=AF.Identity), reads=[rpy], writes=[r_ysb[s]])
                    self.resid_epilogue(ysb[s][:], [r_ysb[s]], xt[b][:, s, :], r_xt[b][s], Grow, r_G, ysb[s], r_ysb[s], ss2[s], r_ss2[s], junk2, r_junk2)
                    kb.dma("pool", dst[t0 + s * 128:t0 + (s + 1) * 128, :], xt[b][:, s, :], reads=[r_xt[b][s]])

    def phase_pool(self, src, dst):
        nc, kb = self.nc, self.kb
        T = 256
        NT = S // T
        HH = 8
        W = T + 2 * HH
        l = 1
        with ExitStack() as es:
            PW = self.sb(es, "p_PW", [128, 4, 2, 256], BF16)
            r_pw = kb.res()
            kb.dma("pool", PW[:], _r(self.I("pool_w"), "g (k p) n -> p g k n", p=128), writes=[r_pw])
            pbrow, r_pb = self.load_rows_bcast(es, "p_pb", self.I("pool_b"), D)
            psrow, r_psr = self.load_rows_bcast(es, "p_ps", self.I("pool_scale"), D)
            Grow, r_G = self.load_rows_bcast(es, "p_G", self.vec[l, 0], D)
            inv = self.sb(es, "p_inv", [128, 2, 4, 256], F32)
            r_inv = kb.res()
            kb.dma("sp", inv[:], _r(self.I("pool_inv"), "a g t -> (a g t)").partition_broadcast(128), writes=[r_inv])
            nb = self.alloc_norm_bufs(es, 2, 2 * HH)
            xt = [self.sb(es, "p_xt%d" % i, [128, 2, D], F32) for i in range(2)]
            r_xt = [[kb.res() for _ in range(2)] for _ in range(2)]
            xh = [self.sb(es, "p_xh%d" % i, [2 * HH, D], F32) for i in range(2)]
            r_xh = [kb.res() for _ in range(2)]
            hTs = [self.sb(es, "p_hT%d" % i, [128, 8, W], F32) for i in range(2)]
            r_hTs = [kb.res() for _ in range(2)]
            bufA = self.sb(es, "p_bA", [128, 2, W], F32)
            bufB = self.sb(es, "p_bB", [128, 2, W], F32)
            r_bA, r_bB = kb.res(), kb.res()
            pl = self.sb(es, "p_pl", [128, 8, T], BF16)
            r_pl = [kb.res() for _ in range(4)]
            ysb = [self.sb(es, "p_ysb%d" % i, [128, D], F32) for i in range(2)]
            r_ysb = [kb.res() for _ in range(2)]
            tmp = [self.sb(es, "p_tmp%d" % i, [128, D], F32) for i in range(2)]
            r_tmp = [kb.res() for _ in range(2)]
            ss2 = [self.sb(es, "p_ss2%d" % i, [128, 4], F32) for i in range(2)]
            r_ss2 = [[kb.res() for _ in range(3)] for _ in range(2)]
            junk2 = self.sb(es, "p_junk2", [128, 512], BF16)
            r_junk2 = kb.res()
            psT = [self.psB[2], self.psB[3]]
            r_psT = [self.r_psB[2], self.r_psB[3]]
            psH = self.psB[1][:, 0:128]
            r_psH = self.r_psB[1]
            for b in range(2):
                kb.op("dve", lambda e, b=b: e.memset(xh[b][:], 0.0), writes=[r_xh[b], self._rxh2(b)])

            def load(t):
                b = t % 2
                t0 = t * T
                for s in range(2):
                    kb.dma("sp", xt[b][:, s, :], src[t0 + s * 128:t0 + (s + 1) * 128, :], writes=[r_xt[b][s]])
                if 0 < t < NT - 1:
                    kb.dma("sp", xh[b][0:HH, :], src[t0 - HH:t0, :], writes=[r_xh[b]])
                    kb.dma("sp", xh[b][HH:2 * HH, :], src[t0 + T:t0 + T + HH, :], writes=[self._rxh2(b)])
                elif t > 0:
                    kb.dma("sp", xh[b][0:HH, :], src[t0 - HH:t0, :], writes=[r_xh[b]])
                else:
                    kb.dma("sp", xh[b][HH:2 * HH, :], src[t0 + T:t0 + T + HH, :], writes=[self._rxh2(b)])

            def do_norm(t):
                b = t % 2
                rxh = Res()
                kb._merge(rxh.w, r_xh[b].w)
                kb._merge(rxh.w, self._rxh2(b).w)
                self.norm_T(nb, xt[b], r_xt[b], 2, xh[b], rxh, 2 * HH, (t > 0, t < NT - 1), hTs[b], r_hTs[b], HH, 0, HH + T,
                            l, 0, 0, psT, r_psT, psH, r_psH)
                kb._merge(r_xh[b].r, rxh.r)
                kb._merge(self._rxh2(b).r, rxh.r)

            load(0)
            do_norm(0)
            for t in range(NT):
                b = t % 2
                t0 = t * T
                hT, r_hT = hTs[b], r_hTs[b]
                if t + 1 < NT:
                    load(t + 1)
                for g in range(4):
                    hg = hT[:, 2 * g:2 * g + 2, :]
                    kb.op("dve", lambda e, hg=hg: e.tensor_tensor(out=bufA[:, :, 1:W], in0=hg[:, :, 0:W - 1], in1=hg[:, :, 1:W], op=ALU.add),
                          reads=[r_hT], writes=[r_bA])
                    cur, rcur, oth, roth = bufA, r_bA, bufB, r_bB
                    lo, hi = 1, W
                    for lvl in range(1, g + 1):
                        sh = 1 << (lvl - 1)
                        nlo, nhi = lo + sh, hi - sh
                        kb.op("dve", lambda e, cur=cur, oth=oth, sh=sh, nlo=nlo, nhi=nhi: e.tensor_tensor(
                            out=oth[:, :, nlo:nhi], in0=cur[:, :, nlo - sh:nhi - sh], in1=cur[:, :, nlo + sh:nhi + sh], op=ALU.add),
                            reads=[rcur], writes=[roth])
                        cur, rcur, oth, roth = oth, roth, cur, rcur
                        lo, hi = nlo, nhi
                    w = 2 << g
                    if 0 < t < NT - 1:
                        kb.op("dve", lambda e, cur=cur, hg=hg, g=g, w=w: e.scalar_tensor_tensor(
                            out=pl[:, 2 * g:2 * g + 2, :], in0=cur[:, :, HH:HH + T], scalar=1.0 / w, in1=hg[:, :, HH:HH + T],
                            op0=ALU.mult, op1=ALU.subtract), reads=[rcur, r_hT], writes=[r_pl[g]])
                    else:
                        a = 0 if t == 0 else 1
                        kb.op("dve", lambda e, cur=cur, oth=oth, g=g, a=a: e.tensor_tensor(
                            out=oth[:, :, HH:HH + T], in0=cur[:, :, HH:HH + T], in1=inv[:, a, g, :].unsqueeze(1).to_broadcast([128, 2, T]),
                            op=ALU.mult), reads=[rcur, r_inv], writes=[roth])
                        kb.op("dve", lambda e, oth=oth, hg=hg, g=g: e.tensor_tensor(
                            out=pl[:, 2 * g:2 * g + 2, :], in0=oth[:, :, HH:HH + T], in1=hg[:, :, HH:HH + T], op=ALU.subtract),
                            reads=[roth, r_hT], writes=[r_pl[g]])
                if t + 1 < NT:
                    do_norm(t + 1)
                for s in range(2):
                    py = self.psA[s]
                    rpy = self.r_psA[s]
                    for g in range(4):
                        for kk in range(2):
                            kb.op("pe", lambda e, g=g, kk=kk, s=s, py=py: e.matmul(py[:, g * 256:(g + 1) * 256], lhsT=pl[:, 2 * g + kk, s * 128:(s + 1) * 128],
                                                                                 rhs=PW[:, g, kk, :], start=(kk == 0), stop=(kk == 1)),
                                  reads=[r_pl[g], r_pw], writes=[rpy], inc=(g == 3 and kk == 1))
                    kb.op("dve", lambda e, s=s, py=py: e.tensor_tensor(out=ysb[s][:], in0=py[:, :], in1=pbrow[:], op=ALU.add),
                          reads=[rpy, r_pb], writes=[r_ysb[s]])
                    kb.op("pool", lambda e, s=s: e.tensor_tensor(out=ysb[s][:], in0=ysb[s][:], in1=psrow[:], op=ALU.mult),
                          reads=[r_ysb[s], r_psr], writes=[r_ysb[s]])
                    self.resid_epilogue(ysb[s][:], [r_ysb[s]], xt[b][:, s, :], r_xt[b][s], Grow, r_G, ysb[s], r_ysb[s], ss2[s], r_ss2[s], junk2, r_junk2)
                    kb.dma("pool", dst[t0 + s * 128:t0 + (s + 1) * 128, :], xt[b][:, s, :], reads=[r_xt[b][s]])

    def _rxh2(self, b):
        if not hasattr(self, "_rxh2_l"):
            self._rxh2_l = [self.kb.res(), self.kb.res()]
        return self._rxh2_l[b]

    def phase_l0_mixer(self, x_in, ctx_in, dst):
        kb = self.kb
        ST = S + LC
        self.qT = self.dscr("qT", [512, S], BF16)
        self.kT = self.dscr("kT", [512, ST], BF16)
        self.v_tok = self.dscr("v_tok", [ST, 512], BF16)
        self.sz_tok = self.dscr("sz_tok", [S, D], BF16)
        self.xs_tok = self.dscr("xs_tok", [ST, D], BF16)
        self.B_tok = self.dscr("B_tok", [ST, 512], BF16)
        self.BT = self.dscr("BT", [512, ST], BF16)
        self.CT = self.dscr("CT", [512, ST], BF16)
        self.dt_tok = self.dscr("dt_tok", [ST, 32], F32)
        self.yf = self.dscr("yf", [S, D], F32)
        self.ssm_tok = self.dscr("ssm_tok", [S, D], BF16)
        self.att_tok = self.dscr("att_tok", [S, 512], BF16)
        self.l0_proj(x_in, ctx_in)
        kb.barrier()
        if self.stop_after == "proj":
            return
        self.l0_ssd(0)
        kb.barrier()
        self.l0_ssd(1)
        kb.barrier()
        if self.stop_after == "ssd":
            return
        self.l0_att()
        kb.barrier()
        if self.stop_after == "att":
            return
        self.l0_out(x_in, dst)

    def l0_proj(self, x_in, ctx_in):
        kb = self.kb
        T = 256
        NT = S // T
        w_in = self.I("w_in")
        with ExitStack() as es:
            W = self.sb(es, "Win", [128, 8, 4640], BF16)
            r_w = [kb.res() for _ in range(8)]
            for kc in range(8):
                kb.dma("pool", W[:, kc, :], w_in[kc * 128:(kc + 1) * 128, :], writes=[r_w[kc]])
            cw = self.sb(es, "c_cw", [128, 16, 3], F32)
            cb = self.sb(es, "c_cb", [128, 16], F32)
            r_cw = [kb.res(), kb.res()]
            kb.dma("sp", cw[:], self.I("conv_wT"), writes=[r_cw[0]])
            kb.dma("sp", cb[:], self.I("conv_bT"), writes=[r_cw[1]])
            nb = self.alloc_norm_bufs(es, 2, 2)
            xt = [self.sb(es, "j_xt%d" % i, [128, 2, D], F32) for i in range(2)]
            r_xt = [[kb.res() for _ in range(2)] for _ in range(2)]
            xh = [self.sb(es, "j_xh%d" % i, [2, D], F32) for i in range(2)]
            r_xh = [kb.res() for _ in range(2)]
            hTs = [self.sb(es, "j_hT%d" % i, [128, 8, T + 2], BF16) for i in range(2)]
            r_hTs = [kb.res() for _ in range(2)]
            ub = [self.sb(es, "j_ub%d" % i, [128, T + 2], F32) for i in range(2)]
            acc = [self.sb(es, "j_acc%d" % i, [128, T], F32) for i in range(2)]
            sx = [self.sb(es, "j_sx%d" % i, [128, T], BF16) for i in range(2)]
            r_ub = [kb.res() for _ in range(2)]
            r_acc = [kb.res() for _ in range(2)]
            r_sx = [kb.res() for _ in range(2)]
            fst = [self.sb(es, "j_fst%d" % i, [128, T], BF16) for i in range(2)]
            r_fst = [kb.res() for _ in range(2)]
            xs_st = self.sb(es, "j_xsst", [128, 2, D], BF16)
            b_st = self.sb(es, "j_bst", [128, 2, 512], BF16)
            sz_st = self.sb(es, "j_szst", [128, 2, D], BF16)
            v_st = self.sb(es, "j_vst", [128, 2, 512], BF16)
            dt_st = self.sb(es, "j_dtst", [128, 2, 32], F32)
            r_xsst, r_bst, r_szst, r_vst, r_dtst = (kb.res() for _ in range(5))
            psT = [self.psB[2], self.psB[3]]
            r_psT = [self.r_psB[2], self.r_psB[3]]
            psH = self.psB[3][:, 256:272]
            r_psH = self.r_psB[3]
            bankF = [self.psB[0], self.psB[1]]
            r_bankF = [self.r_psB[0], self.r_psB[1]]
            bankX = [self.psA[1][:, 0:512], self.psA[1][:, 512:1024]]
            r_bankX = [kb.res(), kb.res()]
            bankM = [self.psA[0][:, 0:512], self.psA[0][:, 512:1024]]
            r_bankM = [kb.res(), kb.res()]
            for b in range(2):
                kb.op("dve", lambda e, b=b: e.memset(xh[b][:], 0.0), writes=[r_xh[b]])

            def srcrows(t):
                return (x_in, t * T) if t < NT else (ctx_in, 0)

            def load(t, b):
                src, t0 = srcrows(t)
                for s in range(2):
                    kb.dma("sp", xt[b][:, s, :], src[t0 + s * 128:t0 + (s + 1) * 128, :], writes=[r_xt[b][s]])
                if t >= NT:
                    return
                if 0 < t < NT - 1:
                    kb.dma("sp", xh[b][0:2, :], src[t0 - 1:t0 + T + 1:T + 1, :], writes=[r_xh[b]])
                elif t > 0:
                    kb.dma("sp", xh[b][0:1, :], src[t0 - 1:t0, :], writes=[r_xh[b]])
                else:
                    kb.dma("sp", xh[b][1:2, :], src[t0 + T:t0 + T + 1, :], writes=[r_xh[b]])

            tiles = list(range(NT + 1))
            if self.max_tiles is not None:
                tiles = list(range(self.max_tiles)) + [NT]

            def do_norm(ti):
                t = tiles[ti]
                b = ti % 2
                isctx = t == NT
                halo = (False, False) if isctx else (t > 0, t < NT - 1)
                self.norm_T(nb, xt[b], r_xt[b], 2, xh[b], r_xh[b], 2, halo, hTs[b], r_hTs[b], 1, 0, T + 1,
                            0, 0, 1 if isctx else 0, psT, r_psT, psH, r_psH)

            load(tiles[0], 0)
            do_norm(0)
            fi = 0
            for ti, t in enumerate(tiles):
                b = ti % 2
                hT, r_hT = hTs[b], r_hTs[b]
                isctx = t == NT
                g0 = S if isctx else t * T
                if ti + 1 < len(tiles):
                    load(tiles[ti + 1], (ti + 1) % 2)
                chunks = []
                if not isctx:
                    chunks += [("q", c, c * 128) for c in range(4)]
                chunks += [("k", c, 1536 + c * 128) for c in range(4)]
                for kind, c, col in chunks:
                    pb = fi % 2
                    fi += 1
                    bk, rbk = bankF[pb], r_bankF[pb]
                    for kc in range(8):
                        kb.op("pe", lambda e, kc=kc, bk=bk, col=col: e.matmul(
                            bk[:, 0:T], lhsT=W[:, kc, col:col + 128], rhs=hT[:, kc, 1:T + 1], start=(kc == 0), stop=(kc == 7)),
                            reads=[r_w[kc], r_hT], writes=[rbk], inc=(kc == 7))
                    st, rst = fst[pb], r_fst[pb]
                    kb.op("act", lambda e, st=st, bk=bk, kind=kind: e.activation(out=st[:], in_=bk[:, 0:T], func=AF.Identity,
                                                                               scale=(0.125 if kind == "q" else 1.0)),
                          reads=[rbk], writes=[rst])
                    dstT = self.qT if kind == "q" else self.kT
                    kb.dma("pool", dstT[c * 128:(c + 1) * 128, g0:g0 + T], st[:], reads=[rst])
                fbase = fi
                fi += 16
                for i in range(18):
                    if i < 16:
                        pb = (fbase + i) % 2
                        bk, rbk = bankF[pb], r_bankF[pb]
                        col = 2560 + i * 128
                        for kc in range(8):
                            kb.op("pe", lambda e, kc=kc, bk=bk, col=col: e.matmul(
                                bk[:, 0:T + 2], lhsT=W[:, kc, col:col + 128], rhs=hT[:, kc, 0:T + 2], start=(kc == 0), stop=(kc == 7)),
                                reads=[r_w[kc], r_hT], writes=[rbk], inc=(kc == 7))
                    if 1 <= i <= 16:
                        c = i - 1
                        cb_ = c % 2
                        kb.op("act", lambda e, cb_=cb_: e.activation(out=sx[cb_][:], in_=acc[cb_][:], func=AF.Silu), reads=[r_acc[cb_]], writes=[r_sx[cb_]])
                        if c >= 8:
                            dstT = self.BT if c < 12 else self.CT
                            cc = (c - 8) % 4
                            kb.dma("pool", dstT[cc * 128:(cc + 1) * 128, g0:g0 + T], sx[cb_][:], reads=[r_sx[cb_]])
                    if i < 16:
                        ib = i % 2
                        kb.op("act", lambda e, ib=ib, bk=bk, i=i: e.activation(out=acc[ib][:], in_=bk[:, 1:T + 1], func=AF.Identity,
                                                                            bias=cb[:, i:i + 1], scale=cw[:, i, 1:2]),
                              reads=[rbk] + r_cw, writes=[r_acc[ib]])
                    if 1 <= i <= 12:
                        c = i - 1
                        cb_ = c % 2
                        for s in range(2):
                            kb.op("pe", lambda e, s=s, cb_=cb_: e.matmul(bankX[cb_][:, s * 128:(s + 1) * 128], lhsT=sx[cb_][:, s * 128:(s + 1) * 128],
                                                                       rhs=self.identb, start=True, stop=True),
                                  reads=[r_sx[cb_], self.r_cst], writes=[r_bankX[cb_]], inc=(s == 1))
                    if 2 <= i <= 13:
                        c = i - 2
                        cb_ = c % 2
                        if c < 8:
                            kb.op("dve", lambda e, c=c, cb_=cb_: e.tensor_copy(out=xs_st[:, :, c * 128:(c + 1) * 128],
                                                                              in_=_r(bankX[cb_][:, 0:256], "p (s f) -> p s f", s=2)),
                                  reads=[r_bankX[cb_]], writes=[r_xsst])
                        else:
                            kb.op("dve", lambda e, c=c, cb_=cb_: e.tensor_copy(out=b_st[:, :, (c - 8) * 128:(c - 7) * 128],
                                                                              in_=_r(bankX[cb_][:, 0:256], "p (s f) -> p s f", s=2)),
                                  reads=[r_bankX[cb_]], writes=[r_bst])
                    if i < 16:
                        a, c = acc[ib], i
                        kb.op("dve", lambda e, bk=bk, a=a, c=c: e.scalar_tensor_tensor(out=a[:], in0=bk[:, 0:T], scalar=cw[:, c, 0:1], in1=a[:],
                                                                                    op0=ALU.mult, op1=ALU.add), reads=[rbk, r_acc[ib]] + r_cw, writes=[r_acc[ib]])
                        kb.op("dve", lambda e, bk=bk, a=a, c=c: e.scalar_tensor_tensor(out=a[:], in0=bk[:, 2:T + 2], scalar=cw[:, c, 2:3], in1=a[:],
                                                                                    op0=ALU.mult, op1=ALU.add), reads=[rbk, r_acc[ib]] + r_cw, writes=[r_acc[ib]])
                    if i == 8 and ti + 1 < len(tiles):
                        do_norm(ti + 1)
                kb.dma("pool", _r(self.xs_tok[g0:g0 + T, :], "(s p) d -> p s d", p=128), xs_st[:], reads=[r_xsst])
                kb.dma("pool", _r(self.B_tok[g0:g0 + T, :], "(s p) d -> p s d", p=128), b_st[:], reads=[r_bst])
                mi = 0
                for s in range(2):
                    tm = [("v", 2048, 512), ("d", 4608, 32)]
                    if not isctx:
                        tm = [("z", 512, 512), ("z", 1024, 512)] + tm
                    for kind, col, n in tm:
                        bk, rbk = bankM[mi % 2], r_bankM[mi % 2]
                        mi += 1
                        for kc in range(8):
                            kb.op("pe", lambda e, kc=kc, bk=bk, col=col, n=n, s=s: e.matmul(
                                bk[:, 0:n], lhsT=hT[:, kc, 1 + s * 128:1 + (s + 1) * 128], rhs=W[:, kc, col:col + n], start=(kc == 0), stop=(kc == 7)),
                                reads=[r_w[kc], r_hT], writes=[rbk], inc=(kc == 7))
                        if kind == "z":
                            kb.op("act", lambda e, bk=bk, col=col, s=s: e.activation(out=sz_st[:, s, col - 512:col], in_=bk[:, 0:512], func=AF.Silu),
                                  reads=[rbk], writes=[r_szst])
                        elif kind == "v":
                            kb.op("act", lambda e, bk=bk, s=s: e.activation(out=v_st[:, s, :], in_=bk[:, 0:512], func=AF.Identity),
                                  reads=[rbk], writes=[r_vst])
                        else:
                            kb.op("dve", lambda e, bk=bk, s=s: e.tensor_copy(out=dt_st[:, s, :], in_=bk[:, 0:32]), reads=[rbk], writes=[r_dtst])
                if not isctx:
                    kb.dma("pool", _r(self.sz_tok[g0:g0 + T, :], "(s p) d -> p s d", p=128), sz_st[:], reads=[r_szst])
                kb.dma("pool", _r(self.v_tok[g0:g0 + T, :], "(s p) d -> p s d", p=128), v_st[:], reads=[r_vst])
                kb.dma("pool", _r(self.dt_tok[g0:g0 + T, :], "(s p) d -> p s d", p=128), dt_st[:], reads=[r_dtst])

    def l0_ssd(self, d):
        kb = self.kb
        NCH = S // 128
        with ExitStack() as es:
            rows = self.sb(es, "d_rows", [128, 3, 32], F32)
            r_rows = kb.res()
            kb.dma("sp", rows[:], _r(self.I("ssm_rows"), "a b -> (a b)").partition_broadcast(128), writes=[r_rows])
            negA = self.sb(es, "d_negA", [128, 32], F32)
            dsum = self.sb(es, "d_dsum", [128, 16], F32)
            r_negA = kb.res()
            kb.op("act", lambda e: e.activation(out=negA[:], in_=rows[:, 0, :], func=AF.Exp), reads=[r_rows], writes=[r_negA])
            kb.op("dve", lambda e: e.tensor_scalar(out=negA[:], in0=negA[:], scalar1=-1.0, scalar2=None, op0=ALU.mult), reads=[r_negA], writes=[r_negA])
            kb.op("dve", lambda e: e.tensor_tensor(out=dsum[:], in0=rows[:, 2, 0:16], in1=rows[:, 2, 16:32], op=ALU.add), reads=[r_rows], writes=[r_negA])
            ngrow, r_ng = self.load_rows_bcast(es, "d_ng", self.I("ssm_norm_g"), D)
            xs = [self.sb(es, "d_xs%d" % i, [128, D], BF16) for i in range(2)]
            Bt = [self.sb(es, "d_Bt%d" % i, [128, 512], BF16) for i in range(2)]
            BTt = [self.sb(es, "d_BTt%d" % i, [128, 4, 128], BF16) for i in range(2)]
            CTt = [self.sb(es, "d_CTt%d" % i, [128, 4, 128], BF16) for i in range(2)]
            dtr = [self.sb(es, "d_dtr%d" % i, [128, 32], F32) for i in range(2)]
            yfl = [self.sb(es, "d_yfl%d" % i, [128, D], F32) for i in range(2)]
            szl = [self.sb(es, "d_szl%d" % i, [128, D], BF16) for i in range(2)]
            r_ld = [[kb.res() for _ in range(7)] for _ in range(2)]
            H = self.sb(es, "d_H", [128, D], F32)
            Hb = self.sb(es, "d_Hb", [128, D], BF16)
            r_H, r_Hb = kb.res(), kb.res()
            kb.op("dve", lambda e: e.memset(H[:], 0.0), writes=[r_H])
            kb.op("pool", lambda e: e.memset(Hb[:], 0.0), writes=[r_Hb])
            sms = [self.sb(es, "d_sm%d" % i, [128, 8, 16], F32) for i in range(2)]
            r_sms = [[kb.res() for _ in range(8)] for _ in range(2)]
            dws = [self.sb(es, "d_dw%d" % i, [128, 16], F32) for i in range(2)]
            r_dws = [kb.res() for _ in range(2)]
            ysbs = [self.sb(es, "d_ysb%d" % i, [128, D], F32) for i in range(2)]
            r_ysbs = [kb.res() for _ in range(2)]
            Lh = self.sb(es, "d_Lh", [128, 16, 128], F32)
            r_Lh = kb.res()
            seg = self.sb(es, "d_seg", [128, 4, 128], F32)
            r_seg = kb.res()
            cbm = self.sb(es, "d_cbm", [128, 4, 128], F32)
            r_cbm = kb.res()
            M = self.sb(es, "d_M", [128, 16, 128], BF16)
            r_M = [kb.res() for _ in range(4)]
            xdt = self.sb(es, "d_xdt", [128, D], BF16)
            xdtws = [self.sb(es, "d_xdtw%d" % i, [128, D], BF16) for i in range(2)]
            r_xdt = kb.res()
            r_xdtws = [kb.res() for _ in range(2)]
            yc = self.sb(es, "d_yc", [128, D], F32)
            yt = self.sb(es, "d_yt", [128, D], F32)
            r_yc, r_yt = kb.res(), kb.res()
            ss4 = self.sb(es, "d_ss4", [128, 8], F32)
            r_ss4 = [kb.res() for _ in range(3)]
            junk = self.sb(es, "d_junk", [128, 256], BF16)
            r_junk = kb.res()
            so = self.sb(es, "d_so", [128, D], BF16)
            r_so = kb.res()
            one1 = self.sb(es, "d_one", [128, 1], F32)
            kb.op("pool", lambda e: e.memset(one1[:], 1.0), writes=[r_negA])
            y_ps, r_yps = self.psA[0], [self.r_psA[0], kb.res()]
            yo_ps, r_yops = self.psA[1], [self.r_psA[1], kb.res()]
            D_ps, r_Dps = self.psB[0], self.r_psB[0]
            cb_ps, r_cbps = self.psB[1], self.r_psB[1]
            sm_ps, r_smps = self.psB[2], self.r_psB[2]
            S_ps, r_Sps = self.psB[3], self.r_psB[3]
            Rm = self.cst[:, 1 + d, :]
            Mk = self.cst[:, 3 + d, :]
            ones = self.cst[:, 5, :]
            lat = list(range(NCH)) if d == 0 else list(range(NCH - 1, -1, -1))
            if self.max_tiles is not None:
                lat = lat[:self.max_tiles]
            order = [("c", 0), ("c", 1)] if d == 0 else [("c", 1), ("c", 0)]
            order += [("l", c) for c in lat]

            def tok0(kc):
                return S + kc[1] * 128 if kc[0] == "c" else kc[1] * 128

            def load(i):
                b = i % 2
                kc = order[i]
                t0 = tok0(kc)
                kb.dma("sp", xs[b][:], self.xs_tok[t0:t0 + 128, :], writes=[r_ld[b][0]])
                kb.dma("sp", Bt[b][:], self.B_tok[t0:t0 + 128, :], writes=[r_ld[b][1]])
                kb.dma("sp", dtr[b][:], self.dt_tok[t0:t0 + 128, :], writes=[r_ld[b][2]])
                if kc[0] == "l":
                    kb.dma("sp", BTt[b][:], _r(self.BT[:, t0:t0 + 128], "(g n) t -> n g t", n=128), writes=[r_ld[b][3]])
                    kb.dma("sp", CTt[b][:], _r(self.CT[:, t0:t0 + 128], "(g n) t -> n g t", n=128), writes=[r_ld[b][4]])
                    if d == 1:
                        kb.dma("sp", yfl[b][:], self.yf[t0:t0 + 128, :], writes=[r_ld[b][5]])
                        kb.dma("sp", szl[b][:], self.sz_tok[t0:t0 + 128, :], writes=[r_ld[b][6]])

            def partA(i):
                b = i % 2
                kc = order[i]
                islat = kc[0] == "l"
                rl = r_ld[b]
                sm, r_sm, dw, r_dw, xdtw, r_xdtw = sms[b], r_sms[b], dws[b], r_dws[b], xdtws[b], r_xdtws[b]
                hs = slice(d * 16, (d + 1) * 16)
                kb.op("dve", lambda e: e.tensor_tensor(out=sm[:, 0, :], in0=dtr[b][:, hs], in1=rows[:, 1, hs], op=ALU.add), reads=[rl[2], r_rows], writes=[r_sm[0]])
                kb.op("act", lambda e: e.activation(out=sm[:, 1, :], in_=sm[:, 0, :], func=AF.Exp), reads=[r_sm[0]], writes=[r_sm[1]])
                kb.op("act", lambda e: e.activation(out=sm[:, 2, :], in_=sm[:, 1, :], func=AF.Ln, bias=one1[:]), reads=[r_sm[1], r_negA], writes=[r_sm[2]])
                kb.op("dve", lambda e: e.tensor_tensor(out=sm[:, 3, :], in0=sm[:, 2, :], in1=negA[:, hs], op=ALU.mult), reads=[r_sm[2], r_negA], writes=[r_sm[3]])
                kb.op("pe", lambda e: e.matmul(sm_ps[:, 0:16], lhsT=Rm, rhs=sm[:, 3, :], start=True, stop=True), reads=[r_sm[3], self.r_cst], writes=[r_smps], inc=False)
                kb.op("pe", lambda e: e.matmul(sm_ps[:, 16:32], lhsT=ones, rhs=sm[:, 3, :], start=True, stop=True), reads=[r_sm[3], self.r_cst], writes=[r_smps])
                kb.op("dve", lambda e: e.tensor_copy(out=sm[:, 4, :], in_=sm_ps[:, 0:16]), reads=[r_smps], writes=[r_sm[4]])
                kb.op("act", lambda e: e.activation(out=sm[:, 7, :], in_=sm_ps[:, 16:32], func=AF.Exp), reads=[r_smps], writes=[r_sm[7]])
                kb.op("dve", lambda e: e.tensor_tensor(out=sm[:, 0, :], in0=sm_ps[:, 16:32], in1=sm[:, 4, :], op=ALU.subtract), reads=[r_smps, r_sm[4]], writes=[r_sm[0]])
                kb.op("act", lambda e: e.activation(out=sm[:, 6, :], in_=sm[:, 4, :], func=AF.Exp), reads=[r_sm[4]], writes=[r_sm[6]])
                kb.op("act", lambda e: e.activation(out=sm[:, 5, :], in_=sm[:, 0, :], func=AF.Exp), reads=[r_sm[0]], writes=[r_sm[5]])
                xs3 = _r(xs[b][:], "p (h q) -> p h q", h=16)
                if islat:
                    kb.op("dve", lambda e: e.tensor_tensor(out=Lh[:], in0=Mk.unsqueeze(1).to_broadcast([128, 16, 128]),
                                                           in1=sm[:, 3, :].unsqueeze(2).to_broadcast([128, 16, 128]), op=ALU.mult),
                          reads=[r_sm[3], self.r_cst], writes=[r_Lh])
                    for g in range(4):
                        kb.op("pe", lambda e, g=g: e.matmul(cb_ps[:, g * 128:(g + 1) * 128], lhsT=BTt[b][:, g, :], rhs=CTt[b][:, g, :], start=True, stop=True),
                              reads=[rl[3], rl[4]], writes=[r_cbps], inc=(g == 3))
                    kb.op("pool", lambda e: e.tensor_tensor(out=_r(xdt[:], "p (h q) -> p h q", h=16), in0=xs3, in1=sm[:, 2, :].unsqueeze(2).to_broadcast([128, 16, 64]), op=ALU.mult),
                          reads=[rl[0], r_sm[2]], writes=[r_xdt])
                    kb.op("dve", lambda e: e.tensor_tensor(out=cbm[:], in0=_r(cb_ps[:, :], "p (g l) -> p g l", g=4),
                                                           in1=Rm.unsqueeze(1).to_broadcast([128, 4, 128]), op=ALU.mult),
                          reads=[r_cbps, self.r_cst], writes=[r_cbm])
                kb.op("dve", lambda e: e.tensor_tensor(out=dw[:], in0=sm[:, 2, :], in1=sm[:, 5, :], op=ALU.mult), reads=[r_sm[2], r_sm[5]], writes=[r_dw])
                kb.op("pool", lambda e: e.tensor_tensor(out=_r(xdtw[:], "p (h q) -> p h q", h=16), in0=xs3, in1=dw[:].unsqueeze(2).to_broadcast([128, 16, 64]), op=ALU.mult),
                      reads=[rl[0], r_dw], writes=[r_xdtw])
                if islat:
                    for g in range(4):
                        for r4 in range(4):
                            h = g * 4 + r4
                            kb.op("pe", lambda e, h=h, r4=r4: e.matmul(D_ps[:, r4 * 128:(r4 + 1) * 128], lhsT=Lh[:, h, :], rhs=Rm, start=True, stop=True),
                                  reads=[r_Lh, self.r_cst], writes=[r_Dps], inc=(r4 == 3))
                        kb.op("act", lambda e: e.activation(out=_r(seg[:], "p r l -> p (r l)"), in_=D_ps[:, :], func=AF.Exp), reads=[r_Dps], writes=[r_seg])
                        kb.op("dve", lambda e, g=g: e.tensor_tensor(out=M[:, g * 4:(g + 1) * 4, :], in0=seg[:], in1=cbm[:, g, :].unsqueeze(1).to_broadcast([128, 4, 128]), op=ALU.mult),
                              reads=[r_seg, r_cbm], writes=[r_M[g]])
                    for h in range(16):
                        kb.op("pe", lambda e, h=h: e.matmul(y_ps[:, h * 64:(h + 1) * 64], lhsT=M[:, h, :], rhs=xdt[:, h * 64:(h + 1) * 64], start=True, stop=True),
                              reads=[r_M[h // 4], r_xdt], writes=r_yps, inc=(h == 15))
                    kb.op("act", lambda e: e.activation(out=ysbs[b][:], in_=y_ps[:, :], func=AF.Identity), reads=r_yps, writes=[r_ysbs[b]])

            def partB(i):
                b = i % 2
                kc = order[i]
                t0 = tok0(kc)
                islat = kc[0] == "l"
                rl = r_ld[b]
                sm, r_sm, xdtw, r_xdtw = sms[b], r_sms[b], xdtws[b], r_xdtws[b]
                xs3 = _r(xs[b][:], "p (h q) -> p h q", h=16)
                if islat:
                    for g in range(4):
                        kb.op("pe", lambda e, g=g: e.matmul(yo_ps[:, g * 256:(g + 1) * 256], lhsT=CTt[b][:, g, :], rhs=Hb[:, g * 256:(g + 1) * 256], start=True, stop=True),
                              reads=[rl[4], r_Hb], writes=r_yops, inc=(g == 3))
                for gp in range(2):
                    for gg in range(2):
                        g = gp * 2 + gg
                        if gp == 0 or True:
                            pass
                    if gp == 0:
                        for gg in range(2):
                            g = gg
                            kb.op("pe", lambda e, g=g, gg=gg: e.matmul(S_ps[:, gg * 256:(gg + 1) * 256], lhsT=Bt[b][:, g * 128:(g + 1) * 128], rhs=xdtw[:, g * 256:(g + 1) * 256], start=True, stop=True),
                                  reads=[rl[1], r_xdtw], writes=[r_Sps], inc=(gg == 1))
                if islat:
                    kb.op("dve", lambda e: e.tensor_tensor(out=_r(yt[:], "p (h q) -> p h q", h=16), in0=_r(yo_ps[:, :], "p (h q) -> p h q", h=16),
                                                           in1=sm[:, 6, :].unsqueeze(2).to_broadcast([128, 16, 64]), op=ALU.mult),
                          reads=r_yops + [r_sm[6]], writes=[r_yt])
                for gp in range(2):
                    if gp == 1:
                        for gg in range(2):
                            g = 2 + gg
                            kb.op("pe", lambda e, g=g, gg=gg: e.matmul(S_ps[:, gg * 256:(gg + 1) * 256], lhsT=Bt[b][:, g * 128:(g + 1) * 128], rhs=xdtw[:, g * 256:(g + 1) * 256], start=True, stop=True),
                                  reads=[rl[1], r_xdtw], writes=[r_Sps], inc=(gg == 1))
                    Hs = H[:, gp * 512:(gp + 1) * 512]
                    kb.op("pool", lambda e, Hs=Hs, gp=gp: e.tensor_tensor(out=_r(Hs, "p (h q) -> p h q", h=8), in0=_r(Hs, "p (h q) -> p h q", h=8),
                                                                        in1=sm[:, 7, gp * 8:(gp + 1) * 8].unsqueeze(2).to_broadcast([128, 8, 64]), op=ALU.mult),
                          reads=[r_H, r_sm[7]], writes=[r_H])
                    kb.op("dve", lambda e, Hs=Hs: e.tensor_tensor(out=Hs, in0=Hs, in1=S_ps[:, :], op=ALU.add), reads=[r_H, r_Sps], writes=[r_H])
                kb.op("act", lambda e: e.activation(out=Hb[:], in_=H[:], func=AF.Identity), reads=[r_H], writes=[r_Hb])
                if not islat:
                    return
                kb.op("pool", lambda e: e.tensor_tensor(out=yc[:], in0=yt[:], in1=ysbs[b][:], op=ALU.add), reads=[r_yt, r_ysbs[b]], writes=[r_yc])
                if d == 0:
                    kb.dma("pool", self.yf[t0:t0 + 128, :], yc[:], reads=[r_yc])
                    return
                kb.op("pool", lambda e: e.tensor_tensor(out=yc[:], in0=yc[:], in1=yfl[b][:], op=ALU.add), reads=[r_yc, rl[5]], writes=[r_yc])
                kb.op("dve", lambda e: e.tensor_tensor(out=_r(yt[:], "p (h q) -> p h q", h=16), in0=xs3, in1=dsum[:].unsqueeze(2).to_broadcast([128, 16, 64]), op=ALU.mult),
                      reads=[rl[0], r_negA], writes=[r_yt])
                kb.op("pool", lambda e: e.tensor_tensor(out=yc[:], in0=yc[:], in1=yt[:], op=ALU.add), reads=[r_yc, r_yt], writes=[r_yc])
                kb.op("dve", lambda e: e.tensor_tensor(out=yc[:], in0=yc[:], in1=szl[b][:], op=ALU.mult), reads=[r_yc, rl[6]], writes=[r_yc])
                for g in range(4):
                    kb.op("act", lambda e, g=g: e.activation(out=junk[:], in_=yc[:, g * 256:(g + 1) * 256], func=AF.Square, accum_out=ss4[:, g:g + 1]),
                          reads=[r_yc], writes=[r_ss4[0], r_junk])
                kb.op("act", lambda e: e.activation(out=ss4[:, 4:8], in_=ss4[:, 0:4], func=AF.Sqrt, bias=self.epsb[:], scale=1.0 / 256), reads=[r_ss4[0], self.r_cst], writes=[r_ss4[1]])
                kb.op("dve", lambda e: e.reciprocal(out=ss4[:, 4:8], in_=ss4[:, 4:8]), reads=[r_ss4[1]], writes=[r_ss4[1]])
                kb.op("dve", lambda e: e.tensor_tensor(out=_r(yt[:], "p (g q) -> p g q", g=4), in0=_r(yc[:], "p (g q) -> p g q", g=4),
                                                       in1=ss4[:, 4:8].unsqueeze(2).to_broadcast([128, 4, 256]), op=ALU.mult), reads=[r_yc, r_ss4[1]], writes=[r_yt])
                kb.op("pool", lambda e: e.tensor_tensor(out=so[:], in0=yt[:], in1=ngrow[:], op=ALU.mult), reads=[r_yt, r_ng], writes=[r_so])
                kb.dma("pool", self.ssm_tok[t0:t0 + 128, :], so[:], reads=[r_so])

            load(0)
            partA(0)
            for i in range(len(order)):
                if i + 1 < len(order):
                    load(i + 1)
                    partA(i + 1)
                partB(i)

    def l0_att(self):
        kb = self.kb
        ST = S + LC
        NTL = S // 128
        tab_in = self.I("rpbtab")
        with ExitStack() as es:
            qh = [self.sb(es, "a_q%d" % i, [64, S], BF16) for i in range(2)]
            kh = [self.sb(es, "a_k%d" % i, [64, ST], BF16) for i in range(2)]
            va = [self.sb(es, "a_v%d" % i, [128, 34, 65], BF16) for i in range(2)]
            tab = [self.sb(es, "a_tab%d" % i, [128, 5, 640], F32) for i in range(2)]
            r_hd = [[kb.res() for _ in range(4)] for _ in range(2)]
            sc = [self.sb(es, "a_sc%d" % i, [128, 640], F32) for i in range(2)]
            P = [self.sb(es, "a_P%d" % i, [128, 896], BF16) for i in range(2)]
            r_sc = [kb.res() for _ in range(2)]
            r_P = [kb.res() for _ in range(2)]
            rc = [self.sb(es, "a_rc%d" % i, [128, 1], F32) for i in range(2)]
            r_rc = [kb.res() for _ in range(2)]
            ast = [self.sb(es, "a_st%d" % i, [128, 32, 64], BF16) for i in range(2)]
            r_ast = [kb.res() for _ in range(2)]
            S_ps = [self.psA[0], self.psA[1]]
            r_Sps = [[self.r_psA[0], kb.res()], [self.r_psA[1], kb.res()]]
            o_ps = [self.psB[0], self.psB[1]]
            r_ops = [self.r_psB[0], self.r_psB[1]]
            for i in range(2):
                kb.op("dve", lambda e, i=i: e.memset(va[i][:, :, 64:65], 1.0), writes=[r_hd[i][2]])

            def loadh(h):
                b = h % 2
                kb.dma("sp", qh[b][:], self.qT[h * 64:(h + 1) * 64, :], writes=[r_hd[b][0]])
                kb.dma("sp", kh[b][:], self.kT[h * 64:(h + 1) * 64, :], writes=[r_hd[b][1]])
                for part in range(2):
                    kb.dma("sp", va[b][:, part * 17:(part + 1) * 17, 0:64],
                           _r(self.v_tok[part * 17 * 128:(part + 1) * 17 * 128, h * 64:(h + 1) * 64], "(j p) d -> p j d", p=128),
                           writes=[r_hd[b][2]] if part == 0 else [self._rva2(b)])
                kb.dma("sp", tab[b][:], _r(tab_in[h], "v p c q -> p v (c q)"), writes=[r_hd[b][3]])

            nh = 8
            loadh(0)
            it = 0
            for h in range(nh):
                b = h % 2
                if h + 1 < nh:
                    loadh(h + 1)
                rq, rk, rv, rt = r_hd[b]
                rv2 = self._rva2(b)
                tl = list(range(NTL) if self.max_tiles is None else range(self.max_tiles))

                def emit_S(j, pb):
                    u0 = min(max(2 * j - 4, 0), 54)
                    kt0 = u0 // 2
                    sp_ = S_ps[pb]
                    for c in range(7):
                        k0 = (kt0 + c) * 128 if c < 5 else S + (c - 5) * 128
                        kb.op("pe", lambda e, c=c, k0=k0, sp_=sp_: e.matmul(sp_[:, c * 128:(c + 1) * 128], lhsT=kh[b][:, k0:k0 + 128], rhs=qh[b][:, j * 128:(j + 1) * 128],
                                                                          start=True, stop=True), reads=[rq, rk], writes=r_Sps[pb], inc=(c == 6))

                emit_S(tl[0], it % 2)
                for ji, j in enumerate(tl):
                    pb = it % 2
                    it += 1
                    var = 0 if j == 0 else 1 if j == 1 else 3 if j == 30 else 4 if j == 31 else 2
                    u0 = min(max(2 * j - 4, 0), 54)
                    kt0 = u0 // 2
                    sp_ = S_ps[pb]
                    if ji + 1 < len(tl):
                        emit_S(tl[ji + 1], it % 2)
                    kb.op("dve", lambda e, sp_=sp_, var=var: e.tensor_tensor(out=sc[pb][:, 0:512], in0=sp_[:, 0:512], in1=tab[b][:, var, 0:512], op=ALU.add),
                          reads=r_Sps[pb] + [rt], writes=[r_sc[pb]])
                    kb.op("dve", lambda e, sp_=sp_, var=var: e.tensor_tensor(out=sc[pb][:, 512:640], in0=sp_[:, 512:640], in1=tab[b][:, var, 512:640], op=ALU.add),
                          reads=r_Sps[pb] + [rt], writes=[r_sc[pb]])
                    kb.op("act", lambda e, sp_=sp_: e.activation(out=P[pb][:, 640:896], in_=sp_[:, 640:896], func=AF.Exp), reads=r_Sps[pb], writes=[r_P[pb]])
                    kb.op("act", lambda e: e.activation(out=P[pb][:, 0:640], in_=sc[pb][:], func=AF.Exp), reads=[r_sc[pb]], writes=[r_P[pb]])
                    for c in range(7):
                        kt = kt0 + c if c < 5 else 32 + (c - 5)
                        kb.op("pe", lambda e, c=c, kt=kt: e.matmul(o_ps[pb][:, 0:65], lhsT=P[pb][:, c * 128:(c + 1) * 128], rhs=va[b][:, kt, :], start=(c == 0), stop=(c == 6)),
                              reads=[r_P[pb], rv, rv2], writes=[r_ops[pb]], inc=(c == 6))
                    kb.op("dve", lambda e: e.reciprocal(out=rc[pb][:], in_=o_ps[pb][:, 64:65]), reads=[r_ops[pb]], writes=[r_rc[pb]])
                    kb.op("dve", lambda e, j=j: e.tensor_scalar(out=ast[b][:, j, :], in0=o_ps[pb][:, 0:64], scalar1=rc[pb][:, 0:1], scalar2=None, op0=ALU.mult),
                          reads=[r_ops[pb], r_rc[pb]], writes=[r_ast[b]])
                for part in range(4):
                    kb.dma("pool", _r(self.att_tok[part * 1024:(part + 1) * 1024, h * 64:(h + 1) * 64], "(j p) d -> p j d", p=128),
                           ast[b][:, part * 8:(part + 1) * 8, :], reads=[r_ast[b]])

    def _rva2(self, b):
        if not hasattr(self, "_rva2_l"):
            self._rva2_l = [self.kb.res(), self.kb.res()]
        return self._rva2_l[b]

    def l0_out(self, x_in, dst):
        kb = self.kb
        NTL = S // 128
        with ExitStack() as es:
            Wo = self.sb(es, "o_W", [128, 12, D], BF16)
            r_wo = kb.res()
            kb.dma("pool", Wo[:], _r(self.I("w_out"), "(c p) d -> p c d", p=128), writes=[r_wo])
            Grow, r_G = self.load_rows_bcast(es, "o_G", self.vec[0, 0], D)
            cat = [self.sb(es, "o_cat%d" % i, [128, 1536], BF16) for i in range(2)]
            xt = [self.sb(es, "o_xt%d" % i, [128, D], F32) for i in range(2)]
            r_cat = [[kb.res(), kb.res()] for _ in range(2)]
            r_xt = [kb.res() for _ in range(2)]
            catT = self.sb(es, "o_catT", [128, 12, 128], BF16)
            r_catT = [kb.res() for _ in range(3)]
            tmp = self.sb(es, "o_tmp", [128, D], F32)
            r_tmp = kb.res()
            ss2 = self.sb(es, "o_ss2", [128, 4], F32)
            r_ss2 = [kb.res() for _ in range(3)]
            junk2 = self.sb(es, "o_junk2", [128, 512], BF16)
            r_junk2 = kb.res()
            psT = [self.psB[0], self.psB[1], self.psB[2]]
            r_psT = [self.r_psB[0], self.r_psB[1], self.r_psB[2]]

            def load(i):
                b = i % 2
                kb.dma("sp", cat[b][:, 0:512], self.att_tok[i * 128:(i + 1) * 128, :], writes=[r_cat[b][0]])
                kb.dma("sp", cat[b][:, 512:1536], self.ssm_tok[i * 128:(i + 1) * 128, :], writes=[r_cat[b][1]])
                kb.dma("sp", xt[b][:], x_in[i * 128:(i + 1) * 128, :], writes=[r_xt[b]])

            n = NTL if self.max_tiles is None else self.max_tiles
            load(0)
            for i in range(n):
                b = i % 2
                if i + 1 < n:
                    load(i + 1)
                for q3 in range(3):
                    for cc in range(4):
                        c = q3 * 4 + cc
                        kb.op("pe", lambda e, c=c, cc=cc, q3=q3: e.matmul(psT[q3][:, cc * 128:(cc + 1) * 128], lhsT=cat[b][:, c * 128:(c + 1) * 128], rhs=self.identb,
                                                                        start=True, stop=True), reads=r_cat[b] + [self.r_cst], writes=[r_psT[q3]], inc=(cc == 3))
                    kb.op("act", lambda e, q3=q3: e.activation(out=_r(catT[:, q3 * 4:(q3 + 1) * 4, :], "p c t -> p (c t)"), in_=psT[q3][:, :], func=AF.Identity),
                          reads=[r_psT[q3]], writes=[r_catT[q3]])
                py = self.psA[i % 2]
                rpy = self.r_psA[i % 2]
                for hb in range(2):
                    for c in range(12):
                        kb.op("pe", lambda e, c=c, hb=hb: e.matmul(py[:, hb * 512:(hb + 1) * 512], lhsT=catT[:, c, :], rhs=Wo[:, c, hb * 512:(hb + 1) * 512],
                                                                 start=(c == 0), stop=(c == 11)), reads=[r_catT[c // 4], r_wo], writes=[rpy], inc=(c == 11))
                self.resid_epilogue(py[:, :], [rpy], xt[b][:], r_xt[b], Grow, r_G, tmp, r_tmp, ss2, r_ss2, junk2, r_junk2)
                kb.dma("pool", dst[i * 128:(i + 1) * 128, :], xt[b][:], reads=[r_xt[b]])


def _consts():
    t = np.arange(128)
    c = np.zeros((128, 6, 128), np.float32)
    c[:, 0] = (t[:, None] == t[None, :])
    c[:, 1] = (t[:, None] <= t[None, :])
    c[:, 2] = (t[:, None] >= t[None, :])
    c[:, 3] = (t[:, None] > t[None, :])
    c[:, 4] = (t[:, None] < t[None, :])
    c[:, 5] = 1.0
    return c


def _pool_inv():
    out = np.zeros((2, 4, 256), np.float32)
    for a, base in enumerate((0, S - 256)):
        t = base + np.arange(256)
        for gi, w in enumerate((2, 4, 8, 16)):
            lo = np.clip(t - w // 2, 0, S)
            hi = np.clip(t - w // 2 + w, 0, S)
            out[a, gi] = np.float32(1.0) / (hi - lo).astype(np.float32)
    return out


def _rpb_table(rpb):
    padded = np.concatenate([rpb.reshape(8, -1), np.full((8, 1), NEG, np.float32)], axis=1)
    sent = 15 * 31
    variants = [(0, (0, 1)), (0, (2, 3)), (0, (4, 5)), (54, (60, 61)), (54, (62, 63))]
    k = np.arange(640)
    ki = k // 64
    kc = k % 64
    q = np.arange(128)
    qc = q % 64
    idx = np.zeros((5, 640, 128), np.int64)
    for v, (u0, rows) in enumerate(variants):
        r = np.array(rows)[q // 64]
        rs = np.clip(r - 4, 0, 56)
        cs = np.clip(qc - 8, 0, 48)
        i = u0 + ki
        valid = (i[:, None] >= rs[None, :]) & (i[:, None] < rs[None, :] + 8) & (kc[:, None] >= cs[None, :]) & (kc[:, None] < cs[None, :] + 16)
        rr = i[:, None] - r[None, :] + 7
        cc = kc[:, None] - qc[None, :] + 15
        flat = np.clip(rr, 0, 14) * 31 + np.clip(cc, 0, 30)
        idx[v] = np.where(valid, flat, sent)
    tab = padded[:, idx]
    tab = tab.reshape(8, 5, 5, 128, 128).transpose(0, 1, 3, 2, 4)
    return np.ascontiguousarray(tab, dtype=np.float32)


def make_in_maps(inp):
    f = lambda a: np.ascontiguousarray(np.asarray(a), dtype=np.float32)
    x, c, ctx, c_ctx = f(inp["x"]), f(inp["c"]), f(inp["ctx"]), f(inp["c_ctx"])
    shared = {
        "ada_w": f(inp["ada_w"]),
        "ada_bT": f(f(inp["ada_b"]).reshape(2, 48, 128).transpose(2, 0, 1)),
        "norm_gT": f(f(inp["norm_g"]).reshape(2, 4, 8, 128).transpose(3, 0, 1, 2)),
        "w_in": f(inp["w_in"])[0],
        "w_out": f(inp["w_out"])[0],
        "rpbtab": _rpb_table(f(inp["na_rpb"])[0]),
        "conv_wT": f(f(inp["ssm_conv_w"])[0].reshape(3, 16, 128).transpose(2, 1, 0)),
        "conv_bT": f(f(inp["ssm_conv_b"])[0].reshape(16, 128).transpose(1, 0)),
        "ssm_rows": f(np.stack([f(inp["ssm_a_log"])[0].reshape(32), f(inp["ssm_dt_bias"])[0].reshape(32), f(inp["ssm_d"])[0].reshape(32)])),
        "ssm_norm_g": f(inp["ssm_norm_g"])[0],
        "pool_w": f(inp["pool_w"])[0],
        "pool_b": f(inp["pool_b"])[0].reshape(D),
        "pool_scale": f(inp["pool_scale"])[0],
        "pool_inv": _pool_inv(),
        "ffn_w_up": f(inp["ffn_w_up"]),
        "fconv_wT": f(f(inp["ffn_conv_w"]).reshape(2, 3, 22, 128).transpose(3, 0, 2, 1)),
        "fconv_bT": f(f(inp["ffn_conv_b"]).reshape(2, 22, 128).transpose(2, 0, 1)),
        "ffn_w_down": f(inp["ffn_w_down"]),
        "consts": _consts(),
    }
    maps = []
    for b in range(x.shape[0]):
        cc = np.stack([c[b].reshape(8, 128).T, c_ctx.reshape(8, 128).T], axis=2)
        m = dict(shared)
        m["x"] = f(x[b])
        m["ctx"] = f(ctx[b])
        m["cc"] = f(cc)
        maps.append(m)
    return maps


_NC_CACHE = {}


def kernel(**inputs):
    if "full" not in _NC_CACHE:
        p = Prog()
        _NC_CACHE["full"] = p.build()
        _NC_CACHE["names"] = [k for k in p.dram if k in Prog.SHAPES]
    nc = _NC_CACHE["full"]
    maps = make_in_maps(inputs)
    names = _NC_CACHE["names"]
    maps = [{k: m[k] for k in names} for m in maps]
    res = run_bass_kernel_spmd(nc, maps, core_ids=list(range(NCORES)))
    return np.stack([np.asarray(r["out"], dtype=np.float32) for r in res.results], axis=0)
```

```python
import numpy as np
from contextlib import ExitStack
import concourse.bass as bass
import concourse.mybir as mybir
from concourse.bass_utils import run_bass_kernel_spmd

F32 = mybir.dt.float32
BF16 = mybir.dt.bfloat16
ALU = mybir.AluOpType
AF = mybir.ActivationFunctionType

D = 1024
S = 4096
LC = 256
NCORES = 8
FH = 2816
EPS = 1e-6
NEG = -30000.0


class Res:
    __slots__ = ("w", "r", "name")

    def __init__(self, name=""):
        self.w = {}
        self.r = {}
        self.name = name


class KB:
    def __init__(self):
        nc = bass.Bass("TRN2", target_bir_lowering=False)
        self.nc = nc
        self.h = {"pe": nc.tensor, "act": nc.scalar, "dve": nc.vector, "pool": nc.gpsimd, "sp": nc.sync}
        self.sems = {}
        self.cnt = {}
        for e in ("pe", "act", "dve", "pool"):
            self.sems[e] = nc.alloc_semaphore(name="c_" + e)
            self.cnt[e] = 0
        self.dq = {"sp": [], "pool": [], "act": []}
        for q, n in (("sp", 28), ("pool", 20), ("act", 8)):
            for i in range(n):
                k = "d_%s%d" % (q, i)
                self.sems[k] = nc.alloc_semaphore(name=k)
                self.cnt[k] = 0
                self.dq[q].append(k)
        self.dqi = {"sp": 0, "pool": 0, "act": 0}
        self.sems["bar"] = nc.alloc_semaphore(name="bar")
        self.cnt["bar"] = 0
        self.waited = {e: {} for e in self.h}
        self.nres = 0

    def res(self, name=""):
        return Res(name)

    def _wait(self, E, need):
        for key, v in need.items():
            if key == E and E == "pe":
                continue
            if self.waited[E].get(key, 0) < v:
                self.h[E].wait_ge(self.sems[key], v)
                self.waited[E][key] = v

    @staticmethod
    def _merge(need, d):
        for k, v in d.items():
            if need.get(k, 0) < v:
                need[k] = v

    def op(self, E, fn, reads=(), writes=(), inc=True):
        need = {}
        for r in reads:
            self._merge(need, r.w)
        for w in writes:
            self._merge(need, w.w)
            self._merge(need, w.r)
        self._wait(E, need)
        ins = fn(self.h[E])
        if inc:
            self.cnt[E] += 1
            ins.then_inc(self.sems[E], 1)
            tv = self.cnt[E]
        else:
            tv = self.cnt[E] + 1
        for r in reads:
            if r.r.get(E, 0) < tv:
                r.r[E] = tv
        for w in writes:
            w.w = {E: tv}
            w.r = {}
        return ins

    def dma(self, Q, out, in_, reads=(), writes=(), **kw):
        need = {}
        for r in reads:
            self._merge(need, r.w)
        for w in writes:
            self._merge(need, w.w)
            self._merge(need, w.r)
        i = self.dqi[Q]
        self.dqi[Q] = i + 1
        key = self.dq[Q][i % len(self.dq[Q])]
        if self.cnt[key] > 0:
            need[key] = max(need.get(key, 0), self.cnt[key])
        self._wait(Q, need)
        ins = self.h[Q].dma_start(out=out, in_=in_, **kw)
        self.cnt[key] += 16
        ins.then_inc(self.sems[key], 16)
        tv = self.cnt[key]
        for r in reads:
            r.r[key] = tv
        for w in writes:
            w.w = {key: tv}
            w.r = {}
        return ins

    def barrier(self):
        need = {k: v for k, v in self.cnt.items() if k != "bar" and v > 0}
        self._wait("sp", need)
        self.cnt["bar"] += 1
        self.h["sp"].sem_inc(self.sems["bar"], 1)
        for e in ("pe", "act", "dve", "pool"):
            self.h[e].wait_ge(self.sems["bar"], self.cnt["bar"])
            self.waited[e]["bar"] = self.cnt["bar"]
            for k, v in need.items():
                self.waited[e][k] = max(self.waited[e].get(k, 0), v)


def _r(ap, pat, **kw):
    return ap.rearrange(pat, **kw)


class Prog:
    def __init__(self, dumps=(), stop_after=None, mode="full", max_tiles=None):
        self.mode = mode
        self.max_tiles = max_tiles
        self.kb = KB()
        self.nc = self.kb.nc
        self.dumps = set(dumps)
        self.stop_after = stop_after
        self.dram = {}

    def din(self, name, shape, dt=F32):
        t = self.nc.dram_tensor(name, list(shape), dt, kind="ExternalInput").ap()
        self.dram[name] = t
        return t

    SHAPES = {
        "x": [S, D], "ctx": [LC, D], "cc": [128, 8, 2], "ada_w": [2, D, 6 * D], "ada_bT": [128, 2, 48],
        "norm_gT": [128, 2, 4, 8], "w_in": [D, 4640], "w_out": [1536, D], "rpbtab": [8, 5, 128, 5, 128],
        "conv_wT": [128, 16, 3], "conv_bT": [128, 16], "ssm_rows": [3, 32], "ssm_norm_g": [D],
        "pool_w": [4, 256, 256], "pool_b": [D], "pool_scale": [D], "pool_inv": [2, 4, 256],
        "ffn_w_up": [2, D, 2 * FH], "fconv_wT": [128, 2, 22, 3], "fconv_bT": [128, 2, 22], "ffn_w_down": [2, FH, D],
        "consts": [128, 6, 128],
    }

    def I(self, name):
        if name not in self.dram:
            self.din(name, self.SHAPES[name])
        return self.dram[name]

    def dscr(self, name, shape, dt=F32, out=False):
        kind = "ExternalOutput" if (out or name in self.dumps) else "Internal"
        t = self.nc.dram_tensor(name, list(shape), dt, kind=kind).ap()
        self.dram[name] = t
        return t

    def sb(self, es, name, shape, dt):
        self.kb.nres += 1
        return es.enter_context(self.nc.sbuf_tensor("s%d_%s" % (self.kb.nres, name), list(shape), dt))

    def build(self):
        nc, kb = self.nc, self.kb
        x_in = self.I("x")
        ctx_in = self.I("ctx") if self.mode == "full" else None
        cc_in = self.I("cc")
        ada_w = self.I("ada_w")
        ada_bT = self.I("ada_bT")
        norm_gT = self.I("norm_gT")
        consts_in = self.I("consts")
        out = self.dscr("out", [S, D], out=True)
        xa = self.dscr("xa", [S, D])
        xb = self.dscr("xb", [S, D])
        xc = self.dscr("xc", [S, D])
        self.vec = self.dscr("vec", [2, 2, D])

        with ExitStack() as g:
            self.cst = self.sb(g, "cst", [128, 6, 128], F32)
            self.cstb = self.sb(g, "cstb", [128, 6, 128], BF16)
            self.modv = self.sb(g, "modv", [128, 2, 4, 8, 2], F32)
            self.epsb = self.sb(g, "epsb", [128, 1], F32)
            self.r_cst = kb.res("cst")
            self.r_modv = kb.res("modv")
            self.psA = [g.enter_context(nc.psum_tensor("psA%d" % i, [128, 1024], F32)) for i in range(2)]
            self.psB = [g.enter_context(nc.psum_tensor("psB%d" % i, [128, 512], F32)) for i in range(4)]
            self.r_psA = [kb.res("psA%d" % i) for i in range(2)]
            self.r_psB = [kb.res("psB%d" % i) for i in range(4)]

            kb.dma("sp", self.cst[:], consts_in, writes=[self.r_cst])
            kb.op("dve", lambda e: e.tensor_copy(out=self.cstb[:], in_=self.cst[:]), reads=[self.r_cst], writes=[self.r_cst])
            kb.op("pool", lambda e: e.memset(self.epsb[:], EPS), writes=[self.r_cst])
            self.identb = self.cstb[:, 0, :]

            self.phase_adaln(cc_in, ada_w, ada_bT, norm_gT)
            kb.barrier()
            if self.stop_after == "adaln":
                return self.finish()
            if self.mode == "l1":
                self.phase_pool(x_in, xc)
                kb.barrier()
                self.phase_ffn(1, xc, out)
                return self.finish()
            if self.mode == "ffn0":
                self.phase_ffn(0, x_in, out)
                return self.finish()
            self.phase_l0_mixer(x_in, ctx_in, xa)
            kb.barrier()
            if self.stop_after in ("proj", "ssd", "att", "l0mix"):
                return self.finish()
            self.phase_ffn(0, xa, xb)
            kb.barrier()
            if self.stop_after == "l0ffn":
                return self.finish()
            self.phase_pool(xb, xc)
            kb.barrier()
            if self.stop_after == "l1mix":
                return self.finish()
            self.phase_ffn(1, xc, out)
            kb.barrier()
        return self.finish()

    def finish(self):
        self.kb.barrier()
        return self.nc

    def phase_adaln(self, cc_in, ada_w, ada_bT, norm_gT):
        nc, kb = self.nc, self.kb
        with ExitStack() as es:
            cc = self.sb(es, "cc", [128, 8, 2], F32)
            sg = self.sb(es, "sg", [128, 8, 2], F32)
            scc = self.sb(es, "scc", [128, 8, 2], F32)
            abT = self.sb(es, "abT", [128, 2, 48], F32)
            ngT = self.sb(es, "ngT", [128, 2, 4, 8], F32)
            mod = self.sb(es, "mod", [128, 2, 48, 2], F32)
            gv = self.sb(es, "gv", [128, 2, 2, 8], F32)
            wbuf = [self.sb(es, "adw%d" % i, [128, 8, 768], F32) for i in range(2)]
            r_w = [[kb.res(), kb.res()] for _ in range(2)]
            r_cc, r_small, r_mod, r_gv = kb.res(), kb.res(), kb.res(), kb.res()
            kb.dma("sp", cc[:], cc_in, writes=[r_cc])
            kb.dma("sp", abT[:], ada_bT, writes=[r_small])
            kb.dma("sp", ngT[:], norm_gT, writes=[r_small])
            kb.op("act", lambda e: e.activation(out=sg[:], in_=cc[:], func=AF.Sigmoid), reads=[r_cc], writes=[r_gv])
            kb.op("dve", lambda e: e.tensor_tensor(out=scc[:], in0=cc[:], in1=sg[:], op=ALU.mult), reads=[r_cc, r_gv], writes=[r_cc])
            it = 0
            for l in range(2):
                for jb in range(8):
                    wb, rw = wbuf[it % 2], r_w[it % 2]
                    for half in range(2):
                        kb.dma("sp", wb[:, half * 4:(half + 1) * 4, :],
                               _r(ada_w[l, half * 512:(half + 1) * 512, jb * 768:(jb + 1) * 768], "(k p) n -> p k n", p=128),
                               writes=[rw[half]])
                    ps = self.psB[it % 2]
                    rps = self.r_psB[it % 2]
                    it += 1
                    self._ada_mm(wb, rw, scc, r_cc, ps, rps, mod, r_mod, abT, r_small, l, jb)
            for l in range(2):
                for i, (sci, shi, gi) in enumerate(((8, 0, 0), (32, 24, 2))):
                    kb.op("dve", lambda e, l=l, i=i, sci=sci, gi=gi: e.scalar_tensor_tensor(
                        out=self.modv[:, l, 2 * i, :, :], in0=mod[:, l, sci:sci + 8, :], scalar=1.0,
                        in1=ngT[:, l, gi, :].unsqueeze(2).to_broadcast([128, 8, 2]), op0=ALU.add, op1=ALU.mult),
                        reads=[r_mod, r_small], writes=[self.r_modv])
                    kb.op("dve", lambda e, l=l, i=i, shi=shi: e.tensor_copy(
                        out=self.modv[:, l, 2 * i + 1, :, :], in_=mod[:, l, shi:shi + 8, :]),
                        reads=[r_mod], writes=[self.r_modv])
                for i, (gti, gpi) in enumerate(((16, 1), (40, 3))):
                    kb.op("dve", lambda e, l=l, i=i, gti=gti, gpi=gpi: e.tensor_tensor(
                        out=gv[:, l, i, :], in0=mod[:, l, gti:gti + 8, 0], in1=ngT[:, l, gpi, :], op=ALU.mult),
                        reads=[r_mod, r_small], writes=[r_gv])
            for l in range(2):
                for i in range(2):
                    kb.dma("pool", _r(self.vec[l, i], "(j p) -> p j", p=128), gv[:, l, i, :], reads=[r_gv],
                           allow_slow_non_contiguous=True)

    def _ada_mm(self, wb, rw, scc, r_cc, ps, rps, mod, r_mod, abT, r_small, l, jb):
        kb = self.kb
        for jj in range(6):
            for kc in range(8):
                kb.op("pe", lambda e, jj=jj, kc=kc: e.matmul(ps[:, jj * 2:jj * 2 + 2], lhsT=wb[:, kc, jj * 128:(jj + 1) * 128],
                                                         rhs=scc[:, kc, :], start=(kc == 0), stop=(kc == 7)),
                      reads=[rw[kc // 4], r_cc], writes=[rps], inc=(jj == 5 and kc == 7))
        kb.op("dve", lambda e: e.tensor_tensor(
            out=mod[:, l, jb * 6:(jb + 1) * 6, :], in0=_r(ps[:, 0:12], "p (j n) -> p j n", n=2),
            in1=abT[:, l, jb * 6:(jb + 1) * 6].unsqueeze(2).to_broadcast([128, 6, 2]), op=ALU.add),
            reads=[rps, r_small], writes=[r_mod])

    def alloc_norm_bufs(self, es, nsub, nh):
        nhp = max(nh, 2)
        self.r_nt = [self.kb.res() for _ in range(6)]
        return dict(junk=self.sb(es, "n_junk", [128, D], BF16), xn=self.sb(es, "n_xn", [128, nsub, D], BF16),
                    ss=self.sb(es, "n_ss", [128, 4], F32), rstd=self.sb(es, "n_rstd", [128, 4], F32),
                    xnh=self.sb(es, "n_xnh", [nhp, D], BF16), ssh=self.sb(es, "n_ssh", [nhp, 1], F32),
                    rstdh=self.sb(es, "n_rstdh", [nhp, 1], F32))

    def norm_T(self, nb, xt, r_xts, nsub, xh, r_xh, nh, halo, hT, r_hT, main0, hl0, hr0, l, mi, col, psT, r_psT, psH, r_psH):
        kb = self.kb
        junk, xn, ss, rstd, xnh, ssh, rstdh = (nb[k] for k in ("junk", "xn", "ss", "rstd", "xnh", "ssh", "rstdh"))
        rt = self.r_nt
        ntok = nsub * 128
        for s in range(nsub):
            kb.op("act", lambda e, s=s: e.activation(out=junk[:], in_=xt[:, s, :], func=AF.Square, accum_out=ss[:, s:s + 1]),
                  reads=[r_xts[s]], writes=[rt[0]])
        kb.op("act", lambda e: e.activation(out=rstd[:, 0:nsub], in_=ss[:, 0:nsub], func=AF.Sqrt, bias=self.epsb[:], scale=1.0 / D),
              reads=[rt[0], self.r_cst], writes=[rt[1]])
        kb.op("dve", lambda e: e.reciprocal(out=rstd[:, 0:nsub], in_=rstd[:, 0:nsub]), reads=[rt[1]], writes=[rt[1]])
        for s in range(nsub):
            kb.op("dve", lambda e, s=s: e.tensor_scalar(out=xn[:, s, :], in0=xt[:, s, :], scalar1=rstd[:, s:s + 1], scalar2=None, op0=ALU.mult),
                  reads=[r_xts[s], rt[1]], writes=[rt[2]])
        A = self.modv[:, l, mi, :, col]
        B = self.modv[:, l, mi + 1, :, col]
        for kc in range(8):
            p = psT[kc % 2]
            rp = r_psT[kc % 2]
            for s in range(nsub):
                kb.op("pe", lambda e, s=s, kc=kc, p=p: e.matmul(p[:, s * 128:(s + 1) * 128], lhsT=xn[:, s, kc * 128:(kc + 1) * 128],
                                                             rhs=self.identb, start=True, stop=True),
                      reads=[rt[2], self.r_cst], writes=[rp], inc=(s == nsub - 1))
            kb.op("act", lambda e, kc=kc, p=p: e.activation(out=hT[:, kc, main0:main0 + ntok], in_=p[:, 0:ntok], func=AF.Identity,
                                                          bias=B[:, kc:kc + 1], scale=A[:, kc:kc + 1]),
                  reads=[rp, self.r_modv], writes=[r_hT])
        if nh == 0:
            return
        lv, rv = halo
        hh = nh // 2
        if lv or rv:
            kb.op("act", lambda e: e.activation(out=junk[0:nh, :], in_=xh[0:nh, :], func=AF.Square, accum_out=ssh[0:nh, 0:1]),
                  reads=[r_xh], writes=[rt[3]])
            kb.op("act", lambda e: e.activation(out=rstdh[0:nh, :], in_=ssh[0:nh, :], func=AF.Sqrt, bias=self.epsb[0:nh, :], scale=1.0 / D),
                  reads=[rt[3], self.r_cst], writes=[rt[4]])
            kb.op("dve", lambda e: e.reciprocal(out=rstdh[0:nh, :], in_=rstdh[0:nh, :]), reads=[rt[4]], writes=[rt[4]])
            kb.op("dve", lambda e: e.tensor_scalar(out=xnh[0:nh, :], in0=xh[0:nh, :], scalar1=rstdh[0:nh, 0:1], scalar2=None, op0=ALU.mult),
                  reads=[r_xh, rt[4]], writes=[rt[5]])
            for kc in range(8):
                kb.op("pe", lambda e, kc=kc: e.matmul(psH[:, kc * nh:(kc + 1) * nh], lhsT=xnh[0:nh, kc * 128:(kc + 1) * 128],
                                                   rhs=self.cstb[0:nh, 0, 0:nh], start=True, stop=True),
                      reads=[rt[5], self.r_cst], writes=[r_psH], inc=(kc == 7))
            for kc in range(8):
                for side, c0 in ((0, hl0), (1, hr0)):
                    kb.op("dve", lambda e, kc=kc, side=side, c0=c0: e.tensor_scalar(
                        out=hT[:, kc, c0:c0 + hh], in0=psH[:, kc * nh + side * hh:kc * nh + (side + 1) * hh],
                        scalar1=A[:, kc:kc + 1], scalar2=B[:, kc:kc + 1], op0=ALU.mult, op1=ALU.add),
                        reads=[r_psH, self.r_modv], writes=[r_hT])
        if not lv:
            kb.op("dve", lambda e: e.memset(hT[:, :, hl0:hl0 + hh], 0.0), writes=[r_hT])
        if not rv:
            kb.op("dve", lambda e: e.memset(hT[:, :, hr0:hr0 + hh], 0.0), writes=[r_hT])

    def load_rows_bcast(self, es, name, src_row, n):
        t = self.sb(es, name, [128, n], F32)
        r = self.kb.res(name)
        self.kb.dma("sp", t[:], src_row.partition_broadcast(128), writes=[r])
        return t, r

    def resid_epilogue(self, y, r_y, xt_s, r_xt, Grow, r_G, tmp, r_tmp, ss2, r_ss2, junk, r_junk):
        kb = self.kb
        for hb in range(2):
            kb.op("act", lambda e, hb=hb: e.activation(out=junk[:, 0:512], in_=y[:, hb * 512:(hb + 1) * 512], func=AF.Square,
                                                     accum_out=ss2[:, hb:hb + 1]), reads=r_y, writes=[r_ss2[0], r_junk])
        kb.op("dve", lambda e: e.tensor_tensor(out=ss2[:, 2:3], in0=ss2[:, 0:1], in1=ss2[:, 1:2], op=ALU.add), reads=[r_ss2[0]], writes=[r_ss2[1]])
        kb.op("act", lambda e: e.activation(out=ss2[:, 3:4], in_=ss2[:, 2:3], func=AF.Sqrt, bias=self.epsb[:], scale=1.0 / D),
              reads=[r_ss2[1], self.r_cst], writes=[r_ss2[2]])
        kb.op("dve", lambda e: e.reciprocal(out=ss2[:, 3:4], in_=ss2[:, 3:4]), reads=[r_ss2[2]], writes=[r_ss2[2]])
        kb.op("dve", lambda e: e.scalar_tensor_tensor(out=tmp[:], in0=y, scalar=ss2[:, 3:4], in1=Grow[:], op0=ALU.mult, op1=ALU.mult),
              reads=r_y + [r_ss2[2], r_G], writes=[r_tmp])
        kb.op("pool", lambda e: e.tensor_tensor(out=xt_s, in0=xt_s, in1=tmp[:], op=ALU.add), reads=[r_tmp, r_xt], writes=[r_xt])

    def phase_ffn(self, l, src, dst):
        nc, kb = self.nc, self.kb
        T = 256
        NT = S // T
        with ExitStack() as es:
            Wup = self.sb(es, "Wup", [128, 8, 2 * FH], BF16)
            Wdn = self.sb(es, "Wdn", [128, 22, D], BF16)
            r_wup = [kb.res() for _ in range(8)]
            r_wdn = [kb.res() for _ in range(2)]
            for kc in range(8):
                kb.dma("pool", Wup[:, kc, :], self.I("ffn_w_up")[l, kc * 128:(kc + 1) * 128, :], writes=[r_wup[kc]])
            for hh in range(2):
                kb.dma("pool", Wdn[:, hh * 11:(hh + 1) * 11, :], _r(self.I("ffn_w_down")[l, hh * 1408:(hh + 1) * 1408, :], "(j p) d -> p j d", p=128),
                       writes=[r_wdn[hh]])
            cw = self.sb(es, "f_cw", [128, 22, 3], F32)
            cb = self.sb(es, "f_cb", [128, 22], F32)
            r_cw = [kb.res(), kb.res()]
            kb.dma("sp", cw[:], self.I("fconv_wT")[:, l], writes=[r_cw[0]])
            kb.dma("sp", cb[:], self.I("fconv_bT")[:, l], writes=[r_cw[1]])
            Grow, r_G = self.load_rows_bcast(es, "f_G", self.vec[l, 1], D)
            nb = self.alloc_norm_bufs(es, 2, 2)
            xt = [self.sb(es, "f_xt%d" % i, [128, 2, D], F32) for i in range(2)]
            r_xt = [[kb.res() for _ in range(2)] for _ in range(2)]
            xh = [self.sb(es, "f_xh%d" % i, [2, D], F32) for i in range(2)]
            r_xh = [kb.res() for _ in range(2)]
            hT = [self.sb(es, "f_hT%d" % i, [128, 8, T + 2], BF16) for i in range(2)]
            r_hT = [kb.res() for _ in range(2)]
            ub = [self.sb(es, "f_ub%d" % i, [128, T + 2], F32) for i in range(2)]
            acc = [self.sb(es, "f_acc%d" % i, [128, T], F32) for i in range(2)]
            gl = [self.sb(es, "f_gl%d" % i, [128, T], F32) for i in range(2)]
            r_ub = [kb.res() for _ in range(2)]
            r_acc = [kb.res() for _ in range(2)]
            r_gl = [kb.res() for _ in range(2)]
            gT = self.sb(es, "f_gT", [128, 22, T], BF16)
            r_gT = [kb.res() for _ in range(22)]
            ss2 = [self.sb(es, "f_ss2%d" % i, [128, 4], F32) for i in range(2)]
            r_ss2 = [[kb.res() for _ in range(3)] for _ in range(2)]
            junk2 = self.sb(es, "f_junk2", [128, 512], BF16)
            r_junk2 = kb.res()
            psT = [self.psB[2], self.psB[3]]
            r_psT = [self.r_psB[2], self.r_psB[3]]
            psH = self.psB[3][:, 256:272]
            r_psH = self.r_psB[3]
            bankU = [self.psB[0], self.psA[1][:, 0:512]]
            bankV = [self.psB[1], self.psA[1][:, 512:1024]]
            psU = [bk[:, 0:T] for bk in bankU]
            psUh = [bk[:, T:T + 2] for bk in bankU]
            psV = [bk[:, 0:T] for bk in bankV]
            r_psU = [self.r_psB[0], kb.res()]
            r_psV = [self.r_psB[1], kb.res()]
            r_psUh = r_psU
            for b in range(2):
                kb.op("dve", lambda e, b=b: e.memset(xh[b][:], 0.0), writes=[r_xh[b]])

            def load(t):
                b = t % 2
                t0 = t * T
                for s in range(2):
                    kb.dma("sp", xt[b][:, s, :], src[t0 + s * 128:t0 + (s + 1) * 128, :], writes=[r_xt[b][s]])
                if 0 < t < NT - 1:
                    kb.dma("sp", xh[b][0:2, :], src[t0 - 1:t0 + T + 1:T + 1, :], writes=[r_xh[b]])
                elif t > 0:
                    kb.dma("sp", xh[b][0:1, :], src[t0 - 1:t0, :], writes=[r_xh[b]])
                else:
                    kb.dma("sp", xh[b][1:2, :], src[t0 + T:t0 + T + 1, :], writes=[r_xh[b]])

            ysb = [self.sb(es, "f_ysb%d" % i, [128, D], F32) for i in range(2)]
            r_ysb = [kb.res() for _ in range(2)]
            NTR = NT if self.max_tiles is None else self.max_tiles

            def do_norm(t):
                b = t % 2
                self.norm_T(nb, xt[b], r_xt[b], 2, xh[b], r_xh[b], 2, (t > 0, t < NT - 1), hT[b], r_hT[b], 1, 0, T + 1,
                            l, 2, 0, psT, r_psT, psH, r_psH)

            load(0)
            do_norm(0)
            for t in range(NTR):
                b = t % 2
                t0 = t * T
                if t + 1 < NT:
                    load(t + 1)
                for j in range(23):
                    pb = j % 2
                    qb = (j - 1) % 2
                    if j < 22:
                        for (pp, rr, c0, n0, n1) in ((bankU[pb][:, 0:T + 2], r_psU[pb], j * 128, 0, T + 2),
                                                     (psV[pb], r_psV[pb], FH + j * 128, 1, T + 1)):
                            for kc in range(8):
                                kb.op("pe", lambda e, kc=kc, pp=pp, c0=c0, n0=n0, n1=n1: e.matmul(
                                    pp, lhsT=Wup[:, kc, c0:c0 + 128], rhs=hT[b][:, kc, n0:n1], start=(kc == 0), stop=(kc == 7)),
                                    reads=[r_wup[kc], r_hT[b]], writes=[rr], inc=(kc == 7))
                    if j >= 1:
                        kb.op("act", lambda e, qb=qb: e.activation(out=gl[qb][:], in_=acc[qb][:], func=AF.Gelu), reads=[r_acc[qb]], writes=[r_gl[qb]])
                    if j < 22:
                        a = acc[pb]
                        bu = bankU[pb]
                        kb.op("act", lambda e, a=a, bu=bu, j=j: e.activation(out=a[:], in_=bu[:, 1:T + 1], func=AF.Identity,
                                                                           bias=cb[:, j:j + 1], scale=cw[:, j, 1:2]),
                              reads=[r_psU[pb]] + r_cw, writes=[r_acc[pb]])
                    if j >= 1:
                        kb.op("dve", lambda e, j=j, qb=qb: e.tensor_tensor(out=gT[:, j - 1, :], in0=gl[qb][:], in1=psV[qb], op=ALU.mult),
                              reads=[r_gl[qb], r_psV[qb]], writes=[r_gT[j - 1]])
                    if j < 22:
                        kb.op("dve", lambda e, bu=bu, a=a, j=j: e.scalar_tensor_tensor(out=a[:], in0=bu[:, 0:T], scalar=cw[:, j, 0:1], in1=a[:],
                                                                                    op0=ALU.mult, op1=ALU.add), reads=[r_psU[pb], r_acc[pb]] + r_cw, writes=[r_acc[pb]])
                        kb.op("dve", lambda e, bu=bu, a=a, j=j: e.scalar_tensor_tensor(out=a[:], in0=bu[:, 2:T + 2], scalar=cw[:, j, 2:3], in1=a[:],
                                                                                    op0=ALU.mult, op1=ALU.add), reads=[r_psU[pb], r_acc[pb]] + r_cw, writes=[r_acc[pb]])
                    if j == 8 and t + 1 < NTR:
                        do_norm(t + 1)
                for s in range(2):
                    py = self.psA[0]
                    rpy = self.r_psA[0]
                    for hb in range(2):
                        for j in range(22):
                            kb.op("pe", lambda e, j=j, s=s, hb=hb: e.matmul(py[:, hb * 512:(hb + 1) * 512], lhsT=gT[:, j, s * 128:(s + 1) * 128],
                                                                          rhs=Wdn[:, j, hb * 512:(hb + 1) * 512], start=(j == 0), stop=(j == 21)),
                                  reads=[r_gT[j], r_wdn[j // 11]], writes=[rpy], inc=(j == 21))
                    kb.op("act", lambda e, s=s: e.activation(out=ysb[s][:], in_=py[:, :], func=AF.Identity), reads=[rpy], writes=[r_ysb[s]])
                    self.resid_epilogue(ysb[s][:], [r_ysb[s]], xt[b][:, s, :], r_xt[b][s], Grow, r_G, ysb[s], r_ysb[s], ss2[s], r_ss2[s], junk2, r_junk2)
                    kb.dma("pool", dst[t0 + s * 128:t0 + (s + 1) * 128, :], xt[b][:, s, :], reads=[r_xt[b][s]])

    def phase_pool(self, src, dst):
        nc, kb = self.nc, self.kb
        T = 256
        NT = S // T
        HH = 8
        W = T + 2 * HH
        l = 1
        with ExitStack() as es:
            PW = self.sb(es, "p_PW", [128, 4, 2, 256], BF16)
            r_pw = kb.res()
            kb.dma("pool", PW[:], _r(self.I("pool_w"), "g (k p) n -> p g k n", p=128), writes=[r_pw])
            pbrow, r_pb = self.load_rows_bcast(es, "p_pb", self.I("pool_b"), D)
            psrow, r_psr = self.load_rows_bcast(es, "p_ps", self.I("pool_scale"), D)
            Grow, r_G = self.load_rows_bcast(es, "p_G", self.vec[l, 0], D)
            inv = self.sb(es, "p_inv", [128, 2, 4, 256], F32)
            r_inv = kb.res()
            kb.dma("sp", inv[:], _r(self.I("pool_inv"), "a g t -> (a g t)").partition_broadcast(128), writes=[r_inv])
            nb = self.alloc_norm_bufs(es, 2, 2 * HH)
            xt = [self.sb(es, "p_xt%d" % i, [128, 2, D], F32) for i in range(2)]
            r_xt = [[kb.res() for _ in range(2)] for _ in range(2)]
            xh = [self.sb(es, "p_xh%d" % i, [2 * HH, D], F32) for i in range(2)]
            r_xh = [kb.res() for _ in range(2)]
            hTs = [self.sb(es, "p_hT%d" % i, [128, 8, W], F32) for i in range(2)]
            r_hTs = [kb.res() for _ in range(2)]
            bufA = self.sb(es, "p_bA", [128, 2, W], F32)
            bufB = self.sb(es, "p_bB", [128, 2, W], F32)
            r_bA, r_bB = kb.res(), kb.res()
            pl = self.sb(es, "p_pl", [128, 8, T], BF16)
            r_pl = [kb.res() for _ in range(4)]
            ysb = [self.sb(es, "p_ysb%d" % i, [128, D], F32) for i in range(2)]
            r_ysb = [kb.res() for _ in range(2)]
            tmp = [self.sb(es, "p_tmp%d" % i, [128, D], F32) for i in range(2)]
            r_tmp = [kb.res() for _ in range(2)]
            ss2 = [self.sb(es, "p_ss2%d" % i, [128, 4], F32) for i in range(2)]
            r_ss2 = [[kb.res() for _ in range(3)] for _ in range(2)]
            junk2 = self.sb(es, "p_junk2", [128, 512], BF16)
            r_junk2 = kb.res()
            psT = [self.psB[2], self.psB[3]]
            r_psT = [self.r_psB[2], self.r_psB[3]]
            psH = self.psB[1][:, 0:128]
            r_psH = self.r_psB[1]
            for b in range(2):
                kb.op("dve", lambda e, b=b: e.memset(xh[b][:], 0.0), writes=[r_xh[b], self._rxh2(b)])

            def load(t):
                b = t % 2
                t0 = t * T
                for s in range(2):
                    kb.dma("sp", xt[b][:, s, :], src[t0 + s * 128:t0 + (s + 1) * 128, :], writes=[r_xt[b][s]])
                if 0 < t < NT - 1:
                    kb.dma("sp", xh[b][0:HH, :], src[t0 - HH:t0, :], writes=[r_xh[b]])
                    kb.dma("sp", xh[b][HH:2 * HH, :], src[t0 + T:t0 + T + HH, :], writes=[self._rxh2(b)])
                elif t > 0:
                    kb.dma("sp", xh[b][0:HH, :], src[t0 - HH:t0, :], writes=[r_xh[b]])
                else:
                    kb.dma("sp", xh[b][HH:2 * HH, :], src[t0 + T:t0 + T + HH, :], writes=[self._rxh2(b)])

            def do_norm(t):
                b = t % 2
                rxh = Res()
                kb._merge(rxh.w, r_xh[b].w)
                kb._merge(rxh.w, self._rxh2(b).w)
                self.norm_T(nb, xt[b], r_xt[b], 2, xh[b], rxh, 2 * HH, (t > 0, t < NT - 1), hTs[b], r_hTs[b], HH, 0, HH + T,
                            l, 0, 0, psT, r_psT, psH, r_psH)
                kb._merge(r_xh[b].r, rxh.r)
                kb._merge(self._rxh2(b).r, rxh.r)

            load(0)
            do_norm(0)
            for t in range(NT):
                b = t % 2
                t0 = t * T
                hT, r_hT = hTs[b], r_hTs[b]
                if t + 1 < NT:
                    load(t + 1)
                for g in range(4):
                    hg = hT[:, 2 * g:2 * g + 2, :]
                    kb.op("dve", lambda e, hg=hg: e.tensor_tensor(out=bufA[:, :, 1:W], in0=hg[:, :, 0:W - 1], in1=hg[:, :, 1:W], op=ALU.add),
                          reads=[r_hT], writes=[r_bA])
                    cur, rcur, oth, roth = bufA, r_bA, bufB, r_bB
                    lo, hi = 1, W
                    for lvl in range(1, g + 1):
                        sh = 1 << (lvl - 1)
                        nlo, nhi = lo + sh, hi - sh
                        kb.op("dve", lambda e, cur=cur, oth=oth, sh=sh, nlo=nlo, nhi=nhi: e.tensor_tensor(
                            out=oth[:, :, nlo:nhi], in0=cur[:, :, nlo - sh:nhi - sh], in1=cur[:, :, nlo + sh:nhi + sh], op=ALU.add),
                            reads=[rcur], writes=[roth])
                        cur, rcur, oth, roth = oth, roth, cur, rcur
                        lo, hi = nlo, nhi
                    w = 2 << g
                    if 0 < t < NT - 1:
                        kb.op("dve", lambda e, cur=cur, hg=hg, g=g, w=w: e.scalar_tensor_tensor(
                            out=pl[:, 2 * g:2 * g + 2, :], in0=cur[:, :, HH:HH + T], scalar=1.0 / w, in1=hg[:, :, HH:HH + T],
                            op0=ALU.mult, op1=ALU.subtract), reads=[rcur, r_hT], writes=[r_pl[g]])
                    else:
                        a = 0 if t == 0 else 1
                        kb.op("dve", lambda e, cur=cur, oth=oth, g=g, a=a: e.tensor_tensor(
                            out=oth[:, :, HH:HH + T], in0=cur[:, :, HH:HH + T], in1=inv[:, a, g, :].unsqueeze(1).to_broadcast([128, 2, T]),
                            op=ALU.mult), reads=[rcur, r_inv], writes=[roth])
                        kb.op("dve", lambda e, oth=oth, hg=hg, g=g: e.tensor_tensor(
                            out=pl[:, 2 * g:2 * g + 2, :], in0=oth[:, :, HH:HH + T], in1=hg[:, :, HH:HH + T], op=ALU.subtract),
                            reads=[roth, r_hT], writes=[r_pl[g]])
                if t + 1 < NT:
                    do_norm(t + 1)
                for s in range(2):
                    py = self.psA[s]
                    rpy = self.r_psA[s]
                    for g in range(4):
                        for kk in range(2):
                            kb.op("pe", lambda e, g=g, kk=kk, s=s, py=py: e.matmul(py[:, g * 256:(g + 1) * 256], lhsT=pl[:, 2 * g + kk, s * 128:(s + 1) * 128],
                                                                                 rhs=PW[:, g, kk, :], start=(kk == 0), stop=(kk == 1)),
                                  reads=[r_pl[g], r_pw], writes=[rpy], inc=(g == 3 and kk == 1))
                    kb.op("dve", lambda e, s=s, py=py: e.tensor_tensor(out=ysb[s][:], in0=py[:, :], in1=pbrow[:], op=ALU.add),
                          reads=[rpy, r_pb], writes=[r_ysb[s]])
                    kb.op("pool", lambda e, s=s: e.tensor_tensor(out=ysb[s][:], in0=ysb[s][:], in1=psrow[:], op=ALU.mult),
                          reads=[r_ysb[s], r_psr], writes=[r_ysb[s]])
                    self.resid_epilogue(ysb[s][:], [r_ysb[s]], xt[b][:, s, :], r_xt[b][s], Grow, r_G, ysb[s], r_ysb[s], ss2[s], r_ss2[s], junk2, r_junk2)
                    kb.dma("pool", dst[t0 + s * 128:t0 + (s + 1) * 128, :], xt[b][:, s, :], reads=[r_xt[b][s]])

    def _rxh2(self, b):
        if not hasattr(self, "_rxh2_l"):
            self._rxh2_l = [self.kb.res(), self.kb.res()]
        return self._rxh2_l[b]

    def phase_l0_mixer(self, x_in, ctx_in, dst):
        kb = self.kb
        ST = S + LC
        self.qT = self.dscr("qT", [512, S], BF16)
        self.kT = self.dscr("kT", [512, ST], BF16)
        self.v_tok = self.dscr("v_tok", [ST, 512], BF16)
        self.sz_tok = self.dscr("sz_tok", [S, D], BF16)
        self.xs_tok = self.dscr("xs_tok", [ST, D], BF16)
        self.B_tok = self.dscr("B_tok", [ST, 512], BF16)
        self.BT = self.dscr("BT", [512, ST], BF16)
        self.CT = self.dscr("CT", [512, ST], BF16)
        self.dt_tok = self.dscr("dt_tok", [ST, 32], F32)
        self.yf = self.dscr("yf", [S, D], F32)
        self.ssm_tok = self.dscr("ssm_tok", [S, D], BF16)
        self.att_tok = self.dscr("att_tok", [S, 512], BF16)
        self.l0_proj(x_in, ctx_in)
        kb.barrier()
        if self.stop_after == "proj":
            return
        self.l0_ssd(0)
        kb.barrier()
        self.l0_ssd(1)
        kb.barrier()
        if self.stop_after == "ssd":
            return
        self.l0_att()
        kb.barrier()
        if self.stop_after == "att":
            return
        self.l0_out(x_in, dst)

    def l0_proj(self, x_in, ctx_in):
        kb = self.kb
        T = 256
        NT = S // T
        w_in = self.I("w_in")
        with ExitStack() as es:
            W = self.sb(es, "Win", [128, 8, 4640], BF16)
            r_w = [kb.res() for _ in range(8)]
            for kc in range(8):
                kb.dma("pool", W[:, kc, :], w_in[kc * 128:(kc + 1) * 128, :], writes=[r_w[kc]])
            cw = self.sb(es, "c_cw", [128, 16, 3], F32)
            cb = self.sb(es, "c_cb", [128, 16], F32)
            r_cw = [kb.res(), kb.res()]
            kb.dma("sp", cw[:], self.I("conv_wT"), writes=[r_cw[0]])
            kb.dma("sp", cb[:], self.I("conv_bT"), writes=[r_cw[1]])
            nb = self.alloc_norm_bufs(es, 2, 2)
            xt = [self.sb(es, "j_xt%d" % i, [128, 2, D], F32) for i in range(2)]
            r_xt = [[kb.res() for _ in range(2)] for _ in range(2)]
            xh = [self.sb(es, "j_xh%d" % i, [2, D], F32) for i in range(2)]
            r_xh = [kb.res() for _ in range(2)]
            hTs = [self.sb(es, "j_hT%d" % i, [128, 8, T + 2], BF16) for i in range(2)]
            r_hTs = [kb.res() for _ in range(2)]
            ub = [self.sb(es, "j_ub%d" % i, [128, T + 2], F32) for i in range(2)]
            acc = [self.sb(es, "j_acc%d" % i, [128, T], F32) for i in range(2)]
            sx = [self.sb(es, "j_sx%d" % i, [128, T], BF16) for i in range(2)]
            r_ub = [kb.res() for _ in range(2)]
            r_acc = [kb.res() for _ in range(2)]
            r_sx = [kb.res() for _ in range(2)]
            fst = [self.sb(es, "j_fst%d" % i, [128, T], BF16) for i in range(2)]
            r_fst = [kb.res() for _ in range(2)]
            xs_st = self.sb(es, "j_xsst", [128, 2, D], BF16)
            b_st = self.sb(es, "j_bst", [128, 2, 512], BF16)
            sz_st = self.sb(es, "j_szst", [128, 2, D], BF16)
            v_st = self.sb(es, "j_vst", [128, 2, 512], BF16)
            dt_st = self.sb(es, "j_dtst", [128, 2, 32], F32)
            r_xsst, r_bst, r_szst, r_vst, r_dtst = (kb.res() for _ in range(5))
            psT = [self.psB[2], self.psB[3]]
            r_psT = [self.r_psB[2], self.r_psB[3]]
            psH = self.psB[3][:, 256:272]
            r_psH = self.r_psB[3]
            bankF = [self.psB[0], self.psB[1]]
            r_bankF = [self.r_psB[0], self.r_psB[1]]
            bankX = [self.psA[1][:, 0:512], self.psA[1][:, 512:1024]]
            r_bankX = [kb.res(), kb.res()]
            bankM = [self.psA[0][:, 0:512], self.psA[0][:, 512:1024]]
            r_bankM = [kb.res(), kb.res()]
            for b in range(2):
                kb.op("dve", lambda e, b=b: e.memset(xh[b][:], 0.0), writes=[r_xh[b]])

            def srcrows(t):
                return (x_in, t * T) if t < NT else (ctx_in, 0)

            def load(t, b):
                src, t0 = srcrows(t)
                for s in range(2):
                    kb.dma("sp", xt[b][:, s, :], src[t0 + s * 128:t0 + (s + 1) * 128, :], writes=[r_xt[b][s]])
                if t >= NT:
                    return
                if 0 < t < NT - 1:
                    kb.dma("sp", xh[b][0:2, :], src[t0 - 1:t0 + T + 1:T + 1, :], writes=[r_xh[b]])
                elif t > 0:
                    kb.dma("sp", xh[b][0:1, :], src[t0 - 1:t0, :], writes=[r_xh[b]])
                else:
                    kb.dma("sp", xh[b][1:2, :], src[t0 + T:t0 + T + 1, :], writes=[r_xh[b]])

            tiles = list(range(NT + 1))
            if self.max_tiles is not None:
                tiles = list(range(self.max_tiles)) + [NT]

            def do_norm(ti):
                t = tiles[ti]
                b = ti % 2
                isctx = t == NT
                halo = (False, False) if isctx else (t > 0, t < NT - 1)
                self.norm_T(nb, xt[b], r_xt[b], 2, xh[b], r_xh[b], 2, halo, hTs[b], r_hTs[b], 1, 0, T + 1,
                            0, 0, 1 if isctx else 0, psT, r_psT, psH, r_psH)

            load(tiles[0], 0)
            do_norm(0)
            fi = 0
            for ti, t in enumerate(tiles):
                b = ti % 2
                hT, r_hT = hTs[b], r_hTs[b]
                isctx = t == NT
                g0 = S if isctx else t * T
                if ti + 1 < len(tiles):
                    load(tiles[ti + 1], (ti + 1) % 2)
                chunks = []
                if not isctx:
                    chunks += [("q", c, c * 128) for c in range(4)]
                chunks += [("k", c, 1536 + c * 128) for c in range(4)]
                for kind, c, col in chunks:
                    pb = fi % 2
                    fi += 1
                    bk, rbk = bankF[pb], r_bankF[pb]
                    for kc in range(8):
                        kb.op("pe", lambda e, kc=kc, bk=bk, col=col: e.matmul(
                            bk[:, 0:T], lhsT=W[:, kc, col:col + 128], rhs=hT[:, kc, 1:T + 1], start=(kc == 0), stop=(kc == 7)),
                            reads=[r_w[kc], r_hT], writes=[rbk], inc=(kc == 7))
                    st, rst = fst[pb], r_fst[pb]
                    kb.op("act", lambda e, st=st, bk=bk, kind=kind: e.activation(out=st[:], in_=bk[:, 0:T], func=AF.Identity,
                                                                               scale=(0.125 if kind == "q" else 1.0)),
                          reads=[rbk], writes=[rst])
                    dstT = self.qT if kind == "q" else self.kT
                    kb.dma("pool", dstT[c * 128:(c + 1) * 128, g0:g0 + T], st[:], reads=[rst])
                fbase = fi
                fi += 16
                for i in range(18):
                    if i < 16:
                        pb = (fbase + i) % 2
                        bk, rbk = bankF[pb], r_bankF[pb]
                        col = 2560 + i * 128
                        for kc in range(8):
                            kb.op("pe", lambda e, kc=kc, bk=bk, col=col: e.matmul(
                                bk[:, 0:T + 2], lhsT=W[:, kc, col:col + 128], rhs=hT[:, kc, 0:T + 2], start=(kc == 0), stop=(kc == 7)),
                                reads=[r_w[kc], r_hT], writes=[rbk], inc=(kc == 7))
                    if 1 <= i <= 16:
                        c = i - 1
                        cb_ = c % 2
                        kb.op("act", lambda e, cb_=cb_: e.activation(out=sx[cb_][:], in_=acc[cb_][:], func=AF.Silu), reads=[r_acc[cb_]], writes=[r_sx[cb_]])
                        if c >= 8:
                            dstT = self.BT if c < 12 else self.CT
                            cc = (c - 8) % 4
                            kb.dma("pool", dstT[cc * 128:(cc + 1) * 128, g0:g0 + T], sx[cb_][:], reads=[r_sx[cb_]])
                    if i < 16:
                        ib = i % 2
                        kb.op("act", lambda e, ib=ib, bk=bk, i=i: e.activation(out=acc[ib][:], in_=bk[:, 1:T + 1], func=AF.Identity,
                                                                            bias=cb[:, i:i + 1], scale=cw[:, i, 1:2]),
                              reads=[rbk] + r_cw, writes=[r_acc[ib]])
                    if 1 <= i <= 12:
                        c = i - 1
                        cb_ = c % 2
                        for s in range(2):
                            kb.op("pe", lambda e, s=s, cb_=cb_: e.matmul(bankX[cb_][:, s * 128:(s + 1) * 128], lhsT=sx[cb_][:, s * 128:(s + 1) * 128],
                                                                       rhs=self.identb, start=True, stop=True),
                                  reads=[r_sx[cb_], self.r_cst], writes=[r_bankX[cb_]], inc=(s == 1))
                    if 2 <= i <= 13:
                        c = i - 2
                        cb_ = c % 2
                        if c < 8:
                            kb.op("dve", lambda e, c=c, cb_=cb_: e.tensor_copy(out=xs_st[:, :, c * 128:(c + 1) * 128],
                                                                              in_=_r(bankX[cb_][:, 0:256], "p (s f) -> p s f", s=2)),
                                  reads=[r_bankX[cb_]], writes=[r_xsst])
                        else:
                            kb.op("dve", lambda e, c=c, cb_=cb_: e.tensor_copy(out=b_st[:, :, (c - 8) * 128:(c - 7) * 128],
                                                                              in_=_r(bankX[cb_][:, 0:256], "p (s f) -> p s f", s=2)),
                                  reads=[r_bankX[cb_]], writes=[r_bst])
                    if i < 16:
                        a, c = acc[ib], i
                        kb.op("dve", lambda e, bk=bk, a=a, c=c: e.scalar_tensor_tensor(out=a[:], in0=bk[:, 0:T], scalar=cw[:, c, 0:1], in1=a[:],
                                                                                    op0=ALU.mult, op1=ALU.add), reads=[rbk, r_acc[ib]] + r_cw, writes=[r_acc[ib]])
                        kb.op("dve", lambda e, bk=bk, a=a, c=c: e.scalar_tensor_tensor(out=a[:], in0=bk[:, 2:T + 2], scalar=cw[:, c, 2:3], in1=a[:],
                                                                                    op0=ALU.mult, op1=ALU.add), reads=[rbk, r_acc[ib]] + r_cw, writes=[r_acc[ib]])
                    if i == 8 and ti + 1 < len(tiles):
                        do_norm(ti + 1)
                kb.dma("pool", _r(self.xs_tok[g0:g0 + T, :], "(s p) d -> p s d", p=128), xs_st[:], reads=[r_xsst])
                kb.dma("pool", _r(self.B_tok[g0:g0 + T, :], "(s p) d -> p s d", p=128), b_st[:], reads=[r_bst])
                mi = 0
                for s in range(2):
                    tm = [("v", 2048, 512), ("d", 4608, 32)]
                    if not isctx:
                        tm = [("z", 512, 512), ("z", 1024, 512)] + tm
                    for kind, col, n in tm:
                        bk, rbk = bankM[mi % 2], r_bankM[mi % 2]
                        mi += 1
                        for kc in range(8):
                            kb.op("pe", lambda e, kc=kc, bk=bk, col=col, n=n, s=s: e.matmul(
                                bk[:, 0:n], lhsT=hT[:, kc, 1 + s * 128:1 + (s + 1) * 128], rhs=W[:, kc, col:col + n], start=(kc == 0), stop=(kc == 7)),
                                reads=[r_w[kc], r_hT], writes=[rbk], inc=(kc == 7))
                        if kind == "z":
                            kb.op("act", lambda e, bk=bk, col=col, s=s: e.activation(out=sz_st[:, s, col - 512:col], in_=bk[:, 0:512], func=AF.Silu),
                                  reads=[rbk], writes=[r_szst])
                        elif kind == "v":
                            kb.op("act", lambda e, bk=bk, s=s: e.activation(out=v_st[:, s, :], in_=bk[:, 0:512], func=AF.Identity),
                                  reads=[rbk], writes=[r_vst])
                        else:
                            kb.op("dve", lambda e, bk=bk, s=s: e.tensor_copy(out=dt_st[:, s, :], in_=bk[:, 0:32]), reads=[rbk], writes=[r_dtst])
                if not isctx:
                    kb.dma("pool", _r(self.sz_tok[g0:g0 + T, :], "(s p) d -> p s d", p=128), sz_st[:], reads=[r_szst])
                kb.dma("pool", _r(self.v_tok[g0:g0 + T, :], "(s p) d -> p s d", p=128), v_st[:], reads=[r_vst])
                kb.dma("pool", _r(self.dt_tok[g0:g0 + T, :], "(s p) d -> p s d", p=128), dt_st[:], reads=[r_dtst])

    def l0_ssd(self, d):
        kb = self.kb
        NCH = S // 128
        with ExitStack() as es:
            rows = self.sb(es, "d_rows", [128, 3, 32], F32)
            r_rows = kb.res()
            kb.dma("sp", rows[:], _r(self.I("ssm_rows"), "a b -> (a b)").partition_broadcast(128), writes=[r_rows])
            negA = self.sb(es, "d_negA", [128, 32], F32)
            dsum = self.sb(es, "d_dsum", [128, 16], F32)
            r_negA = kb.res()
            kb.op("act", lambda e: e.activation(out=negA[:], in_=rows[:, 0, :], func=AF.Exp), reads=[r_rows], writes=[r_negA])
            kb.op("dve", lambda e: e.tensor_scalar(out=negA[:], in0=negA[:], scalar1=-1.0, scalar2=None, op0=ALU.mult), reads=[r_negA], writes=[r_negA])
            kb.op("dve", lambda e: e.tensor_tensor(out=dsum[:], in0=rows[:, 2, 0:16], in1=rows[:, 2, 16:32], op=ALU.add), reads=[r_rows], writes=[r_negA])
            ngrow, r_ng = self.load_rows_bcast(es, "d_ng", self.I("ssm_norm_g"), D)
            xs = [self.sb(es, "d_xs%d" % i, [128, D], BF16) for i in range(2)]
            Bt = [self.sb(es, "d_Bt%d" % i, [128, 512], BF16) for i in range(2)]
            BTt = [self.sb(es, "d_BTt%d" % i, [128, 4, 128], BF16) for i in range(2)]
            CTt = [self.sb(es, "d_CTt%d" % i, [128, 4, 128], BF16) for i in range(2)]
            dtr = [self.sb(es, "d_dtr%d" % i, [128, 32], F32) for i in range(2)]
            yfl = [self.sb(es, "d_yfl%d" % i, [128, D], F32) for i in range(2)]
            szl = [self.sb(es, "d_szl%d" % i, [128, D], BF16) for i in range(2)]
            r_ld = [[kb.res() for _ in range(7)] for _ in range(2)]
            H = self.sb(es, "d_H", [128, D], F32)
            Hb = self.sb(es, "d_Hb", [128, D], BF16)
            r_H, r_Hb = kb.res(), kb.res()
            kb.op("dve", lambda e: e.memset(H[:], 0.0), writes=[r_H])
            kb.op("pool", lambda e: e.memset(Hb[:], 0.0), writes=[r_Hb])
            sms = [self.sb(es, "d_sm%d" % i, [128, 8, 16], F32) for i in range(2)]
            r_sms = [[kb.res() for _ in range(8)] for _ in range(2)]
            dws = [self.sb(es, "d_dw%d" % i, [128, 16], F32) for i in range(2)]
            r_dws = [kb.res() for _ in range(2)]
            ysbs = [self.sb(es, "d_ysb%d" % i, [128, D], F32) for i in range(2)]
            r_ysbs = [kb.res() for _ in range(2)]
            Lh = self.sb(es, "d_Lh", [128, 16, 128], F32)
            r_Lh = kb.res()
            seg = self.sb(es, "d_seg", [128, 4, 128], F32)
            r_seg = kb.res()
            cbm = self.sb(es, "d_cbm", [128, 4, 128], F32)
            r_cbm = kb.res()
            M = self.sb(es, "d_M", [128, 16, 128], BF16)
            r_M = [kb.res() for _ in range(4)]
            xdt = self.sb(es, "d_xdt", [128, D], BF16)
            xdtws = [self.sb(es, "d_xdtw%d" % i, [128, D], BF16) for i in range(2)]
            r_xdt = kb.res()
            r_xdtws = [kb.res() for _ in range(2)]
            yc = self.sb(es, "d_yc", [128, D], F32)
            yt = self.sb(es, "d_yt", [128, D], F32)
            r_yc, r_yt = kb.res(), kb.res()
            ss4 = self.sb(es, "d_ss4", [128, 8], F32)
            r_ss4 = [kb.res() for _ in range(3)]
            junk = self.sb(es, "d_junk", [128, 256], BF16)
            r_junk = kb.res()
            so = self.sb(es, "d_so", [128, D], BF16)
            r_so = kb.res()
            one1 = self.sb(es, "d_one", [128, 1], F32)
            kb.op("pool", lambda e: e.memset(one1[:], 1.0), writes=[r_negA])
            y_ps, r_yps = self.psA[0], [self.r_psA[0], kb.res()]
            yo_ps, r_yops = self.psA[1], [self.r_psA[1], kb.res()]
            D_ps, r_Dps = self.psB[0], self.r_psB[0]
            cb_ps, r_cbps = self.psB[1], self.r_psB[1]
            sm_ps, r_smps = self.psB[2], self.r_psB[2]
            S_ps, r_Sps = self.psB[3], self.r_psB[3]
            Rm = self.cst[:, 1 + d, :]
            Mk = self.cst[:, 3 + d, :]
            ones = self.cst[:, 5, :]
            lat = list(range(NCH)) if d == 0 else list(range(NCH - 1, -1, -1))
            if self.max_tiles is not None:
                lat = lat[:self.max_tiles]
            order = [("c", 0), ("c", 1)] if d == 0 else [("c", 1), ("c", 0)]
            order += [("l", c) for c in lat]

            def tok0(kc):
                return S + kc[1] * 128 if kc[0] == "c" else kc[1] * 128

            def load(i):
                b = i % 2
                kc = order[i]
                t0 = tok0(kc)
                kb.dma("sp", xs[b][:], self.xs_tok[t0:t0 + 128, :], writes=[r_ld[b][0]])
                kb.dma("sp", Bt[b][:], self.B_tok[t0:t0 + 128, :], writes=[r_ld[b][1]])
                if kc[0] == "l":
                    kb.dma("sp", BTt[b][:], _r(self.BT[:, t0:t0 + 128], "(g n) t -> n g t", n=128), writes=[r_ld[b][3]])
                    kb.dma("sp", CTt[b][:], _r(self.CT[:, t0:t0 + 128], "(g n) t -> n g t", n=128), writes=[r_ld[b][4]])
                    if d == 1:
                        kb.dma("sp", yfl[b][:], self.yf[t0:t0 + 128, :], writes=[r_ld[b][5]])
                        kb.dma("sp", szl[b][:], self.sz_tok[t0:t0 + 128, :], writes=[r_ld[b][6]])

            NCA = (S + LC) // 128
            dtall = self.sb(es, "d_dtall", [128, NCA, 32], F32)
            V = self.sb(es, "d_V", [128, 7, NCA, 16], F32)
            DEC = self.sb(es, "d_DEC", [128, NCA, 16], F32)
            r_V = kb.res()
            r_dtall = kb.res()
            kb.dma("sp", dtall[:], _r(self.dt_tok, "(c p) h -> p c h", p=128), writes=[r_dtall])
            hs = slice(d * 16, (d + 1) * 16)
            NV = NCA * 16
            fl = lambda k: _r(V[:, k], "p c h -> p (c h)")
            kb.op("dve", lambda e: e.tensor_tensor(out=V[:, 0], in0=dtall[:, :, hs], in1=rows[:, 1, hs].unsqueeze(1).to_broadcast([128, NCA, 16]), op=ALU.add),
                  reads=[r_dtall, r_rows], writes=[r_V])
            kb.op("act", lambda e: e.activation(out=fl(0), in_=fl(0), func=AF.Exp), reads=[r_V], writes=[r_V])
            kb.op("act", lambda e: e.activation(out=fl(1), in_=fl(0), func=AF.Ln, bias=one1[:]), reads=[r_V, r_negA], writes=[r_V])
            kb.op("dve", lambda e: e.tensor_tensor(out=V[:, 2], in0=V[:, 1], in1=negA[:, hs].unsqueeze(1).to_broadcast([128, NCA, 16]), op=ALU.mult),
                  reads=[r_V, r_negA], writes=[r_V])
            for (c0, c1) in ((0, 512), (512, NV)):
                kb.op("pe", lambda e, c0=c0, c1=c1: e.matmul(y_ps[:, c0:c1], lhsT=Rm, rhs=fl(2)[:, c0:c1], start=True, stop=True), reads=[r_V, self.r_cst], writes=r_yps, inc=False)
                kb.op("pe", lambda e, c0=c0, c1=c1: e.matmul(yo_ps[:, c0:c1], lhsT=ones, rhs=fl(2)[:, c0:c1], start=True, stop=True), reads=[r_V, self.r_cst], writes=r_yops, inc=(c0 == 512))
            kb.op("dve", lambda e: e.tensor_copy(out=fl(3), in_=y_ps[:, 0:NV]), reads=r_yps, writes=[r_V])
            kb.op("act", lambda e: e.activation(out=_r(DEC[:], "p c h -> p (c h)"), in_=yo_ps[:, 0:NV], func=AF.Exp), reads=r_yops, writes=[r_V])
            kb.op("dve", lambda e: e.tensor_tensor(out=fl(0), in0=yo_ps[:, 0:NV], in1=fl(3), op=ALU.subtract), reads=r_yops + [r_V], writes=[r_V])
            kb.op("act", lambda e: e.activation(out=fl(5), in_=fl(3), func=AF.Exp), reads=[r_V], writes=[r_V])
            kb.op("act", lambda e: e.activation(out=fl(4), in_=fl(0), func=AF.Exp), reads=[r_V], writes=[r_V])
            kb.op("dve", lambda e: e.tensor_tensor(out=fl(6), in0=fl(1), in1=fl(4), op=ALU.mult), reads=[r_V], writes=[r_V])

            def partA(i):
                b = i % 2
                kc = order[i]
                islat = kc[0] == "l"
                rl = r_ld[b]
                xdtw, r_xdtw = xdtws[b], r_xdtws[b]
                ci = tok0(kc) // 128
                a16, dt16, dw16 = V[:, 2, ci, :], V[:, 1, ci, :], V[:, 6, ci, :]
                xs3 = _r(xs[b][:], "p (h q) -> p h q", h=16)
                if islat:
                    kb.op("dve", lambda e: e.tensor_tensor(out=Lh[:], in0=Mk.unsqueeze(1).to_broadcast([128, 16, 128]),
                                                           in1=a16.unsqueeze(2).to_broadcast([128, 16, 128]), op=ALU.mult),
                          reads=[r_V, self.r_cst], writes=[r_Lh])
                    for g in range(4):
                        kb.op("pe", lambda e, g=g: e.matmul(cb_ps[:, g * 128:(g + 1) * 128], lhsT=BTt[b][:, g, :], rhs=CTt[b][:, g, :], start=True, stop=True),
                              reads=[rl[3], rl[4]], writes=[r_cbps], inc=(g == 3))
                    kb.op("pool", lambda e: e.tensor_tensor(out=_r(xdt[:], "p (h q) -> p h q", h=16), in0=xs3, in1=dt16.unsqueeze(2).to_broadcast([128, 16, 64]), op=ALU.mult),
                          reads=[rl[0], r_V], writes=[r_xdt])
                    kb.op("dve", lambda e: e.tensor_tensor(out=cbm[:], in0=_r(cb_ps[:, :], "p (g l) -> p g l", g=4),
                                                           in1=Rm.unsqueeze(1).to_broadcast([128, 4, 128]), op=ALU.mult),
                          reads=[r_cbps, self.r_cst], writes=[r_cbm])
                kb.op("pool", lambda e: e.tensor_tensor(out=_r(xdtw[:], "p (h q) -> p h q", h=16), in0=xs3, in1=dw16.unsqueeze(2).to_broadcast([128, 16, 64]), op=ALU.mult),
                      reads=[rl[0], r_V], writes=[r_xdtw])
                if islat:
                    for g in range(4):
                        for r4 in range(4):
                            h = g * 4 + r4
                            kb.op("pe", lambda e, h=h, r4=r4: e.matmul(D_ps[:, r4 * 128:(r4 + 1) * 128], lhsT=Lh[:, h, :], rhs=Rm, start=True, stop=True),
                                  reads=[r_Lh, self.r_cst], writes=[r_Dps], inc=(r4 == 3))
                        kb.op("act", lambda e: e.activation(out=_r(seg[:], "p r l -> p (r l)"), in_=D_ps[:, :], func=AF.Exp), reads=[r_Dps], writes=[r_seg])
                        kb.op("dve", lambda e, g=g: e.tensor_tensor(out=M[:, g * 4:(g + 1) * 4, :], in0=seg[:], in1=cbm[:, g, :].unsqueeze(1).to_broadcast([128, 4, 128]), op=ALU.mult),
                              reads=[r_seg, r_cbm], writes=[r_M[g]])
                    for h in range(16):
                        kb.op("pe", lambda e, h=h: e.matmul(y_ps[:, h * 64:(h + 1) * 64], lhsT=M[:, h, :], rhs=xdt[:, h * 64:(h + 1) * 64], start=True, stop=True),
                              reads=[r_M[h // 4], r_xdt], writes=r_yps, inc=(h == 15))
                    kb.op("act", lambda e: e.activation(out=ysbs[b][:], in_=y_ps[:, :], func=AF.Identity), reads=r_yps, writes=[r_ysbs[b]])

            def partB(i):
                b = i % 2
                kc = order[i]
                t0 = tok0(kc)
                islat = kc[0] == "l"
                rl = r_ld[b]
                xdtw, r_xdtw = xdtws[b], r_xdtws[b]
                ci = t0 // 128
                xs3 = _r(xs[b][:], "p (h q) -> p h q", h=16)
                if islat:
                    for g in range(4):
                        kb.op("pe", lambda e, g=g: e.matmul(yo_ps[:, g * 256:(g + 1) * 256], lhsT=CTt[b][:, g, :], rhs=Hb[:, g * 256:(g + 1) * 256], start=True, stop=True),
                              reads=[rl[4], r_Hb], writes=r_yops, inc=(g == 3))
                for gp in range(2):
                    for gg in range(2):
                        g = gp * 2 + gg
                        if gp == 0 or True:
                            pass
                    if gp == 0:
                        for gg in range(2):
                            g = gg
                            kb.op("pe", lambda e, g=g, gg=gg: e.matmul(S_ps[:, gg * 256:(gg + 1) * 256], lhsT=Bt[b][:, g * 128:(g + 1) * 128], rhs=xdtw[:, g * 256:(g + 1) * 256], start=True, stop=True),
                                  reads=[rl[1], r_xdtw], writes=[r_Sps], inc=(gg == 1))
                if islat:
                    kb.op("dve", lambda e: e.tensor_tensor(out=_r(yt[:], "p (h q) -> p h q", h=16), in0=_r(yo_ps[:, :], "p (h q) -> p h q", h=16),
                                                           in1=V[:, 5, ci, :].unsqueeze(2).to_broadcast([128, 16, 64]), op=ALU.mult),
                          reads=r_yops + [r_V], writes=[r_yt])
                for gp in range(2):
                    if gp == 1:
                        for gg in range(2):
                            g = 2 + gg
                            kb.op("pe", lambda e, g=g, gg=gg: e.matmul(S_ps[:, gg * 256:(gg + 1) * 256], lhsT=Bt[b][:, g * 128:(g + 1) * 128], rhs=xdtw[:, g * 256:(g + 1) * 256], start=True, stop=True),
                                  reads=[rl[1], r_xdtw], writes=[r_Sps], inc=(gg == 1))
                    Hs = H[:, gp * 512:(gp + 1) * 512]
                    kb.op("pool", lambda e, Hs=Hs, gp=gp: e.tensor_tensor(out=_r(Hs, "p (h q) -> p h q", h=8), in0=_r(Hs, "p (h q) -> p h q", h=8),
                                                                        in1=DEC[:, ci, gp * 8:(gp + 1) * 8].unsqueeze(2).to_broadcast([128, 8, 64]), op=ALU.mult),
                          reads=[r_H, r_V], writes=[r_H])
                    kb.op("dve", lambda e, Hs=Hs: e.tensor_tensor(out=Hs, in0=Hs, in1=S_ps[:, :], op=ALU.add), reads=[r_H, r_Sps], writes=[r_H])
                kb.op("act", lambda e: e.activation(out=Hb[:], in_=H[:], func=AF.Identity), reads=[r_H], writes=[r_Hb])
                if not islat:
                    return
                kb.op("pool", lambda e: e.tensor_tensor(out=yc[:], in0=yt[:], in1=ysbs[b][:], op=ALU.add), reads=[r_yt, r_ysbs[b]], writes=[r_yc])
                if d == 0:
                    kb.dma("pool", self.yf[t0:t0 + 128, :], yc[:], reads=[r_yc])
                    return
                kb.op("pool", lambda e: e.tensor_tensor(out=yc[:], in0=yc[:], in1=yfl[b][:], op=ALU.add), reads=[r_yc, rl[5]], writes=[r_yc])
                kb.op("dve", lambda e: e.tensor_tensor(out=_r(yt[:], "p (h q) -> p h q", h=16), in0=xs3, in1=dsum[:].unsqueeze(2).to_broadcast([128, 16, 64]), op=ALU.mult),
                      reads=[rl[0], r_negA], writes=[r_yt])
                kb.op("pool", lambda e: e.tensor_tensor(out=yc[:], in0=yc[:], in1=yt[:], op=ALU.add), reads=[r_yc, r_yt], writes=[r_yc])
                kb.op("dve", lambda e: e.tensor_tensor(out=yc[:], in0=yc[:], in1=szl[b][:], op=ALU.mult), reads=[r_yc, rl[6]], writes=[r_yc])
                for g in range(4):
                    kb.op("act", lambda e, g=g: e.activation(out=junk[:], in_=yc[:, g * 256:(g + 1) * 256], func=AF.Square, accum_out=ss4[:, g:g + 1]),
                          reads=[r_yc], writes=[r_ss4[0], r_junk])
                kb.op("act", lambda e: e.activation(out=ss4[:, 4:8], in_=ss4[:, 0:4], func=AF.Sqrt, bias=self.epsb[:], scale=1.0 / 256), reads=[r_ss4[0], self.r_cst], writes=[r_ss4[1]])
                kb.op("dve", lambda e: e.reciprocal(out=ss4[:, 4:8], in_=ss4[:, 4:8]), reads=[r_ss4[1]], writes=[r_ss4[1]])
                kb.op("dve", lambda e: e.tensor_tensor(out=_r(yt[:], "p (g q) -> p g q", g=4), in0=_r(yc[:], "p (g q) -> p g q", g=4),
                                                       in1=ss4[:, 4:8].unsqueeze(2).to_broadcast([128, 4, 256]), op=ALU.mult), reads=[r_yc, r_ss4[1]], writes=[r_yt])
                kb.op("pool", lambda e: e.tensor_tensor(out=so[:], in0=yt[:], in1=ngrow[:], op=ALU.mult), reads=[r_yt, r_ng], writes=[r_so])
                kb.dma("pool", self.ssm_tok[t0:t0 + 128, :], so[:], reads=[r_so])

            load(0)
            partA(0)
            for i in range(len(order)):
                if i + 1 < len(order):
                    load(i + 1)
                    partA(i + 1)
                partB(i)

    def l0_att(self):
        kb = self.kb
        ST = S + LC
        NTL = S // 128
        tab_in = self.I("rpbtab")
        with ExitStack() as es:
            qh = [self.sb(es, "a_q%d" % i, [64, S], BF16) for i in range(2)]
            kh = [self.sb(es, "a_k%d" % i, [64, ST], BF16) for i in range(2)]
            va = [self.sb(es, "a_v%d" % i, [128, 34, 65], BF16) for i in range(2)]
            tab = [self.sb(es, "a_tab%d" % i, [128, 5, 640], F32) for i in range(2)]
            r_hd = [[kb.res() for _ in range(4)] for _ in range(2)]
            sc = [self.sb(es, "a_sc%d" % i, [128, 640], F32) for i in range(2)]
            P = [self.sb(es, "a_P%d" % i, [128, 896], BF16) for i in range(2)]
            r_sc = [kb.res() for _ in range(2)]
            r_P = [kb.res() for _ in range(2)]
            rc = [self.sb(es, "a_rc%d" % i, [128, 1], F32) for i in range(2)]
            r_rc = [kb.res() for _ in range(2)]
            ast = [self.sb(es, "a_st%d" % i, [128, 32, 64], BF16) for i in range(2)]
            r_ast = [kb.res() for _ in range(2)]
            S_ps = [self.psA[0], self.psA[1]]
            r_Sps = [[self.r_psA[0], kb.res()], [self.r_psA[1], kb.res()]]
            o_ps = [self.psB[0], self.psB[1]]
            r_ops = [self.r_psB[0], self.r_psB[1]]
            for i in range(2):
                kb.op("dve", lambda e, i=i: e.memset(va[i][:, :, 64:65], 1.0), writes=[r_hd[i][2]])

            def loadh(h):
                b = h % 2
                kb.dma("sp", qh[b][:], self.qT[h * 64:(h + 1) * 64, :], writes=[r_hd[b][0]])
                kb.dma("sp", kh[b][:], self.kT[h * 64:(h + 1) * 64, :], writes=[r_hd[b][1]])
                for part in range(2):
                    kb.dma("sp", va[b][:, part * 17:(part + 1) * 17, 0:64],
                           _r(self.v_tok[part * 17 * 128:(part + 1) * 17 * 128, h * 64:(h + 1) * 64], "(j p) d -> p j d", p=128),
                           writes=[r_hd[b][2]] if part == 0 else [self._rva2(b)])
                kb.dma("sp", tab[b][:], _r(tab_in[h], "v p c q -> p v (c q)"), writes=[r_hd[b][3]])

            nh = 8
            loadh(0)
            it = 0
            for h in range(nh):
                b = h % 2
                if h + 1 < nh:
                    loadh(h + 1)
                rq, rk, rv, rt = r_hd[b]
                rv2 = self._rva2(b)
                tl = list(range(NTL) if self.max_tiles is None else range(self.max_tiles))

                def emit_S(j, pb):
                    u0 = min(max(2 * j - 4, 0), 54)
                    kt0 = u0 // 2
                    sp_ = S_ps[pb]
                    for c in range(7):
                        k0 = (kt0 + c) * 128 if c < 5 else S + (c - 5) * 128
                        kb.op("pe", lambda e, c=c, k0=k0, sp_=sp_: e.matmul(sp_[:, c * 128:(c + 1) * 128], lhsT=kh[b][:, k0:k0 + 128], rhs=qh[b][:, j * 128:(j + 1) * 128],
                                                                          start=True, stop=True), reads=[rq, rk], writes=r_Sps[pb], inc=(c == 6))

                emit_S(tl[0], it % 2)
                for ji, j in enumerate(tl):
                    pb = it % 2
                    it += 1
                    var = 0 if j == 0 else 1 if j == 1 else 3 if j == 30 else 4 if j == 31 else 2
                    u0 = min(max(2 * j - 4, 0), 54)
                    kt0 = u0 // 2
                    sp_ = S_ps[pb]
                    if ji + 1 < len(tl):
                        emit_S(tl[ji + 1], it % 2)
                    kb.op("dve", lambda e, sp_=sp_, var=var: e.tensor_tensor(out=sc[pb][:, 0:512], in0=sp_[:, 0:512], in1=tab[b][:, var, 0:512], op=ALU.add),
                          reads=r_Sps[pb] + [rt], writes=[r_sc[pb]])
                    kb.op("dve", lambda e, sp_=sp_, var=var: e.tensor_tensor(out=sc[pb][:, 512:640], in0=sp_[:, 512:640], in1=tab[b][:, var, 512:640], op=ALU.add),
                          reads=r_Sps[pb] + [rt], writes=[r_sc[pb]])
                    kb.op("act", lambda e, sp_=sp_: e.activation(out=P[pb][:, 640:896], in_=sp_[:, 640:896], func=AF.Exp), reads=r_Sps[pb], writes=[r_P[pb]])
                    kb.op("act", lambda e: e.activation(out=P[pb][:, 0:640], in_=sc[pb][:], func=AF.Exp), reads=[r_sc[pb]], writes=[r_P[pb]])
                    for c in range(7):
                        kt = kt0 + c if c < 5 else 32 + (c - 5)
                        kb.op("pe", lambda e, c=c, kt=kt: e.matmul(o_ps[pb][:, 0:65], lhsT=P[pb][:, c * 128:(c + 1) * 128], rhs=va[b][:, kt, :], start=(c == 0), stop=(c == 6)),
                              reads=[r_P[pb], rv, rv2], writes=[r_ops[pb]], inc=(c == 6))
                    kb.op("dve", lambda e: e.reciprocal(out=rc[pb][:], in_=o_ps[pb][:, 64:65]), reads=[r_ops[pb]], writes=[r_rc[pb]])
                    kb.op("dve", lambda e, j=j: e.tensor_scalar(out=ast[b][:, j, :], in0=o_ps[pb][:, 0:64], scalar1=rc[pb][:, 0:1], scalar2=None, op0=ALU.mult),
                          reads=[r_ops[pb], r_rc[pb]], writes=[r_ast[b]])
                for part in range(4):
                    kb.dma("pool", _r(self.att_tok[part * 1024:(part + 1) * 1024, h * 64:(h + 1) * 64], "(j p) d -> p j d", p=128),
                           ast[b][:, part * 8:(part + 1) * 8, :], reads=[r_ast[b]])

    def _rva2(self, b):
        if not hasattr(self, "_rva2_l"):
            self._rva2_l = [self.kb.res(), self.kb.res()]
        return self._rva2_l[b]

    def l0_out(self, x_in, dst):
        kb = self.kb
        NTL = S // 128
        with ExitStack() as es:
            Wo = self.sb(es, "o_W", [128, 12, D], BF16)
            r_wo = kb.res()
            kb.dma("pool", Wo[:], _r(self.I("w_out"), "(c p) d -> p c d", p=128), writes=[r_wo])
            Grow, r_G = self.load_rows_bcast(es, "o_G", self.vec[0, 0], D)
            cat = [self.sb(es, "o_cat%d" % i, [128, 1536], BF16) for i in range(2)]
            xt = [self.sb(es, "o_xt%d" % i, [128, D], F32) for i in range(2)]
            r_cat = [[kb.res(), kb.res()] for _ in range(2)]
            r_xt = [kb.res() for _ in range(2)]
            catT = self.sb(es, "o_catT", [128, 12, 128], BF16)
            r_catT = [kb.res() for _ in range(3)]
            tmp = self.sb(es, "o_tmp", [128, D], F32)
            r_tmp = kb.res()
            ss2 = self.sb(es, "o_ss2", [128, 4], F32)
            r_ss2 = [kb.res() for _ in range(3)]
            junk2 = self.sb(es, "o_junk2", [128, 512], BF16)
            r_junk2 = kb.res()
            psT = [self.psB[0], self.psB[1], self.psB[2]]
            r_psT = [self.r_psB[0], self.r_psB[1], self.r_psB[2]]

            def load(i):
                b = i % 2
                kb.dma("sp", cat[b][:, 0:512], self.att_tok[i * 128:(i + 1) * 128, :], writes=[r_cat[b][0]])
                kb.dma("sp", cat[b][:, 512:1536], self.ssm_tok[i * 128:(i + 1) * 128, :], writes=[r_cat[b][1]])
                kb.dma("sp", xt[b][:], x_in[i * 128:(i + 1) * 128, :], writes=[r_xt[b]])

            n = NTL if self.max_tiles is None else self.max_tiles
            load(0)
            for i in range(n):
                b = i % 2
                if i + 1 < n:
                    load(i + 1)
                for q3 in range(3):
                    for cc in range(4):
                        c = q3 * 4 + cc
                        kb.op("pe", lambda e, c=c, cc=cc, q3=q3: e.matmul(psT[q3][:, cc * 128:(cc + 1) * 128], lhsT=cat[b][:, c * 128:(c + 1) * 128], rhs=self.identb,
                                                                        start=True, stop=True), reads=r_cat[b] + [self.r_cst], writes=[r_psT[q3]], inc=(cc == 3))
                    kb.op("act", lambda e, q3=q3: e.activation(out=_r(catT[:, q3 * 4:(q3 + 1) * 4, :], "p c t -> p (c t)"), in_=psT[q3][:, :], func=AF.Identity),
                          reads=[r_psT[q3]], writes=[r_catT[q3]])
                py = self.psA[i % 2]
                rpy = self.r_psA[i % 2]
                for hb in range(2):
                    for c in range(12):
                        kb.op("pe", lambda e, c=c, hb=hb: e.matmul(py[:, hb * 512:(hb + 1) * 512], lhsT=catT[:, c, :], rhs=Wo[:, c, hb * 512:(hb + 1) * 512],
                                                                 start=(c == 0), stop=(c == 11)), reads=[r_catT[c // 4], r_wo], writes=[rpy], inc=(c == 11))
                self.resid_epilogue(py[:, :], [rpy], xt[b][:], r_xt[b], Grow, r_G, tmp, r_tmp, ss2, r_ss2, junk2, r_junk2)
                kb.dma("pool", dst[i * 128:(i + 1) * 128, :], xt[b][:], reads=[r_xt[b]])


def _consts():
    t = np.arange(128)
    c = np.zeros((128, 6, 128), np.float32)
    c[:, 0] = (t[:, None] == t[None, :])
    c[:, 1] = (t[:, None] <= t[None, :])
    c[:, 2] = (t[:, None] >= t[None, :])
    c[:, 3] = (t[:, None] > t[None, :])
    c[:, 4] = (t[:, None] < t[None, :])
    c[:, 5] = 1.0
    return c


def _pool_inv():
    out = np.zeros((2, 4, 256), np.float32)
    for a, base in enumerate((0, S - 256)):
        t = base + np.arange(256)
        for gi, w in enumerate((2, 4, 8, 16)):
            lo = np.clip(t - w // 2, 0, S)
            hi = np.clip(t - w // 2 + w, 0, S)
            out[a, gi] = np.float32(1.0) / (hi - lo).astype(np.float32)
    return out


def _rpb_table(rpb):
    padded = np.concatenate([rpb.reshape(8, -1), np.full((8, 1), NEG, np.float32)], axis=1)
    sent = 15 * 31
    variants = [(0, (0, 1)), (0, (2, 3)), (0, (4, 5)), (54, (60, 61)), (54, (62, 63))]
    k = np.arange(640)
    ki = k // 64
    kc = k % 64
    q = np.arange(128)
    qc = q % 64
    idx = np.zeros((5, 640, 128), np.int64)
    for v, (u0, rows) in enumerate(variants):
        r = np.array(rows)[q // 64]
        rs = np.clip(r - 4, 0, 56)
        cs = np.clip(qc - 8, 0, 48)
        i = u0 + ki
        valid = (i[:, None] >= rs[None, :]) & (i[:, None] < rs[None, :] + 8) & (kc[:, None] >= cs[None, :]) & (kc[:, None] < cs[None, :] + 16)
        rr = i[:, None] - r[None, :] + 7
        cc = kc[:, None] - qc[None, :] + 15
        flat = np.clip(rr, 0, 14) * 31 + np.clip(cc, 0, 30)
        idx[v] = np.where(valid, flat, sent)
    tab = padded[:, idx]
    tab = tab.reshape(8, 5, 5, 128, 128).transpose(0, 1, 3, 2, 4)
    return np.ascontiguousarray(tab, dtype=np.float32)


def make_in_maps(inp):
    f = lambda a: np.ascontiguousarray(np.asarray(a), dtype=np.float32)
    x, c, ctx, c_ctx = f(inp["x"]), f(inp["c"]), f(inp["ctx"]), f(inp["c_ctx"])
    shared = {
        "ada_w": f(inp["ada_w"]),
        "ada_bT": f(f(inp["ada_b"]).reshape(2, 48, 128).transpose(2, 0, 1)),
        "norm_gT": f(f(inp["norm_g"]).reshape(2, 4, 8, 128).transpose(3, 0, 1, 2)),
        "w_in": f(inp["w_in"])[0],
        "w_out": f(inp["w_out"])[0],
        "rpbtab": _rpb_table(f(inp["na_rpb"])[0]),
        "conv_wT": f(f(inp["ssm_conv_w"])[0].reshape(3, 16, 128).transpose(2, 1, 0)),
        "conv_bT": f(f(inp["ssm_conv_b"])[0].reshape(16, 128).transpose(1, 0)),
        "ssm_rows": f(np.stack([f(inp["ssm_a_log"])[0].reshape(32), f(inp["ssm_dt_bias"])[0].reshape(32), f(inp["ssm_d"])[0].reshape(32)])),
        "ssm_norm_g": f(inp["ssm_norm_g"])[0],
        "pool_w": f(inp["pool_w"])[0],
        "pool_b": f(inp["pool_b"])[0].reshape(D),
        "pool_scale": f(inp["pool_scale"])[0],
        "pool_inv": _pool_inv(),
        "ffn_w_up": f(inp["ffn_w_up"]),
        "fconv_wT": f(f(inp["ffn_conv_w"]).reshape(2, 3, 22, 128).transpose(3, 0, 2, 1)),
        "fconv_bT": f(f(inp["ffn_conv_b"]).reshape(2, 22, 128).transpose(2, 0, 1)),
        "ffn_w_down": f(inp["ffn_w_down"]),
        "consts": _consts(),
    }
    maps = []
    for b in range(x.shape[0]):
        cc = np.stack([c[b].reshape(8, 128).T, c_ctx.reshape(8, 128).T], axis=2)
        m = dict(shared)
        m["x"] = f(x[b])
        m["ctx"] = f(ctx[b])
        m["cc"] = f(cc)
        maps.append(m)
    return maps


_NC_CACHE = {}


def kernel(**inputs):
    if "full" not in _NC_CACHE:
        p = Prog()
        _NC_CACHE["full"] = p.build()
        _NC_CACHE["names"] = [k for k in p.dram if k in Prog.SHAPES]
    nc = _NC_CACHE["full"]
    maps = make_in_maps(inputs)
    names = _NC_CACHE["names"]
    maps = [{k: m[k] for k in names} for m in maps]
    res = run_bass_kernel_spmd(nc, maps, core_ids=list(range(NCORES)))
    return np.stack([np.asarray(r["out"], dtype=np.float32) for r in res.results], axis=0)
```

```python
import numpy as np
from contextlib import ExitStack
import concourse.bass as bass
import concourse.mybir as mybir
from concourse.bass_utils import run_bass_kernel_spmd

F32 = mybir.dt.float32
BF16 = mybir.dt.bfloat16
ALU = mybir.AluOpType
AF = mybir.ActivationFunctionType

D = 1024
S = 4096
LC = 256
NCORES = 8
FH = 2816
EPS = 1e-6
NEG = -30000.0


class Res:
    __slots__ = ("w", "r", "name")

    def __init__(self, name=""):
        self.w = {}
        self.r = {}
        self.name = name


class KB:
    def __init__(self):
        nc = bass.Bass("TRN2", target_bir_lowering=False)
        self.nc = nc
        self.h = {"pe": nc.tensor, "act": nc.scalar, "dve": nc.vector, "pool": nc.gpsimd, "sp": nc.sync}
        self.sems = {}
        self.cnt = {}
        for e in ("pe", "act", "dve", "pool"):
            self.sems[e] = nc.alloc_semaphore(name="c_" + e)
            self.cnt[e] = 0
        self.dq = {"sp": [], "pool": [], "act": []}
        for q, n in (("sp", 28), ("pool", 20), ("act", 8)):
            for i in range(n):
                k = "d_%s%d" % (q, i)
                self.sems[k] = nc.alloc_semaphore(name=k)
                self.cnt[k] = 0
                self.dq[q].append(k)
        self.dqi = {"sp": 0, "pool": 0, "act": 0}
        self.sems["bar"] = nc.alloc_semaphore(name="bar")
        self.cnt["bar"] = 0
        self.waited = {e: {} for e in self.h}
        self.nres = 0

    def res(self, name=""):
        return Res(name)

    def _wait(self, E, need):
        for key, v in need.items():
            if key == E and E == "pe":
                continue
            if self.waited[E].get(key, 0) < v:
                self.h[E].wait_ge(self.sems[key], v)
                self.waited[E][key] = v

    @staticmethod
    def _merge(need, d):
        for k, v in d.items():
            if need.get(k, 0) < v:
                need[k] = v

    def op(self, E, fn, reads=(), writes=(), inc=True):
        need = {}
        for r in reads:
            self._merge(need, r.w)
        for w in writes:
            self._merge(need, w.w)
            self._merge(need, w.r)
        self._wait(E, need)
        ins = fn(self.h[E])
        if inc:
            self.cnt[E] += 1
            ins.then_inc(self.sems[E], 1)
            tv = self.cnt[E]
        else:
            tv = self.cnt[E] + 1
        for r in reads:
            if r.r.get(E, 0) < tv:
                r.r[E] = tv
        for w in writes:
            w.w = {E: tv}
            w.r = {}
        return ins

    def dma(self, Q, out, in_, reads=(), writes=(), **kw):
        need = {}
        for r in reads:
            self._merge(need, r.w)
        for w in writes:
            self._merge(need, w.w)
            self._merge(need, w.r)
        i = self.dqi[Q]
        self.dqi[Q] = i + 1
        key = self.dq[Q][i % len(self.dq[Q])]
        if self.cnt[key] > 0:
            need[key] = max(need.get(key, 0), self.cnt[key])
        self._wait(Q, need)
        ins = self.h[Q].dma_start(out=out, in_=in_, **kw)
        self.cnt[key] += 16
        ins.then_inc(self.sems[key], 16)
        tv = self.cnt[key]
        for r in reads:
            r.r[key] = tv
        for w in writes:
            w.w = {key: tv}
            w.r = {}
        return ins

    def barrier(self):
        need = {k: v for k, v in self.cnt.items() if k != "bar" and v > 0}
        self._wait("sp", need)
        self.cnt["bar"] += 1
        self.h["sp"].sem_inc(self.sems["bar"], 1)
        for e in ("pe", "act", "dve", "pool"):
            self.h[e].wait_ge(self.sems["bar"], self.cnt["bar"])
            self.waited[e]["bar"] = self.cnt["bar"]
            for k, v in need.items():
                self.waited[e][k] = max(self.waited[e].get(k, 0), v)


def _r(ap, pat, **kw):
    return ap.rearrange(pat, **kw)


class Prog:
    def __init__(self, dumps=(), stop_after=None, mode="full", max_tiles=None):
        self.mode = mode
        self.max_tiles = max_tiles
        self.kb = KB()
        self.nc = self.kb.nc
        self.dumps = set(dumps)
        self.stop_after = stop_after
        self.dram = {}

    def din(self, name, shape, dt=F32):
        t = self.nc.dram_tensor(name, list(shape), dt, kind="ExternalInput").ap()
        self.dram[name] = t
        return t

    SHAPES = {
        "x": [S, D], "ctx": [LC, D], "cc": [128, 8, 2], "ada_w": [2, D, 6 * D], "ada_bT": [128, 2, 48],
        "norm_gT": [128, 2, 4, 8], "w_in": [D, 4640], "w_out": [1536, D], "rpbtab": [8, 5, 128, 5, 128],
        "conv_wT": [128, 16, 3], "conv_bT": [128, 16], "ssm_rows": [3, 32], "ssm_norm_g": [D],
        "pool_w": [4, 256, 256], "pool_b": [D], "pool_scale": [D], "pool_inv": [2, 4, 256],
        "ffn_w_up": [2, D, 2 * FH], "fconv_wT": [128, 2, 22, 3], "fconv_bT": [128, 2, 22], "ffn_w_down": [2, FH, D],
        "consts": [128, 6, 128],
    }

    def I(self, name):
        if name not in self.dram:
            self.din(name, self.SHAPES[name])
        return self.dram[name]

    def dscr(self, name, shape, dt=F32, out=False):
        kind = "ExternalOutput" if (out or name in self.dumps) else "Internal"
        t = self.nc.dram_tensor(name, list(shape), dt, kind=kind).ap()
        self.dram[name] = t
        return t

    def sb(self, es, name, shape, dt):
        self.kb.nres += 1
        return es.enter_context(self.nc.sbuf_tensor("s%d_%s" % (self.kb.nres, name), list(shape), dt))

    def build(self):
        nc, kb = self.nc, self.kb
        x_in = self.I("x")
        ctx_in = self.I("ctx") if self.mode == "full" else None
        cc_in = self.I("cc")
        ada_w = self.I("ada_w")
        ada_bT = self.I("ada_bT")
        norm_gT = self.I("norm_gT")
        consts_in = self.I("consts")
        out = self.dscr("out", [S, D], out=True)
        xa = self.dscr("xa", [S, D])
        xb = self.dscr("xb", [S, D])
        xc = self.dscr("xc", [S, D])
        self.vec = self.dscr("vec", [2, 2, D])

        with ExitStack() as g:
            self.cst = self.sb(g, "cst", [128, 6, 128], F32)
            self.cstb = self.sb(g, "cstb", [128, 6, 128], BF16)
            self.modv = self.sb(g, "modv", [128, 2, 4, 8, 2], F32)
            self.epsb = self.sb(g, "epsb", [128, 1], F32)
            self.r_cst = kb.res("cst")
            self.r_modv = kb.res("modv")
            self.psA = [g.enter_context(nc.psum_tensor("psA%d" % i, [128, 1024], F32)) for i in range(2)]
            self.psB = [g.enter_context(nc.psum_tensor("psB%d" % i, [128, 512], F32)) for i in range(4)]
            self.r_psA = [kb.res("psA%d" % i) for i in range(2)]
            self.r_psB = [kb.res("psB%d" % i) for i in range(4)]

            kb.dma("sp", self.cst[:], consts_in, writes=[self.r_cst])
            kb.op("dve", lambda e: e.tensor_copy(out=self.cstb[:], in_=self.cst[:]), reads=[self.r_cst], writes=[self.r_cst])
            kb.op("pool", lambda e: e.memset(self.epsb[:], EPS), writes=[self.r_cst])
            self.identb = self.cstb[:, 0, :]

            self.phase_adaln(cc_in, ada_w, ada_bT, norm_gT)
            kb.barrier()
            if self.stop_after == "adaln":
                return self.finish()
            if self.mode == "l1":
                self.phase_pool(x_in, xc)
                kb.barrier()
                self.phase_ffn(1, xc, out)
                return self.finish()
            if self.mode == "ffn0":
                self.phase_ffn(0, x_in, out)
                return self.finish()
            self.phase_l0_mixer(x_in, ctx_in, xa)
            kb.barrier()
            if self.stop_after in ("proj", "ssd", "att", "l0mix"):
                return self.finish()
            self.phase_ffn(0, xa, xb)
            kb.barrier()
            if self.stop_after == "l0ffn":
                return self.finish()
            self.phase_pool(xb, xc)
            kb.barrier()
            if self.stop_after == "l1mix":
                return self.finish()
            self.phase_ffn(1, xc, out)
            kb.barrier()
        return self.finish()

    def finish(self):
        self.kb.barrier()
        return self.nc

    def phase_adaln(self, cc_in, ada_w, ada_bT, norm_gT):
        nc, kb = self.nc, self.kb
        with ExitStack() as es:
            cc = self.sb(es, "cc", [128, 8, 2], F32)
            sg = self.sb(es, "sg", [128, 8, 2], F32)
            scc = self.sb(es, "scc", [128, 8, 2], F32)
            abT = self.sb(es, "abT", [128, 2, 48], F32)
            ngT = self.sb(es, "ngT", [128, 2, 4, 8], F32)
            mod = self.sb(es, "mod", [128, 2, 48, 2], F32)
            gv = self.sb(es, "gv", [128, 2, 2, 8], F32)
            wbuf = [self.sb(es, "adw%d" % i, [128, 8, 768], F32) for i in range(2)]
            r_w = [[kb.res(), kb.res()] for _ in range(2)]
            r_cc, r_small, r_mod, r_gv = kb.res(), kb.res(), kb.res(), kb.res()
            kb.dma("sp", cc[:], cc_in, writes=[r_cc])
            kb.dma("sp", abT[:], ada_bT, writes=[r_small])
            kb.dma("sp", ngT[:], norm_gT, writes=[r_small])
            kb.op("act", lambda e: e.activation(out=sg[:], in_=cc[:], func=AF.Sigmoid), reads=[r_cc], writes=[r_gv])
            kb.op("dve", lambda e: e.tensor_tensor(out=scc[:], in0=cc[:], in1=sg[:], op=ALU.mult), reads=[r_cc, r_gv], writes=[r_cc])
            it = 0
            for l in range(2):
                for jb in range(8):
                    wb, rw = wbuf[it % 2], r_w[it % 2]
                    for half in range(2):
                        kb.dma("sp", wb[:, half * 4:(half + 1) * 4, :],
                               _r(ada_w[l, half * 512:(half + 1) * 512, jb * 768:(jb + 1) * 768], "(k p) n -> p k n", p=128),
                               writes=[rw[half]])
                    ps = self.psB[it % 2]
                    rps = self.r_psB[it % 2]
                    it += 1
                    self._ada_mm(wb, rw, scc, r_cc, ps, rps, mod, r_mod, abT, r_small, l, jb)
            for l in range(2):
                for i, (sci, shi, gi) in enumerate(((8, 0, 0), (32, 24, 2))):
                    kb.op("dve", lambda e, l=l, i=i, sci=sci, gi=gi: e.scalar_tensor_tensor(
                        out=self.modv[:, l, 2 * i, :, :], in0=mod[:, l, sci:sci + 8, :], scalar=1.0,
                        in1=ngT[:, l, gi, :].unsqueeze(2).to_broadcast([128, 8, 2]), op0=ALU.add, op1=ALU.mult),
                        reads=[r_mod, r_small], writes=[self.r_modv])
                    kb.op("dve", lambda e, l=l, i=i, shi=shi: e.tensor_copy(
                        out=self.modv[:, l, 2 * i + 1, :, :], in_=mod[:, l, shi:shi + 8, :]),
                        reads=[r_mod], writes=[self.r_modv])
                for i, (gti, gpi) in enumerate(((16, 1), (40, 3))):
                    kb.op("dve", lambda e, l=l, i=i, gti=gti, gpi=gpi: e.tensor_tensor(
                        out=gv[:, l, i, :], in0=mod[:, l, gti:gti + 8, 0], in1=ngT[:, l, gpi, :], op=ALU.mult),
                        reads=[r_mod, r_small], writes=[r_gv])
            for l in range(2):
                for i in range(2):
                    kb.dma("pool", _r(self.vec[l, i], "(j p) -> p j", p=128), gv[:, l, i, :], reads=[r_gv],
                           allow_slow_non_contiguous=True)

    def _ada_mm(self, wb, rw, scc, r_cc, ps, rps, mod, r_mod, abT, r_small, l, jb):
        kb = self.kb
        for jj in range(6):
            for kc in range(8):
                kb.op("pe", lambda e, jj=jj, kc=kc: e.matmul(ps[:, jj * 2:jj * 2 + 2], lhsT=wb[:, kc, jj * 128:(jj + 1) * 128],
                                                         rhs=scc[:, kc, :], start=(kc == 0), stop=(kc == 7)),
                      reads=[rw[kc // 4], r_cc], writes=[rps], inc=(jj == 5 and kc == 7))
        kb.op("dve", lambda e: e.tensor_tensor(
            out=mod[:, l, jb * 6:(jb + 1) * 6, :], in0=_r(ps[:, 0:12], "p (j n) -> p j n", n=2),
            in1=abT[:, l, jb * 6:(jb + 1) * 6].unsqueeze(2).to_broadcast([128, 6, 2]), op=ALU.add),
            reads=[rps, r_small], writes=[r_mod])

    def alloc_norm_bufs(self, es, nsub, nh):
        nhp = max(nh, 2)
        self.r_nt = [self.kb.res() for _ in range(6)]
        return dict(junk=self.sb(es, "n_junk", [128, D], BF16), xn=self.sb(es, "n_xn", [128, nsub, D], BF16),
                    ss=self.sb(es, "n_ss", [128, 4], F32), rstd=self.sb(es, "n_rstd", [128, 4], F32),
                    xnh=self.sb(es, "n_xnh", [nhp, D], BF16), ssh=self.sb(es, "n_ssh", [nhp, 1], F32),
                    rstdh=self.sb(es, "n_rstdh", [nhp, 1], F32))

    def norm_T(self, nb, xt, r_xts, nsub, xh, r_xh, nh, halo, hT, r_hT, main0, hl0, hr0, l, mi, col, psT, r_psT, psH, r_psH):
        kb = self.kb
        junk, xn, ss, rstd, xnh, ssh, rstdh = (nb[k] for k in ("junk", "xn", "ss", "rstd", "xnh", "ssh", "rstdh"))
        rt = self.r_nt
        ntok = nsub * 128
        for s in range(nsub):
            kb.op("act", lambda e, s=s: e.activation(out=junk[:], in_=xt[:, s, :], func=AF.Square, accum_out=ss[:, s:s + 1]),
                  reads=[r_xts[s]], writes=[rt[0]])
        kb.op("act", lambda e: e.activation(out=rstd[:, 0:nsub], in_=ss[:, 0:nsub], func=AF.Sqrt, bias=self.epsb[:], scale=1.0 / D),
              reads=[rt[0], self.r_cst], writes=[rt[1]])
        kb.op("dve", lambda e: e.reciprocal(out=rstd[:, 0:nsub], in_=rstd[:, 0:nsub]), reads=[rt[1]], writes=[rt[1]])
        for s in range(nsub):
            kb.op("dve", lambda e, s=s: e.tensor_scalar(out=xn[:, s, :], in0=xt[:, s, :], scalar1=rstd[:, s:s + 1], scalar2=None, op0=ALU.mult),
                  reads=[r_xts[s], rt[1]], writes=[rt[2]])
        A = self.modv[:, l, mi, :, col]
        B = self.modv[:, l, mi + 1, :, col]
        for kc in range(8):
            p = psT[kc % 2]
            rp = r_psT[kc % 2]
            for s in range(nsub):
                kb.op("pe", lambda e, s=s, kc=kc, p=p: e.matmul(p[:, s * 128:(s + 1) * 128], lhsT=xn[:, s, kc * 128:(kc + 1) * 128],
                                                             rhs=self.identb, start=True, stop=True),
                      reads=[rt[2], self.r_cst], writes=[rp], inc=(s == nsub - 1))
            kb.op("act", lambda e, kc=kc, p=p: e.activation(out=hT[:, kc, main0:main0 + ntok], in_=p[:, 0:ntok], func=AF.Identity,
                                                          bias=B[:, kc:kc + 1], scale=A[:, kc:kc + 1]),
                  reads=[rp, self.r_modv], writes=[r_hT])
        if nh == 0:
            return
        lv, rv = halo
        hh = nh // 2
        if lv or rv:
            kb.op("act", lambda e: e.activation(out=junk[0:nh, :], in_=xh[0:nh, :], func=AF.Square, accum_out=ssh[0:nh, 0:1]),
                  reads=[r_xh], writes=[rt[3]])
            kb.op("act", lambda e: e.activation(out=rstdh[0:nh, :], in_=ssh[0:nh, :], func=AF.Sqrt, bias=self.epsb[0:nh, :], scale=1.0 / D),
                  reads=[rt[3], self.r_cst], writes=[rt[4]])
            kb.op("dve", lambda e: e.reciprocal(out=rstdh[0:nh, :], in_=rstdh[0:nh, :]), reads=[rt[4]], writes=[rt[4]])
            kb.op("dve", lambda e: e.tensor_scalar(out=xnh[0:nh, :], in0=xh[0:nh, :], scalar1=rstdh[0:nh, 0:1], scalar2=None, op0=ALU.mult),
                  reads=[r_xh, rt[4]], writes=[rt[5]])
            for kc in range(8):
                kb.op("pe", lambda e, kc=kc: e.matmul(psH[:, kc * nh:(kc + 1) * nh], lhsT=xnh[0:nh, kc * 128:(kc + 1) * 128],
                                                   rhs=self.cstb[0:nh, 0, 0:nh], start=True, stop=True),
                      reads=[rt[5], self.r_cst], writes=[r_psH], inc=(kc == 7))
            for kc in range(8):
                for side, c0 in ((0, hl0), (1, hr0)):
                    kb.op("dve", lambda e, kc=kc, side=side, c0=c0: e.tensor_scalar(
                        out=hT[:, kc, c0:c0 + hh], in0=psH[:, kc * nh + side * hh:kc * nh + (side + 1) * hh],
                        scalar1=A[:, kc:kc + 1], scalar2=B[:, kc:kc + 1], op0=ALU.mult, op1=ALU.add),
                        reads=[r_psH, self.r_modv], writes=[r_hT])
        if not lv:
            kb.op("dve", lambda e: e.memset(hT[:, :, hl0:hl0 + hh], 0.0), writes=[r_hT])
        if not rv:
            kb.op("dve", lambda e: e.memset(hT[:, :, hr0:hr0 + hh], 0.0), writes=[r_hT])

    def load_rows_bcast(self, es, name, src_row, n):
        t = self.sb(es, name, [128, n], F32)
        r = self.kb.res(name)
        self.kb.dma("sp", t[:], src_row.partition_broadcast(128), writes=[r])
        return t, r

    def resid_epilogue(self, y, r_y, xt_s, r_xt, Grow, r_G, tmp, r_tmp, ss2, r_ss2, junk, r_junk):
        kb = self.kb
        for hb in range(2):
            kb.op("act", lambda e, hb=hb: e.activation(out=junk[:, 0:512], in_=y[:, hb * 512:(hb + 1) * 512], func=AF.Square,
                                                     accum_out=ss2[:, hb:hb + 1]), reads=r_y, writes=[r_ss2[0], r_junk])
        kb.op("dve", lambda e: e.tensor_tensor(out=ss2[:, 2:3], in0=ss2[:, 0:1], in1=ss2[:, 1:2], op=ALU.add), reads=[r_ss2[0]], writes=[r_ss2[1]])
        kb.op("act", lambda e: e.activation(out=ss2[:, 3:4], in_=ss2[:, 2:3], func=AF.Sqrt, bias=self.epsb[:], scale=1.0 / D),
              reads=[r_ss2[1], self.r_cst], writes=[r_ss2[2]])
        kb.op("dve", lambda e: e.reciprocal(out=ss2[:, 3:4], in_=ss2[:, 3:4]), reads=[r_ss2[2]], writes=[r_ss2[2]])
        kb.op("dve", lambda e: e.scalar_tensor_tensor(out=tmp[:], in0=y, scalar=ss2[:, 3:4], in1=Grow[:], op0=ALU.mult, op1=ALU.mult),
              reads=r_y + [r_ss2[2], r_G], writes=[r_tmp])
        kb.op("pool", lambda e: e.tensor_tensor(out=xt_s, in0=xt_s, in1=tmp[:], op=ALU.add), reads=[r_tmp, r_xt], writes=[r_xt])

    def phase_ffn(self, l, src, dst):
        nc, kb = self.nc, self.kb
        T = 256
        NT = S // T
        with ExitStack() as es:
            Wup = self.sb(es, "Wup", [128, 8, 2 * FH], BF16)
            Wdn = self.sb(es, "Wdn", [128, 22, D], BF16)
            r_wup = [kb.res() for _ in range(8)]
            r_wdn = [kb.res() for _ in range(2)]
            for kc in range(8):
                kb.dma("pool", Wup[:, kc, :], self.I("ffn_w_up")[l, kc * 128:(kc + 1) * 128, :], writes=[r_wup[kc]])
            for hh in range(2):
                kb.dma("pool", Wdn[:, hh * 11:(hh + 1) * 11, :], _r(self.I("ffn_w_down")[l, hh * 1408:(hh + 1) * 1408, :], "(j p) d -> p j d", p=128),
                       writes=[r_wdn[hh]])
            cw = self.sb(es, "f_cw", [128, 22, 3], F32)
            cb = self.sb(es, "f_cb", [128, 22], F32)
            r_cw = [kb.res(), kb.res()]
            kb.dma("sp", cw[:], self.I("fconv_wT")[:, l], writes=[r_cw[0]])
            kb.dma("sp", cb[:], self.I("fconv_bT")[:, l], writes=[r_cw[1]])
            Grow, r_G = self.load_rows_bcast(es, "f_G", self.vec[l, 1], D)
            nb = self.alloc_norm_bufs(es, 2, 2)
            xt = [self.sb(es, "f_xt%d" % i, [128, 2, D], F32) for i in range(2)]
            r_xt = [[kb.res() for _ in range(2)] for _ in range(2)]
            xh = [self.sb(es, "f_xh%d" % i, [2, D], F32) for i in range(2)]
            r_xh = [kb.res() for _ in range(2)]
            hT = [self.sb(es, "f_hT%d" % i, [128, 8, T + 2], BF16) for i in range(2)]
            r_hT = [kb.res() for _ in range(2)]
            ub = [self.sb(es, "f_ub%d" % i, [128, T + 2], F32) for i in range(2)]
            acc = [self.sb(es, "f_acc%d" % i, [128, T], F32) for i in range(2)]
            gl = [self.sb(es, "f_gl%d" % i, [128, T], F32) for i in range(2)]
            r_ub = [kb.res() for _ in range(2)]
            r_acc = [kb.res() for _ in range(2)]
            r_gl = [kb.res() for _ in range(2)]
            gT = self.sb(es, "f_gT", [128, 22, T], BF16)
            r_gT = [kb.res() for _ in range(22)]
            ss2 = [self.sb(es, "f_ss2%d" % i, [128, 4], F32) for i in range(2)]
            r_ss2 = [[kb.res() for _ in range(3)] for _ in range(2)]
            junk2 = self.sb(es, "f_junk2", [128, 512], BF16)
            r_junk2 = kb.res()
            psT = [self.psB[2], self.psB[3]]
            r_psT = [self.r_psB[2], self.r_psB[3]]
            psH = self.psB[3][:, 256:272]
            r_psH = self.r_psB[3]
            bankU = [self.psB[0], self.psA[1][:, 0:512]]
            bankV = [self.psB[1], self.psA[1][:, 512:1024]]
            psU = [bk[:, 0:T] for bk in bankU]
            psUh = [bk[:, T:T + 2] for bk in bankU]
            psV = [bk[:, 0:T] for bk in bankV]
            r_psU = [self.r_psB[0], kb.res()]
            r_psV = [self.r_psB[1], kb.res()]
            r_psUh = r_psU
            for b in range(2):
                kb.op("dve", lambda e, b=b: e.memset(xh[b][:], 0.0), writes=[r_xh[b]])

            def load(t):
                b = t % 2
                t0 = t * T
                for s in range(2):
                    kb.dma("sp", xt[b][:, s, :], src[t0 + s * 128:t0 + (s + 1) * 128, :], writes=[r_xt[b][s]])
                if 0 < t < NT - 1:
                    kb.dma("sp", xh[b][0:2, :], src[t0 - 1:t0 + T + 1:T + 1, :], writes=[r_xh[b]])
                elif t > 0:
                    kb.dma("sp", xh[b][0:1, :], src[t0 - 1:t0, :], writes=[r_xh[b]])
                else:
                    kb.dma("sp", xh[b][1:2, :], src[t0 + T:t0 + T + 1, :], writes=[r_xh[b]])

            ysb = [self.sb(es, "f_ysb%d" % i, [128, D], F32) for i in range(2)]
            r_ysb = [kb.res() for _ in range(2)]
            NTR = NT if self.max_tiles is None else self.max_tiles

            def do_norm(t):
                b = t % 2
                self.norm_T(nb, xt[b], r_xt[b], 2, xh[b], r_xh[b], 2, (t > 0, t < NT - 1), hT[b], r_hT[b], 1, 0, T + 1,
                            l, 2, 0, psT, r_psT, psH, r_psH)

            load(0)
            do_norm(0)
            for t in range(NTR):
                b = t % 2
                t0 = t * T
                if t + 1 < NT:
                    load(t + 1)
                for j in range(23):
                    pb = j % 2
                    qb = (j - 1) % 2
                    if j < 22:
                        for (pp, rr, c0, n0, n1) in ((bankU[pb][:, 0:T + 2], r_psU[pb], j * 128, 0, T + 2),
                                                     (psV[pb], r_psV[pb], FH + j * 128, 1, T + 1)):
                            for kc in range(8):
                                kb.op("pe", lambda e, kc=kc, pp=pp, c0=c0, n0=n0, n1=n1: e.matmul(
                                    pp, lhsT=Wup[:, kc, c0:c0 + 128], rhs=hT[b][:, kc, n0:n1], start=(kc == 0), stop=(kc == 7)),
                                    reads=[r_wup[kc], r_hT[b]], writes=[rr], inc=(kc == 7))
                    if j >= 1:
                        kb.op("act", lambda e, qb=qb: e.activation(out=gl[qb][:], in_=acc[qb][:], func=AF.Gelu), reads=[r_acc[qb]], writes=[r_gl[qb]])
                    if j < 22:
                        a = acc[pb]
                        bu = bankU[pb]
                        kb.op("act", lambda e, a=a, bu=bu, j=j: e.activation(out=a[:], in_=bu[:, 1:T + 1], func=AF.Identity,
                                                                           bias=cb[:, j:j + 1], scale=cw[:, j, 1:2]),
                              reads=[r_psU[pb]] + r_cw, writes=[r_acc[pb]])
                    if j >= 1:
                        kb.op("dve", lambda e, j=j, qb=qb: e.tensor_tensor(out=gT[:, j - 1, :], in0=gl[qb][:], in1=psV[qb], op=ALU.mult),
                              reads=[r_gl[qb], r_psV[qb]], writes=[r_gT[j - 1]])
                    if j < 22:
                        kb.op("dve", lambda e, bu=bu, a=a, j=j: e.scalar_tensor_tensor(out=a[:], in0=bu[:, 0:T], scalar=cw[:, j, 0:1], in1=a[:],
                                                                                    op0=ALU.mult, op1=ALU.add), reads=[r_psU[pb], r_acc[pb]] + r_cw, writes=[r_acc[pb]])
                        kb.op("dve", lambda e, bu=bu, a=a, j=j: e.scalar_tensor_tensor(out=a[:], in0=bu[:, 2:T + 2], scalar=cw[:, j, 2:3], in1=a[:],
                                                                                    op0=ALU.mult, op1=ALU.add), reads=[r_psU[pb], r_acc[pb]] + r_cw, writes=[r_acc[pb]])
                    if j == 8 and t + 1 < NTR:
                        do_norm(t + 1)
                for s in range(2):
                    py = self.psA[0]
                    rpy = self.r_psA[0]
                    for hb in range(2):
                        for j in range(22):
                            kb.op("pe", lambda e, j=j, s=s, hb=hb: e.matmul(py[:, hb * 512:(hb + 1) * 512], lhsT=gT[:, j, s * 128:(s + 1) * 128],
                                                                          rhs=Wdn[:, j, hb * 512:(hb + 1) * 512], start=(j == 0), stop=(j == 21)),
                                  reads=[r_gT[j], r_wdn[j // 11]], writes=[rpy], inc=(j == 21))
                    kb.op("act", lambda e, s=s: e.activation(out=ysb[s][:], in_=py[:, :], func=AF.Identity), reads=[rpy], writes=[r_ysb[s]])
                    self.resid_epilogue(ysb[s][:], [r_ysb[s]], xt[b][:, s, :], r_xt[b][s], Grow, r_G, ysb[s], r_ysb[s], ss2[s], r_ss2[s], junk2, r_junk2)
                    kb.dma("pool", dst[t0 + s * 128:t0 + (s + 1) * 128, :], xt[b][:, s, :], reads=[r_xt[b][s]])

    def phase_pool(self, src, dst):
        nc, kb = self.nc, self.kb
        T = 256
        NT = S // T
        HH = 8
        W = T + 2 * HH
        l = 1
        with ExitStack() as es:
            PW = self.sb(es, "p_PW", [128, 4, 2, 256], BF16)
            r_pw = kb.res()
            kb.dma("pool", PW[:], _r(self.I("pool_w"), "g (k p) n -> p g k n", p=128), writes=[r_pw])
            pbrow, r_pb = self.load_rows_bcast(es, "p_pb", self.I("pool_b"), D)
            psrow, r_psr = self.load_rows_bcast(es, "p_ps", self.I("pool_scale"), D)
            Grow, r_G = self.load_rows_bcast(es, "p_G", self.vec[l, 0], D)
            inv = self.sb(es, "p_inv", [128, 2, 4, 256], F32)
            r_inv = kb.res()
            kb.dma("sp", inv[:], _r(self.I("pool_inv"), "a g t -> (a g t)").partition_broadcast(128), writes=[r_inv])
            nb = self.alloc_norm_bufs(es, 2, 2 * HH)
            xt = [self.sb(es, "p_xt%d" % i, [128, 2, D], F32) for i in range(2)]
            r_xt = [[kb.res() for _ in range(2)] for _ in range(2)]
            xh = [self.sb(es, "p_xh%d" % i, [2 * HH, D], F32) for i in range(2)]
            r_xh = [kb.res() for _ in range(2)]
            hTs = [self.sb(es, "p_hT%d" % i, [128, 8, W], F32) for i in range(2)]
            r_hTs = [kb.res() for _ in range(2)]
            bufA = self.sb(es, "p_bA", [128, 2, W], F32)
            bufB = self.sb(es, "p_bB", [128, 2, W], F32)
            r_bA, r_bB = kb.res(), kb.res()
            pl = self.sb(es, "p_pl", [128, 8, T], BF16)
            r_pl = [kb.res() for _ in range(4)]
            ysb = [self.sb(es, "p_ysb%d" % i, [128, D], F32) for i in range(2)]
            r_ysb = [kb.res() for _ in range(2)]
            tmp = [self.sb(es, "p_tmp%d" % i, [128, D], F32) for i in range(2)]
            r_tmp = [kb.res() for _ in range(2)]
            ss2 = [self.sb(es, "p_ss2%d" % i, [128, 4], F32) for i in range(2)]
            r_ss2 = [[kb.res() for _ in range(3)] for _ in range(2)]
            junk2 = self.sb(es, "p_junk2", [128, 512], BF16)
            r_junk2 = kb.res()
            psT = [self.psB[2], self.psB[3]]
            r_psT = [self.r_psB[2], self.r_psB[3]]
            psH = self.psB[1][:, 0:128]
            r_psH = self.r_psB[1]
            for b in range(2):
                kb.op("dve", lambda e, b=b: e.memset(xh[b][:], 0.0), writes=[r_xh[b], self._rxh2(b)])

            def load(t):
                b = t % 2
                t0 = t * T
                for s in range(2):
                    kb.dma("sp", xt[b][:, s, :], src[t0 + s * 128:t0 + (s + 1) * 128, :], writes=[r_xt[b][s]])
                if 0 < t < NT - 1:
                    kb.dma("sp", xh[b][0:HH, :], src[t0 - HH:t0, :], writes=[r_xh[b]])
                    kb.dma("sp", xh[b][HH:2 * HH, :], src[t0 + T:t0 + T + HH, :], writes=[self._rxh2(b)])
                elif t > 0:
                    kb.dma("sp", xh[b][0:HH, :], src[t0 - HH:t0, :], writes=[r_xh[b]])
                else:
                    kb.dma("sp", xh[b][HH:2 * HH, :], src[t0 + T:t0 + T + HH, :], writes=[self._rxh2(b)])

            def do_norm(t):
                b = t % 2
                rxh = Res()
                kb._merge(rxh.w, r_xh[b].w)
                kb._merge(rxh.w, self._rxh2(b).w)
                self.norm_T(nb, xt[b], r_xt[b], 2, xh[b], rxh, 2 * HH, (t > 0, t < NT - 1), hTs[b], r_hTs[b], HH, 0, HH + T,
                            l, 0, 0, psT, r_psT, psH, r_psH)
                kb._merge(r_xh[b].r, rxh.r)
                kb._merge(self._rxh2(b).r, rxh.r)

            load(0)
            do_norm(0)
            for t in range(NT):
                b = t % 2
                t0 = t * T
                hT, r_hT = hTs[b], r_hTs[b]
                if t + 1 < NT:
                    load(t + 1)
                for g in range(4):
                    hg = hT[:, 2 * g:2 * g + 2, :]
                    kb.op("dve", lambda e, hg=hg: e.tensor_tensor(out=bufA[:, :, 1:W], in0=hg[:, :, 0:W - 1], in1=hg[:, :, 1:W], op=ALU.add),
                          reads=[r_hT], writes=[r_bA])
                    cur, rcur, oth, roth = bufA, r_bA, bufB, r_bB
                    lo, hi = 1, W
                    for lvl in range(1, g + 1):
                        sh = 1 << (lvl - 1)
                        nlo, nhi = lo + sh, hi - sh
                        kb.op("dve", lambda e, cur=cur, oth=oth, sh=sh, nlo=nlo, nhi=nhi: e.tensor_tensor(
                            out=oth[:, :, nlo:nhi], in0=cur[:, :, nlo - sh:nhi - sh], in1=cur[:, :, nlo + sh:nhi + sh], op=ALU.add),
                            reads=[rcur], writes=[roth])
                        cur, rcur, oth, roth = oth, roth, cur, rcur
                        lo, hi = nlo, nhi
                    w = 2 << g
                    if 0 < t < NT - 1:
                        kb.op("dve", lambda e, cur=cur, hg=hg, g=g, w=w: e.scalar_tensor_tensor(
                            out=pl[:, 2 * g:2 * g + 2, :], in0=cur[:, :, HH:HH + T], scalar=1.0 / w, in1=hg[:, :, HH:HH + T],
                            op0=ALU.mult, op1=ALU.subtract), reads=[rcur, r_hT], writes=[r_pl[g]])
                    else:
                        a = 0 if t == 0 else 1
                        kb.op("dve", lambda e, cur=cur, oth=oth, g=g, a=a: e.tensor_tensor(
                            out=oth[:, :, HH:HH + T], in0=cur[:, :, HH:HH + T], in1=inv[:, a, g, :].unsqueeze(1).to_broadcast([128, 2, T]),
                            op=ALU.mult), reads=[rcur, r_inv], writes=[roth])
                        kb.op("dve", lambda e, oth=oth, hg=hg, g=g: e.tensor_tensor(
                            out=pl[:, 2 * g:2 * g + 2, :], in0=oth[:, :, HH:HH + T], in1=hg[:, :, HH:HH + T], op=ALU.subtract),
                            reads=[roth, r_hT], writes=[r_pl[g]])
                if t + 1 < NT:
                    do_norm(t + 1)
                for s in range(2):
                    py = self.psA[s]
                    rpy = self.r_psA[s]
                    for g in range(4):
                        for kk in range(2):
                            kb.op("pe", lambda e, g=g, kk=kk, s=s, py=py: e.matmul(py[:, g * 256:(g + 1) * 256], lhsT=pl[:, 2 * g + kk, s * 128:(s + 1) * 128],
                                                                                 rhs=PW[:, g, kk, :], start=(kk == 0), stop=(kk == 1)),
                                  reads=[r_pl[g], r_pw], writes=[rpy], inc=(g == 3 and kk == 1))
                    kb.op("dve", lambda e, s=s, py=py: e.tensor_tensor(out=ysb[s][:], in0=py[:, :], in1=pbrow[:], op=ALU.add),
                          reads=[rpy, r_pb], writes=[r_ysb[s]])
                    kb.op("pool", lambda e, s=s: e.tensor_tensor(out=ysb[s][:], in0=ysb[s][:], in1=psrow[:], op=ALU.mult),
                          reads=[r_ysb[s], r_psr], writes=[r_ysb[s]])
                    self.resid_epilogue(ysb[s][:], [r_ysb[s]], xt[b][:, s, :], r_xt[b][s], Grow, r_G, ysb[s], r_ysb[s], ss2[s], r_ss2[s], junk2, r_junk2)
                    kb.dma("pool", dst[t0 + s * 128:t0 + (s + 1) * 128, :], xt[b][:, s, :], reads=[r_xt[b][s]])

    def _rxh2(self, b):
        if not hasattr(self, "_rxh2_l"):
            self._rxh2_l = [self.kb.res(), self.kb.res()]
        return self._rxh2_l[b]

    def phase_l0_mixer(self, x_in, ctx_in, dst):
        kb = self.kb
        ST = S + LC
        self.qT = self.dscr("qT", [512, S], BF16)
        self.kT = self.dscr("kT", [512, ST], BF16)
        self.v_tok = self.dscr("v_tok", [ST, 512], BF16)
        self.sz_tok = self.dscr("sz_tok", [S, D], BF16)
        self.xs_tok = self.dscr("xs_tok", [ST, D], BF16)
        self.B_tok = self.dscr("B_tok", [ST, 512], BF16)
        self.BT = self.dscr("BT", [512, ST], BF16)
        self.CT = self.dscr("CT", [512, ST], BF16)
        self.dt_tok = self.dscr("dt_tok", [ST, 32], F32)
        self.yf = self.dscr("yf", [S, D], F32)
        self.ssm_tok = self.dscr("ssm_tok", [S, D], BF16)
        self.att_tok = self.dscr("att_tok", [S, 512], BF16)
        self.l0_proj(x_in, ctx_in)
        kb.barrier()
        if self.stop_after == "proj":
            return
        self.l0_ssd(0)
        kb.barrier()
        self.l0_ssd(1)
        kb.barrier()
        if self.stop_after == "ssd":
            return
        self.l0_att()
        kb.barrier()
        if self.stop_after == "att":
            return
        self.l0_out(x_in, dst)

    def l0_proj(self, x_in, ctx_in):
        kb = self.kb
        T = 256
        NT = S // T
        w_in = self.I("w_in")
        with ExitStack() as es:
            W = self.sb(es, "Win", [128, 8, 4640], BF16)
            r_w = [kb.res() for _ in range(8)]
            for kc in range(8):
                kb.dma("pool", W[:, kc, :], w_in[kc * 128:(kc + 1) * 128, :], writes=[r_w[kc]])
            cw = self.sb(es, "c_cw", [128, 16, 3], F32)
            cb = self.sb(es, "c_cb", [128, 16], F32)
            r_cw = [kb.res(), kb.res()]
            kb.dma("sp", cw[:], self.I("conv_wT"), writes=[r_cw[0]])
            kb.dma("sp", cb[:], self.I("conv_bT"), writes=[r_cw[1]])
            nb = self.alloc_norm_bufs(es, 2, 2)
            xt = [self.sb(es, "j_xt%d" % i, [128, 2, D], F32) for i in range(2)]
            r_xt = [[kb.res() for _ in range(2)] for _ in range(2)]
            xh = [self.sb(es, "j_xh%d" % i, [2, D], F32) for i in range(2)]
            r_xh = [kb.res() for _ in range(2)]
            hTs = [self.sb(es, "j_hT%d" % i, [128, 8, T + 2], BF16) for i in range(2)]
            r_hTs = [kb.res() for _ in range(2)]
            ub = [self.sb(es, "j_ub%d" % i, [128, T + 2], F32) for i in range(2)]
            acc = [self.sb(es, "j_acc%d" % i, [128, T], F32) for i in range(2)]
            sx = [self.sb(es, "j_sx%d" % i, [128, T], BF16) for i in range(2)]
            r_ub = [kb.res() for _ in range(2)]
            r_acc = [kb.res() for _ in range(2)]
            r_sx = [kb.res() for _ in range(2)]
            fst = [self.sb(es, "j_fst%d" % i, [128, T], BF16) for i in range(2)]
            r_fst = [kb.res() for _ in range(2)]
            xs_st = self.sb(es, "j_xsst", [128, 2, D], BF16)
            b_st = self.sb(es, "j_bst", [128, 2, 512], BF16)
            sz_st = self.sb(es, "j_szst", [128, 2, D], BF16)
            v_st = self.sb(es, "j_vst", [128, 2, 512], BF16)
            dt_st = self.sb(es, "j_dtst", [128, 2, 32], F32)
            r_xsst, r_bst, r_szst, r_vst, r_dtst = (kb.res() for _ in range(5))
            psT = [self.psB[2], self.psB[3]]
            r_psT = [self.r_psB[2], self.r_psB[3]]
            psH = self.psB[3][:, 256:272]
            r_psH = self.r_psB[3]
            bankF = [self.psB[0], self.psB[1]]
            r_bankF = [self.r_psB[0], self.r_psB[1]]
            bankX = [self.psA[1][:, 0:512], self.psA[1][:, 512:1024]]
            r_bankX = [kb.res(), kb.res()]
            bankM = [self.psA[0][:, 0:512], self.psA[0][:, 512:1024]]
            r_bankM = [kb.res(), kb.res()]
            for b in range(2):
                kb.op("dve", lambda e, b=b: e.memset(xh[b][:], 0.0), writes=[r_xh[b]])

            def srcrows(t):
                return (x_in, t * T) if t < NT else (ctx_in, 0)

            def load(t, b):
                src, t0 = srcrows(t)
                for s in range(2):
                    kb.dma("sp", xt[b][:, s, :], src[t0 + s * 128:t0 + (s + 1) * 128, :], writes=[r_xt[b][s]])
                if t >= NT:
                    return
                if 0 < t < NT - 1:
                    kb.dma("sp", xh[b][0:2, :], src[t0 - 1:t0 + T + 1:T + 1, :], writes=[r_xh[b]])
                elif t > 0:
                    kb.dma("sp", xh[b][0:1, :], src[t0 - 1:t0, :], writes=[r_xh[b]])
                else:
                    kb.dma("sp", xh[b][1:2, :], src[t0 + T:t0 + T + 1, :], writes=[r_xh[b]])

            tiles = list(range(NT + 1))
            if self.max_tiles is not None:
                tiles = list(range(self.max_tiles)) + [NT]

            def do_norm(ti):
                t = tiles[ti]
                b = ti % 2
                isctx = t == NT
                halo = (False, False) if isctx else (t > 0, t < NT - 1)
                self.norm_T(nb, xt[b], r_xt[b], 2, xh[b], r_xh[b], 2, halo, hTs[b], r_hTs[b], 1, 0, T + 1,
                            0, 0, 1 if isctx else 0, psT, r_psT, psH, r_psH)

            load(tiles[0], 0)
            do_norm(0)
            fi = 0
            for ti, t in enumerate(tiles):
                b = ti % 2
                hT, r_hT = hTs[b], r_hTs[b]
                isctx = t == NT
                g0 = S if isctx else t * T
                if ti + 1 < len(tiles):
                    load(tiles[ti + 1], (ti + 1) % 2)
                chunks = []
                if not isctx:
                    chunks += [("q", c, c * 128) for c in range(4)]
                chunks += [("k", c, 1536 + c * 128) for c in range(4)]
                for kind, c, col in chunks:
                    pb = fi % 2
                    fi += 1
                    bk, rbk = bankF[pb], r_bankF[pb]
                    for kc in range(8):
                        kb.op("pe", lambda e, kc=kc, bk=bk, col=col: e.matmul(
                            bk[:, 0:T], lhsT=W[:, kc, col:col + 128], rhs=hT[:, kc, 1:T + 1], start=(kc == 0), stop=(kc == 7)),
                            reads=[r_w[kc], r_hT], writes=[rbk], inc=(kc == 7))
                    st, rst = fst[pb], r_fst[pb]
                    kb.op("act", lambda e, st=st, bk=bk, kind=kind: e.activation(out=st[:], in_=bk[:, 0:T], func=AF.Identity,
                                                                               scale=(0.125 if kind == "q" else 1.0)),
                          reads=[rbk], writes=[rst])
                    dstT = self.qT if kind == "q" else self.kT
                    kb.dma("pool", dstT[c * 128:(c + 1) * 128, g0:g0 + T], st[:], reads=[rst])
                fbase = fi
                fi += 16
                for i in range(18):
                    if i < 16:
                        pb = (fbase + i) % 2
                        bk, rbk = bankF[pb], r_bankF[pb]
                        col = 2560 + i * 128
                        for kc in range(8):
                            kb.op("pe", lambda e, kc=kc, bk=bk, col=col: e.matmul(
                                bk[:, 0:T + 2], lhsT=W[:, kc, col:col + 128], rhs=hT[:, kc, 0:T + 2], start=(kc == 0), stop=(kc == 7)),
                                reads=[r_w[kc], r_hT], writes=[rbk], inc=(kc == 7))
                    if 1 <= i <= 16:
                        c = i - 1
                        cb_ = c % 2
                        kb.op("act", lambda e, cb_=cb_: e.activation(out=sx[cb_][:], in_=acc[cb_][:], func=AF.Silu), reads=[r_acc[cb_]], writes=[r_sx[cb_]])
                        if c >= 8:
                            dstT = self.BT if c < 12 else self.CT
                            cc = (c - 8) % 4
                            kb.dma("pool", dstT[cc * 128:(cc + 1) * 128, g0:g0 + T], sx[cb_][:], reads=[r_sx[cb_]])
                    if i < 16:
                        ib = i % 2
                        kb.op("act", lambda e, ib=ib, bk=bk, i=i: e.activation(out=acc[ib][:], in_=bk[:, 1:T + 1], func=AF.Identity,
                                                                            bias=cb[:, i:i + 1], scale=cw[:, i, 1:2]),
                              reads=[rbk] + r_cw, writes=[r_acc[ib]])
                    if 1 <= i <= 12:
                        c = i - 1
                        cb_ = c % 2
                        for s in range(2):
                            kb.op("pe", lambda e, s=s, cb_=cb_: e.matmul(bankX[cb_][:, s * 128:(s + 1) * 128], lhsT=sx[cb_][:, s * 128:(s + 1) * 128],
                                                                       rhs=self.identb, start=True, stop=True),
                                  reads=[r_sx[cb_], self.r_cst], writes=[r_bankX[cb_]], inc=(s == 1))
                    if 2 <= i <= 13:
                        c = i - 2
                        cb_ = c % 2
                        if c < 8:
                            kb.op("dve", lambda e, c=c, cb_=cb_: e.tensor_copy(out=xs_st[:, :, c * 128:(c + 1) * 128],
                                                                              in_=_r(bankX[cb_][:, 0:256], "p (s f) -> p s f", s=2)),
                                  reads=[r_bankX[cb_]], writes=[r_xsst])
                        else:
                            kb.op("dve", lambda e, c=c, cb_=cb_: e.tensor_copy(out=b_st[:, :, (c - 8) * 128:(c - 7) * 128],
                                                                              in_=_r(bankX[cb_][:, 0:256], "p (s f) -> p s f", s=2)),
                                  reads=[r_bankX[cb_]], writes=[r_bst])
                    if i < 16:
                        a, c = acc[ib], i
                        kb.op("dve", lambda e, bk=bk, a=a, c=c: e.scalar_tensor_tensor(out=a[:], in0=bk[:, 0:T], scalar=cw[:, c, 0:1], in1=a[:],
                                                                                    op0=ALU.mult, op1=ALU.add), reads=[rbk, r_acc[ib]] + r_cw, writes=[r_acc[ib]])
                        kb.op("dve", lambda e, bk=bk, a=a, c=c: e.scalar_tensor_tensor(out=a[:], in0=bk[:, 2:T + 2], scalar=cw[:, c, 2:3], in1=a[:],
                                                                                    op0=ALU.mult, op1=ALU.add), reads=[rbk, r_acc[ib]] + r_cw, writes=[r_acc[ib]])
                    if i == 8 and ti + 1 < len(tiles):
                        do_norm(ti + 1)
                kb.dma("pool", _r(self.xs_tok[g0:g0 + T, :], "(s p) d -> p s d", p=128), xs_st[:], reads=[r_xsst])
                kb.dma("pool", _r(self.B_tok[g0:g0 + T, :], "(s p) d -> p s d", p=128), b_st[:], reads=[r_bst])
                mi = 0
                for s in range(2):
                    tm = [("v", 2048, 512), ("d", 4608, 32)]
                    if not isctx:
                        tm = [("z", 512, 512), ("z", 1024, 512)] + tm
                    for kind, col, n in tm:
                        bk, rbk = bankM[mi % 2], r_bankM[mi % 2]
                        mi += 1
                        for kc in range(8):
                            kb.op("pe", lambda e, kc=kc, bk=bk, col=col, n=n, s=s: e.matmul(
                                bk[:, 0:n], lhsT=hT[:, kc, 1 + s * 128:1 + (s + 1) * 128], rhs=W[:, kc, col:col + n], start=(kc == 0), stop=(kc == 7)),
                                reads=[r_w[kc], r_hT], writes=[rbk], inc=(kc == 7))
                        if kind == "z":
                            kb.op("act", lambda e, bk=bk, col=col, s=s: e.activation(out=sz_st[:, s, col - 512:col], in_=bk[:, 0:512], func=AF.Silu),
                                  reads=[rbk], writes=[r_szst])
                        elif kind == "v":
                            kb.op("act", lambda e, bk=bk, s=s: e.activation(out=v_st[:, s, :], in_=bk[:, 0:512], func=AF.Identity),
                                  reads=[rbk], writes=[r_vst])
                        else:
                            kb.op("dve", lambda e, bk=bk, s=s: e.tensor_copy(out=dt_st[:, s, :], in_=bk[:, 0:32]), reads=[rbk], writes=[r_dtst])
                if not isctx:
                    kb.dma("pool", _r(self.sz_tok[g0:g0 + T, :], "(s p) d -> p s d", p=128), sz_st[:], reads=[r_szst])
                kb.dma("pool", _r(self.v_tok[g0:g0 + T, :], "(s p) d -> p s d", p=128), v_st[:], reads=[r_vst])
                kb.dma("pool", _r(self.dt_tok[g0:g0 + T, :], "(s p) d -> p s d", p=128), dt_st[:], reads=[r_dtst])

    def l0_ssd(self, d):
        kb = self.kb
        NCH = S // 128
        with ExitStack() as es:
            rows = self.sb(es, "d_rows", [128, 3, 32], F32)
            r_rows = kb.res()
            kb.dma("sp", rows[:], _r(self.I("ssm_rows"), "a b -> (a b)").partition_broadcast(128), writes=[r_rows])
            negA = self.sb(es, "d_negA", [128, 32], F32)
            dsum = self.sb(es, "d_dsum", [128, 16], F32)
            r_negA = kb.res()
            kb.op("act", lambda e: e.activation(out=negA[:], in_=rows[:, 0, :], func=AF.Exp), reads=[r_rows], writes=[r_negA])
            kb.op("dve", lambda e: e.tensor_scalar(out=negA[:], in0=negA[:], scalar1=-1.0, scalar2=None, op0=ALU.mult), reads=[r_negA], writes=[r_negA])
            kb.op("dve", lambda e: e.tensor_tensor(out=dsum[:], in0=rows[:, 2, 0:16], in1=rows[:, 2, 16:32], op=ALU.add), reads=[r_rows], writes=[r_negA])
            ngrow, r_ng = self.load_rows_bcast(es, "d_ng", self.I("ssm_norm_g"), D)
            xs = [self.sb(es, "d_xs%d" % i, [128, D], BF16) for i in range(2)]
            Bt = [self.sb(es, "d_Bt%d" % i, [128, 512], BF16) for i in range(2)]
            BTt = [self.sb(es, "d_BTt%d" % i, [128, 4, 128], BF16) for i in range(2)]
            CTt = [self.sb(es, "d_CTt%d" % i, [128, 4, 128], BF16) for i in range(2)]
            dtr = [self.sb(es, "d_dtr%d" % i, [128, 32], F32) for i in range(2)]
            yfl = [self.sb(es, "d_yfl%d" % i, [128, D], F32) for i in range(2)]
            szl = [self.sb(es, "d_szl%d" % i, [128, D], BF16) for i in range(2)]
            r_ld = [[kb.res() for _ in range(7)] for _ in range(2)]
            H = self.sb(es, "d_H", [128, D], F32)
            Hb = self.sb(es, "d_Hb", [128, D], BF16)
            r_H, r_Hb = kb.res(), kb.res()
            kb.op("dve", lambda e: e.memset(H[:], 0.0), writes=[r_H])
            kb.op("pool", lambda e: e.memset(Hb[:], 0.0), writes=[r_Hb])
            sms = [self.sb(es, "d_sm%d" % i, [128, 8, 16], F32) for i in range(2)]
            r_sms = [[kb.res() for _ in range(8)] for _ in range(2)]
            dws = [self.sb(es, "d_dw%d" % i, [128, 16], F32) for i in range(2)]
            r_dws = [kb.res() for _ in range(2)]
            ysbs = [self.sb(es, "d_ysb%d" % i, [128, D], F32) for i in range(2)]
            r_ysbs = [kb.res() for _ in range(2)]
            Lh = self.sb(es, "d_Lh", [128, 16, 128], F32)
            r_Lh = kb.res()
            seg = self.sb(es, "d_seg", [128, 4, 128], F32)
            r_seg = kb.res()
            cbm = self.sb(es, "d_cbm", [128, 4, 128], F32)
            r_cbm = kb.res()
            M = self.sb(es, "d_M", [128, 16, 128], BF16)
            r_M = [kb.res() for _ in range(4)]
            xdt = self.sb(es, "d_xdt", [128, D], BF16)
            xdtws = [self.sb(es, "d_xdtw%d" % i, [128, D], BF16) for i in range(2)]
            r_xdt = kb.res()
            r_xdtws = [kb.res() for _ in range(2)]
            yc = self.sb(es, "d_yc", [128, D], F32)
            yt = self.sb(es, "d_yt", [128, D], F32)
            r_yc, r_yt = kb.res(), kb.res()
            ss4 = self.sb(es, "d_ss4", [128, 8], F32)
            r_ss4 = [kb.res() for _ in range(3)]
            junk = self.sb(es, "d_junk", [128, 256], BF16)
            r_junk = kb.res()
            so = self.sb(es, "d_so", [128, D], BF16)
            r_so = kb.res()
            one1 = self.sb(es, "d_one", [128, 1], F32)
            kb.op("pool", lambda e: e.memset(one1[:], 1.0), writes=[r_negA])
            y_ps, r_yps = self.psA[0], [self.r_psA[0], kb.res()]
            yo_ps, r_yops = self.psA[1], [self.r_psA[1], kb.res()]
            D_ps, r_Dps = self.psB[0], self.r_psB[0]
            cb_ps, r_cbps = self.psB[1], self.r_psB[1]
            sm_ps, r_smps = self.psB[2], self.r_psB[2]
            S_ps, r_Sps = self.psB[3], self.r_psB[3]
            Rm = self.cst[:, 1 + d, :]
            Mk = self.cst[:, 3 + d, :]
            ones = self.cst[:, 5, :]
            lat = list(range(NCH)) if d == 0 else list(range(NCH - 1, -1, -1))
            if self.max_tiles is not None:
                lat = lat[:self.max_tiles]
            order = [("c", 0), ("c", 1)] if d == 0 else [("c", 1), ("c", 0)]
            order += [("l", c) for c in lat]

            def tok0(kc):
                return S + kc[1] * 128 if kc[0] == "c" else kc[1] * 128

            def load(i):
                b = i % 2
                kc = order[i]
                t0 = tok0(kc)
                kb.dma("sp", xs[b][:], self.xs_tok[t0:t0 + 128, :], writes=[r_ld[b][0]])
                kb.dma("sp", Bt[b][:], self.B_tok[t0:t0 + 128, :], writes=[r_ld[b][1]])
                if kc[0] == "l":
                    kb.dma("sp", BTt[b][:], _r(self.BT[:, t0:t0 + 128], "(g n) t -> n g t", n=128), writes=[r_ld[b][3]])
                    kb.dma("sp", CTt[b][:], _r(self.CT[:, t0:t0 + 128], "(g n) t -> n g t", n=128), writes=[r_ld[b][4]])
                    if d == 1:
                        kb.dma("sp", yfl[b][:], self.yf[t0:t0 + 128, :], writes=[r_ld[b][5]])
                        kb.dma("sp", szl[b][:], self.sz_tok[t0:t0 + 128, :], writes=[r_ld[b][6]])

            NCA = (S + LC) // 128
            dtall = self.sb(es, "d_dtall", [128, NCA, 32], F32)
            V = self.sb(es, "d_V", [128, 7, NCA, 16], F32)
            DEC = self.sb(es, "d_DEC", [128, NCA, 16], F32)
            r_V = kb.res()
            r_dtall = kb.res()
            kb.dma("sp", dtall[:], _r(self.dt_tok, "(c p) h -> p c h", p=128), writes=[r_dtall])
            hs = slice(d * 16, (d + 1) * 16)
            NV = NCA * 16
            fl = lambda k: _r(V[:, k], "p c h -> p (c h)")
            kb.op("dve", lambda e: e.tensor_tensor(out=V[:, 0], in0=dtall[:, :, hs], in1=rows[:, 1, hs].unsqueeze(1).to_broadcast([128, NCA, 16]), op=ALU.add),
                  reads=[r_dtall, r_rows], writes=[r_V])
            kb.op("act", lambda e: e.activation(out=fl(0), in_=fl(0), func=AF.Exp), reads=[r_V], writes=[r_V])
            kb.op("act", lambda e: e.activation(out=fl(1), in_=fl(0), func=AF.Ln, bias=one1[:]), reads=[r_V, r_negA], writes=[r_V])
            kb.op("dve", lambda e: e.tensor_tensor(out=V[:, 2], in0=V[:, 1], in1=negA[:, hs].unsqueeze(1).to_broadcast([128, NCA, 16]), op=ALU.mult),
                  reads=[r_V, r_negA], writes=[r_V])
            for (c0, c1) in ((0, 512), (512, NV)):
                kb.op("pe", lambda e, c0=c0, c1=c1: e.matmul(y_ps[:, c0:c1], lhsT=Rm, rhs=fl(2)[:, c0:c1], start=True, stop=True), reads=[r_V, self.r_cst], writes=r_yps, inc=False)
                kb.op("pe", lambda e, c0=c0, c1=c1: e.matmul(yo_ps[:, c0:c1], lhsT=ones, rhs=fl(2)[:, c0:c1], start=True, stop=True), reads=[r_V, self.r_cst], writes=r_yops, inc=(c0 == 512))
            kb.op("dve", lambda e: e.tensor_copy(out=fl(3), in_=y_ps[:, 0:NV]), reads=r_yps, writes=[r_V])
            kb.op("act", lambda e: e.activation(out=_r(DEC[:], "p c h -> p (c h)"), in_=yo_ps[:, 0:NV], func=AF.Exp), reads=r_yops, writes=[r_V])
            kb.op("dve", lambda e: e.tensor_tensor(out=fl(0), in0=yo_ps[:, 0:NV], in1=fl(3), op=ALU.subtract), reads=r_yops + [r_V], writes=[r_V])
            kb.op("act", lambda e: e.activation(out=fl(5), in_=fl(3), func=AF.Exp), reads=[r_V], writes=[r_V])
            kb.op("act", lambda e: e.activation(out=fl(4), in_=fl(0), func=AF.Exp), reads=[r_V], writes=[r_V])
            kb.op("dve", lambda e: e.tensor_tensor(out=fl(6), in0=fl(1), in1=fl(4), op=ALU.mult), reads=[r_V], writes=[r_V])

            def partA(i):
                b = i % 2
                kc = order[i]
                islat = kc[0] == "l"
                rl = r_ld[b]
                xdtw, r_xdtw = xdtws[b], r_xdtws[b]
                ci = tok0(kc) // 128
                a16, dt16, dw16 = V[:, 2, ci, :], V[:, 1, ci, :], V[:, 6, ci, :]
                xs3 = _r(xs[b][:], "p (h q) -> p h q", h=16)
                if islat:
                    kb.op("dve", lambda e: e.tensor_tensor(out=Lh[:], in0=Mk.unsqueeze(1).to_broadcast([128, 16, 128]),
                                                           in1=a16.unsqueeze(2).to_broadcast([128, 16, 128]), op=ALU.mult),
                          reads=[r_V, self.r_cst], writes=[r_Lh])
                    for g in range(4):
                        kb.op("pe", lambda e, g=g: e.matmul(cb_ps[:, g * 128:(g + 1) * 128], lhsT=BTt[b][:, g, :], rhs=CTt[b][:, g, :], start=True, stop=True),
                              reads=[rl[3], rl[4]], writes=[r_cbps], inc=(g == 3))
                    kb.op("pool", lambda e: e.tensor_tensor(out=_r(xdt[:], "p (h q) -> p h q", h=16), in0=xs3, in1=dt16.unsqueeze(2).to_broadcast([128, 16, 64]), op=ALU.mult),
                          reads=[rl[0], r_V], writes=[r_xdt])
                    kb.op("dve", lambda e: e.tensor_tensor(out=cbm[:], in0=_r(cb_ps[:, :], "p (g l) -> p g l", g=4),
                                                           in1=Rm.unsqueeze(1).to_broadcast([128, 4, 128]), op=ALU.mult),
                          reads=[r_cbps, self.r_cst], writes=[r_cbm])
                kb.op("pool", lambda e: e.tensor_tensor(out=_r(xdtw[:], "p (h q) -> p h q", h=16), in0=xs3, in1=dw16.unsqueeze(2).to_broadcast([128, 16, 64]), op=ALU.mult),
                      reads=[rl[0], r_V], writes=[r_xdtw])
                if islat:
                    for g in range(4):
                        for r4 in range(4):
                            h = g * 4 + r4
                            kb.op("pe", lambda e, h=h, r4=r4: e.matmul(D_ps[:, r4 * 128:(r4 + 1) * 128], lhsT=Lh[:, h, :], rhs=Rm, start=True, stop=True),
                                  reads=[r_Lh, self.r_cst], writes=[r_Dps], inc=(r4 == 3))
                        kb.op("act", lambda e: e.activation(out=_r(seg[:], "p r l -> p (r l)"), in_=D_ps[:, :], func=AF.Exp), reads=[r_Dps], writes=[r_seg])
                        kb.op("dve", lambda e, g=g: e.tensor_tensor(out=M[:, g * 4:(g + 1) * 4, :], in0=seg[:], in1=cbm[:, g, :].unsqueeze(1).to_broadcast([128, 4, 128]), op=ALU.mult),
                              reads=[r_seg, r_cbm], writes=[r_M[g]])
                    for h in range(16):
                        kb.op("pe", lambda e, h=h: e.matmul(y_ps[:, h * 64:(h + 1) * 64], lhsT=M[:, h, :], rhs=xdt[:, h * 64:(h + 1) * 64], start=True, stop=True),
                              reads=[r_M[h // 4], r_xdt], writes=r_yps, inc=(h == 15))
                    kb.op("act", lambda e: e.activation(out=ysbs[b][:], in_=y_ps[:, :], func=AF.Identity), reads=r_yps, writes=[r_ysbs[b]])

            S_banks = [self.psB[3], self.psB[2]]
            r_S_banks = [self.r_psB[3], self.r_psB[2]]
            def partB(i):
                b = i % 2
                kc = order[i]
                t0 = tok0(kc)
                islat = kc[0] == "l"
                rl = r_ld[b]
                xdtw, r_xdtw = xdtws[b], r_xdtws[b]
                ci = t0 // 128
                xs3 = _r(xs[b][:], "p (h q) -> p h q", h=16)
                if islat:
                    for g in range(4):
                        kb.op("pe", lambda e, g=g: e.matmul(yo_ps[:, g * 256:(g + 1) * 256], lhsT=CTt[b][:, g, :], rhs=Hb[:, g * 256:(g + 1) * 256], start=True, stop=True),
                              reads=[rl[4], r_Hb], writes=r_yops, inc=(g == 3))
                for g in range(4):
                    sp_, rsp_ = S_banks[g // 2], r_S_banks[g // 2]
                    kb.op("pe", lambda e, g=g, sp_=sp_: e.matmul(sp_[:, (g % 2) * 256:(g % 2 + 1) * 256], lhsT=Bt[b][:, g * 128:(g + 1) * 128], rhs=xdtw[:, g * 256:(g + 1) * 256], start=True, stop=True),
                          reads=[rl[1], r_xdtw], writes=[rsp_], inc=(g % 2 == 1))
                kb.op("pool", lambda e: e.tensor_tensor(out=_r(H[:], "p (h q) -> p h q", h=16), in0=_r(H[:], "p (h q) -> p h q", h=16),
                                                        in1=DEC[:, ci, :].unsqueeze(2).to_broadcast([128, 16, 64]), op=ALU.mult),
                      reads=[r_H, r_V], writes=[r_H])
                if islat:
                    kb.op("dve", lambda e: e.tensor_tensor(out=_r(yt[:], "p (h q) -> p h q", h=16), in0=_r(yo_ps[:, :], "p (h q) -> p h q", h=16),
                                                           in1=V[:, 5, ci, :].unsqueeze(2).to_broadcast([128, 16, 64]), op=ALU.mult),
                          reads=r_yops + [r_V], writes=[r_yt])
                for gp in range(2):
                    Hs = H[:, gp * 512:(gp + 1) * 512]
                    kb.op("dve", lambda e, Hs=Hs, gp=gp: e.tensor_tensor(out=Hs, in0=Hs, in1=S_banks[gp][:, :], op=ALU.add), reads=[r_H, r_S_banks[gp]], writes=[r_H])
                kb.op("act", lambda e: e.activation(out=Hb[:], in_=H[:], func=AF.Identity), reads=[r_H], writes=[r_Hb])
                if not islat:
                    return
                kb.op("pool", lambda e: e.tensor_tensor(out=yc[:], in0=yt[:], in1=ysbs[b][:], op=ALU.add), reads=[r_yt, r_ysbs[b]], writes=[r_yc])
                if d == 0:
                    kb.dma("pool", self.yf[t0:t0 + 128, :], yc[:], reads=[r_yc])
                    return
                kb.op("pool", lambda e: e.tensor_tensor(out=yc[:], in0=yc[:], in1=yfl[b][:], op=ALU.add), reads=[r_yc, rl[5]], writes=[r_yc])
                kb.op("dve", lambda e: e.tensor_tensor(out=_r(yt[:], "p (h q) -> p h q", h=16), in0=xs3, in1=dsum[:].unsqueeze(2).to_broadcast([128, 16, 64]), op=ALU.mult),
                      reads=[rl[0], r_negA], writes=[r_yt])
                kb.op("pool", lambda e: e.tensor_tensor(out=yc[:], in0=yc[:], in1=yt[:], op=ALU.add), reads=[r_yc, r_yt], writes=[r_yc])
                kb.op("dve", lambda e: e.tensor_tensor(out=yc[:], in0=yc[:], in1=szl[b][:], op=ALU.mult), reads=[r_yc, rl[6]], writes=[r_yc])
                for g in range(4):
                    kb.op("act", lambda e, g=g: e.activation(out=junk[:], in_=yc[:, g * 256:(g + 1) * 256], func=AF.Square, accum_out=ss4[:, g:g + 1]),
                          reads=[r_yc], writes=[r_ss4[0], r_junk])
                kb.op("act", lambda e: e.activation(out=ss4[:, 4:8], in_=ss4[:, 0:4], func=AF.Sqrt, bias=self.epsb[:], scale=1.0 / 256), reads=[r_ss4[0], self.r_cst], writes=[r_ss4[1]])
                kb.op("dve", lambda e: e.reciprocal(out=ss4[:, 4:8], in_=ss4[:, 4:8]), reads=[r_ss4[1]], writes=[r_ss4[1]])
                kb.op("dve", lambda e: e.tensor_tensor(out=_r(yt[:], "p (g q) -> p g q", g=4), in0=_r(yc[:], "p (g q) -> p g q", g=4),
                                                       in1=ss4[:, 4:8].unsqueeze(2).to_broadcast([128, 4, 256]), op=ALU.mult), reads=[r_yc, r_ss4[1]], writes=[r_yt])
                kb.op("pool", lambda e: e.tensor_tensor(out=so[:], in0=yt[:], in1=ngrow[:], op=ALU.mult), reads=[r_yt, r_ng], writes=[r_so])
                kb.dma("pool", self.ssm_tok[t0:t0 + 128, :], so[:], reads=[r_so])

            load(0)
            partA(0)
            for i in range(len(order)):
                if i + 1 < len(order):
                    load(i + 1)
                    partA(i + 1)
                partB(i)

    def l0_att(self):
        kb = self.kb
        ST = S + LC
        NTL = S // 128
        tab_in = self.I("rpbtab")
        with ExitStack() as es:
            qh = [self.sb(es, "a_q%d" % i, [64, S], BF16) for i in range(2)]
            kh = [self.sb(es, "a_k%d" % i, [64, ST], BF16) for i in range(2)]
            va = [self.sb(es, "a_v%d" % i, [128, 34, 65], BF16) for i in range(2)]
            tab = [self.sb(es, "a_tab%d" % i, [128, 5, 640], F32) for i in range(2)]
            r_hd = [[kb.res() for _ in range(4)] for _ in range(2)]
            sc = [self.sb(es, "a_sc%d" % i, [128, 640], F32) for i in range(2)]
            P = [self.sb(es, "a_P%d" % i, [128, 896], BF16) for i in range(2)]
            r_sc = [kb.res() for _ in range(2)]
            r_P = [kb.res() for _ in range(2)]
            rc = [self.sb(es, "a_rc%d" % i, [128, 1], F32) for i in range(2)]
            r_rc = [kb.res() for _ in range(2)]
            ast = [self.sb(es, "a_st%d" % i, [128, 32, 64], BF16) for i in range(2)]
            r_ast = [kb.res() for _ in range(2)]
            S_ps = [self.psA[0], self.psA[1]]
            r_Sps = [[self.r_psA[0], kb.res()], [self.r_psA[1], kb.res()]]
            o_ps = [self.psB[0], self.psB[1]]
            r_ops = [self.r_psB[0], self.r_psB[1]]
            for i in range(2):
                kb.op("dve", lambda e, i=i: e.memset(va[i][:, :, 64:65], 1.0), writes=[r_hd[i][2]])

            def loadh(h):
                b = h % 2
                kb.dma("sp", qh[b][:], self.qT[h * 64:(h + 1) * 64, :], writes=[r_hd[b][0]])
                kb.dma("sp", kh[b][:], self.kT[h * 64:(h + 1) * 64, :], writes=[r_hd[b][1]])
                for part in range(2):
                    kb.dma("sp", va[b][:, part * 17:(part + 1) * 17, 0:64],
                           _r(self.v_tok[part * 17 * 128:(part + 1) * 17 * 128, h * 64:(h + 1) * 64], "(j p) d -> p j d", p=128),
                           writes=[r_hd[b][2]] if part == 0 else [self._rva2(b)])
                kb.dma("sp", tab[b][:], _r(tab_in[h], "v p c q -> p v (c q)"), writes=[r_hd[b][3]])

            nh = 8
            loadh(0)
            it = 0
            for h in range(nh):
                b = h % 2
                if h + 1 < nh:
                    loadh(h + 1)
                rq, rk, rv, rt = r_hd[b]
                rv2 = self._rva2(b)
                tl = list(range(NTL) if self.max_tiles is None else range(self.max_tiles))

                def emit_S(j, pb):
                    u0 = min(max(2 * j - 4, 0), 54)
                    kt0 = u0 // 2
                    sp_ = S_ps[pb]
                    for c in range(7):
                        k0 = (kt0 + c) * 128 if c < 5 else S + (c - 5) * 128
                        kb.op("pe", lambda e, c=c, k0=k0, sp_=sp_: e.matmul(sp_[:, c * 128:(c + 1) * 128], lhsT=kh[b][:, k0:k0 + 128], rhs=qh[b][:, j * 128:(j + 1) * 128],
                                                                          start=True, stop=True), reads=[rq, rk], writes=r_Sps[pb], inc=(c == 6))

                emit_S(tl[0], it % 2)
                for ji, j in enumerate(tl):
                    pb = it % 2
                    it += 1
                    var = 0 if j == 0 else 1 if j == 1 else 3 if j == 30 else 4 if j == 31 else 2
                    u0 = min(max(2 * j - 4, 0), 54)
                    kt0 = u0 // 2
                    sp_ = S_ps[pb]
                    if ji + 1 < len(tl):
                        emit_S(tl[ji + 1], it % 2)
                    kb.op("dve", lambda e, sp_=sp_, var=var: e.tensor_tensor(out=sc[pb][:, 0:512], in0=sp_[:, 0:512], in1=tab[b][:, var, 0:512], op=ALU.add),
                          reads=r_Sps[pb] + [rt], writes=[r_sc[pb]])
                    kb.op("dve", lambda e, sp_=sp_, var=var: e.tensor_tensor(out=sc[pb][:, 512:640], in0=sp_[:, 512:640], in1=tab[b][:, var, 512:640], op=ALU.add),
                          reads=r_Sps[pb] + [rt], writes=[r_sc[pb]])
                    kb.op("act", lambda e, sp_=sp_: e.activation(out=P[pb][:, 640:896], in_=sp_[:, 640:896], func=AF.Exp), reads=r_Sps[pb], writes=[r_P[pb]])
                    kb.op("act", lambda e: e.activation(out=P[pb][:, 0:640], in_=sc[pb][:], func=AF.Exp), reads=[r_sc[pb]], writes=[r_P[pb]])
                    for c in range(7):
                        kt = kt0 + c if c < 5 else 32 + (c - 5)
                        kb.op("pe", lambda e, c=c, kt=kt: e.matmul(o_ps[pb][:, 0:65], lhsT=P[pb][:, c * 128:(c + 1) * 128], rhs=va[b][:, kt, :], start=(c == 0), stop=(c == 6)),
                              reads=[r_P[pb], rv, rv2], writes=[r_ops[pb]], inc=(c == 6))
                    kb.op("dve", lambda e: e.reciprocal(out=rc[pb][:], in_=o_ps[pb][:, 64:65]), reads=[r_ops[pb]], writes=[r_rc[pb]])
                    kb.op("dve", lambda e, j=j: e.tensor_scalar(out=ast[b][:, j, :], in0=o_ps[pb][:, 0:64], scalar1=rc[pb][:, 0:1], scalar2=None, op0=ALU.mult),
                          reads=[r_ops[pb], r_rc[pb]], writes=[r_ast[b]])
                for part in range(4):
                    kb.dma("pool", _r(self.att_tok[part * 1024:(part + 1) * 1024, h * 64:(h + 1) * 64], "(j p) d -> p j d", p=128),
                           ast[b][:, part * 8:(part + 1) * 8, :], reads=[r_ast[b]])

    def _rva2(self, b):
        if not hasattr(self, "_rva2_l"):
            self._rva2_l = [self.kb.res(), self.kb.res()]
        return self._rva2_l[b]

    def l0_out(self, x_in, dst):
        kb = self.kb
        NTL = S // 128
        with ExitStack() as es:
            Wo = self.sb(es, "o_W", [128, 12, D], BF16)
            r_wo = kb.res()
            kb.dma("pool", Wo[:], _r(self.I("w_out"), "(c p) d -> p c d", p=128), writes=[r_wo])
            Grow, r_G = self.load_rows_bcast(es, "o_G", self.vec[0, 0], D)
            cat = [self.sb(es, "o_cat%d" % i, [128, 1536], BF16) for i in range(2)]
            xt = [self.sb(es, "o_xt%d" % i, [128, D], F32) for i in range(2)]
            r_cat = [[kb.res(), kb.res()] for _ in range(2)]
            r_xt = [kb.res() for _ in range(2)]
            catT = self.sb(es, "o_catT", [128, 12, 128], BF16)
            r_catT = [kb.res() for _ in range(3)]
            tmp = self.sb(es, "o_tmp", [128, D], F32)
            r_tmp = kb.res()
            ss2 = self.sb(es, "o_ss2", [128, 4], F32)
            r_ss2 = [kb.res() for _ in range(3)]
            junk2 = self.sb(es, "o_junk2", [128, 512], BF16)
            r_junk2 = kb.res()
            psT = [self.psB[0], self.psB[1], self.psB[2]]
            r_psT = [self.r_psB[0], self.r_psB[1], self.r_psB[2]]

            def load(i):
                b = i % 2
                kb.dma("sp", cat[b][:, 0:512], self.att_tok[i * 128:(i + 1) * 128, :], writes=[r_cat[b][0]])
                kb.dma("sp", cat[b][:, 512:1536], self.ssm_tok[i * 128:(i + 1) * 128, :], writes=[r_cat[b][1]])
                kb.dma("sp", xt[b][:], x_in[i * 128:(i + 1) * 128, :], writes=[r_xt[b]])

            n = NTL if self.max_tiles is None else self.max_tiles
            load(0)
            for i in range(n):
                b = i % 2
                if i + 1 < n:
                    load(i + 1)
                for q3 in range(3):
                    for cc in range(4):
                        c = q3 * 4 + cc
                        kb.op("pe", lambda e, c=c, cc=cc, q3=q3: e.matmul(psT[q3][:, cc * 128:(cc + 1) * 128], lhsT=cat[b][:, c * 128:(c + 1) * 128], rhs=self.identb,
                                                                        start=True, stop=True), reads=r_cat[b] + [self.r_cst], writes=[r_psT[q3]], inc=(cc == 3))
                    kb.op("act", lambda e, q3=q3: e.activation(out=_r(catT[:, q3 * 4:(q3 + 1) * 4, :], "p c t -> p (c t)"), in_=psT[q3][:, :], func=AF.Identity),
                          reads=[r_psT[q3]], writes=[r_catT[q3]])
                py = self.psA[i % 2]
                rpy = self.r_psA[i % 2]
                for hb in range(2):
                    for c in range(12):
                        kb.op("pe", lambda e, c=c, hb=hb: e.matmul(py[:, hb * 512:(hb + 1) * 512], lhsT=catT[:, c, :], rhs=Wo[:, c, hb * 512:(hb + 1) * 512],
                                                                 start=(c == 0), stop=(c == 11)), reads=[r_catT[c // 4], r_wo], writes=[rpy], inc=(c == 11))
                self.resid_epilogue(py[:, :], [rpy], xt[b][:], r_xt[b], Grow, r_G, tmp, r_tmp, ss2, r_ss2, junk2, r_junk2)
                kb.dma("pool", dst[i * 128:(i + 1) * 128, :], xt[b][:], reads=[r_xt[b]])


def _consts():
    t = np.arange(128)
    c = np.zeros((128, 6, 128), np.float32)
    c[:, 0] = (t[:, None] == t[None, :])
    c[:, 1] = (t[:, None] <= t[None, :])
    c[:, 2] = (t[:, None] >= t[None, :])
    c[:, 3] = (t[:, None] > t[None, :])
    c[:, 4] = (t[:, None] < t[None, :])
    c[:, 5] = 1.0
    return c


def _pool_inv():
    out = np.zeros((2, 4, 256), np.float32)
    for a, base in enumerate((0, S - 256)):
        t = base + np.arange(256)
        for gi, w in enumerate((2, 4, 8, 16)):
            lo = np.clip(t - w // 2, 0, S)
            hi = np.clip(t - w // 2 + w, 0, S)
            out[a, gi] = np.float32(1.0) / (hi - lo).astype(np.float32)
    return out


def _rpb_table(rpb):
    padded = np.concatenate([rpb.reshape(8, -1), np.full((8, 1), NEG, np.float32)], axis=1)
    sent = 15 * 31
    variants = [(0, (0, 1)), (0, (2, 3)), (0, (4, 5)), (54, (60, 61)), (54, (62, 63))]
    k = np.arange(640)
    ki = k // 64
    kc = k % 64
    q = np.arange(128)
    qc = q % 64
    idx = np.zeros((5, 640, 128), np.int64)
    for v, (u0, rows) in enumerate(variants):
        r = np.array(rows)[q // 64]
        rs = np.clip(r - 4, 0, 56)
        cs = np.clip(qc - 8, 0, 48)
        i = u0 + ki
        valid = (i[:, None] >= rs[None, :]) & (i[:, None] < rs[None, :] + 8) & (kc[:, None] >= cs[None, :]) & (kc[:, None] < cs[None, :] + 16)
        rr = i[:, None] - r[None, :] + 7
        cc = kc[:, None] - qc[None, :] + 15
        flat = np.clip(rr, 0, 14) * 31 + np.clip(cc, 0, 30)
        idx[v] = np.where(valid, flat, sent)
    tab = padded[:, idx]
    tab = tab.reshape(8, 5, 5, 128, 128).transpose(0, 1, 3, 2, 4)
    return np.ascontiguousarray(tab, dtype=np.float32)


def make_in_maps(inp):
    f = lambda a: np.ascontiguousarray(np.asarray(a), dtype=np.float32)
    x, c, ctx, c_ctx = f(inp["x"]), f(inp["c"]), f(inp["ctx"]), f(inp["c_ctx"])
    shared = {
        "ada_w": f(inp["ada_w"]),
        "ada_bT": f(f(inp["ada_b"]).reshape(2, 48, 128).transpose(2, 0, 1)),
        "norm_gT": f(f(inp["norm_g"]).reshape(2, 4, 8, 128).transpose(3, 0, 1, 2)),
        "w_in": f(inp["w_in"])[0],
        "w_out": f(inp["w_out"])[0],
        "rpbtab": _rpb_table(f(inp["na_rpb"])[0]),
        "conv_wT": f(f(inp["ssm_conv_w"])[0].reshape(3, 16, 128).transpose(2, 1, 0)),
        "conv_bT": f(f(inp["ssm_conv_b"])[0].reshape(16, 128).transpose(1, 0)),
        "ssm_rows": f(np.stack([f(inp["ssm_a_log"])[0].reshape(32), f(inp["ssm_dt_bias"])[0].reshape(32), f(inp["ssm_d"])[0].reshape(32)])),
        "ssm_norm_g": f(inp["ssm_norm_g"])[0],
        "pool_w": f(inp["pool_w"])[0],
        "pool_b": f(inp["pool_b"])[0].reshape(D),
        "pool_scale": f(inp["pool_scale"])[0],
        "pool_inv": _pool_inv(),
        "ffn_w_up": f(inp["ffn_w_up"]),
        "fconv_wT": f(f(inp["ffn_conv_w"]).reshape(2, 3, 22, 128).transpose(3, 0, 2, 1)),
        "fconv_bT": f(f(inp["ffn_conv_b"]).reshape(2, 22, 128).transpose(2, 0, 1)),
        "ffn_w_down": f(inp["ffn_w_down"]),
        "consts": _consts(),
    }
    maps = []
    for b in range(x.shape[0]):
        cc = np.stack([c[b].reshape(8, 128).T, c_ctx.reshape(8, 128).T], axis=2)
        m = dict(shared)
        m["x"] = f(x[b])
        m["ctx"] = f(ctx[b])
        m["cc"] = f(cc)
        maps.append(m)
    return maps


_NC_CACHE = {}


def kernel(**inputs):
    if "full" not in _NC_CACHE:
        p = Prog()
        _NC_CACHE["full"] = p.build()
        _NC_CACHE["names"] = [k for k in p.dram if k in Prog.SHAPES]
    nc = _NC_CACHE["full"]
    maps = make_in_maps(inputs)
    names = _NC_CACHE["names"]
    maps = [{k: m[k] for k in names} for m in maps]
    res = run_bass_kernel_spmd(nc, maps, core_ids=list(range(NCORES)))
    return np.stack([np.asarray(r["out"], dtype=np.float32) for r in res.results], axis=0)
```
